# Optimizing a Trainium2 kernel written in Bass

```python
import jax, jax.numpy as jnp
from jax import lax
import numpy as np

D_MODEL = 1024
BATCH = 2
SEQ = 8192
DEPTH = 2
DEC_BATCH = 32
DEC_SEQ = 16
PAST_LEN = 2048

CHUNK = 64
HEAD_DIM = 64
MIX_HEADS = D_MODEL // HEAD_DIM
A_GROUPS = MIX_HEADS // 4
C_HEADS = (MIX_HEADS - A_GROUPS) // 2
B_HEADS = MIX_HEADS - A_GROUPS - C_HEADS
B_KV_HEADS = 1
A_WIDTH = A_GROUPS * HEAD_DIM
B_WIDTH = B_HEADS * HEAD_DIM
B_KV_WIDTH = B_KV_HEADS * HEAD_DIM
C_WIDTH = C_HEADS * HEAD_DIM
D_MIX = A_WIDTH + B_WIDTH + C_WIDTH
A_CHUNK = 128
IDX_HEADS = 8
IDX_DIM = 32
TOPK_MAX = 256
Q_BLOCK = 128
ROPE_BASE = 10000.0
D_FF = 2816
CONV_W = 3
LN_EPS = 1e-5
ALPHA = (2 * DEPTH) ** 0.25
BETA = (8 * DEPTH) ** -0.25
PROJ_SIZES = (A_WIDTH, A_WIDTH, B_WIDTH, B_KV_WIDTH, B_KV_WIDTH, IDX_HEADS * IDX_DIM, IDX_DIM,
              IDX_HEADS, C_WIDTH, C_WIDTH, C_WIDTH, C_WIDTH)
PROJ_SPLITS = tuple(int(s) for s in np.cumsum(PROJ_SIZES)[:-1])
D_PROJ = int(sum(PROJ_SIZES))

kernel_name = 'hybrid_streaming_encoder_step'

F32 = jnp.float32


def layer_norm(x, g, b):
    xf = x.astype(F32)
    mu = xf.mean(-1, keepdims=True)
    var = jnp.square(xf - mu).mean(-1, keepdims=True)
    return ((xf - mu) * lax.rsqrt(var + LN_EPS) * g + b).astype(x.dtype)


def head_norm(y, g):
    yf = y.astype(F32)
    mu = yf.mean(-1, keepdims=True)
    var = jnp.square(yf - mu).mean(-1, keepdims=True)
    return (yf - mu) * lax.rsqrt(var + LN_EPS) * g.reshape(C_HEADS, HEAD_DIM)


def rotary(x, pos):
    half = HEAD_DIM // 2
    freqs = ROPE_BASE ** (-jnp.arange(half, dtype=F32) / half)
    ang = pos.astype(F32)[:, None] * freqs
    cos, sin = jnp.cos(ang)[None, :, None, :], jnp.sin(ang)[None, :, None, :]
    xf = x.astype(F32)
    x1, x2 = xf[..., :half], xf[..., half:]
    return jnp.concatenate([x1 * cos - x2 * sin, x1 * sin + x2 * cos], -1).astype(x.dtype)


def mixer_a(u_raw, v_raw, ln_g, ln_b, ws, bs, chunk_len):
    u = jax.nn.gelu(u_raw)
    v = jax.nn.gelu(v_raw)
    bn, t, _ = v.shape
    n = t // chunk_len
    vn = layer_norm(v.reshape(bn, t, A_GROUPS, HEAD_DIM), ln_g.reshape(A_GROUPS, HEAD_DIM),
                    ln_b.reshape(A_GROUPS, HEAD_DIM))
    vc = vn.reshape(bn, n, chunk_len, A_GROUPS, HEAD_DIM)
    w = ws[:, :chunk_len, :chunk_len] * jnp.tril(jnp.ones((chunk_len, chunk_len), ws.dtype))
    s = jnp.einsum('gij,bnjgc->bnigc', w, vc) + bs[:, :chunk_len].T[None, None, :, :, None]
    return u * s.reshape(bn, t, A_WIDTH), vn.reshape(bn, t, A_WIDTH)


def dsa_attend(q, qi, wi, k, v, ki, limit, k_sel):
    bn, t, _ = q.shape
    l = k.shape[1]
    qi = qi.reshape(bn, t, IDX_HEADS, IDX_DIM)
    rel = jax.nn.relu(jnp.einsum('bthd,bsd->bths', qi, ki).astype(F32) * IDX_DIM ** -0.5)
    score = jnp.einsum('bths,bth->bts', rel, wi.astype(F32)) * IDX_HEADS ** -0.5
    adm = jnp.arange(l)[None, :] < limit[:, None]
    score = jnp.where(adm[None], score, -jnp.inf)
    _, idx = lax.top_k(score, k_sel)
    valid = idx < limit[None, :, None]
    gather = jax.vmap(lambda a, i: a[i])
    ks = gather(k, idx).reshape(bn, t, k_sel, B_KV_HEADS, HEAD_DIM)
    vs = gather(v, idx).reshape(bn, t, k_sel, B_KV_HEADS, HEAD_DIM)
    qh = q.reshape(bn, t, B_KV_HEADS, B_HEADS // B_KV_HEADS, HEAD_DIM)
    logits = jnp.einsum('btngd,btknd->btngk', qh, ks).astype(F32) * HEAD_DIM ** -0.5
    logits = jnp.where(valid[:, :, None, None, :], logits, -jnp.inf)
    p = jax.nn.softmax(logits, axis=-1).astype(v.dtype)
    o = jnp.einsum('btngk,btknd->btngd', p, vs)
    return o.reshape(bn, t, B_WIDTH)


def dsa_prompt(q, qi, wi, k, v, ki):
    bn, s, _ = q.shape
    nb = s // Q_BLOCK
    k_sel = min(TOPK_MAX, s // 4)

    def blocks(a):
        return a.reshape(bn, nb, Q_BLOCK, a.shape[-1]).swapaxes(0, 1)

    pos = jnp.arange(s).reshape(nb, Q_BLOCK)
    limit = (pos // CHUNK + 1) * CHUNK

    def one(args):
        qb, qib, wib, lim = args
        return dsa_attend(qb, qib, wib, k, v, ki, lim, k_sel)

    o = lax.map(one, (blocks(q), blocks(qi), blocks(wi), limit))
    return o.swapaxes(0, 1).reshape(bn, s, B_WIDTH)


def retention_terms(q, k, v, log_g):
    c = q.shape[2]
    i = jnp.arange(c, dtype=F32)
    diff = i[:, None] - i[None, :]
    decay = jnp.where(diff >= 0, jnp.exp(jnp.maximum(diff, 0.0)[None] * log_g[:, None, None]), 0.0)
    s = jnp.einsum('bnihd,bnjhd->bnhij', q, k) * decay
    intra = jnp.einsum('bnhij,bnjhe->bnihe', s, v)
    kv = jnp.einsum('bnjhd,hj,bnjhe->bnhde', k, jnp.exp((c - 1 - i)[None] * log_g[:, None]), v)
    q_decay = jnp.exp((i + 1)[None] * log_g[:, None])
    chunk_decay = jnp.exp(c * log_g)
    return intra, kv, q_decay, chunk_decay


def mixer_c(qr, kr, vr, gr, gn_g, pos, r0, chunk_len):
    bn, t, _ = qr.shape
    n = t // chunk_len
    log_g = jnp.log(1.0 - 2.0 ** (-5.0 - jnp.arange(C_HEADS, dtype=F32)))
    q = rotary(qr.reshape(bn, t, C_HEADS, HEAD_DIM), pos)
    k = rotary(kr.reshape(bn, t, C_HEADS, HEAD_DIM), pos) * HEAD_DIM ** -0.5
    sh = (bn, n, chunk_len, C_HEADS, HEAD_DIM)
    q = q.reshape(sh)
    intra, kv, q_decay, chunk_decay = retention_terms(q, k.reshape(sh), vr.reshape(sh), log_g)

    def step(r, kv_n):
        return chunk_decay[None, :, None, None] * r + kv_n, r

    r_final, r_prev = lax.scan(step, r0.astype(F32), kv.swapaxes(0, 1).astype(F32))
    cross = jnp.einsum('bnihd,nbhde,hi->bnihe', q, r_prev, q_decay)
    y = head_norm((intra + cross).reshape(bn, t, C_HEADS, HEAD_DIM), gn_g)
    return jax.nn.silu(gr) * y.reshape(bn, t, C_WIDTH), r_final


def conv_ffn(x, conv_prev, w_gate, w_up, conv_w, conv_b, w_down):
    t = x.shape[1]
    hg = x @ w_gate
    hu = x @ w_up
    ext = jnp.concatenate([conv_prev.astype(hg.dtype), hg], axis=1)
    conv = conv_b + sum(conv_w[j] * ext[:, j:j + t] for j in range(CONV_W))
    return (jax.nn.gelu(conv) * hu) @ w_down, ext[:, t:]


def finish(x, mix, w_out, ln1_g, ln1_b, conv_prev, w_gate, w_up, conv_w, conv_b, w_down, ln2_g, ln2_b):
    x = layer_norm(ALPHA * x + (mix @ w_out).astype(x.dtype), ln1_g, ln1_b)
    f, conv_state = conv_ffn(x, conv_prev, w_gate, w_up, conv_w, conv_b, w_down)
    x = layer_norm(ALPHA * x + f.astype(x.dtype), ln2_g, ln2_b)
    return x, conv_state


def setup_inputs(seed: int = 0) -> dict:
    key = jax.random.key(seed)
    ks = jax.random.split(key, 24)

    def nrm(k, shape, scale):
        return jax.random.normal(k, shape, F32) * scale

    return {
        'x_prompt': nrm(ks[0], (BATCH, SEQ, D_MODEL), 1.0),
        'x_sample': nrm(ks[1], (DEC_BATCH, DEC_SEQ, D_MODEL), 1.0),
        'cache_b_k': nrm(ks[2], (DEPTH, DEC_BATCH, PAST_LEN, B_KV_WIDTH), 1.0),
        'cache_b_v': nrm(ks[3], (DEPTH, DEC_BATCH, PAST_LEN, B_KV_WIDTH), 1.0),
        'cache_b_kidx': nrm(ks[4], (DEPTH, DEC_BATCH, PAST_LEN, IDX_DIM), 1.0),
        'state_ret': nrm(ks[5], (DEPTH, DEC_BATCH, C_HEADS, HEAD_DIM, HEAD_DIM), 0.5),
        'state_ffn_conv': nrm(ks[6], (DEPTH, DEC_BATCH, CONV_W - 1, D_FF), 1.0),
        'w_in': nrm(ks[7], (DEPTH, D_MODEL, D_PROJ), D_MODEL ** -0.5),
        'a_ln_g': 1.0 + nrm(ks[8], (DEPTH, A_WIDTH), 0.01),
        'a_ln_b': nrm(ks[9], (DEPTH, A_WIDTH), 0.01),
        'a_ws': nrm(ks[10], (DEPTH, A_GROUPS, A_CHUNK, A_CHUNK), A_CHUNK ** -0.5),
        'a_bs': 1.0 + nrm(ks[11], (DEPTH, A_GROUPS, A_CHUNK), 0.01),
        'c_gn_g': 1.0 + nrm(ks[12], (DEPTH, C_WIDTH), 0.01),
        'w_out': nrm(ks[13], (DEPTH, D_MIX, D_MODEL), BETA * D_MIX ** -0.5),
        'ln1_g': 1.0 + nrm(ks[14], (DEPTH, D_MODEL), 0.01),
        'ln1_b': nrm(ks[15], (DEPTH, D_MODEL), 0.01),
        'w_gate': nrm(ks[16], (DEPTH, D_MODEL, D_FF), D_MODEL ** -0.5),
        'w_up': nrm(ks[17], (DEPTH, D_MODEL, D_FF), D_MODEL ** -0.5),
        'conv_w': nrm(ks[18], (DEPTH, CONV_W, D_FF), CONV_W ** -0.5),
        'conv_b': nrm(ks[19], (DEPTH, D_FF), 0.01),
        'w_down': nrm(ks[20], (DEPTH, D_FF, D_MODEL), BETA * D_FF ** -0.5),
        'ln2_g': 1.0 + nrm(ks[21], (DEPTH, D_MODEL), 0.01),
        'ln2_b': nrm(ks[22], (DEPTH, D_MODEL), 0.01),
    }


def reference(x_prompt, x_sample, cache_b_k, cache_b_v, cache_b_kidx, state_ret, state_ffn_conv,
              w_in, a_ln_g, a_ln_b, a_ws, a_bs, c_gn_g, w_out, ln1_g, ln1_b,
              w_gate, w_up, conv_w, conv_b, w_down, ln2_g, ln2_b):
    bp, s, _ = x_prompt.shape
    bs_, t, _ = x_sample.shape
    past = cache_b_k.shape[2]
    l_keys = past + t
    k_sel_s = min(TOPK_MAX, l_keys // 4)
    pos_p = jnp.arange(s)
    pos_s = past + jnp.arange(t)
    xp, xs = x_prompt, x_sample
    kp, vp, kip, rp, cp = [], [], [], [], []
    ks_, vs_, kis, rs, cs, avs = [], [], [], [], [], []
    for l in range(DEPTH):
        ua, va, qb, kb, vb, qib, kib, wib, qc, kc, vc, gc = jnp.split(xp @ w_in[l], PROJ_SPLITS, axis=-1)
        oa, _ = mixer_a(ua, va, a_ln_g[l], a_ln_b[l], a_ws[l], a_bs[l], A_CHUNK)
        ob = dsa_prompt(qb, qib, wib, kb, vb, kib)
        oc, r_p = mixer_c(qc, kc, vc, gc, c_gn_g[l], pos_p,
                          jnp.zeros((bp, C_HEADS, HEAD_DIM, HEAD_DIM), F32), CHUNK)
        mix = jnp.concatenate([oa, ob.astype(oa.dtype), oc.astype(oa.dtype)], axis=-1)
        xp, c_p = finish(xp, mix, w_out[l], ln1_g[l], ln1_b[l],
                         jnp.zeros((bp, CONV_W - 1, D_FF), xp.dtype),
                         w_gate[l], w_up[l], conv_w[l], conv_b[l], w_down[l], ln2_g[l], ln2_b[l])
        kp.append(kb); vp.append(vb); kip.append(kib); rp.append(r_p); cp.append(c_p)
        ua, va, qb, kb, vb, qib, kib, wib, qc, kc, vc, gc = jnp.split(xs @ w_in[l], PROJ_SPLITS, axis=-1)
        oa, av = mixer_a(ua, va, a_ln_g[l], a_ln_b[l], a_ws[l], a_bs[l], t)
        k_full = jnp.concatenate([cache_b_k[l].astype(kb.dtype), kb], axis=1)
        v_full = jnp.concatenate([cache_b_v[l].astype(vb.dtype), vb], axis=1)
        ki_full = jnp.concatenate([cache_b_kidx[l].astype(kib.dtype), kib], axis=1)
        ob = dsa_attend(qb, qib, wib, k_full, v_full, ki_full,
                        jnp.full((t,), l_keys, jnp.int32), k_sel_s)
        oc, r_s = mixer_c(qc, kc, vc, gc, c_gn_g[l], pos_s, state_ret[l], t)
        mix = jnp.concatenate([oa, ob.astype(oa.dtype), oc.astype(oa.dtype)], axis=-1)
        xs, c_s = finish(xs, mix, w_out[l], ln1_g[l], ln1_b[l], state_ffn_conv[l],
                         w_gate[l], w_up[l], conv_w[l], conv_b[l], w_down[l], ln2_g[l], ln2_b[l])
        ks_.append(kb); vs_.append(vb); kis.append(kib); rs.append(r_s); cs.append(c_s); avs.append(av)
    return (xp, xs, jnp.stack(kp), jnp.stack(vp), jnp.stack(kip), jnp.stack(rp), jnp.stack(cp),
            jnp.stack(ks_), jnp.stack(vs_), jnp.stack(kis), jnp.stack(rs), jnp.stack(cs), jnp.stack(avs))
```

```python
import numpy as np
import os
EXP = os.environ.get('EXP', '')
from contextlib import ExitStack
import concourse.bass as bass
import concourse.mybir as mybir
from concourse.bass_utils import run_bass_kernel_spmd

F32 = mybir.dt.float32
BF16 = mybir.dt.bfloat16
AF = mybir.ActivationFunctionType
ALU = mybir.AluOpType
AX = mybir.AxisListType

D = 1024
NT = 16
TS = 128
NTOK = NT * TS
NS = 64
DFF = 2816
NFC = DFF // 128
DEPTH = 2
ALPHA = (2 * DEPTH) ** 0.25
EPS = 1e-5
GROUPS = [[0, 1, 2, 3], [4, 5, 6, 7]]
NEG = -1.0e30


class Prog:
    NSLOT = 8

    def __init__(self, nc):
        self.nc = nc
        self.ops = []
        self.buf = {}
        self.slot_last = {}
        self.slot_rr = {"sp": 0, "pool": 0, "act": 0}
        self.last = {}
        self.dma_pending = []

    def _deps(self, eng, r, w):
        deps = set()
        for k in r:
            b = self.buf.setdefault(k, [None, []])
            if b[0] is not None:
                deps.add(b[0])
            if isinstance(k, str) and k.startswith("ps"):
                deps.update(d for d in b[1] if self.ops[d]["eng"] != eng)
        for k in w:
            b = self.buf.setdefault(k, [None, []])
            if b[0] is not None:
                deps.add(b[0])
            deps.update(b[1])
        if eng == "pe":
            deps = {d for d in deps if not (self.ops[d]["eng"] == "pe" and self.ops[d]["kind"] == "op")}
        return deps

    def _record(self, oid, r, w):
        me = self.ops[oid]
        for k in r:
            rl = self.buf[k][1]
            if me["kind"] == "op":
                rl[:] = [d for d in rl if not (self.ops[d]["kind"] == "op" and self.ops[d]["eng"] == me["eng"])]
            rl.append(oid)
        for k in w:
            self.buf[k] = [oid, []]

    def op(self, eng, fn, r=(), w=()):
        deps = self._deps(eng, r, w)
        oid = len(self.ops)
        self.ops.append(dict(eng=eng, kind="op", fn=fn, deps=deps))
        self._record(oid, r, w)
        self.last[eng] = oid
        return oid

    def barrier(self):
        ids = set(self.last.values()) | set(self.dma_pending)
        for e in ["pe", "act", "dve", "pool", "sp"]:
            self.ops.append(dict(eng=e, kind="bar", deps=set(ids)))
        self.buf = {}
        self.dma_pending = []

    def dma(self, q, out, in_, r=(), w=()):
        deps = self._deps(q, r, w)
        slot = self.slot_rr[q] % self.NSLOT
        self.slot_rr[q] += 1
        prev = self.slot_last.get((q, slot))
        if prev is not None:
            deps.add(prev)
        oid = len(self.ops)
        self.ops.append(dict(eng=q, kind="dma", out=out, in_=in_, deps=deps, slot=slot))
        self.slot_last[(q, slot)] = oid
        self._record(oid, r, w)
        self.dma_pending.append(oid)
        return oid

    def cc(self, ins, outs, r=(), w=()):
        deps = self._deps("pool", r, w)
        oid = len(self.ops)
        self.ops.append(dict(eng="pool", kind="cc", ins=ins, outs=outs, deps=deps))
        self._record(oid, r, w)
        self.dma_pending.append(oid)
        return oid

    def emit(self, stack):
        nc = self.nc
        ops = self.ops
        needed = set()
        for o in ops:
            needed.update(o["deps"])
        engs = ["pe", "act", "dve", "pool", "sp"]
        sem_e = {e: stack.enter_context(nc.semaphore("pg_" + e)) for e in engs}
        sem_d = {(q, s): stack.enter_context(nc.semaphore(f"dq_{q}{s}"))
                 for q in ("sp", "pool", "act") for s in range(self.NSLOT)}
        sem_cc = stack.enter_context(nc.semaphore("ccsem"))
        cnt = {e: 0 for e in engs}
        dcnt = {k: 0 for k in sem_d}
        ccn = 0
        ev = {}
        for i, o in enumerate(ops):
            if o["kind"] == "op":
                if i in needed:
                    cnt[o["eng"]] += 1
                    ev[i] = (sem_e[o["eng"]], cnt[o["eng"]], ("e", o["eng"]))
            elif o["kind"] == "dma":
                k = (o["eng"], o["slot"])
                dcnt[k] += 16
                ev[i] = (sem_d[k], dcnt[k], ("d",) + k)
            elif o["kind"] == "bar":
                pass
            else:
                ccn += 1
                ev[i] = (sem_cc, ccn, ("c",))
        per = {e: [] for e in engs}
        known = {e: {} for e in engs}
        for i, o in enumerate(ops):
            e = o["eng"]
            best = {}
            for d in o["deps"]:
                s, v, key = ev[d]
                if known[e].get(key, 0) >= v:
                    continue
                if key not in best or best[key][1] < v:
                    best[key] = (s, v)
            for key, (s, v) in best.items():
                per[e].append(("wait", s, v))
                known[e][key] = v
            per[e].append(("ins", i))
        for (q, s), c in dcnt.items():
            if c:
                per[q].append(("wait", sem_d[(q, s)], c))
        if ccn:
            per["pool"].append(("wait", sem_cc, ccn))
        if os.environ.get("DUMP"):
            for e in engs:
                print("ENGINE", e)
                for it in per[e]:
                    if it[0] == "wait":
                        print("   wait", it[1], it[2])
                    else:
                        o = ops[it[1]]
                        print("   ", it[1], o["kind"], o.get("tag", ""), "sig" if it[1] in ev else "", ev.get(it[1], ("", ""))[1])
        self.stats = {e: sum(1 for x in per[e] if x[0] == "ins") for e in engs}
        self.stats["sem"] = dict(cnt)

        if os.environ.get("CHECK"):
            semv = {}
            pos = {e: 0 for e in engs}
            prog = True
            while prog:
                prog = False
                for e in engs:
                    while pos[e] < len(per[e]):
                        it = per[e][pos[e]]
                        if it[0] == "wait":
                            if semv.get(it[1].num, 0) >= it[2]:
                                pos[e] += 1
                                prog = True
                            else:
                                break
                        else:
                            i = it[1]
                            if i in ev:
                                o = ops[i]
                                inc = 16 if o["kind"] == "dma" else 1
                                semv[ev[i][0].num] = semv.get(ev[i][0].num, 0) + inc
                            pos[e] += 1
                            prog = True
            for e in engs:
                if pos[e] < len(per[e]):
                    it = per[e][pos[e]]
                    print("DEADLOCK", e, "at", pos[e], "/", len(per[e]), it[0], it[1] if it[0] == "wait" else "", it[2] if it[0] == "wait" else "",
                          "have", semv.get(it[1].num, 0) if it[0] == "wait" else "")
            print("CHECK done", {e: (pos[e], len(per[e])) for e in engs})

        def run(E, lst):
            for it in lst:
                if it[0] == "wait":
                    E.wait_ge(it[1], it[2])
                else:
                    i = it[1]
                    o = ops[i]
                    if o["kind"] == "op":
                        ins = o["fn"](E)
                        if i in ev:
                            ins.then_inc(ev[i][0], 1)
                    elif o["kind"] == "dma":
                        E.dma_start(out=o["out"], in_=o["in_"]).then_inc(ev[i][0], 16)
                    elif o["kind"] == "bar":
                        pass
                    else:
                        E.collective_compute("AllGather", ALU.bypass, replica_groups=GROUPS,
                                             ins=[a.opt() for a in o["ins"]], outs=[a.opt() for a in o["outs"]]).then_inc(ev[i][0])

        block = stack.enter_context(nc.Block())

        @block.sync
        def _(E):
            run(E, per["sp"])

        @block.scalar
        def _(E):
            run(E, per["act"])

        @block.vector
        def _(E):
            run(E, per["dve"])

        @block.gpsimd
        def _(E):
            run(E, per["pool"])

        @block.tensor
        def _(E):
            run(E, per["pe"])


class Seq:
    def __init__(self, name, n, ntile, off):
        self.name, self.n, self.ntile, self.off = name, n, ntile, off
        self.G = 4
        self.ng = ntile // 4
        self.N = 4 * n


PR = Seq("p", 128, NT, 0)
SM = Seq("s", 16, 4, NTOK)
NBIS = 16
CH = dict(ua=0, va=2, qc=4, qcs=7, kc=10, kcs=13, gc=16, vc=19, qb=22, qib=25, wo=27)


class Builder:
    def __init__(self, stage=99):
        self.stage = stage
        self.nc = bass.Bass("TRN2", target_bir_lowering=False)
        self.P = Prog(self.nc)
        self.stack = ExitStack()
        self.rr = 0
        self.wi = 0

    def din(self, name, shape, dt=F32):
        return self.nc.dram_tensor(name, list(shape), dt, kind="ExternalInput").ap()

    def dout(self, name, shape, dt=F32):
        return self.nc.dram_tensor(name, list(shape), dt, kind="ExternalOutput").ap()

    def dscr(self, name, shape, dt=F32):
        return self.nc.dram_tensor(name, list(shape), dt).ap()

    def sb(self, name, shape, dt=F32, st=None):
        self.uid = getattr(self, "uid", 0) + 1
        return (st or self.stack).enter_context(self.nc.sbuf_tensor(f"s{self.uid}_{name}", list(shape), dt))

    def ps(self, name):
        return self.stack.enter_context(self.nc.psum_tensor(name, [128, 512], F32))

    def mm(self, out, lhsT, rhs, start, stop, r, w):
        self.P.op("pe", lambda E: E.matmul(out, lhsT, rhs, start=start, stop=stop, skip_group_check=True), r=r, w=w)

    def tr(self, out, in_, ident, r, w):
        self.P.op("pe", lambda E: E.transpose(out, in_, ident), r=r, w=w)

    def copy(self, eng, out, in_, r, w):
        if eng == "act":
            self.P.op("act", lambda E: E.activation(out=out, in_=in_, func=AF.Copy), r=r, w=w)
        else:
            self.P.op(eng, lambda E: E.tensor_copy(out=out, in_=in_), r=r, w=w)

    def act(self, out, in_, func, r, w, bias=0.0, scale=1.0):
        self.P.op("act", lambda E: E.activation(out=out, in_=in_, func=func, bias=bias, scale=scale), r=r, w=w)

    def tt(self, eng, out, in0, in1, op, r, w):
        self.P.op(eng, lambda E: E.tensor_tensor(out=out, in0=in0, in1=in1, op=op), r=r, w=w)

    def ts(self, eng, out, in0, s1, s2, op0, op1=None, r=(), w=(), accum_out=None):
        kw = {}
        if accum_out is not None:
            kw["accum_out"] = accum_out
        if op1 is None:
            self.P.op(eng, lambda E: E.tensor_scalar(out=out, in0=in0, scalar1=s1, scalar2=None, op0=op0, **kw), r=r, w=w)
        else:
            self.P.op(eng, lambda E: E.tensor_scalar(out=out, in0=in0, scalar1=s1, scalar2=s2, op0=op0, op1=op1, **kw), r=r, w=w)

    def stt(self, eng, out, in0, scalar, in1, op0, op1, r, w):
        self.P.op(eng, lambda E: E.scalar_tensor_tensor(out=out, in0=in0, scalar=scalar, in1=in1, op0=op0, op1=op1), r=r, w=w)

    def red(self, out, in_, op, r, w):
        self.P.op("dve", lambda E: E.tensor_reduce(out=out, in_=in_, axis=AX.X, op=op), r=r, w=w)

    def memset(self, eng, ap, val, w):
        self.P.op(eng, lambda E: E.memset(ap, val), r=(), w=w)

    def alt(self):
        self.rr += 1
        return "act" if self.rr % 2 else "dve"

    def wchunk(self, src):
        i = self.wi % len(self.wring)
        self.wi += 1
        t = self.wring[i]
        self.P.dma("pool", t[:], src, w=[f"wr{i}"])
        return t, f"wr{i}"

    def proj(self, pst, psk, wt, wk, c0, M, xb, xk, N, po=0):
        for kc in range(8):
            self.mm(pst[po:po + M, 0:N], wt[:, kc, c0:c0 + M], xb[:, kc, 0:N], kc == 0, kc == 7, r=[wk, xk], w=[psk])

    def stats(self, xs, keys, N, ones, tag):
        pm, pe2 = self.pss[5], self.pss[6]
        nx = len(xs)
        for i, (x, k) in enumerate(zip(xs, keys)):
            sq = self.sqb[i % 2]
            self.act(sq[:, 0:N], x, AF.Square, r=[k], w=[f"sqb{i % 2}"])
            self.mm(pm[:, 0:N], ones[:], x, i == 0, i == nx - 1, r=[k, "consts"], w=["ps5"])
            self.mm(pe2[:, 0:N], ones[:], sq[:, 0:N], i == 0, i == nx - 1, r=[f"sqb{i % 2}", "consts"], w=["ps6"])
        mean, rstd, tmp = self.st_mean, self.st_rstd, self.st_tmp
        self.act(tmp[:, 0:N], pm[:, 0:N], AF.Square, r=["ps5"], w=["st_tmp"])
        self.copy("act", mean[:, 0:N], pm[:, 0:N], r=["ps5"], w=["st_mean"])
        self.tt("dve", tmp[:, 0:N], pe2[:, 0:N], tmp[:, 0:N], ALU.subtract, r=["ps6", "st_tmp"], w=["st_tmp"])
        self.ts("dve", tmp[:, 0:N], tmp[:, 0:N], 0.0, EPS, ALU.max, ALU.add, r=["st_tmp"], w=["st_tmp"])
        self.act(tmp[:, 0:N], tmp[:, 0:N], AF.Sqrt, r=["st_tmp"], w=["st_tmp"])
        self.P.op("dve", lambda E: E.reciprocal(out=rstd[:, 0:N], in_=tmp[:, 0:N]), r=["st_tmp"], w=["st_rstd"])
        return mean, rstd

    def build(self):
        nc, P = self.nc, self.P
        st = self.stage
        S = self.stack
        xp = self.din("xp", [NTOK, D])
        xs_in = self.din("xs", [NS, D])
        identf = self.din("identf", [128, 128])
        cosd = self.din("cosT", [128, NTOK + NS])
        sind = self.din("sinT", [128, NTOK + NS])
        kdecd = self.din("kdec", [2, 128, 384])
        dtd = self.din("dtT", [128, 6, 128])
        qdecPd = self.din("qdecP", [128, 3, 512])
        qdecSd = self.din("qdecS", [128, 3, 64])
        seld = self.din("sel", [128, 768])
        sel16d = self.din("sel16", [16, 96])
        cmd = self.din("cm", [128, 8, 128])
        e16d = self.din("e16", [128, 16])
        bd64d = self.din("bd64", [128, 128])
        onesd = self.din("onesd", [128, 128])
        umd = self.din("umask", [128, 128])
        dmaskd = self.din("dmask", [128, 512])
        rcoefd = self.din("rcoef", [128, 9, 192])
        sel5d = self.din("sel5", [128, 5])
        cdecSd = self.din("cdecS", [128, 192])
        wA = [self.din(f"wA{l}", [128, 8, 1416]) for l in range(DEPTH)]
        wB = [self.din(f"wB{l}", [35, 128, 8, 128]) for l in range(DEPTH)]
        wCg = [self.din(f"wCg{l}", [NFC, 128, 8, 128]) for l in range(DEPTH)]
        wCu = [self.din(f"wCu{l}", [NFC, 128, 8, 128]) for l in range(DEPTH)]
        wCd = [self.din(f"wCd{l}", [8, 128, NFC, 128]) for l in range(DEPTH)]
        vecd = [self.din(f"vec{l}", [128, 127]) for l in range(DEPTH)]
        bsPd = [self.din(f"bsP{l}", [128, 2, 512]) for l in range(DEPTH)]
        bsSd = [self.din(f"bsS{l}", [128, 2, 64]) for l in range(DEPTH)]
        wmTd = [self.din(f"wmT{l}", [128, 4, 128]) for l in range(DEPTH)]
        ckd = [self.din(f"ck{l}", [4, 2048, 64]) for l in range(DEPTH)]
        cvd = [self.din(f"cv{l}", [4, 2048, 64]) for l in range(DEPTH)]
        ckid = [self.din(f"cki{l}", [4, 2048, 32]) for l in range(DEPTH)]
        stSd = [self.din(f"stS{l}", [128, 4, 192]) for l in range(DEPTH)]
        cvpd = [self.din(f"cvp{l}", [128, NFC, 4, 2]) for l in range(DEPTH)]
        o_y = self.dout("o_y", [NTOK, D])
        o_ys = self.dout("o_ys", [NS, D])
        o_k = self.dout("o_k", [DEPTH, NTOK, 64])
        o_v = self.dout("o_v", [DEPTH, NTOK, 64])
        o_ki = self.dout("o_ki", [DEPTH, NTOK, 32])
        o_ret = self.dout("o_ret", [DEPTH, 128, 192])
        o_conv = self.dout("o_conv", [DEPTH, 128, NFC, 2])
        o_ks = self.dout("o_ks", [DEPTH, NS, 64])
        o_vs = self.dout("o_vs", [DEPTH, NS, 64])
        o_kis = self.dout("o_kis", [DEPTH, NS, 32])
        o_rets = self.dout("o_rets", [DEPTH, 128, 4, 192])
        o_convs = self.dout("o_convs", [DEPTH, 128, NFC, 4, 2])
        o_avs = self.dout("o_avs", [DEPTH, NS, 256])
        XT = self.dscr("XT", [128, 8, NTOK + NS])
        X1T = self.dscr("X1T", [128, 8, NTOK + NS])
        bncA32 = self.dscr("bncA", [NT, 10240])
        gatA32 = self.dscr("gatA", [4 * NT, 10240])
        bncA = bncA32.bitcast(BF16)
        gatA = gatA32.bitcast(BF16)
        bncK2 = [self.dscr(f"bncK{i}", [8 * 128, 192]) for i in range(2)]
        gatK2 = [self.dscr(f"gatK{i}", [4 * 8 * 128, 192]) for i in range(2)]
        bncB32 = self.dscr("bncB", [NT, 1024])
        gatB32 = self.dscr("gatB", [4 * NT, 1024])
        bncB = bncB32.bitcast(BF16)
        gatB = gatB32.bitcast(BF16)

        identF = self.sb("identF", [128, 128])
        identB = self.sb("identB", [128, 128], BF16)
        bd64 = self.sb("bd64", [128, 128])
        ones = self.sb("ones", [128, 128])
        P.dma("sp", identF[:], identf, w=["consts"])
        P.dma("sp", bd64[:], bd64d, w=["consts"])
        P.dma("sp", ones[:], onesd, w=["consts"])
        P.dma("pool", identB[:], identf, w=["consts"])
        kdecT = self.sb("kdecT", [128, 2, 384])
        P.dma("sp", kdecT[:], kdecd.rearrange("a p f -> p a f"), w=["consts"])
        wtok = {"p": self.sb("wtokp", [128, NT, 8]), "s": self.sb("wtoks", [128, 4, 8])}
        vecs = self.sb("vecs", [128, 127])
        self.pss = pss = [self.ps(f"ps{i}") for i in range(8)]
        self.sqb = [self.sb(f"sqb{i}", [128, 512]) for i in range(2)]
        self.st_mean = self.sb("st_mean", [128, 512])
        self.st_rstd = self.sb("st_rstd", [128, 512])
        self.st_tmp = self.sb("st_tmp", [128, 512])
        ktS = self.sb("ktS", [128, 4, 16], BF16)
        vS = self.sb("vS", [16, 4, 64], BF16)
        convo = self.sb("convo", [128, NFC, 2])
        convos = self.sb("convos", [128, NFC, 4, 2])
        V_AG, V_AB, V_GN, V_L1G, V_L1B, V_L2G, V_L2B, V_CW, V_CB = 0, 2, 4, 7, 15, 23, 31, 39, 105

        with ExitStack() as ph:
            xin = [self.sb(f"xin{i}", [128, D], st=ph) for i in range(2)]
            xtg = [self.sb(f"xtg{i}", [128, 8, 128], st=ph) for i in range(2)]
            for sq_, src in ((PR, xp), (SM, xs_in)):
                n = sq_.n
                for m in range(sq_.ntile):
                    b = m % 2
                    P.dma("sp", xin[b][0:n, :], src[m * n:(m + 1) * n, :], w=[f"xin{b}"])
                    for c in range(8):
                        pb = pss[c % 4]
                        self.tr(pb[:, 0:n], xin[b][0:n, c * 128:(c + 1) * 128], identF[0:n, 0:n], r=[f"xin{b}", "consts"], w=[f"ps{c % 4}"])
                        self.copy(self.alt(), xtg[b][:, c, 0:n], pb[:, 0:n], r=[f"ps{c % 4}"], w=[f"xtg{b}"])
                    P.dma("sp", XT[:, :, sq_.off + m * n:sq_.off + (m + 1) * n], xtg[b][:, :, 0:n], r=[f"xtg{b}"], w=["XT"])
            P.barrier()

        for l in range(DEPTH if st >= 9 else 1):
            P.dma("sp", vecs[:], vecd[l], w=["vecs"])
            with ExitStack() as ph:
                WA = self.sb("WA", [128, 8, 1416], BF16, st=ph)
                for kc in range(8):
                    P.dma("pool", WA[:, kc, :], wA[l][:, kc, :], w=["WA"])
                xTf = [self.sb(f"xTf{i}", [128, 8, 128], st=ph) for i in range(2)]
                xTb = [self.sb(f"xTb{i}", [128, 8, 128], BF16, st=ph) for i in range(2)]
                vtok = self.sb("vtok", [128, 384], BF16, st=ph)
                vb16 = self.sb("vb16", [128, 64], BF16, st=ph)
                kvo = [self.sb(f"kvo{i}", [128, 168], st=ph) for i in range(2)]
                ktb = self.sb("ktb", [128, 128], BF16, st=ph)
                rt1 = self.sb("rt1", [128, 3, 128], st=ph)
                krot = self.sb("krot", [128, 3, 128], st=ph)
                kd = self.sb("kd", [128, 384], BF16, st=ph)
                kvsb = [self.sb(f"kvsb{i}", [128, 192], st=ph) for i in range(2)]
                csa = [self.sb(f"csa{i}", [128, 2, 128], st=ph) for i in range(2)]
                stS = self.sb("stS", [128, 4, 192], st=ph)
                cdecS = self.sb("cdecS", [128, 192], st=ph)
                P.dma("sp", stS[:], stSd[l], w=["stS"])
                P.dma("sp", cdecS[:], cdecSd, w=["cdecS"])
                it = 0
                for sq_ in (PR, SM):
                    n = sq_.n
                    ok, ov, oki = (o_k, o_v, o_ki) if sq_ is PR else (o_ks, o_vs, o_kis)
                    kdi = 0 if sq_ is PR else 1
                    for m in range(sq_.ntile):
                        b = it % 2
                        it += 1
                        cols = slice(sq_.off + m * n, sq_.off + (m + 1) * n)
                        rows = slice(m * n, (m + 1) * n)
                        P.dma("sp", xTf[b][:, :, 0:n], XT[:, :, cols], r=["XT"], w=[f"xTf{b}"])
                        self.copy("act", xTb[b][:, :, 0:n], xTf[b][:, :, 0:n], r=[f"xTf{b}"], w=[f"xTb{b}"])
                        xb = xTb[b]
                        rx = [f"xTb{b}", "WA"]
                        for kc in range(8):
                            self.mm(pss[0][0:n, 0:512], xb[:, kc, 0:n], WA[:, kc, 0:512], kc == 0, kc == 7, r=rx, w=["ps0"])
                        for kc in range(8):
                            self.mm(pss[1][0:n, 0:40], xb[:, kc, 0:n], WA[:, kc, 512:552], kc == 0, kc == 7, r=rx, w=["ps1"])
                        self.copy("act", vtok[0:n, :], pss[0][0:n, 0:384], r=["ps0"], w=["vtok"])
                        self.copy("act", vb16[0:n, :], pss[0][0:n, 448:512], r=["ps0"], w=["vb16"])
                        self.copy("dve", kvo[b][0:n, 0:128], pss[0][0:n, 384:512], r=["ps0"], w=[f"kvo{b}"])
                        self.copy("dve", kvo[b][0:n, 128:168], pss[1][0:n, 0:40], r=["ps1"], w=[f"kvo{b}"])
                        self.copy("dve", wtok[sq_.name][0:n, m, :], kvo[b][0:n, 160:168], r=[f"kvo{b}"], w=["wtok"])
                        P.dma("sp", ok[l, rows, :], kvo[b][0:n, 0:64], r=[f"kvo{b}"])
                        P.dma("sp", ov[l, rows, :], kvo[b][0:n, 64:128], r=[f"kvo{b}"])
                        P.dma("sp", oki[l, rows, :], kvo[b][0:n, 128:160], r=[f"kvo{b}"])
                        for kc in range(8):
                            self.mm(pss[2][0:64, 0:n], WA[:, kc, 552:616], xb[:, kc, 0:n], kc == 0, kc == 7, r=rx, w=["ps2"])
                        for kc in range(8):
                            self.mm(pss[2][64:96, 0:n], WA[:, kc, 616:648], xb[:, kc, 0:n], kc == 0, kc == 7, r=rx, w=["ps2"])
                        if sq_ is PR:
                            self.copy("act", ktb[0:96, 0:n], pss[2][0:96, 0:n], r=["ps2"], w=["ktb"])
                            P.dma("sp", bncA[m, 0:12288].rearrange("(d t) -> d t", t=128), ktb[0:96, :], r=["ktb"], w=["bncA"])
                            P.dma("sp", bncA[m, 12288:20480].rearrange("(s e) -> s e", e=64), vb16[:], r=["vb16"], w=["bncA"])
                        else:
                            self.copy("act", ktS[0:96, m, :], pss[2][0:96, 0:n], r=["ps2"], w=["ktS"])
                            self.copy("dve", vS[0:n, m, :], vb16[0:n, :], r=["vb16"], w=["vS"])
                        for p in range(3):
                            for kc in range(8):
                                self.mm(pss[3][:, p * 128:p * 128 + n], WA[:, kc, 648 + p * 128:648 + (p + 1) * 128], xb[:, kc, 0:n],
                                        kc == 0, kc == 7, r=rx, w=["ps3"])
                        for p in range(3):
                            for kc in range(8):
                                self.mm(pss[4][:, p * 128:p * 128 + n], WA[:, kc, 1032 + p * 128:1032 + (p + 1) * 128], xb[:, kc, 0:n],
                                        kc == 0, kc == 7, r=rx, w=["ps4"])
                        P.dma("sp", csa[b][:, 0, 0:n], cosd[:, cols], w=[f"csa{b}"])
                        P.dma("sp", csa[b][:, 1, 0:n], sind[:, cols], w=[f"csa{b}"])
                        cb = csa[b][:, 0, 0:n].unsqueeze(1).to_broadcast([128, 3, n])
                        sbb = csa[b][:, 1, 0:n].unsqueeze(1).to_broadcast([128, 3, n])
                        p3 = pss[3][:, 0:384].rearrange("p (c t) -> p c t", c=3)[:, :, 0:n]
                        p4 = pss[4][:, 0:384].rearrange("p (c t) -> p c t", c=3)[:, :, 0:n]
                        self.tt("dve", rt1[:, :, 0:n], p4, sbb, ALU.mult, r=["ps4", f"csa{b}"], w=["rt1"])
                        self.tt("dve", krot[:, :, 0:n], p3, cb, ALU.mult, r=["ps3", f"csa{b}"], w=["krot"])
                        self.tt("pool", krot[:, :, 0:n], krot[:, :, 0:n], rt1[:, :, 0:n], ALU.add, r=["krot", "rt1"], w=["krot"])
                        for p in range(3):
                            self.tr(pss[5][0:n, p * 128:(p + 1) * 128], krot[:, p, 0:n], identF[:], r=["krot", "consts"], w=["ps5"])
                        self.tt("dve", kd[0:n, :], pss[5][0:n, 0:384], kdecT[0:n, kdi, :], ALU.mult, r=["ps5", "consts"], w=["kd"])
                        for h in range(6):
                            po = (h % 2) * 64
                            self.mm(pss[6][po:po + 64, (h // 2) * 64:(h // 2) * 64 + 64], kd[0:n, h * 64:(h + 1) * 64], vtok[0:n, h * 64:(h + 1) * 64],
                                    True, True, r=["kd", "vtok"], w=["ps6"])
                        if sq_ is PR:
                            self.copy("act", kvsb[b][:], pss[6][:, 0:192], r=["ps6"], w=[f"kvsb{b}"])
                            P.dma("sp", bncK2[m // 8][(m % 8) * 128:(m % 8 + 1) * 128, :], kvsb[b][:], r=[f"kvsb{b}"], w=[f"bncK{m // 8}"])
                        else:
                            self.tt("dve", kvsb[b][:], stS[:, m, :], cdecS[:], ALU.mult, r=["stS", "cdecS"], w=[f"kvsb{b}"])
                            self.tt("dve", kvsb[b][:], kvsb[b][:], pss[6][:, 0:192], ALU.add, r=[f"kvsb{b}", "ps6"], w=[f"kvsb{b}"])
                            P.dma("sp", o_rets[l, :, m, :], kvsb[b][:], r=[f"kvsb{b}"])
                P.cc([bncA32], [gatA32], r=["bncA"], w=["gatA"])
                for i in range(2):
                    P.cc([bncK2[i]], [gatK2[i]], r=[f"bncK{i}"], w=[f"gatK{i}"])
                P.barrier()
            if st <= 1:
                break
            self.phaseB(l, locals())
            if st <= 5:
                break
            self.phaseC(l, locals())

        P.emit(self.stack)
        return nc
    def phaseB(self, l, L):
        P, pss = self.P, self.pss
        st = self.stage
        ones, bd64, identF, identB, vecs = L["ones"], L["bd64"], L["identF"], L["identB"], L["vecs"]
        XT, X1T, gatA, gatK2, bncB = L["XT"], L["X1T"], L["gatA"], L["gatK2"], L["bncB"]
        wB = L["wB"][l]
        V_AG, V_AB, V_GN, V_L1G, V_L1B = L["V_AG"], L["V_AB"], L["V_GN"], L["V_L1G"], L["V_L1B"]
        wtok = L["wtok"]
        with ExitStack() as pp:
            KTI = self.sb("KTI", [128, 16, 4, 128], BF16, st=pp)
            VA = self.sb("VA", [128, 64, 65], BF16, st=pp)
            rst = self.sb("rst", [128, 16, 192], BF16, st=pp)
            self.memset("dve", VA[:], 1.0, w=["VA"])
            for j in range(4):
                for m in range(NT):
                    P.dma("sp", KTI[0:96, m, j, :], gatA[j * 16 + m, 0:12288].rearrange("(d t) -> d t", t=128), r=["gatA"], w=["KTI"])
                    P.dma("sp", VA[:, m * 4 + j, 0:64], gatA[j * 16 + m, 12288:20480].rearrange("(s e) -> s e", e=64), r=["gatA"], w=["VA"])
            with ExitStack() as sc:
                rc = self.sb("rc", [128, 9, 192], st=sc)
                kvg = [self.sb(f"kvg{i}", [128, 4, 192], st=sc) for i in range(2)]
                Sst = self.sb("Sst", [128, 192], st=sc)
                ta = self.sb("ta", [128, 192], st=sc)
                tb = self.sb("tb", [128, 192], st=sc)
                P.dma("sp", rc[:], L["rcoefd"], w=["rc"])
                self.memset("pool", Sst[:], 0.0, w=["Sst"])
                for m in range(NT):
                    b = m % 2
                    P.dma("sp", kvg[b][:], gatK2[m // 8].rearrange("(j m p) f -> p j m f", j=4, m=8)[:, :, m % 8, :], r=[f"gatK{m // 8}"], w=[f"kvg{b}"])
                    self.tt("pool", ta[:], Sst[:], rc[:, 0, :], ALU.mult, r=["Sst", "rc"], w=["ta"])
                    for jp in range(3):
                        self.tt("pool", tb[:], kvg[b][:, jp, :], rc[:, 1 + jp, :], ALU.mult, r=[f"kvg{b}", "rc"], w=["tb"])
                        self.tt("pool", ta[:], ta[:], tb[:], ALU.add, r=["ta", "tb"], w=["ta"])
                    self.copy("pool", rst[:, m, :], ta[:], r=["ta"], w=["rst"])
                    self.tt("pool", ta[:], Sst[:], rc[:, 4, :], ALU.mult, r=["Sst", "rc"], w=["ta"])
                    for jp in range(4):
                        self.tt("pool", tb[:], kvg[b][:, jp, :], rc[:, 5 + jp, :], ALU.mult, r=[f"kvg{b}", "rc"], w=["tb"])
                        self.tt("pool", ta[:], ta[:], tb[:], ALU.add, r=["ta", "tb"], w=["ta"])
                    self.copy("pool", Sst[:], ta[:], r=["ta"], w=["Sst"])
                P.dma("sp", L["o_ret"][l], Sst[:], r=["Sst"])
                P.barrier()
            if st <= 2:
                return
            self.phaseB2(l, L, KTI, VA, rst)

    def phaseB2(self, l, L, KTI, VA, rst):
        P, pss = self.P, self.pss
        st = self.stage
        ones, bd64, identF, identB, vecs = L["ones"], L["bd64"], L["identF"], L["identB"], L["vecs"]
        XT, X1T, gatA, gatK2, bncB = L["XT"], L["X1T"], L["gatA"], L["gatK2"], L["bncB"]
        wB = L["wB"][l]
        V_AG, V_AB, V_GN, V_L1G, V_L1B = L["V_AG"], L["V_AB"], L["V_GN"], L["V_L1G"], L["V_L1B"]
        wtok = L["wtok"]
        with ExitStack() as ph:
            sb = lambda name, shape, dt=F32: self.sb(name, shape, dt, st=ph)
            self.wring = [sb(f"wr{i}", [128, 8, 128], BF16) for i in range(3)]
            scores = sb("scores", [128, 8192])
            junk = sb("junk", [128, 3840], mybir.dt.uint8)
            xfg = sb("xfg", [128, 8, 512])
            xbg = sb("xbg", [128, 8, 512], BF16)
            mixT = sb("mixT", [128, 8, 512], BF16)
            uT = sb("uT", [128, 2, 512])
            gT = sb("gT", [128, 2, 512])
            vtokA = sb("vtokA", [128, 4, 256], BF16)
            avf = scores[0:16, 0:1024].rearrange("p (t f) -> p t f", t=4)
            csg = sb("csg", [128, 2, 512])
            qrot = sb("qrot", [128, 3, 512], BF16)
            qd = sb("qd", [128, 3, 512], BF16)
            krot = sb("krotB", [128, 3, 512], BF16)
            sil = sb("sil", [128, 3, 512], BF16)
            vtokB = sb("vtokB", [128, 4, 384], BF16)
            Sm = sb("Sm", [128, 6, 128], BF16)
            ysb = sb("ysb", [128, 384])
            ycn = sb("ycn", [128, 384])
            DT = sb("DT", [128, 6, 128])
            qdecP = sb("qdecP", [128, 3, 128])
            qdecS = sb("qdecS", [128, 3, 16])
            qq = sb("qq", [128, 4096], BF16)
            Amat = sb("Amat", [128, 128], BF16)
            Wd = sb("Wd", [128, 8, 128], BF16)
            rz = [sb(f"rz{i}", [128, 512], BF16) for i in range(4)]
            pT = [sb(f"pT{i}", [128, 768], BF16) for i in range(2)]
            col = sb("col", [128, 16])
            ob = sb("ob", [128, 384])
            CM = sb("CM", [128, 8, 128], BF16)
            E16 = sb("E16", [128, 16])
            dmask = sb("dmask", [128, 512])
            WmT = sb("WmT", [128, 4, 128], BF16)
            wmf = scores[:, 0:512].rearrange("p (g i) -> p g i", g=4)
            um = scores[:, 512:640]
            bsP = sb("bsP", [128, 2, 128])
            bsS = sb("bsS", [128, 2, 16])
            qi2T = qq[64:96, :].rearrange("p (t h) -> p t h", h=8)
            P.dma("sp", DT[:], L["dtd"], w=["DT"])
            P.dma("sp", qdecP[:], L["qdecPd"][:, :, 0:128], w=["qdec"])
            P.dma("sp", qdecS[:], L["qdecSd"][:, :, 0:16], w=["qdec"])
            P.dma("pool", CM[:], L["cmd"], w=["CM"])
            P.dma("sp", E16[:], L["e16d"], w=["E16"])
            P.dma("sp", dmask[:], L["dmaskd"], w=["dmask"])
            P.dma("sp", wmf, L["wmTd"][l], w=["scores"])
            P.dma("sp", um, L["umd"], w=["scores"])
            P.dma("sp", bsP[:], L["bsPd"][l][:, :, 0:128], w=["bs"])
            P.dma("sp", bsS[:], L["bsSd"][l][:, :, 0:16], w=["bs"])
            self.tt("dve", WmT[:], wmf, um.unsqueeze(1).to_broadcast([128, 4, 128]), ALU.mult, r=["scores"], w=["WmT"])

            def chunk(ci):
                return self.wchunk(wB[ci])

            def group(sq_, g, keysrc):
                n, N = sq_.n, sq_.N
                c0 = sq_.off + g * N
                gcols = slice(c0, c0 + N)
                xk = [f"xfg{c}" for c in range(8)]
                P.dma("sp", xfg[:, :, 0:N], XT[:, :, gcols], r=["XT"], w=xk)
                self.copy("act", xbg[:, :, 0:N], xfg[:, :, 0:N], r=xk, w=["xbg"])
                P.dma("sp", csg[:, 0, 0:N], L["cosd"][:, gcols], w=["csg"])
                P.dma("sp", csg[:, 1, 0:N], L["sind"][:, gcols], w=["csg"])
                bs = bsP if sq_ is PR else bsS
                qdec = qdecP if sq_ is PR else qdecS

                for ch in range(2):
                    wt, wk = chunk(CH["ua"] + ch)
                    self.proj(pss[ch], f"ps{ch}", wt, wk, 0, 128, xbg, "xbg", N)
                    self.act(uT[:, ch, 0:N], pss[ch][:, 0:N], AF.Gelu_apprx_tanh, r=[f"ps{ch}"], w=["uT"])
                for ch in range(2):
                    wt, wk = chunk(CH["va"] + ch)
                    self.proj(pss[ch], f"ps{ch}", wt, wk, 0, 128, xbg, "xbg", N)
                    self.act(gT[:, ch, 0:N], pss[ch][:, 0:N], AF.Gelu_apprx_tanh, r=[f"ps{ch}"], w=["gT"])
                if 'a2' in EXP:
                    return
                for ch in range(2):
                    mean, rstd = self.stats([gT[:, ch, 0:N]], ["gT"], N, bd64, "a")
                    self.tt("dve", gT[:, ch, 0:N], gT[:, ch, 0:N], mean[:, 0:N], ALU.subtract, r=["gT", "st_mean"], w=["gT"])
                    self.tt("dve", gT[:, ch, 0:N], gT[:, ch, 0:N], rstd[:, 0:N], ALU.mult, r=["gT", "st_rstd"], w=["gT"])
                    self.ts("dve", gT[:, ch, 0:N], gT[:, ch, 0:N], vecs[:, V_AG + ch:V_AG + ch + 1], vecs[:, V_AB + ch:V_AB + ch + 1],
                            ALU.mult, ALU.add, r=["gT", "vecs"], w=["gT"])
                if 'a3' in EXP:
                    return
                for t in range(4):
                    pb = pss[2 + t % 2]
                    for ch in range(2):
                        self.tr(pb[0:n, ch * 128:(ch + 1) * 128], gT[:, ch, t * n:(t + 1) * n], identF[:], r=["gT", "consts"], w=[f"ps{2 + t % 2}"])
                    self.copy(self.alt(), vtokA[0:n, t, :], pb[0:n, 0:256], r=[f"ps{2 + t % 2}"], w=["vtokA"])
                    if sq_ is SM:
                        self.copy("dve", avf[0:n, t, :], pb[0:n, 0:256], r=[f"ps{2 + t % 2}"], w=["scores"])
                if sq_ is SM:
                    P.dma("sp", L["o_avs"][l].rearrange("(b t) f -> t b f", t=16), avf, r=["scores"])
                if 'a4' in EXP:
                    return
                for t in range(4):
                    for gr in range(4):
                        po = (gr % 2) * 64
                        self.mm(pss[gr // 2][po:po + 64, t * n:(t + 1) * n], vtokA[0:n, t, gr * 64:(gr + 1) * 64], WmT[0:n, gr, 0:n],
                                True, True, r=["vtokA", "WmT"], w=[f"ps{gr // 2}"])
                if 'a5' in EXP:
                    return
                for ch in range(2):
                    bb = bs[:, ch, 0:n].unsqueeze(1).to_broadcast([128, 4, n])
                    self.copy("act", gT[:, ch, 0:N], pss[ch][:, 0:N], r=[f"ps{ch}"], w=["gT"])
                    g3 = gT[:, ch, 0:N].rearrange("p (t i) -> p t i", t=4)
                    self.tt("dve", g3, g3, bb, ALU.add, r=["gT", "bs"], w=["gT"])
                    if 'a6' in EXP:
                        continue
                    self.tt("dve", mixT[:, ch, 0:N], gT[:, ch, 0:N], uT[:, ch, 0:N], ALU.mult, r=["gT", "uT"], w=["mixT"])

                if 'A' in EXP:
                    return
                def rotary(cq, cqs, dst, with_qd):
                    for p in range(3):
                        wt, wk = chunk(cq + p)
                        self.proj(pss[0], "ps0", wt, wk, 0, 128, xbg, "xbg", N)
                        wt, wk = chunk(cqs + p)
                        self.proj(pss[1], "ps1", wt, wk, 0, 128, xbg, "xbg", N)
                        t1, t2 = self.sqb[0], self.sqb[1]
                        self.tt("dve", t1[:, 0:N], pss[0][:, 0:N], csg[:, 0, 0:N], ALU.mult, r=["ps0", "csg"], w=["sqb0"])
                        self.tt("dve", t2[:, 0:N], pss[1][:, 0:N], csg[:, 1, 0:N], ALU.mult, r=["ps1", "csg"], w=["sqb1"])
                        self.tt("pool", t1[:, 0:N], t1[:, 0:N], t2[:, 0:N], ALU.add, r=["sqb0", "sqb1"], w=["sqb0"])
                        self.copy("act", dst[:, p, 0:N], t1[:, 0:N], r=["sqb0"], w=["qrot" if with_qd else "krotB"])
                        if with_qd:
                            qb_ = qdec[:, p, 0:n].unsqueeze(1).to_broadcast([128, 4, n])
                            self.tt("dve", qd[:, p, 0:N].rearrange("p (t i) -> p t i", t=4), t1[:, 0:N].rearrange("p (t i) -> p t i", t=4), qb_,
                                    ALU.mult, r=["sqb0", "qdec"], w=["qd"])

                rotary(CH["qc"], CH["qcs"], qrot, True)
                if 'r1' in EXP:
                    return
                rotary(CH["kc"], CH["kcs"], krot, False)
                for p in range(3):
                    wt, wk = chunk(CH["gc"] + p)
                    self.proj(pss[p % 2], f"ps{p % 2}", wt, wk, 0, 128, xbg, "xbg", N)
                    self.act(sil[:, p, 0:N], pss[p % 2][:, 0:N], AF.Silu, r=[f"ps{p % 2}"], w=["sil"])
                for p in range(3):
                    wt, wk = chunk(CH["vc"] + p)
                    for t in range(4):
                        pb = pss[2 + t % 2]
                        for kc in range(8):
                            self.mm(pb[0:n, 0:128], xbg[:, kc, t * n:(t + 1) * n], wt[:, kc, :], kc == 0, kc == 7, r=["xbg", wk], w=[f"ps{2 + t % 2}"])
                        self.copy(self.alt(), vtokB[0:n, t, p * 128:(p + 1) * 128], pb[0:n, 0:128], r=[f"ps{2 + t % 2}"], w=["vtokB"])
                if 'r2' in EXP:
                    return
                for t in range(4):
                    tc_ = slice(t * n, (t + 1) * n)
                    m = g * 4 + t
                    for h in range(6):
                        po, p = (h % 2) * 64, h // 2
                        pb, pk = (pss[2], "ps2") if h % 2 == 0 else (pss[3], "ps3")
                        self.mm(pb[0:n, p * 128:p * 128 + n], krot[po:po + 64, p, tc_], qrot[po:po + 64, p, tc_], True, True, r=["krotB", "qrot"], w=[pk])
                    s1, s2 = self.sqb[0], self.sqb[1]
                    self.copy("act", s1[0:n, 0:384], pss[2][0:n, 0:384], r=["ps2"], w=["sqb0"])
                    self.copy("act", s2[0:n, 0:384], pss[3][0:n, 0:384], r=["ps3"], w=["sqb1"])
                    for h in range(6):
                        src, sk = (s1, "sqb0") if h % 2 == 0 else (s2, "sqb1")
                        p = h // 2
                        self.tt("dve", Sm[0:n, h, 0:n], src[0:n, p * 128:p * 128 + n], DT[0:n, h, 0:n], ALU.mult, r=[sk, "DT"], w=["Sm"])
                    if 'r3' in EXP:
                        continue
                    rsrc = keysrc["rst"](m)
                    for h in range(6):
                        po, p = (h % 2) * 64, h // 2
                        pb, pk = (pss[0], "ps0") if h % 2 == 0 else (pss[1], "ps1")
                        self.mm(pb[po:po + 64, p * 128:p * 128 + n], vtokB[0:n, t, h * 64:(h + 1) * 64], Sm[0:n, h, 0:n], True, False,
                                r=["vtokB", "Sm"], w=[pk])
                        self.mm(pb[po:po + 64, p * 128:p * 128 + n], rsrc[po:po + 64, p * 64:(p + 1) * 64], qd[po:po + 64, p, tc_], False, True,
                                r=["rst", "qd"], w=[pk])
                    if 'r4' in EXP:
                        continue
                    ysv = ysb[:, 0:3 * n].rearrange("p (c i) -> p c i", c=3)
                    self.copy("act", ysv[0:64], pss[0][0:64, 0:384].rearrange("p (c i) -> p c i", c=3)[:, :, 0:n], r=["ps0"], w=["ysb"])
                    self.copy("act", ysv[64:128], pss[1][64:128, 0:384].rearrange("p (c i) -> p c i", c=3)[:, :, 0:n], r=["ps1"], w=["ysb"])
                    mean, rstd = self.stats([ysb[:, 0:3 * n]], ["ysb"], 3 * n, bd64, "r")
                    self.tt("dve", ycn[:, 0:3 * n], ysb[:, 0:3 * n], mean[:, 0:3 * n], ALU.subtract, r=["ysb", "st_mean"], w=["ycn"])
                    self.tt("dve", ycn[:, 0:3 * n], ycn[:, 0:3 * n], rstd[:, 0:3 * n], ALU.mult, r=["ycn", "st_rstd"], w=["ycn"])
                    for p in range(3):
                        self.stt("dve", mixT[:, 5 + p, tc_], ycn[:, p * n:(p + 1) * n], vecs[:, V_GN + p:V_GN + p + 1], sil[:, p, tc_], ALU.mult, ALU.mult,
                                 r=["ycn", "vecs", "sil"], w=["mixT"])

                if 'R' in EXP:
                    return
                for c in range(3):
                    wt, wk = chunk(CH["qb"] + c)
                    for hh in range(2):
                        self.proj(pss[hh], f"ps{hh}", wt, wk, hh * 64, 64, xbg, "xbg", N)
                        q2v = qq[0:64, 0:24 * n].rearrange("p (t h i) -> p t h i", t=4, h=6)
                        self.copy(self.alt(), q2v[:, :, 2 * c + hh, :], pss[hh][0:64, 0:N].rearrange("p (t i) -> p t i", t=4), r=[f"ps{hh}"], w=["qq"])
                for c in range(2):
                    wt, wk = chunk(CH["qib"] + c)
                    for hh in range(4):
                        pb = pss[hh % 2]
                        self.proj(pb, f"ps{hh % 2}", wt, wk, hh * 32, 32, xbg, "xbg", N, po=64)
                        self.copy(self.alt(), qi2T[:, 0:N, 4 * c + hh], pb[64:96, 0:N], r=[f"ps{hh % 2}"], w=["qq"])
                for t in range(4):
                    self.dsa_tile(sq_, g, t, l, L, keysrc)

                if 'D' in EXP:
                    return
                for c in range(8):
                    wt, wk = chunk(CH["wo"] + c)
                    pb, pk = pss[c % 2], f"ps{c % 2}"
                    for kc in range(8):
                        self.mm(pb[:, 0:N], wt[:, kc, :], mixT[:, kc, 0:N], kc == 0, kc == 7, r=[wk, "mixT"], w=[pk])
                    self.stt("dve", xfg[:, c, 0:N], xfg[:, c, 0:N], ALPHA, pb[:, 0:N], ALU.mult, ALU.add, r=[xk[c], pk], w=[xk[c]])
                mean, rstd = self.stats([xfg[:, c, 0:N] for c in range(8)], xk, N, ones, "l1")
                for c in range(8):
                    self.tt("dve", xfg[:, c, 0:N], xfg[:, c, 0:N], mean[:, 0:N], ALU.subtract, r=[xk[c], "st_mean"], w=[xk[c]])
                    self.tt("pool", xfg[:, c, 0:N], xfg[:, c, 0:N], rstd[:, 0:N], ALU.mult, r=[xk[c], "st_rstd"], w=[xk[c]])
                    self.act(xfg[:, c, 0:N], xfg[:, c, 0:N], AF.Identity, r=[xk[c], "vecs"], w=[xk[c]],
                             scale=vecs[:, V_L1G + c:V_L1G + c + 1], bias=vecs[:, V_L1B + c:V_L1B + c + 1])
                P.dma("sp", X1T[:, :, gcols], xfg[:, :, 0:N], r=xk, w=["X1T"])
                if sq_ is PR:
                    bnd = self.bnd
                    for t in range(4):
                        self.copy("act", bnd[:, t, :, :], xfg[:, :, t * 128 + 126:t * 128 + 128], r=xk, w=["bnd"])
                    P.dma("sp", bncB[g * 4:(g + 1) * 4, :].rearrange("t (p x) -> p t x", x=16), bnd[:].rearrange("p t k c -> p t (k c)"),
                          r=["bnd"], w=["bncB"])

            self.bnd = sb("bnd", [128, 4, 8, 2], BF16)
            self._scores, self._junk = scores, junk
            self._junk2 = sb("junk2", [128, 4368], mybir.dt.uint8)
            self._sacc = sb("sacc", [128, 1])
            self._mbf = [sb(f"mbf{i}", [128, 128]) for i in range(2)]
            self._mT = [sb(f"mT{i}", [128, 128], BF16) for i in range(2)]
            self._dsa = dict(Amat=Amat, Wd=Wd, rz=rz, pT=pT, col=col, ob=ob, CM=CM, E16=E16, dmask=dmask,
                             qq=qq, qi2T=qi2T, mixT=mixT, wtok=wtok, identF=identF, identB=identB)

            if 'a1' in EXP:
                P.barrier()
                return
            ksrc = dict(rst=lambda m: rst[:, m, :], kind="p", KTI=KTI, VA=VA)
            for g in range(PR.ng if st >= 4 else 1):
                group(PR, g, ksrc)
            P.barrier()
            if st <= 3:
                return
            with ExitStack() as ss:
                KTIs = self.sb("KTIs", [128, 2064], BF16, st=ss)
                VAs = self.sb("VAs", [128, 17, 65], BF16, st=ss)
                cK = self.sb("cK", [128, 16, 96], st=ss)
                rstS = self.sb("rstS", [128, 4, 192], BF16, st=ss)
                stSf = self.sb("stSf", [128, 4, 192], st=ss)
                P.dma("sp", stSf[:], L["stSd"][l], w=["stSf"])
                self.copy("act", rstS[:], stSf[:], r=["stSf"], w=["rst"])
                ksrc = dict(rst=lambda m: rstS[:, m, :], kind="s", KTIs=KTIs, VAs=VAs, cK=cK, ck=L["ckd"][l], cv=L["cvd"][l], cki=L["ckid"][l],
                            ktS=L["ktS"], vS=L["vS"])
                group(SM, 0, ksrc)
                P.barrier()

    def dsa_tile(self, sq_, g, t, l, L, ks):
        P, pss, d = self.P, self.pss, self._dsa
        n = sq_.n
        ng = n // 16
        m = g * 4 + t
        tc_ = slice(t * n, (t + 1) * n)
        scores, junk = self._scores, self._junk
        Amat, Wd, rz, pT, col, ob = d["Amat"], d["Wd"], d["rz"], d["pT"], d["col"], d["ob"]
        identF, identB, mixT = d["identF"], d["identB"], d["mixT"]
        qi2T = d["qi2T"]
        q2f = d["qq"][0:64, 0:24 * n].rearrange("p (t x) -> p t x", t=4)
        if ks["kind"] == "p":
            KTI, VA = ks["KTI"], ks["VA"]
            kkey, vkey = "KTI", "VA"
            iblocks = [(KTI[64:96, kb].rearrange("p j t -> p (j t)"), 512, kb * 512) for kb in range(m + 1)]
            ablocks = [(KTI[0:64, kb, jj, :], VA[:, kb * 4 + jj, :], 128, (kb * 4 + jj) * 128) for kb in range(m + 1) for jj in range(4)]
            Lk = (m + 1) * 512
        else:
            KTIs, VAs, cK = ks["KTIs"], ks["VAs"], ks["cK"]
            kkey, vkey = "KTIs", "VAs"
            b = t
            for k in range(16):
                P.dma("sp", cK[:, k, 0:64], ks["ck"][b][k * 128:(k + 1) * 128, :], w=["cK"])
                P.dma("sp", cK[:, k, 64:96], ks["cki"][b][k * 128:(k + 1) * 128, :], w=["cK"])
            for k in range(16):
                pb, pk = pss[k % 2], f"ps{k % 2}"
                self.tr(pb[0:96, 0:128], cK[:, k, :], identF[:], r=["cK", "consts"], w=[pk])
                self.copy(self.alt(), KTIs[0:96, k * 128:(k + 1) * 128], pb[0:96, 0:128], r=[pk], w=["KTIs"])
            self.copy("dve", KTIs[0:96, 2048:2064], ks["ktS"][0:96, b, :], r=["ktS"], w=["KTIs"])
            self.memset("dve", VAs[:], 1.0, w=["VAs"])
            for k in range(16):
                P.dma("pool", VAs[:, k, 0:64], ks["cv"][b][k * 128:(k + 1) * 128, :], w=["VAs"])
            self.copy("dve", VAs[0:16, 16, 0:64], ks["vS"][0:16, b, :], r=["vS"], w=["VAs"])

            iblocks = [(KTIs[64:96, kb * 512:(kb + 1) * 512], 512, kb * 512) for kb in range(4)] + [(KTIs[64:96, 2048:2064], 16, 2048)]
            ablocks = [(KTIs[0:64, k * 128:(k + 1) * 128], VAs[:, k, :], 128, k * 128) for k in range(16)] + [(KTIs[0:64, 2048:2064], VAs[0:16, 16, :], 16, 2048)]
            Lk = 2064
        wv = d["wtok"][sq_.name]
        for a in range(16):
            self.ts("dve", Amat[0:n, a * 8:(a + 1) * 8], wv[0:n, m, :], d["E16"][0:n, a:a + 1], None, ALU.mult, r=["E16", "wtok"], w=["Amat"])
        self.mm(pss[4][:, 0:n], Amat[0:n, :], identB[0:n, 0:n], True, True, r=["Amat", "consts"], w=["ps4"])
        wall = self.sqb[1]
        self.copy("act", wall[:, 0:n], pss[4][:, 0:n], r=["ps4"], w=["sqb1"])
        for gq in range(ng):
            self.tt("dve", Wd[:, gq, 0:n], wall[:, 0:n], d["CM"][:, gq, 0:n], ALU.mult, r=["sqb1", "CM"], w=["Wd"])
        tmpm = self.sqb[0]
        steps = [(bi, gq) for bi in range(len(iblocks)) for gq in range(ng)]

        def emit_z(si):
            bi, gq = steps[si]
            rhs_ap, nk, c0 = iblocks[bi]
            zb = (2, 3, 0, 1)[si % 4]
            lhs = qi2T[:, t * n + 16 * gq:t * n + 16 * gq + 16, :].rearrange("p a h -> p (a h)")
            self.mm(pss[zb][:, 0:nk], lhs, rhs_ap, True, True, r=["qq", kkey], w=[f"ps{zb}"])

        for si in range(min(2, len(steps))):
            emit_z(si)
        for si, (bi, gq) in enumerate(steps):
            rhs_ap, nk, c0 = iblocks[bi]
            zb = (2, 3, 0, 1)[si % 4]
            pz, pzk = pss[zb], f"ps{zb}"
            rzb, rk = rz[si % 4], f"rz{si % 4}"
            if self.alt() == "act":
                self.act(rzb[:, 0:nk], pz[:, 0:nk], AF.Relu, r=[pzk], w=[rk])
            else:
                self.ts("dve", rzb[:, 0:nk], pz[:, 0:nk], 0.0, None, ALU.max, r=[pzk], w=[rk])
            self.mm(pss[4][0:n, 0:nk], Wd[:, gq, 0:n], rzb[:, 0:nk], gq == 0, gq == ng - 1, r=["Wd", rk], w=["ps4"])
            if si + 2 < len(steps):
                emit_z(si + 2)
            if gq == ng - 1:
                if ks["kind"] == "p" and bi == len(iblocks) - 1:
                    self.tt("dve", tmpm[0:n, 0:nk], pss[4][0:n, 0:nk], d["dmask"][0:n, 0:nk], ALU.subtract, r=["ps4", "dmask"], w=["sqb0"])
                    self.tt("dve", scores[0:n, c0:c0 + nk], pss[4][0:n, 0:nk], d["dmask"][0:n, 0:nk], ALU.add, r=["ps4", "dmask"], w=["scores"])
                else:
                    self.copy(self.alt(), scores[0:n, c0:c0 + nk], pss[4][0:n, 0:nk], r=["ps4"], w=["scores"])
        lo, w0, mid, cnt, gk, t5 = (col[0:n, i:i + 1] for i in range(6))
        if ks["kind"] == "p":
            self.red(lo, tmpm[0:n, 0:512], ALU.min, r=["sqb0"], w=["col"])
            if m > 0:
                self.red(t5, scores[0:n, 0:m * 512], ALU.min, r=["scores"], w=["col"])
                self.tt("dve", lo, lo, t5, ALU.min, r=["col"], w=["col"])
        else:
            self.red(lo, scores[0:n, 0:Lk], ALU.min, r=["scores"], w=["col"])
        self.red(w0, scores[0:n, 0:Lk], ALU.max, r=["scores"], w=["col"])
        self.tt("dve", w0, w0, lo, ALU.subtract, r=["col"], w=["col"])
        Ld = (Lk * 15 // 32) // 16 * 16 if Lk >= 1024 else Lk
        La = Lk - Ld
        sc_ap, jk_ap = scores[0:n, 0:Ld], junk[0:n, 0:Ld]
        if La:
            sa_ap, ja_ap = scores[0:n, Ld:Lk], self._junk2[0:n, 0:La]
            sacc = self._sacc[0:n, 0:1]
        thr = 255.5 - 0.5 * La
        for k in range(NBIS):
            hw = 2.0 ** (-(k + 1))
            self.ts("dve", mid, w0, hw, lo, ALU.mult, ALU.add, r=["col"], w=["mid"])
            P.op("dve", lambda E: E.tensor_scalar(out=jk_ap, in0=sc_ap, scalar1=mid, scalar2=None, op0=ALU.is_ge, op1=ALU.add, accum_out=cnt),
                 r=["scores", "mid"], w=["junk", "col"])
            if La:
                P.op("act", lambda E: E.activation(out=ja_ap, in_=sa_ap, func=AF.Sign, bias=mid, scale=-1.0, accum_out=sacc),
                     r=["scores", "mid"], w=["junk2", "sacc"])
                self.stt("dve", cnt, sacc, -0.5, cnt, ALU.mult, ALU.add, r=["col", "sacc"], w=["col"])
            self.ts("dve", gk, cnt, thr, hw, ALU.is_ge, ALU.mult, r=["col"], w=["col"])
            self.stt("dve", lo, gk, w0, lo, ALU.mult, ALU.add, r=["col", "mid"], w=["col"])
        pO = pss[7]
        nb = len(ablocks)

        mbf = self._mbf
        mT = self._mT

        def emit_logits(bi):
            kt, v, nk, c0 = ablocks[bi]
            i2 = bi % 2
            self.ts("dve", mbf[i2][0:n, 0:nk], scores[0:n, c0:c0 + nk], lo, None, ALU.is_ge, r=["scores", "col"], w=[f"mbf{i2}"])
            tb = 2 + i2
            self.tr(pss[tb][0:nk, 0:n], mbf[i2][0:n, 0:nk], identF[0:n, 0:n], r=[f"mbf{i2}", "consts"], w=[f"ps{tb}"])
            self.copy("act", mT[i2][0:nk, 0:n], pss[tb][0:nk, 0:n], r=[f"ps{tb}"], w=[f"mT{i2}"])
            la, lb = (5, 6) if i2 == 0 else (0, 1)
            self.mm(pss[la][0:nk, 0:4 * n], kt, q2f[:, t, 0:4 * n], True, True, r=[kkey, "qq"], w=[f"ps{la}"])
            self.mm(pss[lb][0:nk, 0:2 * n], kt, q2f[:, t, 4 * n:6 * n], True, True, r=[kkey, "qq"], w=[f"ps{lb}"])

        emit_logits(0)
        for bi, (kt, v, nk, c0) in enumerate(ablocks):
            if bi + 1 < nb:
                emit_logits(bi + 1)
            i2 = bi % 2
            la, lb = (5, 6) if i2 == 0 else (0, 1)
            ptb, pk = pT[i2], f"pT{i2}"
            self.act(ptb[0:nk, 0:4 * n], pss[la][0:nk, 0:4 * n], AF.Exp, r=[f"ps{la}"], w=[pk], scale=0.125)
            self.act(ptb[0:nk, 4 * n:6 * n], pss[lb][0:nk, 0:2 * n], AF.Exp, r=[f"ps{lb}"], w=[pk], scale=0.125)
            p3 = ptb[0:nk, 0:6 * n].rearrange("p (h i) -> p h i", h=6)
            self.tt("dve", p3, p3, mT[i2][0:nk, 0:n].unsqueeze(1).to_broadcast([nk, 6, n]), ALU.mult, r=[pk, f"mT{i2}"], w=[pk])
            for h in range(6):
                self.mm(pO[0:n, h * 65:(h + 1) * 65], ptb[0:nk, h * n:(h + 1) * n], v[0:nk, :], bi == 0 and h == 0, bi == nb - 1 and h == 5,
                        r=[pk, vkey], w=["ps7"])
        posb = self.sqb[1]
        self.copy("act", posb[0:n, 0:390], pO[0:n, 0:390], r=["ps7"], w=["sqb1"])
        pov = posb[0:n, 0:390].rearrange("p (h e) -> p h e", e=65)
        rden = col[0:n, 8:14]
        for h in range(6):
            P.op("dve", (lambda hh: (lambda E: E.reciprocal(out=col[0:n, 8 + hh:9 + hh], in_=posb[0:n, hh * 65 + 64:hh * 65 + 65])))(h), r=["sqb1"], w=["col"])
        for h in range(6):
            self.ts("dve", ob[0:n, h * 64:(h + 1) * 64], posb[0:n, h * 65:h * 65 + 64], col[0:n, 8 + h:9 + h], None, ALU.mult, r=["sqb1", "col"], w=["ob"])
        for c in range(3):
            pb, pk = pss[c % 2], f"ps{c % 2}"
            self.tr(pb[:, 0:n], ob[0:n, c * 128:(c + 1) * 128], identF[0:n, 0:n], r=["ob", "consts"], w=[pk])
            self.copy(self.alt(), mixT[:, 2 + c, tc_], pb[:, 0:n], r=[pk], w=["mixT"])

    def phaseC(self, l, L):
        P, pss = self.P, self.pss
        st = self.stage
        ones, identF, vecs = L["ones"], L["identF"], L["vecs"]
        XT, X1T, gatB, bncB32, gatB32 = L["XT"], L["X1T"], L["gatB"], L["bncB32"], L["gatB32"]
        wCg, wCu, wCd = L["wCg"][l], L["wCu"][l], L["wCd"][l]
        V_L2G, V_L2B, V_CW, V_CB = L["V_L2G"], L["V_L2B"], L["V_CW"], L["V_CB"]
        convo, convos = L["convo"], L["convos"]
        last = (l == DEPTH - 1)
        P.cc([bncB32], [gatB32], r=["bncB"], w=["gatB"])
        P.barrier()
        with ExitStack() as ph:
            sb = lambda name, shape, dt=F32: self.sb(name, shape, dt, st=ph)
            self.wring = [sb(f"wr{i}", [128, 8, 128], BF16) for i in range(4)]
            wdr = [sb(f"wdr{i}", [128, NFC, 128], BF16) for i in range(2)]
            x1f = sb("x1f", [128, 8, 512])
            x1b = sb("x1b", [128, 8, 512], BF16)
            hT = sb("hT", [128, NFC, 512], BF16)
            hgx = sb("hgx", [128, 4, 130])
            c1 = sb("c1", [128, 512])
            gl = sb("gl", [128, 512])
            prevb = sb("prevb", [128, 8, 4, 2], BF16)
            gbt = sb("gbt", [128, 5, 4, 16], BF16)
            acc = sb("acc", [128, 4, 16])
            tmp8 = sb("tmp8", [128, 8])
            sel5 = sb("sel5", [128, 5])
            cvpS = sb("cvpS", [128, NFC, 4, 2])
            ytok = [sb(f"ytok{i}", [128, D]) for i in range(2)]
            P.dma("sp", sel5[:], L["sel5d"], w=["sel5"])
            P.dma("sp", cvpS[:], L["cvpd"][l], w=["cvpS"])
            gBv = gatB.rearrange("r (p x) -> p r x", x=16)
            yi = 0
            for sq_ in (PR, SM):
                n, N = sq_.n, sq_.N
                for g in range(sq_.ng):
                    c0 = sq_.off + g * N
                    gcols = slice(c0, c0 + N)
                    xk = [f"x1f{c}" for c in range(8)]
                    P.dma("sp", x1f[:, :, 0:N], X1T[:, :, gcols], r=["X1T"], w=xk)
                    self.copy("act", x1b[:, :, 0:N], x1f[:, :, 0:N], r=xk, w=["x1b"])
                    if sq_ is PR:
                        for k in range(4):
                            P.dma("sp", gbt[:, k, :, :], gBv[:, k * 16 + 4 * g:k * 16 + 4 * g + 4, :], r=["gatB"], w=["gbt"])
                        if g == 0:
                            self.memset("dve", gbt[:, 4, 0, :], 0.0, w=["gbt"])
                            P.dma("sp", gbt[:, 4, 1:4, :], gBv[:, 48:51, :], r=["gatB"], w=["gbt"])
                        else:
                            P.dma("sp", gbt[:, 4, :, :], gBv[:, 48 + 4 * g - 1:48 + 4 * g + 3, :], r=["gatB"], w=["gbt"])
                        self.ts("dve", acc[:], gbt[:, 0, :, :], sel5[:, 0:1], None, ALU.mult, r=["gbt", "sel5"], w=["acc"])
                        for k in range(1, 5):
                            self.stt("dve", acc[:], gbt[:, k, :, :], sel5[:, k:k + 1], acc[:], ALU.mult, ALU.add, r=["gbt", "sel5", "acc"], w=["acc"])
                        for t in range(4):
                            self.copy("dve", prevb[:, :, t, :], acc[:, t, :].rearrange("p (k c) -> p k c", c=2), r=["acc"], w=["prevb"])
                    for f in range(NFC):
                        wg, wgk = self.wchunk(wCg[f])
                        wu, wuk = self.wchunk(wCu[f])
                        for kc in range(8):
                            self.mm(pss[0][:, 0:N], wg[:, kc, :], x1b[:, kc, 0:N], kc == 0, kc == 7, r=[wgk, "x1b"], w=["ps0"])
                        if sq_ is PR:
                            for kc in range(8):
                                self.mm(pss[2][:, 0:8], wg[:, kc, :], prevb[:, kc, :, :].rearrange("p t c -> p (t c)"), kc == 0, kc == 7,
                                        r=[wgk, "prevb"], w=["ps2"])
                        for kc in range(8):
                            self.mm(pss[1][:, 0:N], wu[:, kc, :], x1b[:, kc, 0:N], kc == 0, kc == 7, r=[wuk, "x1b"], w=["ps1"])
                        for t in range(4):
                            self.copy("act", hgx[:, t, 2:2 + n], pss[0][:, t * n:(t + 1) * n], r=["ps0"], w=["hgx"])
                        if sq_ is PR:
                            self.copy("act", tmp8[:], pss[2][:, 0:8], r=["ps2"], w=["tmp8"])
                            self.copy("dve", hgx[:, :, 0:2], tmp8[:].rearrange("p (t c) -> p t c", c=2), r=["tmp8"], w=["hgx"])
                        else:
                            self.copy("dve", hgx[:, :, 0:2], cvpS[:, f, :, :], r=["cvpS"], w=["hgx"])
                        cw = lambda j: vecs[:, V_CW + f * 3 + j:V_CW + f * 3 + j + 1]
                        c1v = c1[:, 0:N].rearrange("p (t i) -> p t i", t=4)
                        self.ts("dve", c1v, hgx[:, :, 2:2 + n], cw(2), vecs[:, V_CB + f:V_CB + f + 1], ALU.mult, ALU.add, r=["hgx", "vecs"], w=["c1"])
                        self.stt("dve", c1v, hgx[:, :, 1:1 + n], cw(1), c1v, ALU.mult, ALU.add, r=["hgx", "vecs", "c1"], w=["c1"])
                        self.stt("dve", c1v, hgx[:, :, 0:n], cw(0), c1v, ALU.mult, ALU.add, r=["hgx", "vecs", "c1"], w=["c1"])
                        self.act(gl[:, 0:N], c1[:, 0:N], AF.Gelu_apprx_tanh, r=["c1"], w=["gl"])
                        self.tt("dve", hT[:, f, 0:N], gl[:, 0:N], pss[1][:, 0:N], ALU.mult, r=["gl", "ps1"], w=["hT"])
                        if sq_ is PR and g == 3:
                            self.copy("dve", convo[:, f, :], hgx[:, 3, n:n + 2], r=["hgx"], w=["convo"])
                        if sq_ is SM:
                            self.copy("dve", convos[:, f, :, :], hgx[:, :, n:n + 2], r=["hgx"], w=["convos"])
                    for c in range(8):
                        i = c % 2
                        P.dma("pool", wdr[i][:], wCd[c], w=[f"wdr{i}"])
                        pb, pk = pss[c % 2], f"ps{c % 2}"
                        for f in range(NFC):
                            self.mm(pb[:, 0:N], wdr[i][:, f, :], hT[:, f, 0:N], f == 0, f == NFC - 1, r=[f"wdr{i}", "hT"], w=[pk])
                        self.stt("dve", x1f[:, c, 0:N], x1f[:, c, 0:N], ALPHA, pb[:, 0:N], ALU.mult, ALU.add, r=[xk[c], pk], w=[xk[c]])
                    mean, rstd = self.stats([x1f[:, c, 0:N] for c in range(8)], xk, N, ones, "l2")
                    for c in range(8):
                        self.tt("dve", x1f[:, c, 0:N], x1f[:, c, 0:N], mean[:, 0:N], ALU.subtract, r=[xk[c], "st_mean"], w=[xk[c]])
                        self.tt("pool", x1f[:, c, 0:N], x1f[:, c, 0:N], rstd[:, 0:N], ALU.mult, r=[xk[c], "st_rstd"], w=[xk[c]])
                        self.act(x1f[:, c, 0:N], x1f[:, c, 0:N], AF.Identity, r=[xk[c], "vecs"], w=[xk[c]],
                                 scale=vecs[:, V_L2G + c:V_L2G + c + 1], bias=vecs[:, V_L2B + c:V_L2B + c + 1])
                    if not last:
                        P.dma("sp", XT[:, :, gcols], x1f[:, :, 0:N], r=xk, w=["XT"])
                    else:
                        oy = L["o_y"] if sq_ is PR else L["o_ys"]
                        for t in range(4):
                            yb = ytok[yi % 2]
                            yk = f"ytok{yi % 2}"
                            yi += 1
                            for c in range(8):
                                bank = (2, 3, 4, 7)[c % 4]
                                self.tr(pss[bank][0:n, 0:128], x1f[:, c, t * n:(t + 1) * n], identF[:], r=[xk[c], "consts"], w=[f"ps{bank}"])
                                self.copy(self.alt(), yb[0:n, c * 128:(c + 1) * 128], pss[bank][0:n, 0:128], r=[f"ps{bank}"], w=[yk])
                            r0 = (g * 4 + t) * n
                            P.dma("sp", oy[r0:r0 + n, :], yb[0:n, :], r=[yk])
            P.dma("sp", L["o_conv"][l], convo[:], r=["convo"])
            P.dma("sp", L["o_convs"][l], convos[:], r=["convos"])
            P.barrier()
SPL = dict(ua=0, va=256, qb=512, kb=896, vb=960, qib=1024, kib=1280, wib=1312, qc=1320, kc=1704, vc=2088, gc=2472)
LOG_G = np.log(1.0 - 2.0 ** (-5.0 - np.arange(6, dtype=np.float64)))


def _chunked(w):
    return np.ascontiguousarray(w.reshape(8, 128, -1).transpose(1, 0, 2))


def _swap_heads(w):
    m = w.reshape(w.shape[0], -1, 2, 32)
    return np.ascontiguousarray(m[:, :, ::-1, :]).reshape(w.shape[0], -1)


def _rot_tables(pos):
    half = 32
    freqs = (10000.0 ** (-np.arange(half, dtype=np.float32) / half)).astype(np.float32)
    ang = pos.astype(np.float32)[None, :] * freqs[:, None]
    cos = np.cos(ang)
    sin = np.sin(ang)
    cosT = np.concatenate([cos, cos, cos, cos], 0).astype(np.float32)
    sinT = np.concatenate([-sin, sin, -sin, sin], 0).astype(np.float32)
    return cosT, sinT


def _pairlay(per_head):
    t = np.zeros((128, 192), np.float64)
    for h in range(6):
        t[(h % 2) * 64:(h % 2) * 64 + 64, (h // 2) * 64:(h // 2) * 64 + 64] = per_head[h]
    return t


def _const_tables():
    c = {}
    c["identf"] = np.eye(128, dtype=np.float32)
    jj = np.arange(128, dtype=np.float64)
    kd0 = np.exp((127 - jj)[:, None] * LOG_G[None, :]) * 0.125
    kd1 = np.zeros((128, 6))
    kd1[:16] = np.exp((15 - jj[:16])[:, None] * LOG_G[None, :]) * 0.125
    c["kdec"] = np.stack([np.repeat(kd0, 64, 1), np.repeat(kd1, 64, 1)]).astype(np.float32)
    diff = jj[None, :] - jj[:, None]
    dt = np.where(diff[:, None, :] >= 0, np.exp(np.maximum(diff, 0)[:, None, :] * LOG_G[None, :, None]), 0.0) * 0.125
    c["dtT"] = dt.astype(np.float32)
    lane_h = lambda pair: np.array([2 * pair + (p // 64) for p in range(128)])
    qp = np.zeros((128, 3, 512))
    qs = np.zeros((128, 3, 64))
    for pair in range(3):
        lg = LOG_G[lane_h(pair)]
        qp[:, pair, :] = np.exp(((np.arange(512) % 128) + 1)[None, :] * lg[:, None])
        qs[:, pair, :] = np.exp(((np.arange(64) % 16) + 1)[None, :] * lg[:, None])
    c["qdecP"], c["qdecS"] = qp.astype(np.float32), qs.astype(np.float32)
    c["sel"] = (np.arange(128)[:, None] == (np.arange(768) % 128)[None, :]).astype(np.float32)
    c["sel16"] = (np.arange(16)[:, None] == (np.arange(96) % 16)[None, :]).astype(np.float32)
    c["cm"] = np.broadcast_to(((np.arange(128) // 16)[None, None, :] == np.arange(8)[None, :, None]), (128, 8, 128)).astype(np.float32).copy()
    c["e16"] = ((np.arange(128) % 16)[:, None] == np.arange(16)[None, :]).astype(np.float32)
    bd = np.zeros((128, 128), np.float32)
    bd[:64, :64] = 1 / 64
    bd[64:, 64:] = 1 / 64
    c["bd64"] = bd
    c["onesd"] = np.full((128, 128), 1 / 1024, np.float32)
    c["umask"] = (np.arange(128)[None, :] >= np.arange(128)[:, None]).astype(np.float32)
    c["cdecS"] = _pairlay(np.exp(16 * LOG_G)).astype(np.float32)
    return c


def make_inputs(inp):
    maps = []
    cst = _const_tables()
    per_layer = []
    for l in range(DEPTH):
        W = inp["w_in"][l]
        sl = lambda k, n: W[:, SPL[k]:SPL[k] + n]
        d = {}
        d["wA"] = _chunked(np.concatenate([sl("vc", 384), sl("kb", 64), sl("vb", 64), sl("kib", 32), sl("wib", 8),
                                           sl("kb", 64), sl("kib", 32), sl("kc", 384), _swap_heads(sl("kc", 384))], 1))
        wb = np.concatenate([sl("ua", 256), sl("va", 256), sl("qc", 384), _swap_heads(sl("qc", 384)), sl("kc", 384),
                             _swap_heads(sl("kc", 384)), sl("gc", 384), sl("vc", 384), sl("qb", 384), sl("qib", 256), inp["w_out"][l]], 1)
        d["wB"] = np.ascontiguousarray(_chunked(wb).reshape(128, 8, 35, 128).transpose(2, 0, 1, 3))
        d["wCg"] = np.ascontiguousarray(_chunked(inp["w_gate"][l]).reshape(128, 8, NFC, 128).transpose(2, 0, 1, 3))
        d["wCu"] = np.ascontiguousarray(_chunked(inp["w_up"][l]).reshape(128, 8, NFC, 128).transpose(2, 0, 1, 3))
        wd = inp["w_down"][l].reshape(NFC, 128, 8, 128)
        d["wCd"] = np.ascontiguousarray(wd.transpose(2, 1, 0, 3))
        fm = lambda v, k: v.reshape(k, 128).T
        cw = inp["conv_w"][l].reshape(3, NFC, 128).transpose(2, 1, 0).reshape(128, NFC * 3)
        d["vec"] = np.ascontiguousarray(np.concatenate([
            fm(inp["a_ln_g"][l], 2), fm(inp["a_ln_b"][l], 2), fm(inp["c_gn_g"][l], 3), fm(inp["ln1_g"][l], 8), fm(inp["ln1_b"][l], 8),
            fm(inp["ln2_g"][l], 8), fm(inp["ln2_b"][l], 8), cw, fm(inp["conv_b"][l], NFC)], 1).astype(np.float32))
        bs = inp["a_bs"][l]
        g_of = np.array([[ch * 2 + p // 64 for ch in range(2)] for p in range(128)])
        d["bsP"] = np.ascontiguousarray(np.tile(bs[g_of], (1, 1, 4)).astype(np.float32))
        d["bsS"] = np.ascontiguousarray(np.tile(bs[g_of][:, :, :16], (1, 1, 4)).astype(np.float32))
        d["wmT"] = np.ascontiguousarray(inp["a_ws"][l].transpose(2, 0, 1))
        per_layer.append(d)
    for c in range(8):
        b, j = c // 4, c % 4
        m = dict(cst)
        tiles = [4 * mm + j for mm in range(NT)]
        pos = np.concatenate([np.arange(t * 128, (t + 1) * 128) for t in tiles])
        m["xp"] = np.ascontiguousarray(inp["x_prompt"][b][pos])
        m["xs"] = np.ascontiguousarray(inp["x_sample"][4 * c:4 * c + 4].reshape(NS, D))
        pos_all = np.concatenate([pos, np.tile(2048 + np.arange(16), 4)])
        m["cosT"], m["sinT"] = _rot_tables(pos_all)
        dm = np.zeros((128, 512), np.float32)
        for jp in range(4):
            if jp > j:
                dm[:, jp * 128:(jp + 1) * 128] = NEG
            elif jp == j:
                dm[0:64, jp * 128 + 64:(jp + 1) * 128] = NEG
        m["dmask"] = dm
        rc = np.zeros((128, 9, 192))
        rc[:, 0] = _pairlay(np.exp(128 * j * LOG_G))
        for jp in range(3):
            if jp < j:
                rc[:, 1 + jp] = _pairlay(np.exp(128 * (j - 1 - jp) * LOG_G))
        rc[:, 4] = _pairlay(np.exp(512 * LOG_G))
        for jp in range(4):
            rc[:, 5 + jp] = _pairlay(np.exp(128 * (3 - jp) * LOG_G))
        m["rcoef"] = rc.astype(np.float32)
        s5 = np.zeros((128, 5), np.float32)
        s5[:, (j - 1) if j >= 1 else 4] = 1.0
        m["sel5"] = s5
        for l in range(DEPTH):
            for k, v in per_layer[l].items():
                m[f"{k}{l}"] = v
            m[f"ck{l}"] = np.ascontiguousarray(inp["cache_b_k"][l, 4 * c:4 * c + 4])
            m[f"cv{l}"] = np.ascontiguousarray(inp["cache_b_v"][l, 4 * c:4 * c + 4])
            m[f"cki{l}"] = np.ascontiguousarray(inp["cache_b_kidx"][l, 4 * c:4 * c + 4])
            sr = inp["state_ret"][l, 4 * c:4 * c + 4]
            t = np.zeros((128, 4, 192), np.float32)
            for h in range(6):
                t[(h % 2) * 64:(h % 2) * 64 + 64, :, (h // 2) * 64:(h // 2) * 64 + 64] = sr[:, h].transpose(1, 0, 2)
            m[f"stS{l}"] = t
            cv = inp["state_ffn_conv"][l, 4 * c:4 * c + 4]
            m[f"cvp{l}"] = np.ascontiguousarray(cv.reshape(4, 2, NFC, 128).transpose(3, 2, 0, 1))
        maps.append(m)
    return maps


def _unpair(t):
    out = np.zeros(t.shape[:-2] + (6, 64, 64), np.float32)
    for h in range(6):
        out[..., h, :, :] = t[..., (h % 2) * 64:(h % 2) * 64 + 64, (h // 2) * 64:(h // 2) * 64 + 64]
    return out


def assemble(R):
    y = np.zeros((2, 8192, D), np.float32)
    ys = np.zeros((32, 16, D), np.float32)
    kp = np.zeros((DEPTH, 2, 8192, 64), np.float32)
    vp = np.zeros((DEPTH, 2, 8192, 64), np.float32)
    kip = np.zeros((DEPTH, 2, 8192, 32), np.float32)
    rp = np.zeros((DEPTH, 2, 6, 64, 64), np.float32)
    cp = np.zeros((DEPTH, 2, 2, DFF), np.float32)
    ks = np.zeros((DEPTH, 32, 16, 64), np.float32)
    vs = np.zeros((DEPTH, 32, 16, 64), np.float32)
    kis = np.zeros((DEPTH, 32, 16, 32), np.float32)
    rs = np.zeros((DEPTH, 32, 6, 64, 64), np.float32)
    cs = np.zeros((DEPTH, 32, 2, DFF), np.float32)
    avs = np.zeros((DEPTH, 32, 16, 256), np.float32)
    for c in range(8):
        b, j = c // 4, c % 4
        r = R[c]
        pos = np.concatenate([np.arange((4 * mm + j) * 128, (4 * mm + j + 1) * 128) for mm in range(NT)])
        y[b, pos] = r["o_y"]
        ys[4 * c:4 * c + 4] = r["o_ys"].reshape(4, 16, D)
        kp[:, b, pos] = r["o_k"]
        vp[:, b, pos] = r["o_v"]
        kip[:, b, pos] = r["o_ki"]
        if j == 0:
            rp[:, b] = _unpair(r["o_ret"])
        if j == 3:
            cp[:, b] = r["o_conv"].transpose(0, 3, 2, 1).reshape(DEPTH, 2, DFF)
        ks[:, 4 * c:4 * c + 4] = r["o_ks"].reshape(DEPTH, 4, 16, 64)
        vs[:, 4 * c:4 * c + 4] = r["o_vs"].reshape(DEPTH, 4, 16, 64)
        kis[:, 4 * c:4 * c + 4] = r["o_kis"].reshape(DEPTH, 4, 16, 32)
        rs[:, 4 * c:4 * c + 4] = _unpair(r["o_rets"].transpose(0, 2, 1, 3))
        cs[:, 4 * c:4 * c + 4] = r["o_convs"].transpose(0, 3, 4, 2, 1).reshape(DEPTH, 4, 2, DFF)
        avs[:, 4 * c:4 * c + 4] = r["o_avs"].reshape(DEPTH, 4, 16, 256)
    return (y, ys, kp, vp, kip, rp, cp, ks, vs, kis, rs, cs, avs)


_CACHE = {}


def kernel(**inputs):
    inp = {k: np.asarray(v, dtype=np.float32) for k, v in inputs.items()}
    stage = _CACHE.get("stage", 99)
    if "nc" not in _CACHE:
        bld = Builder(stage)
        _CACHE["nc"] = bld.build()
        _CACHE["bld"] = bld
    nc = _CACHE["nc"]
    maps = make_inputs(inp)
    res = run_bass_kernel_spmd(nc, maps, core_ids=list(range(8)))
    _CACHE["raw"] = res.results
    return assemble(res.results)
```

```python
import numpy as np
import os
EXP = os.environ.get('EXP', '')
from contextlib import ExitStack
import concourse.bass as bass
import concourse.mybir as mybir
from concourse.bass_utils import run_bass_kernel_spmd

F32 = mybir.dt.float32
BF16 = mybir.dt.bfloat16
AF = mybir.ActivationFunctionType
ALU = mybir.AluOpType
AX = mybir.AxisListType

D = 1024
NT = 16
TS = 128
NTOK = NT * TS
NS = 64
DFF = 2816
NFC = DFF // 128
DEPTH = 2
ALPHA = (2 * DEPTH) ** 0.25
EPS = 1e-5
GROUPS = [[0, 1, 2, 3], [4, 5, 6, 7]]
NEG = -1.0e30


class Prog:
    NSLOT = 8

    def __init__(self, nc):
        self.nc = nc
        self.ops = []
        self.buf = {}
        self.slot_last = {}
        self.slot_rr = {"sp": 0, "pool": 0, "act": 0}
        self.last = {}
        self.dma_pending = []

    def _deps(self, eng, r, w):
        deps = set()
        for k in r:
            b = self.buf.setdefault(k, [None, []])
            if b[0] is not None:
                deps.add(b[0])
            if isinstance(k, str) and k.startswith("ps"):
                deps.update(d for d in b[1] if self.ops[d]["eng"] != eng)
        for k in w:
            b = self.buf.setdefault(k, [None, []])
            if b[0] is not None:
                deps.add(b[0])
            deps.update(b[1])
        if eng == "pe":
            deps = {d for d in deps if not (self.ops[d]["eng"] == "pe" and self.ops[d]["kind"] == "op")}
        return deps

    def _record(self, oid, r, w):
        me = self.ops[oid]
        for k in r:
            rl = self.buf[k][1]
            if me["kind"] == "op":
                rl[:] = [d for d in rl if not (self.ops[d]["kind"] == "op" and self.ops[d]["eng"] == me["eng"])]
            rl.append(oid)
        for k in w:
            self.buf[k] = [oid, []]

    def op(self, eng, fn, r=(), w=()):
        deps = self._deps(eng, r, w)
        oid = len(self.ops)
        self.ops.append(dict(eng=eng, kind="op", fn=fn, deps=deps))
        self._record(oid, r, w)
        self.last[eng] = oid
        return oid

    def barrier(self):
        ids = set(self.last.values()) | set(self.dma_pending)
        for e in ["pe", "act", "dve", "pool", "sp"]:
            self.ops.append(dict(eng=e, kind="bar", deps=set(ids)))
        self.buf = {}
        self.dma_pending = []

    def dma(self, q, out, in_, r=(), w=()):
        deps = self._deps(q, r, w)
        slot = self.slot_rr[q] % self.NSLOT
        self.slot_rr[q] += 1
        prev = self.slot_last.get((q, slot))
        if prev is not None:
            deps.add(prev)
        oid = len(self.ops)
        self.ops.append(dict(eng=q, kind="dma", out=out, in_=in_, deps=deps, slot=slot))
        self.slot_last[(q, slot)] = oid
        self._record(oid, r, w)
        self.dma_pending.append(oid)
        return oid

    def cc(self, ins, outs, r=(), w=()):
        deps = self._deps("pool", r, w)
        oid = len(self.ops)
        self.ops.append(dict(eng="pool", kind="cc", ins=ins, outs=outs, deps=deps))
        self._record(oid, r, w)
        self.dma_pending.append(oid)
        return oid

    def emit(self, stack):
        nc = self.nc
        ops = self.ops
        needed = set()
        for o in ops:
            needed.update(o["deps"])
        engs = ["pe", "act", "dve", "pool", "sp"]
        sem_e = {e: stack.enter_context(nc.semaphore("pg_" + e)) for e in engs}
        sem_d = {(q, s): stack.enter_context(nc.semaphore(f"dq_{q}{s}"))
                 for q in ("sp", "pool", "act") for s in range(self.NSLOT)}
        sem_cc = stack.enter_context(nc.semaphore("ccsem"))
        cnt = {e: 0 for e in engs}
        dcnt = {k: 0 for k in sem_d}
        ccn = 0
        ev = {}
        for i, o in enumerate(ops):
            if o["kind"] == "op":
                if i in needed:
                    cnt[o["eng"]] += 1
                    ev[i] = (sem_e[o["eng"]], cnt[o["eng"]], ("e", o["eng"]))
            elif o["kind"] == "dma":
                k = (o["eng"], o["slot"])
                dcnt[k] += 16
                ev[i] = (sem_d[k], dcnt[k], ("d",) + k)
            elif o["kind"] == "bar":
                pass
            else:
                ccn += 1
                ev[i] = (sem_cc, ccn, ("c",))
        per = {e: [] for e in engs}
        known = {e: {} for e in engs}
        for i, o in enumerate(ops):
            e = o["eng"]
            best = {}
            for d in o["deps"]:
                s, v, key = ev[d]
                if known[e].get(key, 0) >= v:
                    continue
                if key not in best or best[key][1] < v:
                    best[key] = (s, v)
            for key, (s, v) in best.items():
                per[e].append(("wait", s, v))
                known[e][key] = v
            per[e].append(("ins", i))
        for (q, s), c in dcnt.items():
            if c:
                per[q].append(("wait", sem_d[(q, s)], c))
        if ccn:
            per["pool"].append(("wait", sem_cc, ccn))
        if os.environ.get("DUMP"):
            for e in engs:
                print("ENGINE", e)
                for it in per[e]:
                    if it[0] == "wait":
                        print("   wait", it[1], it[2])
                    else:
                        o = ops[it[1]]
                        print("   ", it[1], o["kind"], o.get("tag", ""), "sig" if it[1] in ev else "", ev.get(it[1], ("", ""))[1])
        self.stats = {e: sum(1 for x in per[e] if x[0] == "ins") for e in engs}
        self.stats["sem"] = dict(cnt)

        if os.environ.get("CHECK"):
            semv = {}
            pos = {e: 0 for e in engs}
            prog = True
            while prog:
                prog = False
                for e in engs:
                    while pos[e] < len(per[e]):
                        it = per[e][pos[e]]
                        if it[0] == "wait":
                            if semv.get(it[1].num, 0) >= it[2]:
                                pos[e] += 1
                                prog = True
                            else:
                                break
                        else:
                            i = it[1]
                            if i in ev:
                                o = ops[i]
                                inc = 16 if o["kind"] == "dma" else 1
                                semv[ev[i][0].num] = semv.get(ev[i][0].num, 0) + inc
                            pos[e] += 1
                            prog = True
            for e in engs:
                if pos[e] < len(per[e]):
                    it = per[e][pos[e]]
                    print("DEADLOCK", e, "at", pos[e], "/", len(per[e]), it[0], it[1] if it[0] == "wait" else "", it[2] if it[0] == "wait" else "",
                          "have", semv.get(it[1].num, 0) if it[0] == "wait" else "")
            print("CHECK done", {e: (pos[e], len(per[e])) for e in engs})

        def run(E, lst):
            for it in lst:
                if it[0] == "wait":
                    E.wait_ge(it[1], it[2])
                else:
                    i = it[1]
                    o = ops[i]
                    if o["kind"] == "op":
                        ins = o["fn"](E)
                        if i in ev:
                            ins.then_inc(ev[i][0], 1)
                    elif o["kind"] == "dma":
                        E.dma_start(out=o["out"], in_=o["in_"]).then_inc(ev[i][0], 16)
                    elif o["kind"] == "bar":
                        pass
                    else:
                        E.collective_compute("AllGather", ALU.bypass, replica_groups=GROUPS,
                                             ins=[a.opt() for a in o["ins"]], outs=[a.opt() for a in o["outs"]]).then_inc(ev[i][0])

        block = stack.enter_context(nc.Block())

        @block.sync
        def _(E):
            run(E, per["sp"])

        @block.scalar
        def _(E):
            run(E, per["act"])

        @block.vector
        def _(E):
            run(E, per["dve"])

        @block.gpsimd
        def _(E):
            run(E, per["pool"])

        @block.tensor
        def _(E):
            run(E, per["pe"])


class Seq:
    def __init__(self, name, n, ntile, off):
        self.name, self.n, self.ntile, self.off = name, n, ntile, off
        self.G = 4
        self.ng = ntile // 4
        self.N = 4 * n


PR = Seq("p", 128, NT, 0)
SM = Seq("s", 16, 4, NTOK)
NBIS = 16
CH = dict(ua=0, va=2, qc=4, qcs=7, kc=10, kcs=13, gc=16, vc=19, qb=22, qib=25, wo=27)


class Builder:
    def __init__(self, stage=99):
        self.stage = stage
        self.nc = bass.Bass("TRN2", target_bir_lowering=False)
        self.P = Prog(self.nc)
        self.stack = ExitStack()
        self.rr = 0
        self.wi = 0

    def din(self, name, shape, dt=F32):
        return self.nc.dram_tensor(name, list(shape), dt, kind="ExternalInput").ap()

    def dout(self, name, shape, dt=F32):
        return self.nc.dram_tensor(name, list(shape), dt, kind="ExternalOutput").ap()

    def dscr(self, name, shape, dt=F32):
        return self.nc.dram_tensor(name, list(shape), dt).ap()

    def sb(self, name, shape, dt=F32, st=None):
        self.uid = getattr(self, "uid", 0) + 1
        return (st or self.stack).enter_context(self.nc.sbuf_tensor(f"s{self.uid}_{name}", list(shape), dt))

    def ps(self, name):
        return self.stack.enter_context(self.nc.psum_tensor(name, [128, 512], F32))

    def mm(self, out, lhsT, rhs, start, stop, r, w):
        self.P.op("pe", lambda E: E.matmul(out, lhsT, rhs, start=start, stop=stop, skip_group_check=True), r=r, w=w)

    def tr(self, out, in_, ident, r, w):
        self.P.op("pe", lambda E: E.transpose(out, in_, ident), r=r, w=w)

    def copy(self, eng, out, in_, r, w):
        if eng == "act":
            self.P.op("act", lambda E: E.activation(out=out, in_=in_, func=AF.Copy), r=r, w=w)
        else:
            self.P.op(eng, lambda E: E.tensor_copy(out=out, in_=in_), r=r, w=w)

    def act(self, out, in_, func, r, w, bias=0.0, scale=1.0):
        self.P.op("act", lambda E: E.activation(out=out, in_=in_, func=func, bias=bias, scale=scale), r=r, w=w)

    def tt(self, eng, out, in0, in1, op, r, w):
        self.P.op(eng, lambda E: E.tensor_tensor(out=out, in0=in0, in1=in1, op=op), r=r, w=w)

    def ts(self, eng, out, in0, s1, s2, op0, op1=None, r=(), w=(), accum_out=None):
        kw = {}
        if accum_out is not None:
            kw["accum_out"] = accum_out
        if op1 is None:
            self.P.op(eng, lambda E: E.tensor_scalar(out=out, in0=in0, scalar1=s1, scalar2=None, op0=op0, **kw), r=r, w=w)
        else:
            self.P.op(eng, lambda E: E.tensor_scalar(out=out, in0=in0, scalar1=s1, scalar2=s2, op0=op0, op1=op1, **kw), r=r, w=w)

    def stt(self, eng, out, in0, scalar, in1, op0, op1, r, w):
        self.P.op(eng, lambda E: E.scalar_tensor_tensor(out=out, in0=in0, scalar=scalar, in1=in1, op0=op0, op1=op1), r=r, w=w)

    def red(self, out, in_, op, r, w):
        self.P.op("dve", lambda E: E.tensor_reduce(out=out, in_=in_, axis=AX.X, op=op), r=r, w=w)

    def memset(self, eng, ap, val, w):
        self.P.op(eng, lambda E: E.memset(ap, val), r=(), w=w)

    def alt(self):
        self.rr += 1
        return "act" if self.rr % 2 else "dve"

    def wchunk(self, src):
        i = self.wi % len(self.wring)
        self.wi += 1
        t = self.wring[i]
        self.P.dma("pool", t[:], src, w=[f"wr{i}"])
        return t, f"wr{i}"

    def proj(self, pst, psk, wt, wk, c0, M, xb, xk, N, po=0):
        for kc in range(8):
            self.mm(pst[po:po + M, 0:N], wt[:, kc, c0:c0 + M], xb[:, kc, 0:N], kc == 0, kc == 7, r=[wk, xk], w=[psk])

    def stats(self, xs, keys, N, ones, tag):
        pm, pe2 = self.pss[5], self.pss[6]
        nx = len(xs)
        for i, (x, k) in enumerate(zip(xs, keys)):
            sq = self.sqb[i % 2]
            self.act(sq[:, 0:N], x, AF.Square, r=[k], w=[f"sqb{i % 2}"])
            self.mm(pm[:, 0:N], ones[:], x, i == 0, i == nx - 1, r=[k, "consts"], w=["ps5"])
            self.mm(pe2[:, 0:N], ones[:], sq[:, 0:N], i == 0, i == nx - 1, r=[f"sqb{i % 2}", "consts"], w=["ps6"])
        mean, rstd, tmp = self.st_mean, self.st_rstd, self.st_tmp
        self.act(tmp[:, 0:N], pm[:, 0:N], AF.Square, r=["ps5"], w=["st_tmp"])
        self.copy("act", mean[:, 0:N], pm[:, 0:N], r=["ps5"], w=["st_mean"])
        self.tt("dve", tmp[:, 0:N], pe2[:, 0:N], tmp[:, 0:N], ALU.subtract, r=["ps6", "st_tmp"], w=["st_tmp"])
        self.ts("dve", tmp[:, 0:N], tmp[:, 0:N], 0.0, EPS, ALU.max, ALU.add, r=["st_tmp"], w=["st_tmp"])
        self.act(tmp[:, 0:N], tmp[:, 0:N], AF.Sqrt, r=["st_tmp"], w=["st_tmp"])
        self.P.op("dve", lambda E: E.reciprocal(out=rstd[:, 0:N], in_=tmp[:, 0:N]), r=["st_tmp"], w=["st_rstd"])
        return mean, rstd

    def build(self):
        nc, P = self.nc, self.P
        st = self.stage
        S = self.stack
        xp = self.din("xp", [NTOK, D])
        xs_in = self.din("xs", [NS, D])
        identf = self.din("identf", [128, 128])
        cosd = self.din("cosT", [128, NTOK + NS])
        sind = self.din("sinT", [128, NTOK + NS])
        kdecd = self.din("kdec", [2, 128, 384])
        dtd = self.din("dtT", [128, 6, 128])
        qdecPd = self.din("qdecP", [128, 3, 512])
        qdecSd = self.din("qdecS", [128, 3, 64])
        seld = self.din("sel", [128, 768])
        sel16d = self.din("sel16", [16, 96])
        cmd = self.din("cm", [128, 8, 128])
        e16d = self.din("e16", [128, 16])
        bd64d = self.din("bd64", [128, 128])
        onesd = self.din("onesd", [128, 128])
        umd = self.din("umask", [128, 128])
        dmaskd = self.din("dmask", [128, 512])
        rcoefd = self.din("rcoef", [128, 9, 192])
        sel5d = self.din("sel5", [128, 5])
        cdecSd = self.din("cdecS", [128, 192])
        wA = [self.din(f"wA{l}", [128, 8, 1416]) for l in range(DEPTH)]
        wB = [self.din(f"wB{l}", [35, 128, 8, 128]) for l in range(DEPTH)]
        wCg = [self.din(f"wCg{l}", [NFC, 128, 8, 128]) for l in range(DEPTH)]
        wCu = [self.din(f"wCu{l}", [NFC, 128, 8, 128]) for l in range(DEPTH)]
        wCd = [self.din(f"wCd{l}", [8, 128, NFC, 128]) for l in range(DEPTH)]
        vecd = [self.din(f"vec{l}", [128, 127]) for l in range(DEPTH)]
        bsPd = [self.din(f"bsP{l}", [128, 2, 512]) for l in range(DEPTH)]
        bsSd = [self.din(f"bsS{l}", [128, 2, 64]) for l in range(DEPTH)]
        wmTd = [self.din(f"wmT{l}", [128, 4, 128]) for l in range(DEPTH)]
        ckd = [self.din(f"ck{l}", [4, 2048, 64]) for l in range(DEPTH)]
        cvd = [self.din(f"cv{l}", [4, 2048, 64]) for l in range(DEPTH)]
        ckid = [self.din(f"cki{l}", [4, 2048, 32]) for l in range(DEPTH)]
        stSd = [self.din(f"stS{l}", [128, 4, 192]) for l in range(DEPTH)]
        cvpd = [self.din(f"cvp{l}", [128, NFC, 4, 2]) for l in range(DEPTH)]
        o_y = self.dout("o_y", [NTOK, D])
        o_ys = self.dout("o_ys", [NS, D])
        o_k = self.dout("o_k", [DEPTH, NTOK, 64])
        o_v = self.dout("o_v", [DEPTH, NTOK, 64])
        o_ki = self.dout("o_ki", [DEPTH, NTOK, 32])
        o_ret = self.dout("o_ret", [DEPTH, 128, 192])
        o_conv = self.dout("o_conv", [DEPTH, 128, NFC, 2])
        o_ks = self.dout("o_ks", [DEPTH, NS, 64])
        o_vs = self.dout("o_vs", [DEPTH, NS, 64])
        o_kis = self.dout("o_kis", [DEPTH, NS, 32])
        o_rets = self.dout("o_rets", [DEPTH, 128, 4, 192])
        o_convs = self.dout("o_convs", [DEPTH, 128, NFC, 4, 2])
        o_avs = self.dout("o_avs", [DEPTH, NS, 256])
        XT = self.dscr("XT", [128, 8, NTOK + NS])
        X1T = self.dscr("X1T", [128, 8, NTOK + NS])
        bncA32 = self.dscr("bncA", [NT, 10240])
        gatA32 = self.dscr("gatA", [4 * NT, 10240])
        bncA = bncA32.bitcast(BF16)
        gatA = gatA32.bitcast(BF16)
        bncK2 = [self.dscr(f"bncK{i}", [8 * 128, 192]) for i in range(2)]
        gatK2 = [self.dscr(f"gatK{i}", [4 * 8 * 128, 192]) for i in range(2)]
        bncB32 = self.dscr("bncB", [NT, 1024])
        gatB32 = self.dscr("gatB", [4 * NT, 1024])
        bncB = bncB32.bitcast(BF16)
        gatB = gatB32.bitcast(BF16)

        identF = self.sb("identF", [128, 128])
        identB = self.sb("identB", [128, 128], BF16)
        bd64 = self.sb("bd64", [128, 128])
        ones = self.sb("ones", [128, 128])
        P.dma("sp", identF[:], identf, w=["consts"])
        P.dma("sp", bd64[:], bd64d, w=["consts"])
        P.dma("sp", ones[:], onesd, w=["consts"])
        P.dma("pool", identB[:], identf, w=["consts"])
        kdecT = self.sb("kdecT", [128, 2, 384])
        P.dma("sp", kdecT[:], kdecd.rearrange("a p f -> p a f"), w=["consts"])
        wtok = {"p": self.sb("wtokp", [128, NT, 8]), "s": self.sb("wtoks", [128, 4, 8])}
        vecs = self.sb("vecs", [128, 127])
        self.pss = pss = [self.ps(f"ps{i}") for i in range(8)]
        self.sqb = [self.sb(f"sqb{i}", [128, 512]) for i in range(2)]
        self.st_mean = self.sb("st_mean", [128, 512])
        self.st_rstd = self.sb("st_rstd", [128, 512])
        self.st_tmp = self.sb("st_tmp", [128, 512])
        ktS = self.sb("ktS", [128, 4, 16], BF16)
        vS = self.sb("vS", [16, 4, 64], BF16)
        convo = self.sb("convo", [128, NFC, 2])
        convos = self.sb("convos", [128, NFC, 4, 2])
        V_AG, V_AB, V_GN, V_L1G, V_L1B, V_L2G, V_L2B, V_CW, V_CB = 0, 2, 4, 7, 15, 23, 31, 39, 105

        with ExitStack() as ph:
            xin = [self.sb(f"xin{i}", [128, D], st=ph) for i in range(2)]
            xtg = [self.sb(f"xtg{i}", [128, 8, 128], st=ph) for i in range(2)]
            for sq_, src in ((PR, xp), (SM, xs_in)):
                n = sq_.n
                for m in range(sq_.ntile):
                    b = m % 2
                    P.dma("sp", xin[b][0:n, :], src[m * n:(m + 1) * n, :], w=[f"xin{b}"])
                    for c in range(8):
                        pb = pss[c % 4]
                        self.tr(pb[:, 0:n], xin[b][0:n, c * 128:(c + 1) * 128], identF[0:n, 0:n], r=[f"xin{b}", "consts"], w=[f"ps{c % 4}"])
                        self.copy(self.alt(), xtg[b][:, c, 0:n], pb[:, 0:n], r=[f"ps{c % 4}"], w=[f"xtg{b}"])
                    P.dma("sp", XT[:, :, sq_.off + m * n:sq_.off + (m + 1) * n], xtg[b][:, :, 0:n], r=[f"xtg{b}"], w=["XT"])
            P.barrier()

        for l in range(DEPTH if st >= 9 else 1):
            P.dma("sp", vecs[:], vecd[l], w=["vecs"])
            with ExitStack() as ph:
                WA = self.sb("WA", [128, 8, 1416], BF16, st=ph)
                for kc in range(8):
                    P.dma("pool", WA[:, kc, :], wA[l][:, kc, :], w=["WA"])
                xTf = [self.sb(f"xTf{i}", [128, 8, 128], st=ph) for i in range(2)]
                xTb = [self.sb(f"xTb{i}", [128, 8, 128], BF16, st=ph) for i in range(2)]
                vtok = self.sb("vtok", [128, 384], BF16, st=ph)
                vb16 = self.sb("vb16", [128, 64], BF16, st=ph)
                kvo = [self.sb(f"kvo{i}", [128, 168], st=ph) for i in range(2)]
                ktb = self.sb("ktb", [128, 128], BF16, st=ph)
                rt1 = self.sb("rt1", [128, 3, 128], st=ph)
                krot = self.sb("krot", [128, 3, 128], st=ph)
                kd = self.sb("kd", [128, 384], BF16, st=ph)
                kvsb = [self.sb(f"kvsb{i}", [128, 192], st=ph) for i in range(2)]
                csa = [self.sb(f"csa{i}", [128, 2, 128], st=ph) for i in range(2)]
                stS = self.sb("stS", [128, 4, 192], st=ph)
                cdecS = self.sb("cdecS", [128, 192], st=ph)
                P.dma("sp", stS[:], stSd[l], w=["stS"])
                P.dma("sp", cdecS[:], cdecSd, w=["cdecS"])
                it = 0
                for sq_ in (PR, SM):
                    n = sq_.n
                    ok, ov, oki = (o_k, o_v, o_ki) if sq_ is PR else (o_ks, o_vs, o_kis)
                    kdi = 0 if sq_ is PR else 1
                    for m in range(sq_.ntile):
                        b = it % 2
                        it += 1
                        cols = slice(sq_.off + m * n, sq_.off + (m + 1) * n)
                        rows = slice(m * n, (m + 1) * n)
                        P.dma("sp", xTf[b][:, :, 0:n], XT[:, :, cols], r=["XT"], w=[f"xTf{b}"])
                        self.copy("act", xTb[b][:, :, 0:n], xTf[b][:, :, 0:n], r=[f"xTf{b}"], w=[f"xTb{b}"])
                        xb = xTb[b]
                        rx = [f"xTb{b}", "WA"]
                        for kc in range(8):
                            self.mm(pss[0][0:n, 0:512], xb[:, kc, 0:n], WA[:, kc, 0:512], kc == 0, kc == 7, r=rx, w=["ps0"])
                        for kc in range(8):
                            self.mm(pss[1][0:n, 0:40], xb[:, kc, 0:n], WA[:, kc, 512:552], kc == 0, kc == 7, r=rx, w=["ps1"])
                        self.copy("act", vtok[0:n, :], pss[0][0:n, 0:384], r=["ps0"], w=["vtok"])
                        self.copy("act", vb16[0:n, :], pss[0][0:n, 448:512], r=["ps0"], w=["vb16"])
                        self.copy("dve", kvo[b][0:n, 0:128], pss[0][0:n, 384:512], r=["ps0"], w=[f"kvo{b}"])
                        self.copy("dve", kvo[b][0:n, 128:168], pss[1][0:n, 0:40], r=["ps1"], w=[f"kvo{b}"])
                        self.copy("dve", wtok[sq_.name][0:n, m, :], kvo[b][0:n, 160:168], r=[f"kvo{b}"], w=["wtok"])
                        P.dma("sp", ok[l, rows, :], kvo[b][0:n, 0:64], r=[f"kvo{b}"])
                        P.dma("sp", ov[l, rows, :], kvo[b][0:n, 64:128], r=[f"kvo{b}"])
                        P.dma("sp", oki[l, rows, :], kvo[b][0:n, 128:160], r=[f"kvo{b}"])
                        for kc in range(8):
                            self.mm(pss[2][0:64, 0:n], WA[:, kc, 552:616], xb[:, kc, 0:n], kc == 0, kc == 7, r=rx, w=["ps2"])
                        for kc in range(8):
                            self.mm(pss[2][64:96, 0:n], WA[:, kc, 616:648], xb[:, kc, 0:n], kc == 0, kc == 7, r=rx, w=["ps2"])
                        if sq_ is PR:
                            self.copy("act", ktb[0:96, 0:n], pss[2][0:96, 0:n], r=["ps2"], w=["ktb"])
                            P.dma("sp", bncA[m, 0:12288].rearrange("(d t) -> d t", t=128), ktb[0:96, :], r=["ktb"], w=["bncA"])
                            P.dma("sp", bncA[m, 12288:20480].rearrange("(s e) -> s e", e=64), vb16[:], r=["vb16"], w=["bncA"])
                        else:
                            self.copy("act", ktS[0:96, m, :], pss[2][0:96, 0:n], r=["ps2"], w=["ktS"])
                            self.copy("dve", vS[0:n, m, :], vb16[0:n, :], r=["vb16"], w=["vS"])
                        for p in range(3):
                            for kc in range(8):
                                self.mm(pss[3][:, p * 128:p * 128 + n], WA[:, kc, 648 + p * 128:648 + (p + 1) * 128], xb[:, kc, 0:n],
                                        kc == 0, kc == 7, r=rx, w=["ps3"])
                        for p in range(3):
                            for kc in range(8):
                                self.mm(pss[4][:, p * 128:p * 128 + n], WA[:, kc, 1032 + p * 128:1032 + (p + 1) * 128], xb[:, kc, 0:n],
                                        kc == 0, kc == 7, r=rx, w=["ps4"])
                        P.dma("sp", csa[b][:, 0, 0:n], cosd[:, cols], w=[f"csa{b}"])
                        P.dma("sp", csa[b][:, 1, 0:n], sind[:, cols], w=[f"csa{b}"])
                        cb = csa[b][:, 0, 0:n].unsqueeze(1).to_broadcast([128, 3, n])
                        sbb = csa[b][:, 1, 0:n].unsqueeze(1).to_broadcast([128, 3, n])
                        p3 = pss[3][:, 0:384].rearrange("p (c t) -> p c t", c=3)[:, :, 0:n]
                        p4 = pss[4][:, 0:384].rearrange("p (c t) -> p c t", c=3)[:, :, 0:n]
                        self.tt("dve", rt1[:, :, 0:n], p4, sbb, ALU.mult, r=["ps4", f"csa{b}"], w=["rt1"])
                        self.tt("dve", krot[:, :, 0:n], p3, cb, ALU.mult, r=["ps3", f"csa{b}"], w=["krot"])
                        self.tt("pool", krot[:, :, 0:n], krot[:, :, 0:n], rt1[:, :, 0:n], ALU.add, r=["krot", "rt1"], w=["krot"])
                        for p in range(3):
                            self.tr(pss[5][0:n, p * 128:(p + 1) * 128], krot[:, p, 0:n], identF[:], r=["krot", "consts"], w=["ps5"])
                        self.tt("dve", kd[0:n, :], pss[5][0:n, 0:384], kdecT[0:n, kdi, :], ALU.mult, r=["ps5", "consts"], w=["kd"])
                        for h in range(6):
                            po = (h % 2) * 64
                            self.mm(pss[6][po:po + 64, (h // 2) * 64:(h // 2) * 64 + 64], kd[0:n, h * 64:(h + 1) * 64], vtok[0:n, h * 64:(h + 1) * 64],
                                    True, True, r=["kd", "vtok"], w=["ps6"])
                        if sq_ is PR:
                            self.copy("act", kvsb[b][:], pss[6][:, 0:192], r=["ps6"], w=[f"kvsb{b}"])
                            P.dma("sp", bncK2[m // 8][(m % 8) * 128:(m % 8 + 1) * 128, :], kvsb[b][:], r=[f"kvsb{b}"], w=[f"bncK{m // 8}"])
                        else:
                            self.tt("dve", kvsb[b][:], stS[:, m, :], cdecS[:], ALU.mult, r=["stS", "cdecS"], w=[f"kvsb{b}"])
                            self.tt("dve", kvsb[b][:], kvsb[b][:], pss[6][:, 0:192], ALU.add, r=[f"kvsb{b}", "ps6"], w=[f"kvsb{b}"])
                            P.dma("sp", o_rets[l, :, m, :], kvsb[b][:], r=[f"kvsb{b}"])
                P.cc([bncA32], [gatA32], r=["bncA"], w=["gatA"])
                for i in range(2):
                    P.cc([bncK2[i]], [gatK2[i]], r=[f"bncK{i}"], w=[f"gatK{i}"])
                P.barrier()
            if st <= 1:
                break
            self.phaseB(l, locals())
            if st <= 5:
                break
            self.phaseC(l, locals())

        P.emit(self.stack)
        return nc
    def phaseB(self, l, L):
        P, pss = self.P, self.pss
        st = self.stage
        ones, bd64, identF, identB, vecs = L["ones"], L["bd64"], L["identF"], L["identB"], L["vecs"]
        XT, X1T, gatA, gatK2, bncB = L["XT"], L["X1T"], L["gatA"], L["gatK2"], L["bncB"]
        wB = L["wB"][l]
        V_AG, V_AB, V_GN, V_L1G, V_L1B = L["V_AG"], L["V_AB"], L["V_GN"], L["V_L1G"], L["V_L1B"]
        wtok = L["wtok"]
        with ExitStack() as pp:
            KTI = self.sb("KTI", [128, 16, 4, 128], BF16, st=pp)
            VA = self.sb("VA", [128, 64, 65], BF16, st=pp)
            rst = self.sb("rst", [128, 16, 192], BF16, st=pp)
            self.memset("dve", VA[:], 1.0, w=["VA"])
            for j in range(4):
                for m in range(NT):
                    P.dma("sp", KTI[0:96, m, j, :], gatA[j * 16 + m, 0:12288].rearrange("(d t) -> d t", t=128), r=["gatA"], w=["KTI"])
                    P.dma("sp", VA[:, m * 4 + j, 0:64], gatA[j * 16 + m, 12288:20480].rearrange("(s e) -> s e", e=64), r=["gatA"], w=["VA"])
            with ExitStack() as sc:
                rc = self.sb("rc", [128, 9, 192], st=sc)
                kvg = [self.sb(f"kvg{i}", [128, 4, 192], st=sc) for i in range(2)]
                Sst = self.sb("Sst", [128, 192], st=sc)
                ta = self.sb("ta", [128, 192], st=sc)
                tb = self.sb("tb", [128, 192], st=sc)
                P.dma("sp", rc[:], L["rcoefd"], w=["rc"])
                self.memset("dve", Sst[:], 0.0, w=["Sst"])
                for m in range(NT):
                    b = m % 2
                    P.dma("sp", kvg[b][:], gatK2[m // 8].rearrange("(j m p) f -> p j m f", j=4, m=8)[:, :, m % 8, :], r=[f"gatK{m // 8}"], w=[f"kvg{b}"])
                    self.tt("dve", ta[:], Sst[:], rc[:, 0, :], ALU.mult, r=["Sst", "rc"], w=["ta"])
                    for jp in range(3):
                        self.tt("dve", tb[:], kvg[b][:, jp, :], rc[:, 1 + jp, :], ALU.mult, r=[f"kvg{b}", "rc"], w=["tb"])
                        self.tt("dve", ta[:], ta[:], tb[:], ALU.add, r=["ta", "tb"], w=["ta"])
                    self.copy("dve", rst[:, m, :], ta[:], r=["ta"], w=["rst"])
                    self.tt("dve", ta[:], Sst[:], rc[:, 4, :], ALU.mult, r=["Sst", "rc"], w=["ta"])
                    for jp in range(4):
                        self.tt("dve", tb[:], kvg[b][:, jp, :], rc[:, 5 + jp, :], ALU.mult, r=[f"kvg{b}", "rc"], w=["tb"])
                        self.tt("dve", ta[:], ta[:], tb[:], ALU.add, r=["ta", "tb"], w=["ta"])
                    self.copy("dve", Sst[:], ta[:], r=["ta"], w=["Sst"])
                P.dma("sp", L["o_ret"][l], Sst[:], r=["Sst"])
                P.barrier()
            if st <= 2:
                return
            self.phaseB2(l, L, KTI, VA, rst)

    def phaseB2(self, l, L, KTI, VA, rst):
        P, pss = self.P, self.pss
        st = self.stage
        ones, bd64, identF, identB, vecs = L["ones"], L["bd64"], L["identF"], L["identB"], L["vecs"]
        XT, X1T, gatA, gatK2, bncB = L["XT"], L["X1T"], L["gatA"], L["gatK2"], L["bncB"]
        wB = L["wB"][l]
        V_AG, V_AB, V_GN, V_L1G, V_L1B = L["V_AG"], L["V_AB"], L["V_GN"], L["V_L1G"], L["V_L1B"]
        wtok = L["wtok"]
        with ExitStack() as ph:
            sb = lambda name, shape, dt=F32: self.sb(name, shape, dt, st=ph)
            self.wring = [sb(f"wr{i}", [128, 8, 128], BF16) for i in range(3)]
            scores = sb("scores", [128, 8192])
            junk = sb("junk", [128, 3840], mybir.dt.uint8)
            xfg = sb("xfg", [128, 8, 512])
            xbg = sb("xbg", [128, 8, 512], BF16)
            mixT = sb("mixT", [128, 8, 512], BF16)
            uT = sb("uT", [128, 2, 512])
            gT = sb("gT", [128, 2, 512])
            vtokA = sb("vtokA", [128, 4, 256], BF16)
            avf = scores[0:16, 0:1024].rearrange("p (t f) -> p t f", t=4)
            csg = sb("csg", [128, 2, 512])
            qrot = sb("qrot", [128, 3, 512], BF16)
            qd = sb("qd", [128, 3, 512], BF16)
            krot = sb("krotB", [128, 3, 512], BF16)
            sil = sb("sil", [128, 3, 512], BF16)
            vtokB = sb("vtokB", [128, 4, 384], BF16)
            Sm = sb("Sm", [128, 6, 128], BF16)
            ysb = sb("ysb", [128, 384])
            ycn = sb("ycn", [128, 384])
            DT = sb("DT", [128, 6, 128])
            qdecP = sb("qdecP", [128, 3, 128])
            qdecS = sb("qdecS", [128, 3, 16])
            qq = sb("qq", [128, 4096], BF16)
            Amat = sb("Amat", [128, 128], BF16)
            Wd = sb("Wd", [128, 8, 128], BF16)
            rz = [sb(f"rz{i}", [128, 512], BF16) for i in range(4)]
            pT = [sb(f"pT{i}", [128, 768], BF16) for i in range(2)]
            mb = [sb(f"mb{i}", [128, 128], BF16) for i in range(2)]
            col = sb("col", [128, 16])
            ob = sb("ob", [128, 384])
            Sel = sb("Sel", [128, 768], BF16)
            Sel16 = sb("Sel16", [16, 96], BF16)
            CM = sb("CM", [128, 8, 128], BF16)
            E16 = sb("E16", [128, 16])
            dmask = sb("dmask", [128, 512])
            WmT = sb("WmT", [128, 4, 128], BF16)
            wmf = scores[:, 0:512].rearrange("p (g i) -> p g i", g=4)
            um = scores[:, 512:640]
            bsP = sb("bsP", [128, 2, 128])
            bsS = sb("bsS", [128, 2, 16])
            qi2T = qq[64:96, :].rearrange("p (t h) -> p t h", h=8)
            P.dma("sp", DT[:], L["dtd"], w=["DT"])
            P.dma("sp", qdecP[:], L["qdecPd"][:, :, 0:128], w=["qdec"])
            P.dma("sp", qdecS[:], L["qdecSd"][:, :, 0:16], w=["qdec"])
            P.dma("pool", Sel[:], L["seld"], w=["Sel"])
            P.dma("pool", Sel16[:], L["sel16d"], w=["Sel"])
            P.dma("pool", CM[:], L["cmd"], w=["CM"])
            P.dma("sp", E16[:], L["e16d"], w=["E16"])
            P.dma("sp", dmask[:], L["dmaskd"], w=["dmask"])
            P.dma("sp", wmf, L["wmTd"][l], w=["scores"])
            P.dma("sp", um, L["umd"], w=["scores"])
            P.dma("sp", bsP[:], L["bsPd"][l][:, :, 0:128], w=["bs"])
            P.dma("sp", bsS[:], L["bsSd"][l][:, :, 0:16], w=["bs"])
            self.tt("dve", WmT[:], wmf, um.unsqueeze(1).to_broadcast([128, 4, 128]), ALU.mult, r=["scores"], w=["WmT"])

            def chunk(ci):
                return self.wchunk(wB[ci])

            def group(sq_, g, keysrc):
                n, N = sq_.n, sq_.N
                c0 = sq_.off + g * N
                gcols = slice(c0, c0 + N)
                xk = [f"xfg{c}" for c in range(8)]
                P.dma("sp", xfg[:, :, 0:N], XT[:, :, gcols], r=["XT"], w=xk)
                self.copy("act", xbg[:, :, 0:N], xfg[:, :, 0:N], r=xk, w=["xbg"])
                P.dma("sp", csg[:, 0, 0:N], L["cosd"][:, gcols], w=["csg"])
                P.dma("sp", csg[:, 1, 0:N], L["sind"][:, gcols], w=["csg"])
                bs = bsP if sq_ is PR else bsS
                qdec = qdecP if sq_ is PR else qdecS

                for ch in range(2):
                    wt, wk = chunk(CH["ua"] + ch)
                    self.proj(pss[ch], f"ps{ch}", wt, wk, 0, 128, xbg, "xbg", N)
                    self.act(uT[:, ch, 0:N], pss[ch][:, 0:N], AF.Gelu_apprx_tanh, r=[f"ps{ch}"], w=["uT"])
                for ch in range(2):
                    wt, wk = chunk(CH["va"] + ch)
                    self.proj(pss[ch], f"ps{ch}", wt, wk, 0, 128, xbg, "xbg", N)
                    self.act(gT[:, ch, 0:N], pss[ch][:, 0:N], AF.Gelu_apprx_tanh, r=[f"ps{ch}"], w=["gT"])
                if 'a2' in EXP:
                    return
                for ch in range(2):
                    mean, rstd = self.stats([gT[:, ch, 0:N]], ["gT"], N, bd64, "a")
                    self.tt("dve", gT[:, ch, 0:N], gT[:, ch, 0:N], mean[:, 0:N], ALU.subtract, r=["gT", "st_mean"], w=["gT"])
                    self.tt("dve", gT[:, ch, 0:N], gT[:, ch, 0:N], rstd[:, 0:N], ALU.mult, r=["gT", "st_rstd"], w=["gT"])
                    self.ts("dve", gT[:, ch, 0:N], gT[:, ch, 0:N], vecs[:, V_AG + ch:V_AG + ch + 1], vecs[:, V_AB + ch:V_AB + ch + 1],
                            ALU.mult, ALU.add, r=["gT", "vecs"], w=["gT"])
                if 'a3' in EXP:
                    return
                for t in range(4):
                    pb = pss[2 + t % 2]
                    for ch in range(2):
                        self.tr(pb[0:n, ch * 128:(ch + 1) * 128], gT[:, ch, t * n:(t + 1) * n], identF[:], r=["gT", "consts"], w=[f"ps{2 + t % 2}"])
                    self.copy(self.alt(), vtokA[0:n, t, :], pb[0:n, 0:256], r=[f"ps{2 + t % 2}"], w=["vtokA"])
                    if sq_ is SM:
                        self.copy("dve", avf[0:n, t, :], pb[0:n, 0:256], r=[f"ps{2 + t % 2}"], w=["scores"])
                if sq_ is SM:
                    P.dma("sp", L["o_avs"][l].rearrange("(b t) f -> t b f", t=16), avf, r=["scores"])
                if 'a4' in EXP:
                    return
                for t in range(4):
                    for gr in range(4):
                        po = (gr % 2) * 64
                        self.mm(pss[gr // 2][po:po + 64, t * n:(t + 1) * n], vtokA[0:n, t, gr * 64:(gr + 1) * 64], WmT[0:n, gr, 0:n],
                                True, True, r=["vtokA", "WmT"], w=[f"ps{gr // 2}"])
                if 'a5' in EXP:
                    return
                for ch in range(2):
                    bb = bs[:, ch, 0:n].unsqueeze(1).to_broadcast([128, 4, n])
                    self.copy("act", gT[:, ch, 0:N], pss[ch][:, 0:N], r=[f"ps{ch}"], w=["gT"])
                    g3 = gT[:, ch, 0:N].rearrange("p (t i) -> p t i", t=4)
                    self.tt("dve", g3, g3, bb, ALU.add, r=["gT", "bs"], w=["gT"])
                    if 'a6' in EXP:
                        continue
                    self.tt("dve", mixT[:, ch, 0:N], gT[:, ch, 0:N], uT[:, ch, 0:N], ALU.mult, r=["gT", "uT"], w=["mixT"])

                if 'A' in EXP:
                    return
                def rotary(cq, cqs, dst, with_qd):
                    for p in range(3):
                        wt, wk = chunk(cq + p)
                        self.proj(pss[0], "ps0", wt, wk, 0, 128, xbg, "xbg", N)
                        wt, wk = chunk(cqs + p)
                        self.proj(pss[1], "ps1", wt, wk, 0, 128, xbg, "xbg", N)
                        t1, t2 = self.sqb[0], self.sqb[1]
                        self.tt("dve", t1[:, 0:N], pss[0][:, 0:N], csg[:, 0, 0:N], ALU.mult, r=["ps0", "csg"], w=["sqb0"])
                        self.tt("dve", t2[:, 0:N], pss[1][:, 0:N], csg[:, 1, 0:N], ALU.mult, r=["ps1", "csg"], w=["sqb1"])
                        self.tt("pool", t1[:, 0:N], t1[:, 0:N], t2[:, 0:N], ALU.add, r=["sqb0", "sqb1"], w=["sqb0"])
                        self.copy("act", dst[:, p, 0:N], t1[:, 0:N], r=["sqb0"], w=["qrot" if with_qd else "krotB"])
                        if with_qd:
                            qb_ = qdec[:, p, 0:n].unsqueeze(1).to_broadcast([128, 4, n])
                            self.tt("dve", qd[:, p, 0:N].rearrange("p (t i) -> p t i", t=4), t1[:, 0:N].rearrange("p (t i) -> p t i", t=4), qb_,
                                    ALU.mult, r=["sqb0", "qdec"], w=["qd"])

                rotary(CH["qc"], CH["qcs"], qrot, True)
                if 'r1' in EXP:
                    return
                rotary(CH["kc"], CH["kcs"], krot, False)
                for p in range(3):
                    wt, wk = chunk(CH["gc"] + p)
                    self.proj(pss[p % 2], f"ps{p % 2}", wt, wk, 0, 128, xbg, "xbg", N)
                    self.act(sil[:, p, 0:N], pss[p % 2][:, 0:N], AF.Silu, r=[f"ps{p % 2}"], w=["sil"])
                for p in range(3):
                    wt, wk = chunk(CH["vc"] + p)
                    for t in range(4):
                        pb = pss[2 + t % 2]
                        for kc in range(8):
                            self.mm(pb[0:n, 0:128], xbg[:, kc, t * n:(t + 1) * n], wt[:, kc, :], kc == 0, kc == 7, r=["xbg", wk], w=[f"ps{2 + t % 2}"])
                        self.copy(self.alt(), vtokB[0:n, t, p * 128:(p + 1) * 128], pb[0:n, 0:128], r=[f"ps{2 + t % 2}"], w=["vtokB"])
                if 'r2' in EXP:
                    return
                for t in range(4):
                    tc_ = slice(t * n, (t + 1) * n)
                    m = g * 4 + t
                    for h in range(6):
                        po, p = (h % 2) * 64, h // 2
                        pb, pk = (pss[2], "ps2") if h % 2 == 0 else (pss[3], "ps3")
                        self.mm(pb[0:n, p * 128:p * 128 + n], krot[po:po + 64, p, tc_], qrot[po:po + 64, p, tc_], True, True, r=["krotB", "qrot"], w=[pk])
                    s1, s2 = self.sqb[0], self.sqb[1]
                    self.copy("act", s1[0:n, 0:384], pss[2][0:n, 0:384], r=["ps2"], w=["sqb0"])
                    self.copy("act", s2[0:n, 0:384], pss[3][0:n, 0:384], r=["ps3"], w=["sqb1"])
                    for h in range(6):
                        src, sk = (s1, "sqb0") if h % 2 == 0 else (s2, "sqb1")
                        p = h // 2
                        self.tt("dve", Sm[0:n, h, 0:n], src[0:n, p * 128:p * 128 + n], DT[0:n, h, 0:n], ALU.mult, r=[sk, "DT"], w=["Sm"])
                    if 'r3' in EXP:
                        continue
                    rsrc = keysrc["rst"](m)
                    for h in range(6):
                        po, p = (h % 2) * 64, h // 2
                        pb, pk = (pss[0], "ps0") if h % 2 == 0 else (pss[1], "ps1")
                        self.mm(pb[po:po + 64, p * 128:p * 128 + n], vtokB[0:n, t, h * 64:(h + 1) * 64], Sm[0:n, h, 0:n], True, False,
                                r=["vtokB", "Sm"], w=[pk])
                        self.mm(pb[po:po + 64, p * 128:p * 128 + n], rsrc[po:po + 64, p * 64:(p + 1) * 64], qd[po:po + 64, p, tc_], False, True,
                                r=["rst", "qd"], w=[pk])
                    if 'r4' in EXP:
                        continue
                    ysv = ysb[:, 0:3 * n].rearrange("p (c i) -> p c i", c=3)
                    self.copy("act", ysv[0:64], pss[0][0:64, 0:384].rearrange("p (c i) -> p c i", c=3)[:, :, 0:n], r=["ps0"], w=["ysb"])
                    self.copy("act", ysv[64:128], pss[1][64:128, 0:384].rearrange("p (c i) -> p c i", c=3)[:, :, 0:n], r=["ps1"], w=["ysb"])
                    mean, rstd = self.stats([ysb[:, 0:3 * n]], ["ysb"], 3 * n, bd64, "r")
                    self.tt("dve", ycn[:, 0:3 * n], ysb[:, 0:3 * n], mean[:, 0:3 * n], ALU.subtract, r=["ysb", "st_mean"], w=["ycn"])
                    self.tt("dve", ycn[:, 0:3 * n], ycn[:, 0:3 * n], rstd[:, 0:3 * n], ALU.mult, r=["ycn", "st_rstd"], w=["ycn"])
                    for p in range(3):
                        self.stt("dve", mixT[:, 5 + p, tc_], ycn[:, p * n:(p + 1) * n], vecs[:, V_GN + p:V_GN + p + 1], sil[:, p, tc_], ALU.mult, ALU.mult,
                                 r=["ycn", "vecs", "sil"], w=["mixT"])

                if 'R' in EXP:
                    return
                for c in range(3):
                    wt, wk = chunk(CH["qb"] + c)
                    for hh in range(2):
                        self.proj(pss[hh], f"ps{hh}", wt, wk, hh * 64, 64, xbg, "xbg", N)
                        q2v = qq[0:64, 0:24 * n].rearrange("p (t h i) -> p t h i", t=4, h=6)
                        self.copy(self.alt(), q2v[:, :, 2 * c + hh, :], pss[hh][0:64, 0:N].rearrange("p (t i) -> p t i", t=4), r=[f"ps{hh}"], w=["qq"])
                for c in range(2):
                    wt, wk = chunk(CH["qib"] + c)
                    for hh in range(4):
                        pb = pss[hh % 2]
                        self.proj(pb, f"ps{hh % 2}", wt, wk, hh * 32, 32, xbg, "xbg", N, po=64)
                        self.copy(self.alt(), qi2T[:, 0:N, 4 * c + hh], pb[64:96, 0:N], r=[f"ps{hh % 2}"], w=["qq"])
                for t in range(4):
                    self.dsa_tile(sq_, g, t, l, L, keysrc)

                if 'D' in EXP:
                    return
                for c in range(8):
                    wt, wk = chunk(CH["wo"] + c)
                    pb, pk = pss[c % 2], f"ps{c % 2}"
                    for kc in range(8):
                        self.mm(pb[:, 0:N], wt[:, kc, :], mixT[:, kc, 0:N], kc == 0, kc == 7, r=[wk, "mixT"], w=[pk])
                    self.stt("dve", xfg[:, c, 0:N], xfg[:, c, 0:N], ALPHA, pb[:, 0:N], ALU.mult, ALU.add, r=[xk[c], pk], w=[xk[c]])
                mean, rstd = self.stats([xfg[:, c, 0:N] for c in range(8)], xk, N, ones, "l1")
                for c in range(8):
                    self.tt("dve", xfg[:, c, 0:N], xfg[:, c, 0:N], mean[:, 0:N], ALU.subtract, r=[xk[c], "st_mean"], w=[xk[c]])
                    self.tt("pool", xfg[:, c, 0:N], xfg[:, c, 0:N], rstd[:, 0:N], ALU.mult, r=[xk[c], "st_rstd"], w=[xk[c]])
                    self.act(xfg[:, c, 0:N], xfg[:, c, 0:N], AF.Identity, r=[xk[c], "vecs"], w=[xk[c]],
                             scale=vecs[:, V_L1G + c:V_L1G + c + 1], bias=vecs[:, V_L1B + c:V_L1B + c + 1])
                P.dma("sp", X1T[:, :, gcols], xfg[:, :, 0:N], r=xk, w=["X1T"])
                if sq_ is PR:
                    bnd = self.bnd
                    for t in range(4):
                        self.copy("act", bnd[:, t, :, :], xfg[:, :, t * 128 + 126:t * 128 + 128], r=xk, w=["bnd"])
                    P.dma("sp", bncB[g * 4:(g + 1) * 4, :].rearrange("t (p x) -> p t x", x=16), bnd[:].rearrange("p t k c -> p t (k c)"),
                          r=["bnd"], w=["bncB"])

            self.bnd = sb("bnd", [128, 4, 8, 2], BF16)
            self._scores, self._junk = scores, junk
            self._junk2 = sb("junk2", [128, 4368], mybir.dt.uint8)
            self._sacc = sb("sacc", [128, 1])
            self._dsa = dict(Amat=Amat, Wd=Wd, rz=rz, pT=pT, mb=mb, col=col, ob=ob, Sel=Sel, Sel16=Sel16, CM=CM, E16=E16, dmask=dmask,
                             qq=qq, qi2T=qi2T, mixT=mixT, wtok=wtok, identF=identF, identB=identB)

            if 'a1' in EXP:
                P.barrier()
                return
            ksrc = dict(rst=lambda m: rst[:, m, :], kind="p", KTI=KTI, VA=VA)
            for g in range(PR.ng if st >= 4 else 1):
                group(PR, g, ksrc)
            P.barrier()
            if st <= 3:
                return
            with ExitStack() as ss:
                KTIs = self.sb("KTIs", [128, 2064], BF16, st=ss)
                VAs = self.sb("VAs", [128, 17, 65], BF16, st=ss)
                cK = self.sb("cK", [128, 16, 96], st=ss)
                rstS = self.sb("rstS", [128, 4, 192], BF16, st=ss)
                stSf = self.sb("stSf", [128, 4, 192], st=ss)
                P.dma("sp", stSf[:], L["stSd"][l], w=["stSf"])
                self.copy("act", rstS[:], stSf[:], r=["stSf"], w=["rst"])
                ksrc = dict(rst=lambda m: rstS[:, m, :], kind="s", KTIs=KTIs, VAs=VAs, cK=cK, ck=L["ckd"][l], cv=L["cvd"][l], cki=L["ckid"][l],
                            ktS=L["ktS"], vS=L["vS"])
                group(SM, 0, ksrc)
                P.barrier()

    def dsa_tile(self, sq_, g, t, l, L, ks):
        P, pss, d = self.P, self.pss, self._dsa
        n = sq_.n
        ng = n // 16
        m = g * 4 + t
        tc_ = slice(t * n, (t + 1) * n)
        scores, junk = self._scores, self._junk
        Amat, Wd, rz, pT, mb, col, ob = d["Amat"], d["Wd"], d["rz"], d["pT"], d["mb"], d["col"], d["ob"]
        identF, identB, mixT = d["identF"], d["identB"], d["mixT"]
        qi2T = d["qi2T"]
        q2f = d["qq"][0:64, 0:24 * n].rearrange("p (t x) -> p t x", t=4)
        SelX = d["Sel"] if n == 128 else d["Sel16"]
        if ks["kind"] == "p":
            KTI, VA = ks["KTI"], ks["VA"]
            kkey, vkey = "KTI", "VA"
            iblocks = [(KTI[64:96, kb].rearrange("p j t -> p (j t)"), 512, kb * 512) for kb in range(m + 1)]
            ablocks = [(KTI[0:64, kb, jj, :], VA[:, kb * 4 + jj, :], 128, (kb * 4 + jj) * 128) for kb in range(m + 1) for jj in range(4)]
            Lk = (m + 1) * 512
        else:
            KTIs, VAs, cK = ks["KTIs"], ks["VAs"], ks["cK"]
            kkey, vkey = "KTIs", "VAs"
            b = t
            for k in range(16):
                P.dma("sp", cK[:, k, 0:64], ks["ck"][b][k * 128:(k + 1) * 128, :], w=["cK"])
                P.dma("sp", cK[:, k, 64:96], ks["cki"][b][k * 128:(k + 1) * 128, :], w=["cK"])
            for k in range(16):
                pb, pk = pss[k % 2], f"ps{k % 2}"
                self.tr(pb[0:96, 0:128], cK[:, k, :], identF[:], r=["cK", "consts"], w=[pk])
                self.copy(self.alt(), KTIs[0:96, k * 128:(k + 1) * 128], pb[0:96, 0:128], r=[pk], w=["KTIs"])
            self.copy("dve", KTIs[0:96, 2048:2064], ks["ktS"][0:96, b, :], r=["ktS"], w=["KTIs"])
            self.memset("dve", VAs[:], 1.0, w=["VAs"])
            for k in range(16):
                P.dma("pool", VAs[:, k, 0:64], ks["cv"][b][k * 128:(k + 1) * 128, :], w=["VAs"])
            self.copy("dve", VAs[0:16, 16, 0:64], ks["vS"][0:16, b, :], r=["vS"], w=["VAs"])

            iblocks = [(KTIs[64:96, kb * 512:(kb + 1) * 512], 512, kb * 512) for kb in range(4)] + [(KTIs[64:96, 2048:2064], 16, 2048)]
            ablocks = [(KTIs[0:64, k * 128:(k + 1) * 128], VAs[:, k, :], 128, k * 128) for k in range(16)] + [(KTIs[0:64, 2048:2064], VAs[0:16, 16, :], 16, 2048)]
            Lk = 2064
        wv = d["wtok"][sq_.name]
        for a in range(16):
            self.ts("dve", Amat[0:n, a * 8:(a + 1) * 8], wv[0:n, m, :], d["E16"][0:n, a:a + 1], None, ALU.mult, r=["E16", "wtok"], w=["Amat"])
        self.mm(pss[4][:, 0:n], Amat[0:n, :], identB[0:n, 0:n], True, True, r=["Amat", "consts"], w=["ps4"])
        wall = self.sqb[1]
        self.copy("act", wall[:, 0:n], pss[4][:, 0:n], r=["ps4"], w=["sqb1"])
        for gq in range(ng):
            self.tt("dve", Wd[:, gq, 0:n], wall[:, 0:n], d["CM"][:, gq, 0:n], ALU.mult, r=["sqb1", "CM"], w=["Wd"])
        tmpm = self.sqb[0]
        steps = [(bi, gq) for bi in range(len(iblocks)) for gq in range(ng)]

        def emit_z(si):
            bi, gq = steps[si]
            rhs_ap, nk, c0 = iblocks[bi]
            zb = (2, 3, 0, 1)[si % 4]
            lhs = qi2T[:, t * n + 16 * gq:t * n + 16 * gq + 16, :].rearrange("p a h -> p (a h)")
            self.mm(pss[zb][:, 0:nk], lhs, rhs_ap, True, True, r=["qq", kkey], w=[f"ps{zb}"])

        for si in range(min(2, len(steps))):
            emit_z(si)
        for si, (bi, gq) in enumerate(steps):
            rhs_ap, nk, c0 = iblocks[bi]
            zb = (2, 3, 0, 1)[si % 4]
            pz, pzk = pss[zb], f"ps{zb}"
            rzb, rk = rz[si % 4], f"rz{si % 4}"
            if self.alt() == "act":
                self.act(rzb[:, 0:nk], pz[:, 0:nk], AF.Relu, r=[pzk], w=[rk])
            else:
                self.ts("dve", rzb[:, 0:nk], pz[:, 0:nk], 0.0, None, ALU.max, r=[pzk], w=[rk])
            self.mm(pss[4][0:n, 0:nk], Wd[:, gq, 0:n], rzb[:, 0:nk], gq == 0, gq == ng - 1, r=["Wd", rk], w=["ps4"])
            if si + 2 < len(steps):
                emit_z(si + 2)
            if gq == ng - 1:
                if ks["kind"] == "p" and bi == len(iblocks) - 1:
                    self.tt("dve", tmpm[0:n, 0:nk], pss[4][0:n, 0:nk], d["dmask"][0:n, 0:nk], ALU.subtract, r=["ps4", "dmask"], w=["sqb0"])
                    self.tt("dve", scores[0:n, c0:c0 + nk], pss[4][0:n, 0:nk], d["dmask"][0:n, 0:nk], ALU.add, r=["ps4", "dmask"], w=["scores"])
                else:
                    self.copy(self.alt(), scores[0:n, c0:c0 + nk], pss[4][0:n, 0:nk], r=["ps4"], w=["scores"])
        lo, w0, mid, cnt, gk, t5 = (col[0:n, i:i + 1] for i in range(6))
        if ks["kind"] == "p":
            self.red(lo, tmpm[0:n, 0:512], ALU.min, r=["sqb0"], w=["col"])
            if m > 0:
                self.red(t5, scores[0:n, 0:m * 512], ALU.min, r=["scores"], w=["col"])
                self.tt("dve", lo, lo, t5, ALU.min, r=["col"], w=["col"])
        else:
            self.red(lo, scores[0:n, 0:Lk], ALU.min, r=["scores"], w=["col"])
        self.red(w0, scores[0:n, 0:Lk], ALU.max, r=["scores"], w=["col"])
        self.tt("dve", w0, w0, lo, ALU.subtract, r=["col"], w=["col"])
        Ld = (Lk * 15 // 32) // 16 * 16 if Lk >= 1024 else Lk
        La = Lk - Ld
        sc_ap, jk_ap = scores[0:n, 0:Ld], junk[0:n, 0:Ld]
        if La:
            sa_ap, ja_ap = scores[0:n, Ld:Lk], self._junk2[0:n, 0:La]
            sacc = self._sacc[0:n, 0:1]
        thr = 255.5 - 0.5 * La
        for k in range(NBIS):
            hw = 2.0 ** (-(k + 1))
            self.ts("dve", mid, w0, hw, lo, ALU.mult, ALU.add, r=["col"], w=["mid"])
            P.op("dve", lambda E: E.tensor_scalar(out=jk_ap, in0=sc_ap, scalar1=mid, scalar2=None, op0=ALU.is_ge, op1=ALU.add, accum_out=cnt),
                 r=["scores", "mid"], w=["junk", "col"])
            if La:
                P.op("act", lambda E: E.activation(out=ja_ap, in_=sa_ap, func=AF.Sign, bias=mid, scale=-1.0, accum_out=sacc),
                     r=["scores", "mid"], w=["junk2", "sacc"])
                self.stt("dve", cnt, sacc, -0.5, cnt, ALU.mult, ALU.add, r=["col", "sacc"], w=["col"])
            self.ts("dve", gk, cnt, thr, hw, ALU.is_ge, ALU.mult, r=["col"], w=["col"])
            self.stt("dve", lo, gk, w0, lo, ALU.mult, ALU.add, r=["col", "mid"], w=["col"])
        pO = pss[7]
        nb = len(ablocks)

        def emit_logits(bi):
            kt, v, nk, c0 = ablocks[bi]
            i2 = bi % 2
            mbb, mk = mb[i2], f"mb{i2}"
            self.ts("dve", mbb[0:n, 0:nk], scores[0:n, c0:c0 + nk], lo, -30000.0, ALU.is_lt, ALU.mult, r=["scores", "col"], w=[mk])
            la, lb = (5, 6) if i2 == 0 else (0, 1)
            self.mm(pss[la][0:nk, 0:4 * n], kt, q2f[:, t, 0:4 * n], True, False, r=[kkey, "qq"], w=[f"ps{la}"])
            self.mm(pss[la][0:nk, 0:4 * n], mbb[0:n, 0:nk], SelX[0:n, 0:4 * n], False, True, r=[mk, "Sel"], w=[f"ps{la}"])
            self.mm(pss[lb][0:nk, 0:2 * n], kt, q2f[:, t, 4 * n:6 * n], True, False, r=[kkey, "qq"], w=[f"ps{lb}"])
            self.mm(pss[lb][0:nk, 0:2 * n], mbb[0:n, 0:nk], SelX[0:n, 4 * n:6 * n], False, True, r=[mk, "Sel"], w=[f"ps{lb}"])

        emit_logits(0)
        for bi, (kt, v, nk, c0) in enumerate(ablocks):
            if bi + 1 < nb:
                emit_logits(bi + 1)
            i2 = bi % 2
            la, lb = (5, 6) if i2 == 0 else (0, 1)
            ptb, pk = pT[i2], f"pT{i2}"
            self.act(ptb[0:nk, 0:4 * n], pss[la][0:nk, 0:4 * n], AF.Exp, r=[f"ps{la}"], w=[pk], scale=0.125)
            self.act(ptb[0:nk, 4 * n:6 * n], pss[lb][0:nk, 0:2 * n], AF.Exp, r=[f"ps{lb}"], w=[pk], scale=0.125)
            for h in range(6):
                self.mm(pO[0:n, h * 65:(h + 1) * 65], ptb[0:nk, h * n:(h + 1) * n], v[0:nk, :], bi == 0 and h == 0, bi == nb - 1 and h == 5,
                        r=[pk, vkey], w=["ps7"])
        posb = self.sqb[1]
        self.copy("act", posb[0:n, 0:390], pO[0:n, 0:390], r=["ps7"], w=["sqb1"])
        pov = posb[0:n, 0:390].rearrange("p (h e) -> p h e", e=65)
        rden = col[0:n, 8:14]
        for h in range(6):
            P.op("dve", (lambda hh: (lambda E: E.reciprocal(out=col[0:n, 8 + hh:9 + hh], in_=posb[0:n, hh * 65 + 64:hh * 65 + 65])))(h), r=["sqb1"], w=["col"])
        for h in range(6):
            self.ts("dve", ob[0:n, h * 64:(h + 1) * 64], posb[0:n, h * 65:h * 65 + 64], col[0:n, 8 + h:9 + h], None, ALU.mult, r=["sqb1", "col"], w=["ob"])
        for c in range(3):
            pb, pk = pss[c % 2], f"ps{c % 2}"
            self.tr(pb[:, 0:n], ob[0:n, c * 128:(c + 1) * 128], identF[0:n, 0:n], r=["ob", "consts"], w=[pk])
            self.copy(self.alt(), mixT[:, 2 + c, tc_], pb[:, 0:n], r=[pk], w=["mixT"])

    def phaseC(self, l, L):
        P, pss = self.P, self.pss
        st = self.stage
        ones, identF, vecs = L["ones"], L["identF"], L["vecs"]
        XT, X1T, gatB, bncB32, gatB32 = L["XT"], L["X1T"], L["gatB"], L["bncB32"], L["gatB32"]
        wCg, wCu, wCd = L["wCg"][l], L["wCu"][l], L["wCd"][l]
        V_L2G, V_L2B, V_CW, V_CB = L["V_L2G"], L["V_L2B"], L["V_CW"], L["V_CB"]
        convo, convos = L["convo"], L["convos"]
        last = (l == DEPTH - 1)
        P.cc([bncB32], [gatB32], r=["bncB"], w=["gatB"])
        P.barrier()
        with ExitStack() as ph:
            sb = lambda name, shape, dt=F32: self.sb(name, shape, dt, st=ph)
            self.wring = [sb(f"wr{i}", [128, 8, 128], BF16) for i in range(4)]
            wdr = [sb(f"wdr{i}", [128, NFC, 128], BF16) for i in range(2)]
            x1f = sb("x1f", [128, 8, 512])
            x1b = sb("x1b", [128, 8, 512], BF16)
            hT = sb("hT", [128, NFC, 512], BF16)
            hgx = sb("hgx", [128, 4, 130])
            c1 = sb("c1", [128, 512])
            gl = sb("gl", [128, 512])
            prevb = sb("prevb", [128, 8, 4, 2], BF16)
            gbt = sb("gbt", [128, 5, 4, 16], BF16)
            acc = sb("acc", [128, 4, 16])
            tmp8 = sb("tmp8", [128, 8])
            sel5 = sb("sel5", [128, 5])
            cvpS = sb("cvpS", [128, NFC, 4, 2])
            ytok = [sb(f"ytok{i}", [128, D]) for i in range(2)]
            P.dma("sp", sel5[:], L["sel5d"], w=["sel5"])
            P.dma("sp", cvpS[:], L["cvpd"][l], w=["cvpS"])
            gBv = gatB.rearrange("r (p x) -> p r x", x=16)
            yi = 0
            for sq_ in (PR, SM):
                n, N = sq_.n, sq_.N
                for g in range(sq_.ng):
                    c0 = sq_.off + g * N
                    gcols = slice(c0, c0 + N)
                    xk = [f"x1f{c}" for c in range(8)]
                    P.dma("sp", x1f[:, :, 0:N], X1T[:, :, gcols], r=["X1T"], w=xk)
                    self.copy("act", x1b[:, :, 0:N], x1f[:, :, 0:N], r=xk, w=["x1b"])
                    if sq_ is PR:
                        for k in range(4):
                            P.dma("sp", gbt[:, k, :, :], gBv[:, k * 16 + 4 * g:k * 16 + 4 * g + 4, :], r=["gatB"], w=["gbt"])
                        if g == 0:
                            self.memset("dve", gbt[:, 4, 0, :], 0.0, w=["gbt"])
                            P.dma("sp", gbt[:, 4, 1:4, :], gBv[:, 48:51, :], r=["gatB"], w=["gbt"])
                        else:
                            P.dma("sp", gbt[:, 4, :, :], gBv[:, 48 + 4 * g - 1:48 + 4 * g + 3, :], r=["gatB"], w=["gbt"])
                        self.ts("dve", acc[:], gbt[:, 0, :, :], sel5[:, 0:1], None, ALU.mult, r=["gbt", "sel5"], w=["acc"])
                        for k in range(1, 5):
                            self.stt("dve", acc[:], gbt[:, k, :, :], sel5[:, k:k + 1], acc[:], ALU.mult, ALU.add, r=["gbt", "sel5", "acc"], w=["acc"])
                        for t in range(4):
                            self.copy("dve", prevb[:, :, t, :], acc[:, t, :].rearrange("p (k c) -> p k c", c=2), r=["acc"], w=["prevb"])
                    for f in range(NFC):
                        wg, wgk = self.wchunk(wCg[f])
                        wu, wuk = self.wchunk(wCu[f])
                        for kc in range(8):
                            self.mm(pss[0][:, 0:N], wg[:, kc, :], x1b[:, kc, 0:N], kc == 0, kc == 7, r=[wgk, "x1b"], w=["ps0"])
                        if sq_ is PR:
                            for kc in range(8):
                                self.mm(pss[2][:, 0:8], wg[:, kc, :], prevb[:, kc, :, :].rearrange("p t c -> p (t c)"), kc == 0, kc == 7,
                                        r=[wgk, "prevb"], w=["ps2"])
                        for kc in range(8):
                            self.mm(pss[1][:, 0:N], wu[:, kc, :], x1b[:, kc, 0:N], kc == 0, kc == 7, r=[wuk, "x1b"], w=["ps1"])
                        for t in range(4):
                            self.copy("act", hgx[:, t, 2:2 + n], pss[0][:, t * n:(t + 1) * n], r=["ps0"], w=["hgx"])
                        if sq_ is PR:
                            self.copy("act", tmp8[:], pss[2][:, 0:8], r=["ps2"], w=["tmp8"])
                            self.copy("dve", hgx[:, :, 0:2], tmp8[:].rearrange("p (t c) -> p t c", c=2), r=["tmp8"], w=["hgx"])
                        else:
                            self.copy("dve", hgx[:, :, 0:2], cvpS[:, f, :, :], r=["cvpS"], w=["hgx"])
                        cw = lambda j: vecs[:, V_CW + f * 3 + j:V_CW + f * 3 + j + 1]
                        c1v = c1[:, 0:N].rearrange("p (t i) -> p t i", t=4)
                        self.ts("dve", c1v, hgx[:, :, 2:2 + n], cw(2), vecs[:, V_CB + f:V_CB + f + 1], ALU.mult, ALU.add, r=["hgx", "vecs"], w=["c1"])
                        self.stt("dve", c1v, hgx[:, :, 1:1 + n], cw(1), c1v, ALU.mult, ALU.add, r=["hgx", "vecs", "c1"], w=["c1"])
                        self.stt("dve", c1v, hgx[:, :, 0:n], cw(0), c1v, ALU.mult, ALU.add, r=["hgx", "vecs", "c1"], w=["c1"])
                        self.act(gl[:, 0:N], c1[:, 0:N], AF.Gelu_apprx_tanh, r=["c1"], w=["gl"])
                        self.tt("dve", hT[:, f, 0:N], gl[:, 0:N], pss[1][:, 0:N], ALU.mult, r=["gl", "ps1"], w=["hT"])
                        if sq_ is PR and g == 3:
                            self.copy("dve", convo[:, f, :], hgx[:, 3, n:n + 2], r=["hgx"], w=["convo"])
                        if sq_ is SM:
                            self.copy("dve", convos[:, f, :, :], hgx[:, :, n:n + 2], r=["hgx"], w=["convos"])
                    for c in range(8):
                        i = c % 2
                        P.dma("pool", wdr[i][:], wCd[c], w=[f"wdr{i}"])
                        pb, pk = pss[c % 2], f"ps{c % 2}"
                        for f in range(NFC):
                            self.mm(pb[:, 0:N], wdr[i][:, f, :], hT[:, f, 0:N], f == 0, f == NFC - 1, r=[f"wdr{i}", "hT"], w=[pk])
                        self.stt("dve", x1f[:, c, 0:N], x1f[:, c, 0:N], ALPHA, pb[:, 0:N], ALU.mult, ALU.add, r=[xk[c], pk], w=[xk[c]])
                    mean, rstd = self.stats([x1f[:, c, 0:N] for c in range(8)], xk, N, ones, "l2")
                    for c in range(8):
                        self.tt("dve", x1f[:, c, 0:N], x1f[:, c, 0:N], mean[:, 0:N], ALU.subtract, r=[xk[c], "st_mean"], w=[xk[c]])
                        self.tt("pool", x1f[:, c, 0:N], x1f[:, c, 0:N], rstd[:, 0:N], ALU.mult, r=[xk[c], "st_rstd"], w=[xk[c]])
                        self.act(x1f[:, c, 0:N], x1f[:, c, 0:N], AF.Identity, r=[xk[c], "vecs"], w=[xk[c]],
                                 scale=vecs[:, V_L2G + c:V_L2G + c + 1], bias=vecs[:, V_L2B + c:V_L2B + c + 1])
                    if not last:
                        P.dma("sp", XT[:, :, gcols], x1f[:, :, 0:N], r=xk, w=["XT"])
                    else:
                        oy = L["o_y"] if sq_ is PR else L["o_ys"]
                        for t in range(4):
                            yb = ytok[yi % 2]
                            yk = f"ytok{yi % 2}"
                            yi += 1
                            for c in range(8):
                                bank = (2, 3, 4, 7)[c % 4]
                                self.tr(pss[bank][0:n, 0:128], x1f[:, c, t * n:(t + 1) * n], identF[:], r=[xk[c], "consts"], w=[f"ps{bank}"])
                                self.copy(self.alt(), yb[0:n, c * 128:(c + 1) * 128], pss[bank][0:n, 0:128], r=[f"ps{bank}"], w=[yk])
                            r0 = (g * 4 + t) * n
                            P.dma("sp", oy[r0:r0 + n, :], yb[0:n, :], r=[yk])
            P.dma("sp", L["o_conv"][l], convo[:], r=["convo"])
            P.dma("sp", L["o_convs"][l], convos[:], r=["convos"])
            P.barrier()
SPL = dict(ua=0, va=256, qb=512, kb=896, vb=960, qib=1024, kib=1280, wib=1312, qc=1320, kc=1704, vc=2088, gc=2472)
LOG_G = np.log(1.0 - 2.0 ** (-5.0 - np.arange(6, dtype=np.float64)))


def _chunked(w):
    return np.ascontiguousarray(w.reshape(8, 128, -1).transpose(1, 0, 2))


def _swap_heads(w):
    m = w.reshape(w.shape[0], -1, 2, 32)
    return np.ascontiguousarray(m[:, :, ::-1, :]).reshape(w.shape[0], -1)


def _rot_tables(pos):
    half = 32
    freqs = (10000.0 ** (-np.arange(half, dtype=np.float32) / half)).astype(np.float32)
    ang = pos.astype(np.float32)[None, :] * freqs[:, None]
    cos = np.cos(ang)
    sin = np.sin(ang)
    cosT = np.concatenate([cos, cos, cos, cos], 0).astype(np.float32)
    sinT = np.concatenate([-sin, sin, -sin, sin], 0).astype(np.float32)
    return cosT, sinT


def _pairlay(per_head):
    t = np.zeros((128, 192), np.float64)
    for h in range(6):
        t[(h % 2) * 64:(h % 2) * 64 + 64, (h // 2) * 64:(h // 2) * 64 + 64] = per_head[h]
    return t


def _const_tables():
    c = {}
    c["identf"] = np.eye(128, dtype=np.float32)
    jj = np.arange(128, dtype=np.float64)
    kd0 = np.exp((127 - jj)[:, None] * LOG_G[None, :]) * 0.125
    kd1 = np.zeros((128, 6))
    kd1[:16] = np.exp((15 - jj[:16])[:, None] * LOG_G[None, :]) * 0.125
    c["kdec"] = np.stack([np.repeat(kd0, 64, 1), np.repeat(kd1, 64, 1)]).astype(np.float32)
    diff = jj[None, :] - jj[:, None]
    dt = np.where(diff[:, None, :] >= 0, np.exp(np.maximum(diff, 0)[:, None, :] * LOG_G[None, :, None]), 0.0) * 0.125
    c["dtT"] = dt.astype(np.float32)
    lane_h = lambda pair: np.array([2 * pair + (p // 64) for p in range(128)])
    qp = np.zeros((128, 3, 512))
    qs = np.zeros((128, 3, 64))
    for pair in range(3):
        lg = LOG_G[lane_h(pair)]
        qp[:, pair, :] = np.exp(((np.arange(512) % 128) + 1)[None, :] * lg[:, None])
        qs[:, pair, :] = np.exp(((np.arange(64) % 16) + 1)[None, :] * lg[:, None])
    c["qdecP"], c["qdecS"] = qp.astype(np.float32), qs.astype(np.float32)
    c["sel"] = (np.arange(128)[:, None] == (np.arange(768) % 128)[None, :]).astype(np.float32)
    c["sel16"] = (np.arange(16)[:, None] == (np.arange(96) % 16)[None, :]).astype(np.float32)
    c["cm"] = np.broadcast_to(((np.arange(128) // 16)[None, None, :] == np.arange(8)[None, :, None]), (128, 8, 128)).astype(np.float32).copy()
    c["e16"] = ((np.arange(128) % 16)[:, None] == np.arange(16)[None, :]).astype(np.float32)
    bd = np.zeros((128, 128), np.float32)
    bd[:64, :64] = 1 / 64
    bd[64:, 64:] = 1 / 64
    c["bd64"] = bd
    c["onesd"] = np.full((128, 128), 1 / 1024, np.float32)
    c["umask"] = (np.arange(128)[None, :] >= np.arange(128)[:, None]).astype(np.float32)
    c["cdecS"] = _pairlay(np.exp(16 * LOG_G)).astype(np.float32)
    return c


def make_inputs(inp):
    maps = []
    cst = _const_tables()
    per_layer = []
    for l in range(DEPTH):
        W = inp["w_in"][l]
        sl = lambda k, n: W[:, SPL[k]:SPL[k] + n]
        d = {}
        d["wA"] = _chunked(np.concatenate([sl("vc", 384), sl("kb", 64), sl("vb", 64), sl("kib", 32), sl("wib", 8),
                                           sl("kb", 64), sl("kib", 32), sl("kc", 384), _swap_heads(sl("kc", 384))], 1))
        wb = np.concatenate([sl("ua", 256), sl("va", 256), sl("qc", 384), _swap_heads(sl("qc", 384)), sl("kc", 384),
                             _swap_heads(sl("kc", 384)), sl("gc", 384), sl("vc", 384), sl("qb", 384), sl("qib", 256), inp["w_out"][l]], 1)
        d["wB"] = np.ascontiguousarray(_chunked(wb).reshape(128, 8, 35, 128).transpose(2, 0, 1, 3))
        d["wCg"] = np.ascontiguousarray(_chunked(inp["w_gate"][l]).reshape(128, 8, NFC, 128).transpose(2, 0, 1, 3))
        d["wCu"] = np.ascontiguousarray(_chunked(inp["w_up"][l]).reshape(128, 8, NFC, 128).transpose(2, 0, 1, 3))
        wd = inp["w_down"][l].reshape(NFC, 128, 8, 128)
        d["wCd"] = np.ascontiguousarray(wd.transpose(2, 1, 0, 3))
        fm = lambda v, k: v.reshape(k, 128).T
        cw = inp["conv_w"][l].reshape(3, NFC, 128).transpose(2, 1, 0).reshape(128, NFC * 3)
        d["vec"] = np.ascontiguousarray(np.concatenate([
            fm(inp["a_ln_g"][l], 2), fm(inp["a_ln_b"][l], 2), fm(inp["c_gn_g"][l], 3), fm(inp["ln1_g"][l], 8), fm(inp["ln1_b"][l], 8),
            fm(inp["ln2_g"][l], 8), fm(inp["ln2_b"][l], 8), cw, fm(inp["conv_b"][l], NFC)], 1).astype(np.float32))
        bs = inp["a_bs"][l]
        g_of = np.array([[ch * 2 + p // 64 for ch in range(2)] for p in range(128)])
        d["bsP"] = np.ascontiguousarray(np.tile(bs[g_of], (1, 1, 4)).astype(np.float32))
        d["bsS"] = np.ascontiguousarray(np.tile(bs[g_of][:, :, :16], (1, 1, 4)).astype(np.float32))
        d["wmT"] = np.ascontiguousarray(inp["a_ws"][l].transpose(2, 0, 1))
        per_layer.append(d)
    for c in range(8):
        b, j = c // 4, c % 4
        m = dict(cst)
        tiles = [4 * mm + j for mm in range(NT)]
        pos = np.concatenate([np.arange(t * 128, (t + 1) * 128) for t in tiles])
        m["xp"] = np.ascontiguousarray(inp["x_prompt"][b][pos])
        m["xs"] = np.ascontiguousarray(inp["x_sample"][4 * c:4 * c + 4].reshape(NS, D))
        pos_all = np.concatenate([pos, np.tile(2048 + np.arange(16), 4)])
        m["cosT"], m["sinT"] = _rot_tables(pos_all)
        dm = np.zeros((128, 512), np.float32)
        for jp in range(4):
            if jp > j:
                dm[:, jp * 128:(jp + 1) * 128] = NEG
            elif jp == j:
                dm[0:64, jp * 128 + 64:(jp + 1) * 128] = NEG
        m["dmask"] = dm
        rc = np.zeros((128, 9, 192))
        rc[:, 0] = _pairlay(np.exp(128 * j * LOG_G))
        for jp in range(3):
            if jp < j:
                rc[:, 1 + jp] = _pairlay(np.exp(128 * (j - 1 - jp) * LOG_G))
        rc[:, 4] = _pairlay(np.exp(512 * LOG_G))
        for jp in range(4):
            rc[:, 5 + jp] = _pairlay(np.exp(128 * (3 - jp) * LOG_G))
        m["rcoef"] = rc.astype(np.float32)
        s5 = np.zeros((128, 5), np.float32)
        s5[:, (j - 1) if j >= 1 else 4] = 1.0
        m["sel5"] = s5
        for l in range(DEPTH):
            for k, v in per_layer[l].items():
                m[f"{k}{l}"] = v
            m[f"ck{l}"] = np.ascontiguousarray(inp["cache_b_k"][l, 4 * c:4 * c + 4])
            m[f"cv{l}"] = np.ascontiguousarray(inp["cache_b_v"][l, 4 * c:4 * c + 4])
            m[f"cki{l}"] = np.ascontiguousarray(inp["cache_b_kidx"][l, 4 * c:4 * c + 4])
            sr = inp["state_ret"][l, 4 * c:4 * c + 4]
            t = np.zeros((128, 4, 192), np.float32)
            for h in range(6):
                t[(h % 2) * 64:(h % 2) * 64 + 64, :, (h // 2) * 64:(h // 2) * 64 + 64] = sr[:, h].transpose(1, 0, 2)
            m[f"stS{l}"] = t
            cv = inp["state_ffn_conv"][l, 4 * c:4 * c + 4]
            m[f"cvp{l}"] = np.ascontiguousarray(cv.reshape(4, 2, NFC, 128).transpose(3, 2, 0, 1))
        maps.append(m)
    return maps


def _unpair(t):
    out = np.zeros(t.shape[:-2] + (6, 64, 64), np.float32)
    for h in range(6):
        out[..., h, :, :] = t[..., (h % 2) * 64:(h % 2) * 64 + 64, (h // 2) * 64:(h // 2) * 64 + 64]
    return out


def assemble(R):
    y = np.zeros((2, 8192, D), np.float32)
    ys = np.zeros((32, 16, D), np.float32)
    kp = np.zeros((DEPTH, 2, 8192, 64), np.float32)
    vp = np.zeros((DEPTH, 2, 8192, 64), np.float32)
    kip = np.zeros((DEPTH, 2, 8192, 32), np.float32)
    rp = np.zeros((DEPTH, 2, 6, 64, 64), np.float32)
    cp = np.zeros((DEPTH, 2, 2, DFF), np.float32)
    ks = np.zeros((DEPTH, 32, 16, 64), np.float32)
    vs = np.zeros((DEPTH, 32, 16, 64), np.float32)
    kis = np.zeros((DEPTH, 32, 16, 32), np.float32)
    rs = np.zeros((DEPTH, 32, 6, 64, 64), np.float32)
    cs = np.zeros((DEPTH, 32, 2, DFF), np.float32)
    avs = np.zeros((DEPTH, 32, 16, 256), np.float32)
    for c in range(8):
        b, j = c // 4, c % 4
        r = R[c]
        pos = np.concatenate([np.arange((4 * mm + j) * 128, (4 * mm + j + 1) * 128) for mm in range(NT)])
        y[b, pos] = r["o_y"]
        ys[4 * c:4 * c + 4] = r["o_ys"].reshape(4, 16, D)
        kp[:, b, pos] = r["o_k"]
        vp[:, b, pos] = r["o_v"]
        kip[:, b, pos] = r["o_ki"]
        if j == 0:
            rp[:, b] = _unpair(r["o_ret"])
        if j == 3:
            cp[:, b] = r["o_conv"].transpose(0, 3, 2, 1).reshape(DEPTH, 2, DFF)
        ks[:, 4 * c:4 * c + 4] = r["o_ks"].reshape(DEPTH, 4, 16, 64)
        vs[:, 4 * c:4 * c + 4] = r["o_vs"].reshape(DEPTH, 4, 16, 64)
        kis[:, 4 * c:4 * c + 4] = r["o_kis"].reshape(DEPTH, 4, 16, 32)
        rs[:, 4 * c:4 * c + 4] = _unpair(r["o_rets"].transpose(0, 2, 1, 3))
        cs[:, 4 * c:4 * c + 4] = r["o_convs"].transpose(0, 3, 4, 2, 1).reshape(DEPTH, 4, 2, DFF)
        avs[:, 4 * c:4 * c + 4] = r["o_avs"].reshape(DEPTH, 4, 16, 256)
    return (y, ys, kp, vp, kip, rp, cp, ks, vs, kis, rs, cs, avs)


_CACHE = {}


def kernel(**inputs):
    inp = {k: np.asarray(v, dtype=np.float32) for k, v in inputs.items()}
    stage = _CACHE.get("stage", 99)
    if "nc" not in _CACHE:
        bld = Builder(stage)
        _CACHE["nc"] = bld.build()
        _CACHE["bld"] = bld
    nc = _CACHE["nc"]
    maps = make_inputs(inp)
    res = run_bass_kernel_spmd(nc, maps, core_ids=list(range(8)))
    _CACHE["raw"] = res.results
    return assemble(res.results)
```

```python
import numpy as np
import os
EXP = os.environ.get('EXP', '')
from contextlib import ExitStack
import concourse.bass as bass
import concourse.mybir as mybir
from concourse.bass_utils import run_bass_kernel_spmd

F32 = mybir.dt.float32
BF16 = mybir.dt.bfloat16
AF = mybir.ActivationFunctionType
ALU = mybir.AluOpType
AX = mybir.AxisListType

D = 1024
NT = 16
TS = 128
NTOK = NT * TS
NS = 64
DFF = 2816
NFC = DFF // 128
DEPTH = 2
ALPHA = (2 * DEPTH) ** 0.25
EPS = 1e-5
GROUPS = [[0, 1, 2, 3], [4, 5, 6, 7]]
NEG = -1.0e30


class Prog:
    NSLOT = 8

    def __init__(self, nc):
        self.nc = nc
        self.ops = []
        self.buf = {}
        self.slot_last = {}
        self.slot_rr = {"sp": 0, "pool": 0, "act": 0}
        self.last = {}
        self.dma_pending = []

    def _deps(self, eng, r, w):
        deps = set()
        for k in r:
            b = self.buf.setdefault(k, [None, []])
            if b[0] is not None:
                deps.add(b[0])
            if isinstance(k, str) and k.startswith("ps"):
                deps.update(d for d in b[1] if self.ops[d]["eng"] != eng)
        for k in w:
            b = self.buf.setdefault(k, [None, []])
            if b[0] is not None:
                deps.add(b[0])
            deps.update(b[1])
        if eng == "pe":
            deps = {d for d in deps if not (self.ops[d]["eng"] == "pe" and self.ops[d]["kind"] == "op")}
        return deps

    def _record(self, oid, r, w):
        me = self.ops[oid]
        for k in r:
            rl = self.buf[k][1]
            if me["kind"] == "op":
                rl[:] = [d for d in rl if not (self.ops[d]["kind"] == "op" and self.ops[d]["eng"] == me["eng"])]
            rl.append(oid)
        for k in w:
            self.buf[k] = [oid, []]

    def op(self, eng, fn, r=(), w=()):
        deps = self._deps(eng, r, w)
        oid = len(self.ops)
        self.ops.append(dict(eng=eng, kind="op", fn=fn, deps=deps))
        self._record(oid, r, w)
        self.last[eng] = oid
        return oid

    def barrier(self):
        ids = set(self.last.values()) | set(self.dma_pending)
        for e in ["pe", "act", "dve", "pool", "sp"]:
            self.ops.append(dict(eng=e, kind="bar", deps=set(ids)))
        self.buf = {}
        self.dma_pending = []

    def dma(self, q, out, in_, r=(), w=()):
        deps = self._deps(q, r, w)
        slot = self.slot_rr[q] % self.NSLOT
        self.slot_rr[q] += 1
        prev = self.slot_last.get((q, slot))
        if prev is not None:
            deps.add(prev)
        oid = len(self.ops)
        self.ops.append(dict(eng=q, kind="dma", out=out, in_=in_, deps=deps, slot=slot))
        self.slot_last[(q, slot)] = oid
        self._record(oid, r, w)
        self.dma_pending.append(oid)
        return oid

    def cc(self, ins, outs, r=(), w=()):
        deps = self._deps("pool", r, w)
        oid = len(self.ops)
        self.ops.append(dict(eng="pool", kind="cc", ins=ins, outs=outs, deps=deps))
        self._record(oid, r, w)
        self.dma_pending.append(oid)
        return oid

    def emit(self, stack):
        nc = self.nc
        ops = self.ops
        needed = set()
        for o in ops:
            needed.update(o["deps"])
        engs = ["pe", "act", "dve", "pool", "sp"]
        sem_e = {e: stack.enter_context(nc.semaphore("pg_" + e)) for e in engs}
        sem_d = {(q, s): stack.enter_context(nc.semaphore(f"dq_{q}{s}"))
                 for q in ("sp", "pool", "act") for s in range(self.NSLOT)}
        sem_cc = stack.enter_context(nc.semaphore("ccsem"))
        cnt = {e: 0 for e in engs}
        dcnt = {k: 0 for k in sem_d}
        ccn = 0
        ev = {}
        for i, o in enumerate(ops):
            if o["kind"] == "op":
                if i in needed:
                    cnt[o["eng"]] += 1
                    ev[i] = (sem_e[o["eng"]], cnt[o["eng"]], ("e", o["eng"]))
            elif o["kind"] == "dma":
                k = (o["eng"], o["slot"])
                dcnt[k] += 16
                ev[i] = (sem_d[k], dcnt[k], ("d",) + k)
            elif o["kind"] == "bar":
                pass
            else:
                ccn += 1
                ev[i] = (sem_cc, ccn, ("c",))
        per = {e: [] for e in engs}
        known = {e: {} for e in engs}
        for i, o in enumerate(ops):
            e = o["eng"]
            best = {}
            for d in o["deps"]:
                s, v, key = ev[d]
                if known[e].get(key, 0) >= v:
                    continue
                if key not in best or best[key][1] < v:
                    best[key] = (s, v)
            for key, (s, v) in best.items():
                per[e].append(("wait", s, v))
                known[e][key] = v
            per[e].append(("ins", i))
        for (q, s), c in dcnt.items():
            if c:
                per[q].append(("wait", sem_d[(q, s)], c))
        if ccn:
            per["pool"].append(("wait", sem_cc, ccn))
        if os.environ.get("DUMP"):
            for e in engs:
                print("ENGINE", e)
                for it in per[e]:
                    if it[0] == "wait":
                        print("   wait", it[1], it[2])
                    else:
                        o = ops[it[1]]
                        print("   ", it[1], o["kind"], o.get("tag", ""), "sig" if it[1] in ev else "", ev.get(it[1], ("", ""))[1])
        self.stats = {e: sum(1 for x in per[e] if x[0] == "ins") for e in engs}
        self.stats["sem"] = dict(cnt)

        if os.environ.get("CHECK"):
            semv = {}
            pos = {e: 0 for e in engs}
            prog = True
            while prog:
                prog = False
                for e in engs:
                    while pos[e] < len(per[e]):
                        it = per[e][pos[e]]
                        if it[0] == "wait":
                            if semv.get(it[1].num, 0) >= it[2]:
                                pos[e] += 1
                                prog = True
                            else:
                                break
                        else:
                            i = it[1]
                            if i in ev:
                                o = ops[i]
                                inc = 16 if o["kind"] == "dma" else 1
                                semv[ev[i][0].num] = semv.get(ev[i][0].num, 0) + inc
                            pos[e] += 1
                            prog = True
            for e in engs:
                if pos[e] < len(per[e]):
                    it = per[e][pos[e]]
                    print("DEADLOCK", e, "at", pos[e], "/", len(per[e]), it[0], it[1] if it[0] == "wait" else "", it[2] if it[0] == "wait" else "",
                          "have", semv.get(it[1].num, 0) if it[0] == "wait" else "")
            print("CHECK done", {e: (pos[e], len(per[e])) for e in engs})

        def run(E, lst):
            for it in lst:
                if it[0] == "wait":
                    E.wait_ge(it[1], it[2])
                else:
                    i = it[1]
                    o = ops[i]
                    if o["kind"] == "op":
                        ins = o["fn"](E)
                        if i in ev:
                            ins.then_inc(ev[i][0], 1)
                    elif o["kind"] == "dma":
                        E.dma_start(out=o["out"], in_=o["in_"]).then_inc(ev[i][0], 16)
                    elif o["kind"] == "bar":
                        pass
                    else:
                        E.collective_compute("AllGather", ALU.bypass, replica_groups=GROUPS,
                                             ins=[a.opt() for a in o["ins"]], outs=[a.opt() for a in o["outs"]]).then_inc(ev[i][0])

        block = stack.enter_context(nc.Block())

        @block.sync
        def _(E):
            run(E, per["sp"])

        @block.scalar
        def _(E):
            run(E, per["act"])

        @block.vector
        def _(E):
            run(E, per["dve"])

        @block.gpsimd
        def _(E):
            run(E, per["pool"])

        @block.tensor
        def _(E):
            run(E, per["pe"])


class Seq:
    def __init__(self, name, n, ntile, off):
        self.name, self.n, self.ntile, self.off = name, n, ntile, off
        self.G = 4
        self.ng = ntile // 4
        self.N = 4 * n


PR = Seq("p", 128, NT, 0)
SM = Seq("s", 16, 4, NTOK)
NBIS = 16
CH = dict(ua=0, va=2, qc=4, qcs=7, kc=10, kcs=13, gc=16, vc=19, qb=22, qib=25, wo=27)


class Builder:
    def __init__(self, stage=99):
        self.stage = stage
        self.nc = bass.Bass("TRN2", target_bir_lowering=False)
        self.P = Prog(self.nc)
        self.stack = ExitStack()
        self.rr = 0
        self.wi = 0

    def din(self, name, shape, dt=F32):
        return self.nc.dram_tensor(name, list(shape), dt, kind="ExternalInput").ap()

    def dout(self, name, shape, dt=F32):
        return self.nc.dram_tensor(name, list(shape), dt, kind="ExternalOutput").ap()

    def dscr(self, name, shape, dt=F32):
        return self.nc.dram_tensor(name, list(shape), dt).ap()

    def sb(self, name, shape, dt=F32, st=None):
        self.uid = getattr(self, "uid", 0) + 1
        return (st or self.stack).enter_context(self.nc.sbuf_tensor(f"s{self.uid}_{name}", list(shape), dt))

    def ps(self, name):
        return self.stack.enter_context(self.nc.psum_tensor(name, [128, 512], F32))

    def mm(self, out, lhsT, rhs, start, stop, r, w):
        self.P.op("pe", lambda E: E.matmul(out, lhsT, rhs, start=start, stop=stop, skip_group_check=True), r=r, w=w)

    def tr(self, out, in_, ident, r, w):
        self.P.op("pe", lambda E: E.transpose(out, in_, ident), r=r, w=w)

    def copy(self, eng, out, in_, r, w):
        if eng == "act":
            self.P.op("act", lambda E: E.activation(out=out, in_=in_, func=AF.Copy), r=r, w=w)
        else:
            self.P.op(eng, lambda E: E.tensor_copy(out=out, in_=in_), r=r, w=w)

    def act(self, out, in_, func, r, w, bias=0.0, scale=1.0):
        self.P.op("act", lambda E: E.activation(out=out, in_=in_, func=func, bias=bias, scale=scale), r=r, w=w)

    def tt(self, eng, out, in0, in1, op, r, w):
        self.P.op(eng, lambda E: E.tensor_tensor(out=out, in0=in0, in1=in1, op=op), r=r, w=w)

    def ts(self, eng, out, in0, s1, s2, op0, op1=None, r=(), w=(), accum_out=None):
        kw = {}
        if accum_out is not None:
            kw["accum_out"] = accum_out
        if op1 is None:
            self.P.op(eng, lambda E: E.tensor_scalar(out=out, in0=in0, scalar1=s1, scalar2=None, op0=op0, **kw), r=r, w=w)
        else:
            self.P.op(eng, lambda E: E.tensor_scalar(out=out, in0=in0, scalar1=s1, scalar2=s2, op0=op0, op1=op1, **kw), r=r, w=w)

    def stt(self, eng, out, in0, scalar, in1, op0, op1, r, w):
        self.P.op(eng, lambda E: E.scalar_tensor_tensor(out=out, in0=in0, scalar=scalar, in1=in1, op0=op0, op1=op1), r=r, w=w)

    def red(self, out, in_, op, r, w):
        self.P.op("dve", lambda E: E.tensor_reduce(out=out, in_=in_, axis=AX.X, op=op), r=r, w=w)

    def memset(self, eng, ap, val, w):
        self.P.op(eng, lambda E: E.memset(ap, val), r=(), w=w)

    def alt(self):
        self.rr += 1
        return "act" if self.rr % 2 else "dve"

    def wchunk(self, src):
        i = self.wi % len(self.wring)
        self.wi += 1
        t = self.wring[i]
        self.P.dma("pool", t[:], src, w=[f"wr{i}"])
        return t, f"wr{i}"

    def proj(self, pst, psk, wt, wk, c0, M, xb, xk, N, po=0):
        for kc in range(8):
            self.mm(pst[po:po + M, 0:N], wt[:, kc, c0:c0 + M], xb[:, kc, 0:N], kc == 0, kc == 7, r=[wk, xk], w=[psk])

    def stats(self, xs, keys, N, ones, tag):
        pm, pe2 = self.pss[5], self.pss[6]
        nx = len(xs)
        for i, (x, k) in enumerate(zip(xs, keys)):
            sq = self.sqb[i % 2]
            self.act(sq[:, 0:N], x, AF.Square, r=[k], w=[f"sqb{i % 2}"])
            self.mm(pm[:, 0:N], ones[:], x, i == 0, i == nx - 1, r=[k, "consts"], w=["ps5"])
            self.mm(pe2[:, 0:N], ones[:], sq[:, 0:N], i == 0, i == nx - 1, r=[f"sqb{i % 2}", "consts"], w=["ps6"])
        mean, rstd, tmp = self.st_mean, self.st_rstd, self.st_tmp
        self.act(tmp[:, 0:N], pm[:, 0:N], AF.Square, r=["ps5"], w=["st_tmp"])
        self.copy("act", mean[:, 0:N], pm[:, 0:N], r=["ps5"], w=["st_mean"])
        self.tt("dve", tmp[:, 0:N], pe2[:, 0:N], tmp[:, 0:N], ALU.subtract, r=["ps6", "st_tmp"], w=["st_tmp"])
        self.ts("dve", tmp[:, 0:N], tmp[:, 0:N], 0.0, EPS, ALU.max, ALU.add, r=["st_tmp"], w=["st_tmp"])
        self.act(tmp[:, 0:N], tmp[:, 0:N], AF.Sqrt, r=["st_tmp"], w=["st_tmp"])
        self.P.op("dve", lambda E: E.reciprocal(out=rstd[:, 0:N], in_=tmp[:, 0:N]), r=["st_tmp"], w=["st_rstd"])
        return mean, rstd

    def build(self):
        nc, P = self.nc, self.P
        st = self.stage
        S = self.stack
        xp = self.din("xp", [NTOK, D])
        xs_in = self.din("xs", [NS, D])
        identf = self.din("identf", [128, 128])
        cosd = self.din("cosT", [128, NTOK + NS])
        sind = self.din("sinT", [128, NTOK + NS])
        kdecd = self.din("kdec", [2, 128, 384])
        dtd = self.din("dtT", [128, 6, 128])
        qdecPd = self.din("qdecP", [128, 3, 512])
        qdecSd = self.din("qdecS", [128, 3, 64])
        seld = self.din("sel", [128, 768])
        sel16d = self.din("sel16", [16, 96])
        cmd = self.din("cm", [128, 8, 128])
        e16d = self.din("e16", [128, 16])
        bd64d = self.din("bd64", [128, 128])
        onesd = self.din("onesd", [128, 128])
        umd = self.din("umask", [128, 128])
        dmaskd = self.din("dmask", [128, 512])
        rcoefd = self.din("rcoef", [128, 9, 192])
        sel5d = self.din("sel5", [128, 5])
        cdecSd = self.din("cdecS", [128, 192])
        wA = [self.din(f"wA{l}", [128, 8, 1416]) for l in range(DEPTH)]
        wB = [self.din(f"wB{l}", [35, 128, 8, 128]) for l in range(DEPTH)]
        wCg = [self.din(f"wCg{l}", [NFC, 128, 8, 128]) for l in range(DEPTH)]
        wCu = [self.din(f"wCu{l}", [NFC, 128, 8, 128]) for l in range(DEPTH)]
        wCd = [self.din(f"wCd{l}", [8, 128, NFC, 128]) for l in range(DEPTH)]
        vecd = [self.din(f"vec{l}", [128, 127]) for l in range(DEPTH)]
        bsPd = [self.din(f"bsP{l}", [128, 2, 512]) for l in range(DEPTH)]
        bsSd = [self.din(f"bsS{l}", [128, 2, 64]) for l in range(DEPTH)]
        wmTd = [self.din(f"wmT{l}", [128, 4, 128]) for l in range(DEPTH)]
        ckd = [self.din(f"ck{l}", [4, 2048, 64]) for l in range(DEPTH)]
        cvd = [self.din(f"cv{l}", [4, 2048, 64]) for l in range(DEPTH)]
        ckid = [self.din(f"cki{l}", [4, 2048, 32]) for l in range(DEPTH)]
        stSd = [self.din(f"stS{l}", [128, 4, 192]) for l in range(DEPTH)]
        cvpd = [self.din(f"cvp{l}", [128, NFC, 4, 2]) for l in range(DEPTH)]
        o_y = self.dout("o_y", [NTOK, D])
        o_ys = self.dout("o_ys", [NS, D])
        o_k = self.dout("o_k", [DEPTH, NTOK, 64])
        o_v = self.dout("o_v", [DEPTH, NTOK, 64])
        o_ki = self.dout("o_ki", [DEPTH, NTOK, 32])
        o_ret = self.dout("o_ret", [DEPTH, 128, 192])
        o_conv = self.dout("o_conv", [DEPTH, 128, NFC, 2])
        o_ks = self.dout("o_ks", [DEPTH, NS, 64])
        o_vs = self.dout("o_vs", [DEPTH, NS, 64])
        o_kis = self.dout("o_kis", [DEPTH, NS, 32])
        o_rets = self.dout("o_rets", [DEPTH, 128, 4, 192])
        o_convs = self.dout("o_convs", [DEPTH, 128, NFC, 4, 2])
        o_avs = self.dout("o_avs", [DEPTH, NS, 256])
        XT = self.dscr("XT", [128, 8, NTOK + NS])
        X1T = self.dscr("X1T", [128, 8, NTOK + NS])
        bncA32 = self.dscr("bncA", [NT, 10240])
        gatA32 = self.dscr("gatA", [4 * NT, 10240])
        bncA = bncA32.bitcast(BF16)
        gatA = gatA32.bitcast(BF16)
        bncK2 = [self.dscr(f"bncK{i}", [8 * 128, 192]) for i in range(2)]
        gatK2 = [self.dscr(f"gatK{i}", [4 * 8 * 128, 192]) for i in range(2)]
        bncB32 = self.dscr("bncB", [NT, 1024])
        gatB32 = self.dscr("gatB", [4 * NT, 1024])
        bncB = bncB32.bitcast(BF16)
        gatB = gatB32.bitcast(BF16)

        identF = self.sb("identF", [128, 128])
        identB = self.sb("identB", [128, 128], BF16)
        bd64 = self.sb("bd64", [128, 128])
        ones = self.sb("ones", [128, 128])
        P.dma("sp", identF[:], identf, w=["consts"])
        P.dma("sp", bd64[:], bd64d, w=["consts"])
        P.dma("sp", ones[:], onesd, w=["consts"])
        P.dma("pool", identB[:], identf, w=["consts"])
        kdecT = self.sb("kdecT", [128, 2, 384])
        P.dma("sp", kdecT[:], kdecd.rearrange("a p f -> p a f"), w=["consts"])
        wtok = {"p": self.sb("wtokp", [128, NT, 8]), "s": self.sb("wtoks", [128, 4, 8])}
        vecs = self.sb("vecs", [128, 127])
        self.pss = pss = [self.ps(f"ps{i}") for i in range(8)]
        self.sqb = [self.sb(f"sqb{i}", [128, 512]) for i in range(2)]
        self.st_mean = self.sb("st_mean", [128, 512])
        self.st_rstd = self.sb("st_rstd", [128, 512])
        self.st_tmp = self.sb("st_tmp", [128, 512])
        ktS = self.sb("ktS", [128, 4, 16], BF16)
        vS = self.sb("vS", [16, 4, 64], BF16)
        convo = self.sb("convo", [128, NFC, 2])
        convos = self.sb("convos", [128, NFC, 4, 2])
        V_AG, V_AB, V_GN, V_L1G, V_L1B, V_L2G, V_L2B, V_CW, V_CB = 0, 2, 4, 7, 15, 23, 31, 39, 105

        with ExitStack() as ph:
            xin = [self.sb(f"xin{i}", [128, D], st=ph) for i in range(2)]
            xtg = [self.sb(f"xtg{i}", [128, 8, 128], st=ph) for i in range(2)]
            for sq_, src in ((PR, xp), (SM, xs_in)):
                n = sq_.n
                for m in range(sq_.ntile):
                    b = m % 2
                    P.dma("sp", xin[b][0:n, :], src[m * n:(m + 1) * n, :], w=[f"xin{b}"])
                    for c in range(8):
                        pb = pss[c % 4]
                        self.tr(pb[:, 0:n], xin[b][0:n, c * 128:(c + 1) * 128], identF[0:n, 0:n], r=[f"xin{b}", "consts"], w=[f"ps{c % 4}"])
                        self.copy(self.alt(), xtg[b][:, c, 0:n], pb[:, 0:n], r=[f"ps{c % 4}"], w=[f"xtg{b}"])
                    P.dma("sp", XT[:, :, sq_.off + m * n:sq_.off + (m + 1) * n], xtg[b][:, :, 0:n], r=[f"xtg{b}"], w=["XT"])
            P.barrier()

        for l in range(DEPTH if st >= 9 else 1):
            P.dma("sp", vecs[:], vecd[l], w=["vecs"])
            with ExitStack() as ph:
                WA = self.sb("WA", [128, 8, 1416], BF16, st=ph)
                for kc in range(8):
                    P.dma("pool", WA[:, kc, :], wA[l][:, kc, :], w=["WA"])
                xTf = [self.sb(f"xTf{i}", [128, 8, 128], st=ph) for i in range(2)]
                xTb = [self.sb(f"xTb{i}", [128, 8, 128], BF16, st=ph) for i in range(2)]
                vtok = self.sb("vtok", [128, 384], BF16, st=ph)
                vb16 = self.sb("vb16", [128, 64], BF16, st=ph)
                kvo = [self.sb(f"kvo{i}", [128, 168], st=ph) for i in range(2)]
                ktb = self.sb("ktb", [128, 128], BF16, st=ph)
                rt1 = self.sb("rt1", [128, 3, 128], st=ph)
                krot = self.sb("krot", [128, 3, 128], st=ph)
                kd = self.sb("kd", [128, 384], BF16, st=ph)
                kvsb = [self.sb(f"kvsb{i}", [128, 192], st=ph) for i in range(2)]
                csa = [self.sb(f"csa{i}", [128, 2, 128], st=ph) for i in range(2)]
                stS = self.sb("stS", [128, 4, 192], st=ph)
                cdecS = self.sb("cdecS", [128, 192], st=ph)
                P.dma("sp", stS[:], stSd[l], w=["stS"])
                P.dma("sp", cdecS[:], cdecSd, w=["cdecS"])
                it = 0
                for sq_ in (PR, SM):
                    n = sq_.n
                    if sq_ is SM:
                        P.cc([bncA32], [gatA32], r=["bncA"], w=["gatA"])
                        for i in range(2):
                            P.cc([bncK2[i]], [gatK2[i]], r=[f"bncK{i}"], w=[f"gatK{i}"])
                    ok, ov, oki = (o_k, o_v, o_ki) if sq_ is PR else (o_ks, o_vs, o_kis)
                    kdi = 0 if sq_ is PR else 1
                    for m in range(sq_.ntile):
                        b = it % 2
                        it += 1
                        cols = slice(sq_.off + m * n, sq_.off + (m + 1) * n)
                        rows = slice(m * n, (m + 1) * n)
                        P.dma("sp", xTf[b][:, :, 0:n], XT[:, :, cols], r=["XT"], w=[f"xTf{b}"])
                        self.copy("act", xTb[b][:, :, 0:n], xTf[b][:, :, 0:n], r=[f"xTf{b}"], w=[f"xTb{b}"])
                        xb = xTb[b]
                        rx = [f"xTb{b}", "WA"]
                        for kc in range(8):
                            self.mm(pss[0][0:n, 0:512], xb[:, kc, 0:n], WA[:, kc, 0:512], kc == 0, kc == 7, r=rx, w=["ps0"])
                        for kc in range(8):
                            self.mm(pss[1][0:n, 0:40], xb[:, kc, 0:n], WA[:, kc, 512:552], kc == 0, kc == 7, r=rx, w=["ps1"])
                        self.copy("act", vtok[0:n, :], pss[0][0:n, 0:384], r=["ps0"], w=["vtok"])
                        self.copy("act", vb16[0:n, :], pss[0][0:n, 448:512], r=["ps0"], w=["vb16"])
                        self.copy("dve", kvo[b][0:n, 0:128], pss[0][0:n, 384:512], r=["ps0"], w=[f"kvo{b}"])
                        self.copy("dve", kvo[b][0:n, 128:168], pss[1][0:n, 0:40], r=["ps1"], w=[f"kvo{b}"])
                        self.copy("dve", wtok[sq_.name][0:n, m, :], kvo[b][0:n, 160:168], r=[f"kvo{b}"], w=["wtok"])
                        P.dma("sp", ok[l, rows, :], kvo[b][0:n, 0:64], r=[f"kvo{b}"])
                        P.dma("sp", ov[l, rows, :], kvo[b][0:n, 64:128], r=[f"kvo{b}"])
                        P.dma("sp", oki[l, rows, :], kvo[b][0:n, 128:160], r=[f"kvo{b}"])
                        for kc in range(8):
                            self.mm(pss[2][0:64, 0:n], WA[:, kc, 552:616], xb[:, kc, 0:n], kc == 0, kc == 7, r=rx, w=["ps2"])
                        for kc in range(8):
                            self.mm(pss[2][64:96, 0:n], WA[:, kc, 616:648], xb[:, kc, 0:n], kc == 0, kc == 7, r=rx, w=["ps2"])
                        if sq_ is PR:
                            self.copy("act", ktb[0:96, 0:n], pss[2][0:96, 0:n], r=["ps2"], w=["ktb"])
                            P.dma("sp", bncA[m, 0:12288].rearrange("(d t) -> d t", t=128), ktb[0:96, :], r=["ktb"], w=["bncA"])
                            P.dma("sp", bncA[m, 12288:20480].rearrange("(s e) -> s e", e=64), vb16[:], r=["vb16"], w=["bncA"])
                        else:
                            self.copy("act", ktS[0:96, m, :], pss[2][0:96, 0:n], r=["ps2"], w=["ktS"])
                            self.copy("dve", vS[0:n, m, :], vb16[0:n, :], r=["vb16"], w=["vS"])
                        for p in range(3):
                            for kc in range(8):
                                self.mm(pss[3][:, p * 128:p * 128 + n], WA[:, kc, 648 + p * 128:648 + (p + 1) * 128], xb[:, kc, 0:n],
                                        kc == 0, kc == 7, r=rx, w=["ps3"])
                        for p in range(3):
                            for kc in range(8):
                                self.mm(pss[4][:, p * 128:p * 128 + n], WA[:, kc, 1032 + p * 128:1032 + (p + 1) * 128], xb[:, kc, 0:n],
                                        kc == 0, kc == 7, r=rx, w=["ps4"])
                        P.dma("sp", csa[b][:, 0, 0:n], cosd[:, cols], w=[f"csa{b}"])
                        P.dma("sp", csa[b][:, 1, 0:n], sind[:, cols], w=[f"csa{b}"])
                        cb = csa[b][:, 0, 0:n].unsqueeze(1).to_broadcast([128, 3, n])
                        sbb = csa[b][:, 1, 0:n].unsqueeze(1).to_broadcast([128, 3, n])
                        p3 = pss[3][:, 0:384].rearrange("p (c t) -> p c t", c=3)[:, :, 0:n]
                        p4 = pss[4][:, 0:384].rearrange("p (c t) -> p c t", c=3)[:, :, 0:n]
                        self.tt("dve", rt1[:, :, 0:n], p4, sbb, ALU.mult, r=["ps4", f"csa{b}"], w=["rt1"])
                        self.tt("dve", krot[:, :, 0:n], p3, cb, ALU.mult, r=["ps3", f"csa{b}"], w=["krot"])
                        self.tt("pool", krot[:, :, 0:n], krot[:, :, 0:n], rt1[:, :, 0:n], ALU.add, r=["krot", "rt1"], w=["krot"])
                        for p in range(3):
                            self.tr(pss[5][0:n, p * 128:(p + 1) * 128], krot[:, p, 0:n], identF[:], r=["krot", "consts"], w=["ps5"])
                        self.tt("dve", kd[0:n, :], pss[5][0:n, 0:384], kdecT[0:n, kdi, :], ALU.mult, r=["ps5", "consts"], w=["kd"])
                        for h in range(6):
                            po = (h % 2) * 64
                            self.mm(pss[6][po:po + 64, (h // 2) * 64:(h // 2) * 64 + 64], kd[0:n, h * 64:(h + 1) * 64], vtok[0:n, h * 64:(h + 1) * 64],
                                    True, True, r=["kd", "vtok"], w=["ps6"])
                        if sq_ is PR:
                            self.copy("act", kvsb[b][:], pss[6][:, 0:192], r=["ps6"], w=[f"kvsb{b}"])
                            P.dma("sp", bncK2[m // 8][(m % 8) * 128:(m % 8 + 1) * 128, :], kvsb[b][:], r=[f"kvsb{b}"], w=[f"bncK{m // 8}"])
                        else:
                            self.tt("dve", kvsb[b][:], stS[:, m, :], cdecS[:], ALU.mult, r=["stS", "cdecS"], w=[f"kvsb{b}"])
                            self.tt("dve", kvsb[b][:], kvsb[b][:], pss[6][:, 0:192], ALU.add, r=[f"kvsb{b}", "ps6"], w=[f"kvsb{b}"])
                            P.dma("sp", o_rets[l, :, m, :], kvsb[b][:], r=[f"kvsb{b}"])
                P.barrier()
            if st <= 1:
                break
            self.phaseB(l, locals())
            if st <= 5:
                break
            self.phaseC(l, locals())

        P.emit(self.stack)
        return nc
    def phaseB(self, l, L):
        P, pss = self.P, self.pss
        st = self.stage
        ones, bd64, identF, identB, vecs = L["ones"], L["bd64"], L["identF"], L["identB"], L["vecs"]
        XT, X1T, gatA, gatK2, bncB = L["XT"], L["X1T"], L["gatA"], L["gatK2"], L["bncB"]
        wB = L["wB"][l]
        V_AG, V_AB, V_GN, V_L1G, V_L1B = L["V_AG"], L["V_AB"], L["V_GN"], L["V_L1G"], L["V_L1B"]
        wtok = L["wtok"]
        with ExitStack() as pp:
            KTI = self.sb("KTI", [128, 16, 4, 128], BF16, st=pp)
            VA = self.sb("VA", [128, 64, 65], BF16, st=pp)
            rst = self.sb("rst", [128, 16, 192], BF16, st=pp)
            self.memset("dve", VA[:], 1.0, w=["VA"])
            for j in range(4):
                for m in range(NT):
                    P.dma("sp", KTI[0:96, m, j, :], gatA[j * 16 + m, 0:12288].rearrange("(d t) -> d t", t=128), r=["gatA"], w=["KTI"])
                    P.dma("sp", VA[:, m * 4 + j, 0:64], gatA[j * 16 + m, 12288:20480].rearrange("(s e) -> s e", e=64), r=["gatA"], w=["VA"])
            with ExitStack() as sc:
                rc = self.sb("rc", [128, 9, 192], st=sc)
                kvg = [self.sb(f"kvg{i}", [128, 4, 192], st=sc) for i in range(2)]
                Sst = self.sb("Sst", [128, 192], st=sc)
                ta = self.sb("ta", [128, 192], st=sc)
                tb = self.sb("tb", [128, 192], st=sc)
                P.dma("sp", rc[:], L["rcoefd"], w=["rc"])
                self.memset("dve", Sst[:], 0.0, w=["Sst"])
                for m in range(NT):
                    b = m % 2
                    P.dma("sp", kvg[b][:], gatK2[m // 8].rearrange("(j m p) f -> p j m f", j=4, m=8)[:, :, m % 8, :], r=[f"gatK{m // 8}"], w=[f"kvg{b}"])
                    self.tt("dve", ta[:], Sst[:], rc[:, 0, :], ALU.mult, r=["Sst", "rc"], w=["ta"])
                    for jp in range(3):
                        self.tt("dve", tb[:], kvg[b][:, jp, :], rc[:, 1 + jp, :], ALU.mult, r=[f"kvg{b}", "rc"], w=["tb"])
                        self.tt("dve", ta[:], ta[:], tb[:], ALU.add, r=["ta", "tb"], w=["ta"])
                    self.copy("dve", rst[:, m, :], ta[:], r=["ta"], w=["rst"])
                    self.tt("dve", ta[:], Sst[:], rc[:, 4, :], ALU.mult, r=["Sst", "rc"], w=["ta"])
                    for jp in range(4):
                        self.tt("dve", tb[:], kvg[b][:, jp, :], rc[:, 5 + jp, :], ALU.mult, r=[f"kvg{b}", "rc"], w=["tb"])
                        self.tt("dve", ta[:], ta[:], tb[:], ALU.add, r=["ta", "tb"], w=["ta"])
                    self.copy("dve", Sst[:], ta[:], r=["ta"], w=["Sst"])
                P.dma("sp", L["o_ret"][l], Sst[:], r=["Sst"])
                P.barrier()
            if st <= 2:
                return
            self.phaseB2(l, L, KTI, VA, rst)

    def phaseB2(self, l, L, KTI, VA, rst):
        P, pss = self.P, self.pss
        st = self.stage
        ones, bd64, identF, identB, vecs = L["ones"], L["bd64"], L["identF"], L["identB"], L["vecs"]
        XT, X1T, gatA, gatK2, bncB = L["XT"], L["X1T"], L["gatA"], L["gatK2"], L["bncB"]
        wB = L["wB"][l]
        V_AG, V_AB, V_GN, V_L1G, V_L1B = L["V_AG"], L["V_AB"], L["V_GN"], L["V_L1G"], L["V_L1B"]
        wtok = L["wtok"]
        with ExitStack() as ph:
            sb = lambda name, shape, dt=F32: self.sb(name, shape, dt, st=ph)
            self.wring = [sb(f"wr{i}", [128, 8, 128], BF16) for i in range(3)]
            scores = sb("scores", [128, 8192])
            junk = sb("junk", [128, 3840], mybir.dt.uint8)
            xfg = sb("xfg", [128, 8, 512])
            xbg = sb("xbg", [128, 8, 512], BF16)
            mixT = sb("mixT", [128, 8, 512], BF16)
            uT = sb("uT", [128, 2, 512])
            gT = sb("gT", [128, 2, 512])
            vtokA = sb("vtokA", [128, 4, 256], BF16)
            avf = scores[0:16, 0:1024].rearrange("p (t f) -> p t f", t=4)
            csg = sb("csg", [128, 2, 512])
            qrot = sb("qrot", [128, 3, 512], BF16)
            qd = sb("qd", [128, 3, 512], BF16)
            krot = sb("krotB", [128, 3, 512], BF16)
            sil = sb("sil", [128, 3, 512], BF16)
            vtokB = sb("vtokB", [128, 4, 384], BF16)
            Sm = sb("Sm", [128, 6, 128], BF16)
            ysb = sb("ysb", [128, 384])
            ycn = sb("ycn", [128, 384])
            DT = sb("DT", [128, 6, 128])
            qdecP = sb("qdecP", [128, 3, 128])
            qdecS = sb("qdecS", [128, 3, 16])
            qq = sb("qq", [128, 4096], BF16)
            Amat = sb("Amat", [128, 128], BF16)
            Wd = sb("Wd", [128, 8, 128], BF16)
            rz = [sb(f"rz{i}", [128, 512], BF16) for i in range(4)]
            pT = [sb(f"pT{i}", [128, 768], BF16) for i in range(2)]
            mb = [sb(f"mb{i}", [128, 128], BF16) for i in range(2)]
            col = sb("col", [128, 16])
            ob = sb("ob", [128, 384])
            Sel = sb("Sel", [128, 768], BF16)
            Sel16 = sb("Sel16", [16, 96], BF16)
            CM = sb("CM", [128, 8, 128], BF16)
            E16 = sb("E16", [128, 16])
            dmask = sb("dmask", [128, 512])
            WmT = sb("WmT", [128, 4, 128], BF16)
            wmf = scores[:, 0:512].rearrange("p (g i) -> p g i", g=4)
            um = scores[:, 512:640]
            bsP = sb("bsP", [128, 2, 128])
            bsS = sb("bsS", [128, 2, 16])
            qi2T = qq[64:96, :].rearrange("p (t h) -> p t h", h=8)
            P.dma("sp", DT[:], L["dtd"], w=["DT"])
            P.dma("sp", qdecP[:], L["qdecPd"][:, :, 0:128], w=["qdec"])
            P.dma("sp", qdecS[:], L["qdecSd"][:, :, 0:16], w=["qdec"])
            P.dma("pool", Sel[:], L["seld"], w=["Sel"])
            P.dma("pool", Sel16[:], L["sel16d"], w=["Sel"])
            P.dma("pool", CM[:], L["cmd"], w=["CM"])
            P.dma("sp", E16[:], L["e16d"], w=["E16"])
            P.dma("sp", dmask[:], L["dmaskd"], w=["dmask"])
            P.dma("sp", wmf, L["wmTd"][l], w=["scores"])
            P.dma("sp", um, L["umd"], w=["scores"])
            P.dma("sp", bsP[:], L["bsPd"][l][:, :, 0:128], w=["bs"])
            P.dma("sp", bsS[:], L["bsSd"][l][:, :, 0:16], w=["bs"])
            self.tt("dve", WmT[:], wmf, um.unsqueeze(1).to_broadcast([128, 4, 128]), ALU.mult, r=["scores"], w=["WmT"])

            def chunk(ci):
                return self.wchunk(wB[ci])

            def group(sq_, g, keysrc):
                n, N = sq_.n, sq_.N
                c0 = sq_.off + g * N
                gcols = slice(c0, c0 + N)
                xk = [f"xfg{c}" for c in range(8)]
                P.dma("sp", xfg[:, :, 0:N], XT[:, :, gcols], r=["XT"], w=xk)
                self.copy("act", xbg[:, :, 0:N], xfg[:, :, 0:N], r=xk, w=["xbg"])
                P.dma("sp", csg[:, 0, 0:N], L["cosd"][:, gcols], w=["csg"])
                P.dma("sp", csg[:, 1, 0:N], L["sind"][:, gcols], w=["csg"])
                bs = bsP if sq_ is PR else bsS
                qdec = qdecP if sq_ is PR else qdecS

                for ch in range(2):
                    wt, wk = chunk(CH["ua"] + ch)
                    self.proj(pss[ch], f"ps{ch}", wt, wk, 0, 128, xbg, "xbg", N)
                    self.act(uT[:, ch, 0:N], pss[ch][:, 0:N], AF.Gelu_apprx_tanh, r=[f"ps{ch}"], w=["uT"])
                for ch in range(2):
                    wt, wk = chunk(CH["va"] + ch)
                    self.proj(pss[ch], f"ps{ch}", wt, wk, 0, 128, xbg, "xbg", N)
                    self.act(gT[:, ch, 0:N], pss[ch][:, 0:N], AF.Gelu_apprx_tanh, r=[f"ps{ch}"], w=["gT"])
                if 'a2' in EXP:
                    return
                for ch in range(2):
                    mean, rstd = self.stats([gT[:, ch, 0:N]], ["gT"], N, bd64, "a")
                    self.tt("dve", gT[:, ch, 0:N], gT[:, ch, 0:N], mean[:, 0:N], ALU.subtract, r=["gT", "st_mean"], w=["gT"])
                    self.tt("dve", gT[:, ch, 0:N], gT[:, ch, 0:N], rstd[:, 0:N], ALU.mult, r=["gT", "st_rstd"], w=["gT"])
                    self.ts("dve", gT[:, ch, 0:N], gT[:, ch, 0:N], vecs[:, V_AG + ch:V_AG + ch + 1], vecs[:, V_AB + ch:V_AB + ch + 1],
                            ALU.mult, ALU.add, r=["gT", "vecs"], w=["gT"])
                if 'a3' in EXP:
                    return
                for t in range(4):
                    pb = pss[2 + t % 2]
                    for ch in range(2):
                        self.tr(pb[0:n, ch * 128:(ch + 1) * 128], gT[:, ch, t * n:(t + 1) * n], identF[:], r=["gT", "consts"], w=[f"ps{2 + t % 2}"])
                    self.copy(self.alt(), vtokA[0:n, t, :], pb[0:n, 0:256], r=[f"ps{2 + t % 2}"], w=["vtokA"])
                    if sq_ is SM:
                        self.copy("dve", avf[0:n, t, :], pb[0:n, 0:256], r=[f"ps{2 + t % 2}"], w=["scores"])
                if sq_ is SM:
                    P.dma("sp", L["o_avs"][l].rearrange("(b t) f -> t b f", t=16), avf, r=["scores"])
                if 'a4' in EXP:
                    return
                for t in range(4):
                    for gr in range(4):
                        po = (gr % 2) * 64
                        self.mm(pss[gr // 2][po:po + 64, t * n:(t + 1) * n], vtokA[0:n, t, gr * 64:(gr + 1) * 64], WmT[0:n, gr, 0:n],
                                True, True, r=["vtokA", "WmT"], w=[f"ps{gr // 2}"])
                if 'a5' in EXP:
                    return
                for ch in range(2):
                    bb = bs[:, ch, 0:n].unsqueeze(1).to_broadcast([128, 4, n])
                    self.copy("act", gT[:, ch, 0:N], pss[ch][:, 0:N], r=[f"ps{ch}"], w=["gT"])
                    g3 = gT[:, ch, 0:N].rearrange("p (t i) -> p t i", t=4)
                    self.tt("dve", g3, g3, bb, ALU.add, r=["gT", "bs"], w=["gT"])
                    if 'a6' in EXP:
                        continue
                    self.tt("dve", mixT[:, ch, 0:N], gT[:, ch, 0:N], uT[:, ch, 0:N], ALU.mult, r=["gT", "uT"], w=["mixT"])

                if 'A' in EXP:
                    return
                def rotary(cq, cqs, dst, with_qd):
                    for p in range(3):
                        wt, wk = chunk(cq + p)
                        self.proj(pss[0], "ps0", wt, wk, 0, 128, xbg, "xbg", N)
                        wt, wk = chunk(cqs + p)
                        self.proj(pss[1], "ps1", wt, wk, 0, 128, xbg, "xbg", N)
                        t1, t2 = self.sqb[0], self.sqb[1]
                        self.tt("dve", t1[:, 0:N], pss[0][:, 0:N], csg[:, 0, 0:N], ALU.mult, r=["ps0", "csg"], w=["sqb0"])
                        self.tt("dve", t2[:, 0:N], pss[1][:, 0:N], csg[:, 1, 0:N], ALU.mult, r=["ps1", "csg"], w=["sqb1"])
                        self.tt("pool", t1[:, 0:N], t1[:, 0:N], t2[:, 0:N], ALU.add, r=["sqb0", "sqb1"], w=["sqb0"])
                        self.copy("act", dst[:, p, 0:N], t1[:, 0:N], r=["sqb0"], w=["qrot" if with_qd else "krotB"])
                        if with_qd:
                            qb_ = qdec[:, p, 0:n].unsqueeze(1).to_broadcast([128, 4, n])
                            self.tt("dve", qd[:, p, 0:N].rearrange("p (t i) -> p t i", t=4), t1[:, 0:N].rearrange("p (t i) -> p t i", t=4), qb_,
                                    ALU.mult, r=["sqb0", "qdec"], w=["qd"])

                rotary(CH["qc"], CH["qcs"], qrot, True)
                if 'r1' in EXP:
                    return
                rotary(CH["kc"], CH["kcs"], krot, False)
                for p in range(3):
                    wt, wk = chunk(CH["gc"] + p)
                    self.proj(pss[p % 2], f"ps{p % 2}", wt, wk, 0, 128, xbg, "xbg", N)
                    self.act(sil[:, p, 0:N], pss[p % 2][:, 0:N], AF.Silu, r=[f"ps{p % 2}"], w=["sil"])
                for p in range(3):
                    wt, wk = chunk(CH["vc"] + p)
                    for t in range(4):
                        pb = pss[2 + t % 2]
                        for kc in range(8):
                            self.mm(pb[0:n, 0:128], xbg[:, kc, t * n:(t + 1) * n], wt[:, kc, :], kc == 0, kc == 7, r=["xbg", wk], w=[f"ps{2 + t % 2}"])
                        self.copy(self.alt(), vtokB[0:n, t, p * 128:(p + 1) * 128], pb[0:n, 0:128], r=[f"ps{2 + t % 2}"], w=["vtokB"])
                if 'r2' in EXP:
                    return
                for t in range(4):
                    tc_ = slice(t * n, (t + 1) * n)
                    m = g * 4 + t
                    for h in range(6):
                        po, p = (h % 2) * 64, h // 2
                        pb, pk = (pss[2], "ps2") if h % 2 == 0 else (pss[3], "ps3")
                        self.mm(pb[0:n, p * 128:p * 128 + n], krot[po:po + 64, p, tc_], qrot[po:po + 64, p, tc_], True, True, r=["krotB", "qrot"], w=[pk])
                    s1, s2 = self.sqb[0], self.sqb[1]
                    self.copy("act", s1[0:n, 0:384], pss[2][0:n, 0:384], r=["ps2"], w=["sqb0"])
                    self.copy("act", s2[0:n, 0:384], pss[3][0:n, 0:384], r=["ps3"], w=["sqb1"])
                    for h in range(6):
                        src, sk = (s1, "sqb0") if h % 2 == 0 else (s2, "sqb1")
                        p = h // 2
                        self.tt("dve", Sm[0:n, h, 0:n], src[0:n, p * 128:p * 128 + n], DT[0:n, h, 0:n], ALU.mult, r=[sk, "DT"], w=["Sm"])
                    if 'r3' in EXP:
                        continue
                    rsrc = keysrc["rst"](m)
                    for h in range(6):
                        po, p = (h % 2) * 64, h // 2
                        pb, pk = (pss[0], "ps0") if h % 2 == 0 else (pss[1], "ps1")
                        self.mm(pb[po:po + 64, p * 128:p * 128 + n], vtokB[0:n, t, h * 64:(h + 1) * 64], Sm[0:n, h, 0:n], True, False,
                                r=["vtokB", "Sm"], w=[pk])
                        self.mm(pb[po:po + 64, p * 128:p * 128 + n], rsrc[po:po + 64, p * 64:(p + 1) * 64], qd[po:po + 64, p, tc_], False, True,
                                r=["rst", "qd"], w=[pk])
                    if 'r4' in EXP:
                        continue
                    ysv = ysb[:, 0:3 * n].rearrange("p (c i) -> p c i", c=3)
                    self.copy("act", ysv[0:64], pss[0][0:64, 0:384].rearrange("p (c i) -> p c i", c=3)[:, :, 0:n], r=["ps0"], w=["ysb"])
                    self.copy("act", ysv[64:128], pss[1][64:128, 0:384].rearrange("p (c i) -> p c i", c=3)[:, :, 0:n], r=["ps1"], w=["ysb"])
                    mean, rstd = self.stats([ysb[:, 0:3 * n]], ["ysb"], 3 * n, bd64, "r")
                    self.tt("dve", ycn[:, 0:3 * n], ysb[:, 0:3 * n], mean[:, 0:3 * n], ALU.subtract, r=["ysb", "st_mean"], w=["ycn"])
                    self.tt("dve", ycn[:, 0:3 * n], ycn[:, 0:3 * n], rstd[:, 0:3 * n], ALU.mult, r=["ycn", "st_rstd"], w=["ycn"])
                    for p in range(3):
                        self.stt("dve", mixT[:, 5 + p, tc_], ycn[:, p * n:(p + 1) * n], vecs[:, V_GN + p:V_GN + p + 1], sil[:, p, tc_], ALU.mult, ALU.mult,
                                 r=["ycn", "vecs", "sil"], w=["mixT"])

                if 'R' in EXP:
                    return
                for c in range(3):
                    wt, wk = chunk(CH["qb"] + c)
                    for hh in range(2):
                        self.proj(pss[hh], f"ps{hh}", wt, wk, hh * 64, 64, xbg, "xbg", N)
                        q2v = qq[0:64, 0:24 * n].rearrange("p (t h i) -> p t h i", t=4, h=6)
                        self.copy(self.alt(), q2v[:, :, 2 * c + hh, :], pss[hh][0:64, 0:N].rearrange("p (t i) -> p t i", t=4), r=[f"ps{hh}"], w=["qq"])
                for c in range(2):
                    wt, wk = chunk(CH["qib"] + c)
                    for hh in range(4):
                        pb = pss[hh % 2]
                        self.proj(pb, f"ps{hh % 2}", wt, wk, hh * 32, 32, xbg, "xbg", N, po=64)
                        self.copy(self.alt(), qi2T[:, 0:N, 4 * c + hh], pb[64:96, 0:N], r=[f"ps{hh % 2}"], w=["qq"])
                for t in range(4):
                    self.dsa_tile(sq_, g, t, l, L, keysrc)

                if 'D' in EXP:
                    return
                for c in range(8):
                    wt, wk = chunk(CH["wo"] + c)
                    pb, pk = pss[c % 2], f"ps{c % 2}"
                    for kc in range(8):
                        self.mm(pb[:, 0:N], wt[:, kc, :], mixT[:, kc, 0:N], kc == 0, kc == 7, r=[wk, "mixT"], w=[pk])
                    self.stt("dve", xfg[:, c, 0:N], xfg[:, c, 0:N], ALPHA, pb[:, 0:N], ALU.mult, ALU.add, r=[xk[c], pk], w=[xk[c]])
                mean, rstd = self.stats([xfg[:, c, 0:N] for c in range(8)], xk, N, ones, "l1")
                for c in range(8):
                    self.tt("dve", xfg[:, c, 0:N], xfg[:, c, 0:N], mean[:, 0:N], ALU.subtract, r=[xk[c], "st_mean"], w=[xk[c]])
                    self.tt("pool", xfg[:, c, 0:N], xfg[:, c, 0:N], rstd[:, 0:N], ALU.mult, r=[xk[c], "st_rstd"], w=[xk[c]])
                    self.act(xfg[:, c, 0:N], xfg[:, c, 0:N], AF.Identity, r=[xk[c], "vecs"], w=[xk[c]],
                             scale=vecs[:, V_L1G + c:V_L1G + c + 1], bias=vecs[:, V_L1B + c:V_L1B + c + 1])
                P.dma("sp", X1T[:, :, gcols], xfg[:, :, 0:N], r=xk, w=["X1T"])
                if sq_ is PR:
                    bnd = self.bnd
                    for t in range(4):
                        self.copy("act", bnd[:, t, :, :], xfg[:, :, t * 128 + 126:t * 128 + 128], r=xk, w=["bnd"])
                    P.dma("sp", bncB[g * 4:(g + 1) * 4, :].rearrange("t (p x) -> p t x", x=16), bnd[:].rearrange("p t k c -> p t (k c)"),
                          r=["bnd"], w=["bncB"])

            self.bnd = sb("bnd", [128, 4, 8, 2], BF16)
            self._scores, self._junk = scores, junk
            self._junk2 = sb("junk2", [128, 4368], mybir.dt.uint8)
            self._sacc = sb("sacc", [128, 1])
            self._dsa = dict(Amat=Amat, Wd=Wd, rz=rz, pT=pT, mb=mb, col=col, ob=ob, Sel=Sel, Sel16=Sel16, CM=CM, E16=E16, dmask=dmask,
                             qq=qq, qi2T=qi2T, mixT=mixT, wtok=wtok, identF=identF, identB=identB)

            if 'a1' in EXP:
                P.barrier()
                return
            ksrc = dict(rst=lambda m: rst[:, m, :], kind="p", KTI=KTI, VA=VA)
            for g in range(PR.ng if st >= 4 else 1):
                group(PR, g, ksrc)
            P.barrier()
            if st >= 9:
                P.cc([L["bncB32"]], [L["gatB32"]], r=[], w=["gatB"])
            if st <= 3:
                return
            with ExitStack() as ss:
                KTIs = self.sb("KTIs", [128, 2064], BF16, st=ss)
                VAs = self.sb("VAs", [128, 17, 65], BF16, st=ss)
                cK = self.sb("cK", [128, 16, 96], st=ss)
                rstS = self.sb("rstS", [128, 4, 192], BF16, st=ss)
                stSf = self.sb("stSf", [128, 4, 192], st=ss)
                P.dma("sp", stSf[:], L["stSd"][l], w=["stSf"])
                self.copy("act", rstS[:], stSf[:], r=["stSf"], w=["rst"])
                ksrc = dict(rst=lambda m: rstS[:, m, :], kind="s", KTIs=KTIs, VAs=VAs, cK=cK, ck=L["ckd"][l], cv=L["cvd"][l], cki=L["ckid"][l],
                            ktS=L["ktS"], vS=L["vS"])
                group(SM, 0, ksrc)
                P.barrier()

    def dsa_tile(self, sq_, g, t, l, L, ks):
        P, pss, d = self.P, self.pss, self._dsa
        n = sq_.n
        ng = n // 16
        m = g * 4 + t
        tc_ = slice(t * n, (t + 1) * n)
        scores, junk = self._scores, self._junk
        Amat, Wd, rz, pT, mb, col, ob = d["Amat"], d["Wd"], d["rz"], d["pT"], d["mb"], d["col"], d["ob"]
        identF, identB, mixT = d["identF"], d["identB"], d["mixT"]
        qi2T = d["qi2T"]
        q2f = d["qq"][0:64, 0:24 * n].rearrange("p (t x) -> p t x", t=4)
        SelX = d["Sel"] if n == 128 else d["Sel16"]
        if ks["kind"] == "p":
            KTI, VA = ks["KTI"], ks["VA"]
            kkey, vkey = "KTI", "VA"
            iblocks = [(KTI[64:96, kb].rearrange("p j t -> p (j t)"), 512, kb * 512) for kb in range(m + 1)]
            ablocks = [(KTI[0:64, kb, jj, :], VA[:, kb * 4 + jj, :], 128, (kb * 4 + jj) * 128) for kb in range(m + 1) for jj in range(4)]
            Lk = (m + 1) * 512
        else:
            KTIs, VAs, cK = ks["KTIs"], ks["VAs"], ks["cK"]
            kkey, vkey = "KTIs", "VAs"
            b = t
            for k in range(16):
                P.dma("sp", cK[:, k, 0:64], ks["ck"][b][k * 128:(k + 1) * 128, :], w=["cK"])
                P.dma("sp", cK[:, k, 64:96], ks["cki"][b][k * 128:(k + 1) * 128, :], w=["cK"])
            for k in range(16):
                pb, pk = pss[k % 2], f"ps{k % 2}"
                self.tr(pb[0:96, 0:128], cK[:, k, :], identF[:], r=["cK", "consts"], w=[pk])
                self.copy(self.alt(), KTIs[0:96, k * 128:(k + 1) * 128], pb[0:96, 0:128], r=[pk], w=["KTIs"])
            self.copy("dve", KTIs[0:96, 2048:2064], ks["ktS"][0:96, b, :], r=["ktS"], w=["KTIs"])
            self.memset("dve", VAs[:], 1.0, w=["VAs"])
            for k in range(16):
                P.dma("pool", VAs[:, k, 0:64], ks["cv"][b][k * 128:(k + 1) * 128, :], w=["VAs"])
            self.copy("dve", VAs[0:16, 16, 0:64], ks["vS"][0:16, b, :], r=["vS"], w=["VAs"])

            iblocks = [(KTIs[64:96, kb * 512:(kb + 1) * 512], 512, kb * 512) for kb in range(4)] + [(KTIs[64:96, 2048:2064], 16, 2048)]
            ablocks = [(KTIs[0:64, k * 128:(k + 1) * 128], VAs[:, k, :], 128, k * 128) for k in range(16)] + [(KTIs[0:64, 2048:2064], VAs[0:16, 16, :], 16, 2048)]
            Lk = 2064
        wv = d["wtok"][sq_.name]
        for a in range(16):
            self.ts("dve", Amat[0:n, a * 8:(a + 1) * 8], wv[0:n, m, :], d["E16"][0:n, a:a + 1], None, ALU.mult, r=["E16", "wtok"], w=["Amat"])
        self.mm(pss[4][:, 0:n], Amat[0:n, :], identB[0:n, 0:n], True, True, r=["Amat", "consts"], w=["ps4"])
        wall = self.sqb[1]
        self.copy("act", wall[:, 0:n], pss[4][:, 0:n], r=["ps4"], w=["sqb1"])
        for gq in range(ng):
            self.tt("dve", Wd[:, gq, 0:n], wall[:, 0:n], d["CM"][:, gq, 0:n], ALU.mult, r=["sqb1", "CM"], w=["Wd"])
        tmpm = self.sqb[0]
        steps = [(bi, gq) for bi in range(len(iblocks)) for gq in range(ng)]

        def emit_z(si):
            bi, gq = steps[si]
            rhs_ap, nk, c0 = iblocks[bi]
            zb = (2, 3, 0, 1)[si % 4]
            lhs = qi2T[:, t * n + 16 * gq:t * n + 16 * gq + 16, :].rearrange("p a h -> p (a h)")
            self.mm(pss[zb][:, 0:nk], lhs, rhs_ap, True, True, r=["qq", kkey], w=[f"ps{zb}"])

        for si in range(min(2, len(steps))):
            emit_z(si)
        for si, (bi, gq) in enumerate(steps):
            rhs_ap, nk, c0 = iblocks[bi]
            zb = (2, 3, 0, 1)[si % 4]
            pz, pzk = pss[zb], f"ps{zb}"
            rzb, rk = rz[si % 4], f"rz{si % 4}"
            if self.alt() == "act":
                self.act(rzb[:, 0:nk], pz[:, 0:nk], AF.Relu, r=[pzk], w=[rk])
            else:
                self.ts("dve", rzb[:, 0:nk], pz[:, 0:nk], 0.0, None, ALU.max, r=[pzk], w=[rk])
            self.mm(pss[4][0:n, 0:nk], Wd[:, gq, 0:n], rzb[:, 0:nk], gq == 0, gq == ng - 1, r=["Wd", rk], w=["ps4"])
            if si + 2 < len(steps):
                emit_z(si + 2)
            if gq == ng - 1:
                if ks["kind"] == "p" and bi == len(iblocks) - 1:
                    self.tt("dve", tmpm[0:n, 0:nk], pss[4][0:n, 0:nk], d["dmask"][0:n, 0:nk], ALU.subtract, r=["ps4", "dmask"], w=["sqb0"])
                    self.tt("dve", scores[0:n, c0:c0 + nk], pss[4][0:n, 0:nk], d["dmask"][0:n, 0:nk], ALU.add, r=["ps4", "dmask"], w=["scores"])
                else:
                    self.copy(self.alt(), scores[0:n, c0:c0 + nk], pss[4][0:n, 0:nk], r=["ps4"], w=["scores"])
        lo, w0, mid, cnt, gk, t5 = (col[0:n, i:i + 1] for i in range(6))
        if ks["kind"] == "p":
            self.red(lo, tmpm[0:n, 0:512], ALU.min, r=["sqb0"], w=["col"])
            if m > 0:
                self.red(t5, scores[0:n, 0:m * 512], ALU.min, r=["scores"], w=["col"])
                self.tt("dve", lo, lo, t5, ALU.min, r=["col"], w=["col"])
        else:
            self.red(lo, scores[0:n, 0:Lk], ALU.min, r=["scores"], w=["col"])
        self.red(w0, scores[0:n, 0:Lk], ALU.max, r=["scores"], w=["col"])
        self.tt("dve", w0, w0, lo, ALU.subtract, r=["col"], w=["col"])
        Ld = (Lk * 15 // 32) // 16 * 16 if Lk >= 1024 else Lk
        La = Lk - Ld
        sc_ap, jk_ap = scores[0:n, 0:Ld], junk[0:n, 0:Ld]
        if La:
            sa_ap, ja_ap = scores[0:n, Ld:Lk], self._junk2[0:n, 0:La]
            sacc = self._sacc[0:n, 0:1]
        thr = 255.5 - 0.5 * La
        for k in range(NBIS):
            hw = 2.0 ** (-(k + 1))
            self.ts("dve", mid, w0, hw, lo, ALU.mult, ALU.add, r=["col"], w=["mid"])
            P.op("dve", lambda E: E.tensor_scalar(out=jk_ap, in0=sc_ap, scalar1=mid, scalar2=None, op0=ALU.is_ge, op1=ALU.add, accum_out=cnt),
                 r=["scores", "mid"], w=["junk", "col"])
            if La:
                P.op("act", lambda E: E.activation(out=ja_ap, in_=sa_ap, func=AF.Sign, bias=mid, scale=-1.0, accum_out=sacc),
                     r=["scores", "mid"], w=["junk2", "sacc"])
                self.stt("dve", cnt, sacc, -0.5, cnt, ALU.mult, ALU.add, r=["col", "sacc"], w=["col"])
            self.ts("dve", gk, cnt, thr, hw, ALU.is_ge, ALU.mult, r=["col"], w=["col"])
            self.stt("dve", lo, gk, w0, lo, ALU.mult, ALU.add, r=["col", "mid"], w=["col"])
        pO = pss[7]
        nb = len(ablocks)

        def emit_logits(bi):
            kt, v, nk, c0 = ablocks[bi]
            i2 = bi % 2
            mbb, mk = mb[i2], f"mb{i2}"
            self.ts("dve", mbb[0:n, 0:nk], scores[0:n, c0:c0 + nk], lo, -30000.0, ALU.is_lt, ALU.mult, r=["scores", "col"], w=[mk])
            la, lb = (5, 6) if i2 == 0 else (0, 1)
            self.mm(pss[la][0:nk, 0:4 * n], kt, q2f[:, t, 0:4 * n], True, False, r=[kkey, "qq"], w=[f"ps{la}"])
            self.mm(pss[la][0:nk, 0:4 * n], mbb[0:n, 0:nk], SelX[0:n, 0:4 * n], False, True, r=[mk, "Sel"], w=[f"ps{la}"])
            self.mm(pss[lb][0:nk, 0:2 * n], kt, q2f[:, t, 4 * n:6 * n], True, False, r=[kkey, "qq"], w=[f"ps{lb}"])
            self.mm(pss[lb][0:nk, 0:2 * n], mbb[0:n, 0:nk], SelX[0:n, 4 * n:6 * n], False, True, r=[mk, "Sel"], w=[f"ps{lb}"])

        emit_logits(0)
        for bi, (kt, v, nk, c0) in enumerate(ablocks):
            if bi + 1 < nb:
                emit_logits(bi + 1)
            i2 = bi % 2
            la, lb = (5, 6) if i2 == 0 else (0, 1)
            ptb, pk = pT[i2], f"pT{i2}"
            self.act(ptb[0:nk, 0:4 * n], pss[la][0:nk, 0:4 * n], AF.Exp, r=[f"ps{la}"], w=[pk], scale=0.125)
            self.act(ptb[0:nk, 4 * n:6 * n], pss[lb][0:nk, 0:2 * n], AF.Exp, r=[f"ps{lb}"], w=[pk], scale=0.125)
            for h in range(6):
                self.mm(pO[0:n, h * 65:(h + 1) * 65], ptb[0:nk, h * n:(h + 1) * n], v[0:nk, :], bi == 0 and h == 0, bi == nb - 1 and h == 5,
                        r=[pk, vkey], w=["ps7"])
        posb = self.sqb[1]
        self.copy("act", posb[0:n, 0:390], pO[0:n, 0:390], r=["ps7"], w=["sqb1"])
        pov = posb[0:n, 0:390].rearrange("p (h e) -> p h e", e=65)
        rden = col[0:n, 8:14]
        for h in range(6):
            P.op("dve", (lambda hh: (lambda E: E.reciprocal(out=col[0:n, 8 + hh:9 + hh], in_=posb[0:n, hh * 65 + 64:hh * 65 + 65])))(h), r=["sqb1"], w=["col"])
        for h in range(6):
            self.ts("dve", ob[0:n, h * 64:(h + 1) * 64], posb[0:n, h * 65:h * 65 + 64], col[0:n, 8 + h:9 + h], None, ALU.mult, r=["sqb1", "col"], w=["ob"])
        for c in range(3):
            pb, pk = pss[c % 2], f"ps{c % 2}"
            self.tr(pb[:, 0:n], ob[0:n, c * 128:(c + 1) * 128], identF[0:n, 0:n], r=["ob", "consts"], w=[pk])
            self.copy(self.alt(), mixT[:, 2 + c, tc_], pb[:, 0:n], r=[pk], w=["mixT"])

    def phaseC(self, l, L):
        P, pss = self.P, self.pss
        st = self.stage
        ones, identF, vecs = L["ones"], L["identF"], L["vecs"]
        XT, X1T, gatB, bncB32, gatB32 = L["XT"], L["X1T"], L["gatB"], L["bncB32"], L["gatB32"]
        wCg, wCu, wCd = L["wCg"][l], L["wCu"][l], L["wCd"][l]
        V_L2G, V_L2B, V_CW, V_CB = L["V_L2G"], L["V_L2B"], L["V_CW"], L["V_CB"]
        convo, convos = L["convo"], L["convos"]
        last = (l == DEPTH - 1)
        P.barrier()
        with ExitStack() as ph:
            sb = lambda name, shape, dt=F32: self.sb(name, shape, dt, st=ph)
            self.wring = [sb(f"wr{i}", [128, 8, 128], BF16) for i in range(4)]
            wdr = [sb(f"wdr{i}", [128, NFC, 128], BF16) for i in range(2)]
            x1f = sb("x1f", [128, 8, 512])
            x1b = sb("x1b", [128, 8, 512], BF16)
            hT = sb("hT", [128, NFC, 512], BF16)
            hgx = sb("hgx", [128, 4, 130])
            c1 = sb("c1", [128, 512])
            gl = sb("gl", [128, 512])
            prevb = sb("prevb", [128, 8, 4, 2], BF16)
            gbt = sb("gbt", [128, 5, 4, 16], BF16)
            acc = sb("acc", [128, 4, 16])
            tmp8 = sb("tmp8", [128, 8])
            sel5 = sb("sel5", [128, 5])
            cvpS = sb("cvpS", [128, NFC, 4, 2])
            ytok = [sb(f"ytok{i}", [128, D]) for i in range(2)]
            P.dma("sp", sel5[:], L["sel5d"], w=["sel5"])
            P.dma("sp", cvpS[:], L["cvpd"][l], w=["cvpS"])
            gBv = gatB.rearrange("r (p x) -> p r x", x=16)
            yi = 0
            for sq_ in (PR, SM):
                n, N = sq_.n, sq_.N
                for g in range(sq_.ng):
                    c0 = sq_.off + g * N
                    gcols = slice(c0, c0 + N)
                    xk = [f"x1f{c}" for c in range(8)]
                    P.dma("sp", x1f[:, :, 0:N], X1T[:, :, gcols], r=["X1T"], w=xk)
                    self.copy("act", x1b[:, :, 0:N], x1f[:, :, 0:N], r=xk, w=["x1b"])
                    if sq_ is PR:
                        for k in range(4):
                            P.dma("sp", gbt[:, k, :, :], gBv[:, k * 16 + 4 * g:k * 16 + 4 * g + 4, :], r=["gatB"], w=["gbt"])
                        if g == 0:
                            self.memset("dve", gbt[:, 4, 0, :], 0.0, w=["gbt"])
                            P.dma("sp", gbt[:, 4, 1:4, :], gBv[:, 48:51, :], r=["gatB"], w=["gbt"])
                        else:
                            P.dma("sp", gbt[:, 4, :, :], gBv[:, 48 + 4 * g - 1:48 + 4 * g + 3, :], r=["gatB"], w=["gbt"])
                        self.ts("dve", acc[:], gbt[:, 0, :, :], sel5[:, 0:1], None, ALU.mult, r=["gbt", "sel5"], w=["acc"])
                        for k in range(1, 5):
                            self.stt("dve", acc[:], gbt[:, k, :, :], sel5[:, k:k + 1], acc[:], ALU.mult, ALU.add, r=["gbt", "sel5", "acc"], w=["acc"])
                        for t in range(4):
                            self.copy("dve", prevb[:, :, t, :], acc[:, t, :].rearrange("p (k c) -> p k c", c=2), r=["acc"], w=["prevb"])
                    for f in range(NFC):
                        wg, wgk = self.wchunk(wCg[f])
                        wu, wuk = self.wchunk(wCu[f])
                        for kc in range(8):
                            self.mm(pss[0][:, 0:N], wg[:, kc, :], x1b[:, kc, 0:N], kc == 0, kc == 7, r=[wgk, "x1b"], w=["ps0"])
                        if sq_ is PR:
                            for kc in range(8):
                                self.mm(pss[2][:, 0:8], wg[:, kc, :], prevb[:, kc, :, :].rearrange("p t c -> p (t c)"), kc == 0, kc == 7,
                                        r=[wgk, "prevb"], w=["ps2"])
                        for kc in range(8):
                            self.mm(pss[1][:, 0:N], wu[:, kc, :], x1b[:, kc, 0:N], kc == 0, kc == 7, r=[wuk, "x1b"], w=["ps1"])
                        for t in range(4):
                            self.copy("act", hgx[:, t, 2:2 + n], pss[0][:, t * n:(t + 1) * n], r=["ps0"], w=["hgx"])
                        if sq_ is PR:
                            self.copy("act", tmp8[:], pss[2][:, 0:8], r=["ps2"], w=["tmp8"])
                            self.copy("dve", hgx[:, :, 0:2], tmp8[:].rearrange("p (t c) -> p t c", c=2), r=["tmp8"], w=["hgx"])
                        else:
                            self.copy("dve", hgx[:, :, 0:2], cvpS[:, f, :, :], r=["cvpS"], w=["hgx"])
                        cw = lambda j: vecs[:, V_CW + f * 3 + j:V_CW + f * 3 + j + 1]
                        c1v = c1[:, 0:N].rearrange("p (t i) -> p t i", t=4)
                        self.ts("dve", c1v, hgx[:, :, 2:2 + n], cw(2), vecs[:, V_CB + f:V_CB + f + 1], ALU.mult, ALU.add, r=["hgx", "vecs"], w=["c1"])
                        self.stt("dve", c1v, hgx[:, :, 1:1 + n], cw(1), c1v, ALU.mult, ALU.add, r=["hgx", "vecs", "c1"], w=["c1"])
                        self.stt("dve", c1v, hgx[:, :, 0:n], cw(0), c1v, ALU.mult, ALU.add, r=["hgx", "vecs", "c1"], w=["c1"])
                        self.act(gl[:, 0:N], c1[:, 0:N], AF.Gelu_apprx_tanh, r=["c1"], w=["gl"])
                        self.tt("dve", hT[:, f, 0:N], gl[:, 0:N], pss[1][:, 0:N], ALU.mult, r=["gl", "ps1"], w=["hT"])
                        if sq_ is PR and g == 3:
                            self.copy("dve", convo[:, f, :], hgx[:, 3, n:n + 2], r=["hgx"], w=["convo"])
                        if sq_ is SM:
                            self.copy("dve", convos[:, f, :, :], hgx[:, :, n:n + 2], r=["hgx"], w=["convos"])
                    for c in range(8):
                        i = c % 2
                        P.dma("pool", wdr[i][:], wCd[c], w=[f"wdr{i}"])
                        pb, pk = pss[c % 2], f"ps{c % 2}"
                        for f in range(NFC):
                            self.mm(pb[:, 0:N], wdr[i][:, f, :], hT[:, f, 0:N], f == 0, f == NFC - 1, r=[f"wdr{i}", "hT"], w=[pk])
                        self.stt("dve", x1f[:, c, 0:N], x1f[:, c, 0:N], ALPHA, pb[:, 0:N], ALU.mult, ALU.add, r=[xk[c], pk], w=[xk[c]])
                    mean, rstd = self.stats([x1f[:, c, 0:N] for c in range(8)], xk, N, ones, "l2")
                    for c in range(8):
                        self.tt("dve", x1f[:, c, 0:N], x1f[:, c, 0:N], mean[:, 0:N], ALU.subtract, r=[xk[c], "st_mean"], w=[xk[c]])
                        self.tt("pool", x1f[:, c, 0:N], x1f[:, c, 0:N], rstd[:, 0:N], ALU.mult, r=[xk[c], "st_rstd"], w=[xk[c]])
                        self.act(x1f[:, c, 0:N], x1f[:, c, 0:N], AF.Identity, r=[xk[c], "vecs"], w=[xk[c]],
                                 scale=vecs[:, V_L2G + c:V_L2G + c + 1], bias=vecs[:, V_L2B + c:V_L2B + c + 1])
                    if not last:
                        P.dma("sp", XT[:, :, gcols], x1f[:, :, 0:N], r=xk, w=["XT"])
                    else:
                        oy = L["o_y"] if sq_ is PR else L["o_ys"]
                        for t in range(4):
                            yb = ytok[yi % 2]
                            yk = f"ytok{yi % 2}"
                            yi += 1
                            for c in range(8):
                                bank = (2, 3, 4, 7)[c % 4]
                                self.tr(pss[bank][0:n, 0:128], x1f[:, c, t * n:(t + 1) * n], identF[:], r=[xk[c], "consts"], w=[f"ps{bank}"])
                                self.copy(self.alt(), yb[0:n, c * 128:(c + 1) * 128], pss[bank][0:n, 0:128], r=[f"ps{bank}"], w=[yk])
                            r0 = (g * 4 + t) * n
                            P.dma("sp", oy[r0:r0 + n, :], yb[0:n, :], r=[yk])
            P.dma("sp", L["o_conv"][l], convo[:], r=["convo"])
            P.dma("sp", L["o_convs"][l], convos[:], r=["convos"])
            P.barrier()
SPL = dict(ua=0, va=256, qb=512, kb=896, vb=960, qib=1024, kib=1280, wib=1312, qc=1320, kc=1704, vc=2088, gc=2472)
LOG_G = np.log(1.0 - 2.0 ** (-5.0 - np.arange(6, dtype=np.float64)))


def _chunked(w):
    return np.ascontiguousarray(w.reshape(8, 128, -1).transpose(1, 0, 2))


def _swap_heads(w):
    m = w.reshape(w.shape[0], -1, 2, 32)
    return np.ascontiguousarray(m[:, :, ::-1, :]).reshape(w.shape[0], -1)


def _rot_tables(pos):
    half = 32
    freqs = (10000.0 ** (-np.arange(half, dtype=np.float32) / half)).astype(np.float32)
    ang = pos.astype(np.float32)[None, :] * freqs[:, None]
    cos = np.cos(ang)
    sin = np.sin(ang)
    cosT = np.concatenate([cos, cos, cos, cos], 0).astype(np.float32)
    sinT = np.concatenate([-sin, sin, -sin, sin], 0).astype(np.float32)
    return cosT, sinT


def _pairlay(per_head):
    t = np.zeros((128, 192), np.float64)
    for h in range(6):
        t[(h % 2) * 64:(h % 2) * 64 + 64, (h // 2) * 64:(h // 2) * 64 + 64] = per_head[h]
    return t


def _const_tables():
    c = {}
    c["identf"] = np.eye(128, dtype=np.float32)
    jj = np.arange(128, dtype=np.float64)
    kd0 = np.exp((127 - jj)[:, None] * LOG_G[None, :]) * 0.125
    kd1 = np.zeros((128, 6))
    kd1[:16] = np.exp((15 - jj[:16])[:, None] * LOG_G[None, :]) * 0.125
    c["kdec"] = np.stack([np.repeat(kd0, 64, 1), np.repeat(kd1, 64, 1)]).astype(np.float32)
    diff = jj[None, :] - jj[:, None]
    dt = np.where(diff[:, None, :] >= 0, np.exp(np.maximum(diff, 0)[:, None, :] * LOG_G[None, :, None]), 0.0) * 0.125
    c["dtT"] = dt.astype(np.float32)
    lane_h = lambda pair: np.array([2 * pair + (p // 64) for p in range(128)])
    qp = np.zeros((128, 3, 512))
    qs = np.zeros((128, 3, 64))
    for pair in range(3):
        lg = LOG_G[lane_h(pair)]
        qp[:, pair, :] = np.exp(((np.arange(512) % 128) + 1)[None, :] * lg[:, None])
        qs[:, pair, :] = np.exp(((np.arange(64) % 16) + 1)[None, :] * lg[:, None])
    c["qdecP"], c["qdecS"] = qp.astype(np.float32), qs.astype(np.float32)
    c["sel"] = (np.arange(128)[:, None] == (np.arange(768) % 128)[None, :]).astype(np.float32)
    c["sel16"] = (np.arange(16)[:, None] == (np.arange(96) % 16)[None, :]).astype(np.float32)
    c["cm"] = np.broadcast_to(((np.arange(128) // 16)[None, None, :] == np.arange(8)[None, :, None]), (128, 8, 128)).astype(np.float32).copy()
    c["e16"] = ((np.arange(128) % 16)[:, None] == np.arange(16)[None, :]).astype(np.float32)
    bd = np.zeros((128, 128), np.float32)
    bd[:64, :64] = 1 / 64
    bd[64:, 64:] = 1 / 64
    c["bd64"] = bd
    c["onesd"] = np.full((128, 128), 1 / 1024, np.float32)
    c["umask"] = (np.arange(128)[None, :] >= np.arange(128)[:, None]).astype(np.float32)
    c["cdecS"] = _pairlay(np.exp(16 * LOG_G)).astype(np.float32)
    return c


def make_inputs(inp):
    maps = []
    cst = _const_tables()
    per_layer = []
    for l in range(DEPTH):
        W = inp["w_in"][l]
        sl = lambda k, n: W[:, SPL[k]:SPL[k] + n]
        d = {}
        d["wA"] = _chunked(np.concatenate([sl("vc", 384), sl("kb", 64), sl("vb", 64), sl("kib", 32), sl("wib", 8),
                                           sl("kb", 64), sl("kib", 32), sl("kc", 384), _swap_heads(sl("kc", 384))], 1))
        wb = np.concatenate([sl("ua", 256), sl("va", 256), sl("qc", 384), _swap_heads(sl("qc", 384)), sl("kc", 384),
                             _swap_heads(sl("kc", 384)), sl("gc", 384), sl("vc", 384), sl("qb", 384), sl("qib", 256), inp["w_out"][l]], 1)
        d["wB"] = np.ascontiguousarray(_chunked(wb).reshape(128, 8, 35, 128).transpose(2, 0, 1, 3))
        d["wCg"] = np.ascontiguousarray(_chunked(inp["w_gate"][l]).reshape(128, 8, NFC, 128).transpose(2, 0, 1, 3))
        d["wCu"] = np.ascontiguousarray(_chunked(inp["w_up"][l]).reshape(128, 8, NFC, 128).transpose(2, 0, 1, 3))
        wd = inp["w_down"][l].reshape(NFC, 128, 8, 128)
        d["wCd"] = np.ascontiguousarray(wd.transpose(2, 1, 0, 3))
        fm = lambda v, k: v.reshape(k, 128).T
        cw = inp["conv_w"][l].reshape(3, NFC, 128).transpose(2, 1, 0).reshape(128, NFC * 3)
        d["vec"] = np.ascontiguousarray(np.concatenate([
            fm(inp["a_ln_g"][l], 2), fm(inp["a_ln_b"][l], 2), fm(inp["c_gn_g"][l], 3), fm(inp["ln1_g"][l], 8), fm(inp["ln1_b"][l], 8),
            fm(inp["ln2_g"][l], 8), fm(inp["ln2_b"][l], 8), cw, fm(inp["conv_b"][l], NFC)], 1).astype(np.float32))
        bs = inp["a_bs"][l]
        g_of = np.array([[ch * 2 + p // 64 for ch in range(2)] for p in range(128)])
        d["bsP"] = np.ascontiguousarray(np.tile(bs[g_of], (1, 1, 4)).astype(np.float32))
        d["bsS"] = np.ascontiguousarray(np.tile(bs[g_of][:, :, :16], (1, 1, 4)).astype(np.float32))
        d["wmT"] = np.ascontiguousarray(inp["a_ws"][l].transpose(2, 0, 1))
        per_layer.append(d)
    for c in range(8):
        b, j = c // 4, c % 4
        m = dict(cst)
        tiles = [4 * mm + j for mm in range(NT)]
        pos = np.concatenate([np.arange(t * 128, (t + 1) * 128) for t in tiles])
        m["xp"] = np.ascontiguousarray(inp["x_prompt"][b][pos])
        m["xs"] = np.ascontiguousarray(inp["x_sample"][4 * c:4 * c + 4].reshape(NS, D))
        pos_all = np.concatenate([pos, np.tile(2048 + np.arange(16), 4)])
        m["cosT"], m["sinT"] = _rot_tables(pos_all)
        dm = np.zeros((128, 512), np.float32)
        for jp in range(4):
            if jp > j:
                dm[:, jp * 128:(jp + 1) * 128] = NEG
            elif jp == j:
                dm[0:64, jp * 128 + 64:(jp + 1) * 128] = NEG
        m["dmask"] = dm
        rc = np.zeros((128, 9, 192))
        rc[:, 0] = _pairlay(np.exp(128 * j * LOG_G))
        for jp in range(3):
            if jp < j:
                rc[:, 1 + jp] = _pairlay(np.exp(128 * (j - 1 - jp) * LOG_G))
        rc[:, 4] = _pairlay(np.exp(512 * LOG_G))
        for jp in range(4):
            rc[:, 5 + jp] = _pairlay(np.exp(128 * (3 - jp) * LOG_G))
        m["rcoef"] = rc.astype(np.float32)
        s5 = np.zeros((128, 5), np.float32)
        s5[:, (j - 1) if j >= 1 else 4] = 1.0
        m["sel5"] = s5
        for l in range(DEPTH):
            for k, v in per_layer[l].items():
                m[f"{k}{l}"] = v
            m[f"ck{l}"] = np.ascontiguousarray(inp["cache_b_k"][l, 4 * c:4 * c + 4])
            m[f"cv{l}"] = np.ascontiguousarray(inp["cache_b_v"][l, 4 * c:4 * c + 4])
            m[f"cki{l}"] = np.ascontiguousarray(inp["cache_b_kidx"][l, 4 * c:4 * c + 4])
            sr = inp["state_ret"][l, 4 * c:4 * c + 4]
            t = np.zeros((128, 4, 192), np.float32)
            for h in range(6):
                t[(h % 2) * 64:(h % 2) * 64 + 64, :, (h // 2) * 64:(h // 2) * 64 + 64] = sr[:, h].transpose(1, 0, 2)
            m[f"stS{l}"] = t
            cv = inp["state_ffn_conv"][l, 4 * c:4 * c + 4]
            m[f"cvp{l}"] = np.ascontiguousarray(cv.reshape(4, 2, NFC, 128).transpose(3, 2, 0, 1))
        maps.append(m)
    return maps


def _unpair(t):
    out = np.zeros(t.shape[:-2] + (6, 64, 64), np.float32)
    for h in range(6):
        out[..., h, :, :] = t[..., (h % 2) * 64:(h % 2) * 64 + 64, (h // 2) * 64:(h // 2) * 64 + 64]
    return out


def assemble(R):
    y = np.zeros((2, 8192, D), np.float32)
    ys = np.zeros((32, 16, D), np.float32)
    kp = np.zeros((DEPTH, 2, 8192, 64), np.float32)
    vp = np.zeros((DEPTH, 2, 8192, 64), np.float32)
    kip = np.zeros((DEPTH, 2, 8192, 32), np.float32)
    rp = np.zeros((DEPTH, 2, 6, 64, 64), np.float32)
    cp = np.zeros((DEPTH, 2, 2, DFF), np.float32)
    ks = np.zeros((DEPTH, 32, 16, 64), np.float32)
    vs = np.zeros((DEPTH, 32, 16, 64), np.float32)
    kis = np.zeros((DEPTH, 32, 16, 32), np.float32)
    rs = np.zeros((DEPTH, 32, 6, 64, 64), np.float32)
    cs = np.zeros((DEPTH, 32, 2, DFF), np.float32)
    avs = np.zeros((DEPTH, 32, 16, 256), np.float32)
    for c in range(8):
        b, j = c // 4, c % 4
        r = R[c]
        pos = np.concatenate([np.arange((4 * mm + j) * 128, (4 * mm + j + 1) * 128) for mm in range(NT)])
        y[b, pos] = r["o_y"]
        ys[4 * c:4 * c + 4] = r["o_ys"].reshape(4, 16, D)
        kp[:, b, pos] = r["o_k"]
        vp[:, b, pos] = r["o_v"]
        kip[:, b, pos] = r["o_ki"]
        if j == 0:
            rp[:, b] = _unpair(r["o_ret"])
        if j == 3:
            cp[:, b] = r["o_conv"].transpose(0, 3, 2, 1).reshape(DEPTH, 2, DFF)
        ks[:, 4 * c:4 * c + 4] = r["o_ks"].reshape(DEPTH, 4, 16, 64)
        vs[:, 4 * c:4 * c + 4] = r["o_vs"].reshape(DEPTH, 4, 16, 64)
        kis[:, 4 * c:4 * c + 4] = r["o_kis"].reshape(DEPTH, 4, 16, 32)
        rs[:, 4 * c:4 * c + 4] = _unpair(r["o_rets"].transpose(0, 2, 1, 3))
        cs[:, 4 * c:4 * c + 4] = r["o_convs"].transpose(0, 3, 4, 2, 1).reshape(DEPTH, 4, 2, DFF)
        avs[:, 4 * c:4 * c + 4] = r["o_avs"].reshape(DEPTH, 4, 16, 256)
    return (y, ys, kp, vp, kip, rp, cp, ks, vs, kis, rs, cs, avs)


_CACHE = {}


def kernel(**inputs):
    inp = {k: np.asarray(v, dtype=np.float32) for k, v in inputs.items()}
    stage = _CACHE.get("stage", 99)
    if "nc" not in _CACHE:
        bld = Builder(stage)
        _CACHE["nc"] = bld.build()
        _CACHE["bld"] = bld
    nc = _CACHE["nc"]
    maps = make_inputs(inp)
    res = run_bass_kernel_spmd(nc, maps, core_ids=list(range(8)))
    _CACHE["raw"] = res.results
    return assemble(res.results)
```

```python
import numpy as np
import os
EXP = os.environ.get('EXP', '')
from contextlib import ExitStack
import concourse.bass as bass
import concourse.mybir as mybir
from concourse.bass_utils import run_bass_kernel_spmd

F32 = mybir.dt.float32
BF16 = mybir.dt.bfloat16
AF = mybir.ActivationFunctionType
ALU = mybir.AluOpType
AX = mybir.AxisListType

D = 1024
NT = 16
TS = 128
NTOK = NT * TS
NS = 64
DFF = 2816
NFC = DFF // 128
DEPTH = 2
ALPHA = (2 * DEPTH) ** 0.25
EPS = 1e-5
GROUPS = [[0, 1, 2, 3], [4, 5, 6, 7]]
NEG = -1.0e30


class Prog:
    NSLOT = 8

    def __init__(self, nc):
        self.nc = nc
        self.ops = []
        self.buf = {}
        self.slot_last = {}
        self.slot_rr = {"sp": 0, "pool": 0, "act": 0}
        self.last = {}
        self.dma_pending = []

    def _deps(self, eng, r, w):
        deps = set()
        for k in r:
            b = self.buf.setdefault(k, [None, []])
            if b[0] is not None:
                deps.add(b[0])
            if isinstance(k, str) and k.startswith("ps"):
                deps.update(d for d in b[1] if self.ops[d]["eng"] != eng)
        for k in w:
            b = self.buf.setdefault(k, [None, []])
            if b[0] is not None:
                deps.add(b[0])
            deps.update(b[1])
        if eng == "pe":
            deps = {d for d in deps if not (self.ops[d]["eng"] == "pe" and self.ops[d]["kind"] == "op")}
        return deps

    def _record(self, oid, r, w):
        me = self.ops[oid]
        for k in r:
            rl = self.buf[k][1]
            if me["kind"] == "op":
                rl[:] = [d for d in rl if not (self.ops[d]["kind"] == "op" and self.ops[d]["eng"] == me["eng"])]
            rl.append(oid)
        for k in w:
            self.buf[k] = [oid, []]

    def op(self, eng, fn, r=(), w=()):
        deps = self._deps(eng, r, w)
        oid = len(self.ops)
        self.ops.append(dict(eng=eng, kind="op", fn=fn, deps=deps))
        self._record(oid, r, w)
        self.last[eng] = oid
        return oid

    def barrier(self):
        ids = set(self.last.values()) | set(self.dma_pending)
        for e in ["pe", "act", "dve", "pool", "sp"]:
            self.ops.append(dict(eng=e, kind="bar", deps=set(ids)))
        self.buf = {}
        self.dma_pending = []

    def dma(self, q, out, in_, r=(), w=()):
        deps = self._deps(q, r, w)
        slot = self.slot_rr[q] % self.NSLOT
        self.slot_rr[q] += 1
        prev = self.slot_last.get((q, slot))
        if prev is not None:
            deps.add(prev)
        oid = len(self.ops)
        self.ops.append(dict(eng=q, kind="dma", out=out, in_=in_, deps=deps, slot=slot))
        self.slot_last[(q, slot)] = oid
        self._record(oid, r, w)
        self.dma_pending.append(oid)
        return oid

    def cc(self, ins, outs, r=(), w=()):
        deps = self._deps("pool", r, w)
        oid = len(self.ops)
        self.ops.append(dict(eng="pool", kind="cc", ins=ins, outs=outs, deps=deps))
        self._record(oid, r, w)
        self.dma_pending.append(oid)
        return oid

    def emit(self, stack):
        nc = self.nc
        ops = self.ops
        needed = set()
        for o in ops:
            needed.update(o["deps"])
        engs = ["pe", "act", "dve", "pool", "sp"]
        sem_e = {e: stack.enter_context(nc.semaphore("pg_" + e)) for e in engs}
        sem_d = {(q, s): stack.enter_context(nc.semaphore(f"dq_{q}{s}"))
                 for q in ("sp", "pool", "act") for s in range(self.NSLOT)}
        sem_cc = stack.enter_context(nc.semaphore("ccsem"))
        cnt = {e: 0 for e in engs}
        dcnt = {k: 0 for k in sem_d}
        ccn = 0
        ev = {}
        for i, o in enumerate(ops):
            if o["kind"] == "op":
                if i in needed:
                    cnt[o["eng"]] += 1
                    ev[i] = (sem_e[o["eng"]], cnt[o["eng"]], ("e", o["eng"]))
            elif o["kind"] == "dma":
                k = (o["eng"], o["slot"])
                dcnt[k] += 16
                ev[i] = (sem_d[k], dcnt[k], ("d",) + k)
            elif o["kind"] == "bar":
                pass
            else:
                ccn += 1
                ev[i] = (sem_cc, ccn, ("c",))
        per = {e: [] for e in engs}
        known = {e: {} for e in engs}
        for i, o in enumerate(ops):
            e = o["eng"]
            best = {}
            for d in o["deps"]:
                s, v, key = ev[d]
                if known[e].get(key, 0) >= v:
                    continue
                if key not in best or best[key][1] < v:
                    best[key] = (s, v)
            for key, (s, v) in best.items():
                per[e].append(("wait", s, v))
                known[e][key] = v
            per[e].append(("ins", i))
        for (q, s), c in dcnt.items():
            if c:
                per[q].append(("wait", sem_d[(q, s)], c))
        if ccn:
            per["pool"].append(("wait", sem_cc, ccn))
        if os.environ.get("DUMP"):
            for e in engs:
                print("ENGINE", e)
                for it in per[e]:
                    if it[0] == "wait":
                        print("   wait", it[1], it[2])
                    else:
                        o = ops[it[1]]
                        print("   ", it[1], o["kind"], o.get("tag", ""), "sig" if it[1] in ev else "", ev.get(it[1], ("", ""))[1])
        self.stats = {e: sum(1 for x in per[e] if x[0] == "ins") for e in engs}
        self.stats["sem"] = dict(cnt)

        if os.environ.get("CHECK"):
            semv = {}
            pos = {e: 0 for e in engs}
            prog = True
            while prog:
                prog = False
                for e in engs:
                    while pos[e] < len(per[e]):
                        it = per[e][pos[e]]
                        if it[0] == "wait":
                            if semv.get(it[1].num, 0) >= it[2]:
                                pos[e] += 1
                                prog = True
                            else:
                                break
                        else:
                            i = it[1]
                            if i in ev:
                                o = ops[i]
                                inc = 16 if o["kind"] == "dma" else 1
                                semv[ev[i][0].num] = semv.get(ev[i][0].num, 0) + inc
                            pos[e] += 1
                            prog = True
            for e in engs:
                if pos[e] < len(per[e]):
                    it = per[e][pos[e]]
                    print("DEADLOCK", e, "at", pos[e], "/", len(per[e]), it[0], it[1] if it[0] == "wait" else "", it[2] if it[0] == "wait" else "",
                          "have", semv.get(it[1].num, 0) if it[0] == "wait" else "")
            print("CHECK done", {e: (pos[e], len(per[e])) for e in engs})

        def run(E, lst):
            for it in lst:
                if it[0] == "wait":
                    E.wait_ge(it[1], it[2])
                else:
                    i = it[1]
                    o = ops[i]
                    if o["kind"] == "op":
                        ins = o["fn"](E)
                        if i in ev:
                            ins.then_inc(ev[i][0], 1)
                    elif o["kind"] == "dma":
                        E.dma_start(out=o["out"], in_=o["in_"]).then_inc(ev[i][0], 16)
                    elif o["kind"] == "bar":
                        pass
                    else:
                        E.collective_compute("AllGather", ALU.bypass, replica_groups=GROUPS,
                                             ins=[a.opt() for a in o["ins"]], outs=[a.opt() for a in o["outs"]]).then_inc(ev[i][0])

        block = stack.enter_context(nc.Block())

        @block.sync
        def _(E):
            run(E, per["sp"])

        @block.scalar
        def _(E):
            run(E, per["act"])

        @block.vector
        def _(E):
            run(E, per["dve"])

        @block.gpsimd
        def _(E):
            run(E, per["pool"])

        @block.tensor
        def _(E):
            run(E, per["pe"])


class Seq:
    def __init__(self, name, n, ntile, off):
        self.name, self.n, self.ntile, self.off = name, n, ntile, off
        self.G = 4
        self.ng = ntile // 4
        self.N = 4 * n


PR = Seq("p", 128, NT, 0)
SM = Seq("s", 16, 4, NTOK)
NBIS = 16
CH = dict(ua=0, va=2, qc=4, qcs=7, kc=10, kcs=13, gc=16, vc=19, qb=22, qib=25, wo=27)


class Builder:
    def __init__(self, stage=99):
        self.stage = stage
        self.nc = bass.Bass("TRN2", target_bir_lowering=False)
        self.P = Prog(self.nc)
        self.stack = ExitStack()
        self.rr = 0
        self.wi = 0

    def din(self, name, shape, dt=F32):
        return self.nc.dram_tensor(name, list(shape), dt, kind="ExternalInput").ap()

    def dout(self, name, shape, dt=F32):
        return self.nc.dram_tensor(name, list(shape), dt, kind="ExternalOutput").ap()

    def dscr(self, name, shape, dt=F32):
        return self.nc.dram_tensor(name, list(shape), dt).ap()

    def sb(self, name, shape, dt=F32, st=None):
        self.uid = getattr(self, "uid", 0) + 1
        return (st or self.stack).enter_context(self.nc.sbuf_tensor(f"s{self.uid}_{name}", list(shape), dt))

    def ps(self, name):
        return self.stack.enter_context(self.nc.psum_tensor(name, [128, 512], F32))

    def mm(self, out, lhsT, rhs, start, stop, r, w):
        self.P.op("pe", lambda E: E.matmul(out, lhsT, rhs, start=start, stop=stop, skip_group_check=True), r=r, w=w)

    def tr(self, out, in_, ident, r, w):
        self.P.op("pe", lambda E: E.transpose(out, in_, ident), r=r, w=w)

    def copy(self, eng, out, in_, r, w):
        if eng == "act":
            self.P.op("act", lambda E: E.activation(out=out, in_=in_, func=AF.Copy), r=r, w=w)
        else:
            self.P.op(eng, lambda E: E.tensor_copy(out=out, in_=in_), r=r, w=w)

    def act(self, out, in_, func, r, w, bias=0.0, scale=1.0):
        self.P.op("act", lambda E: E.activation(out=out, in_=in_, func=func, bias=bias, scale=scale), r=r, w=w)

    def tt(self, eng, out, in0, in1, op, r, w):
        self.P.op(eng, lambda E: E.tensor_tensor(out=out, in0=in0, in1=in1, op=op), r=r, w=w)

    def ts(self, eng, out, in0, s1, s2, op0, op1=None, r=(), w=(), accum_out=None):
        kw = {}
        if accum_out is not None:
            kw["accum_out"] = accum_out
        if op1 is None:
            self.P.op(eng, lambda E: E.tensor_scalar(out=out, in0=in0, scalar1=s1, scalar2=None, op0=op0, **kw), r=r, w=w)
        else:
            self.P.op(eng, lambda E: E.tensor_scalar(out=out, in0=in0, scalar1=s1, scalar2=s2, op0=op0, op1=op1, **kw), r=r, w=w)

    def stt(self, eng, out, in0, scalar, in1, op0, op1, r, w):
        self.P.op(eng, lambda E: E.scalar_tensor_tensor(out=out, in0=in0, scalar=scalar, in1=in1, op0=op0, op1=op1), r=r, w=w)

    def red(self, out, in_, op, r, w):
        self.P.op("dve", lambda E: E.tensor_reduce(out=out, in_=in_, axis=AX.X, op=op), r=r, w=w)

    def memset(self, eng, ap, val, w):
        self.P.op(eng, lambda E: E.memset(ap, val), r=(), w=w)

    def alt(self):
        self.rr += 1
        return "act" if self.rr % 2 else "dve"

    def wchunk(self, src):
        i = self.wi % len(self.wring)
        self.wi += 1
        t = self.wring[i]
        self.P.dma("pool", t[:], src, w=[f"wr{i}"])
        return t, f"wr{i}"

    def proj(self, pst, psk, wt, wk, c0, M, xb, xk, N, po=0):
        for kc in range(8):
            self.mm(pst[po:po + M, 0:N], wt[:, kc, c0:c0 + M], xb[:, kc, 0:N], kc == 0, kc == 7, r=[wk, xk], w=[psk])

    def stats(self, xs, keys, N, ones, tag):
        pm, pe2 = self.pss[5], self.pss[6]
        nx = len(xs)
        for i, (x, k) in enumerate(zip(xs, keys)):
            sq = self.sqb[i % 2]
            self.act(sq[:, 0:N], x, AF.Square, r=[k], w=[f"sqb{i % 2}"])
            self.mm(pm[:, 0:N], ones[:], x, i == 0, i == nx - 1, r=[k, "consts"], w=["ps5"])
            self.mm(pe2[:, 0:N], ones[:], sq[:, 0:N], i == 0, i == nx - 1, r=[f"sqb{i % 2}", "consts"], w=["ps6"])
        mean, rstd, tmp = self.st_mean, self.st_rstd, self.st_tmp
        self.act(tmp[:, 0:N], pm[:, 0:N], AF.Square, r=["ps5"], w=["st_tmp"])
        self.copy("act", mean[:, 0:N], pm[:, 0:N], r=["ps5"], w=["st_mean"])
        self.tt("dve", tmp[:, 0:N], pe2[:, 0:N], tmp[:, 0:N], ALU.subtract, r=["ps6", "st_tmp"], w=["st_tmp"])
        self.ts("dve", tmp[:, 0:N], tmp[:, 0:N], 0.0, EPS, ALU.max, ALU.add, r=["st_tmp"], w=["st_tmp"])
        self.act(tmp[:, 0:N], tmp[:, 0:N], AF.Sqrt, r=["st_tmp"], w=["st_tmp"])
        self.P.op("dve", lambda E: E.reciprocal(out=rstd[:, 0:N], in_=tmp[:, 0:N]), r=["st_tmp"], w=["st_rstd"])
        return mean, rstd

    def build(self):
        nc, P = self.nc, self.P
        st = self.stage
        S = self.stack
        xp = self.din("xp", [NTOK, D])
        xs_in = self.din("xs", [NS, D])
        identf = self.din("identf", [128, 128])
        cosd = self.din("cosT", [128, NTOK + NS])
        sind = self.din("sinT", [128, NTOK + NS])
        kdecd = self.din("kdec", [2, 128, 384])
        dtd = self.din("dtT", [128, 6, 128])
        qdecPd = self.din("qdecP", [128, 3, 512])
        qdecSd = self.din("qdecS", [128, 3, 64])
        seld = self.din("sel", [128, 768])
        sel16d = self.din("sel16", [16, 96])
        cmd = self.din("cm", [128, 8, 128])
        e16d = self.din("e16", [128, 16])
        bd64d = self.din("bd64", [128, 128])
        onesd = self.din("onesd", [128, 128])
        umd = self.din("umask", [128, 128])
        dmaskd = self.din("dmask", [128, 512])
        rcoefd = self.din("rcoef", [128, 9, 192])
        sel5d = self.din("sel5", [128, 5])
        cdecSd = self.din("cdecS", [128, 192])
        wA = [self.din(f"wA{l}", [128, 8, 1416]) for l in range(DEPTH)]
        wB = [self.din(f"wB{l}", [35, 128, 8, 128]) for l in range(DEPTH)]
        wCg = [self.din(f"wCg{l}", [NFC, 128, 8, 128]) for l in range(DEPTH)]
        wCu = [self.din(f"wCu{l}", [NFC, 128, 8, 128]) for l in range(DEPTH)]
        wCd = [self.din(f"wCd{l}", [8, 128, NFC, 128]) for l in range(DEPTH)]
        vecd = [self.din(f"vec{l}", [128, 127]) for l in range(DEPTH)]
        bsPd = [self.din(f"bsP{l}", [128, 2, 512]) for l in range(DEPTH)]
        bsSd = [self.din(f"bsS{l}", [128, 2, 64]) for l in range(DEPTH)]
        wmTd = [self.din(f"wmT{l}", [128, 4, 128]) for l in range(DEPTH)]
        ckd = [self.din(f"ck{l}", [4, 2048, 64]) for l in range(DEPTH)]
        cvd = [self.din(f"cv{l}", [4, 2048, 64]) for l in range(DEPTH)]
        ckid = [self.din(f"cki{l}", [4, 2048, 32]) for l in range(DEPTH)]
        stSd = [self.din(f"stS{l}", [128, 4, 192]) for l in range(DEPTH)]
        cvpd = [self.din(f"cvp{l}", [128, NFC, 4, 2]) for l in range(DEPTH)]
        o_y = self.dout("o_y", [NTOK, D])
        o_ys = self.dout("o_ys", [NS, D])
        o_k = self.dout("o_k", [DEPTH, NTOK, 64])
        o_v = self.dout("o_v", [DEPTH, NTOK, 64])
        o_ki = self.dout("o_ki", [DEPTH, NTOK, 32])
        o_ret = self.dout("o_ret", [DEPTH, 128, 192])
        o_conv = self.dout("o_conv", [DEPTH, 128, NFC, 2])
        o_ks = self.dout("o_ks", [DEPTH, NS, 64])
        o_vs = self.dout("o_vs", [DEPTH, NS, 64])
        o_kis = self.dout("o_kis", [DEPTH, NS, 32])
        o_rets = self.dout("o_rets", [DEPTH, 128, 4, 192])
        o_convs = self.dout("o_convs", [DEPTH, 128, NFC, 4, 2])
        o_avs = self.dout("o_avs", [DEPTH, NS, 256])
        XT = self.dscr("XT", [128, 8, NTOK + NS])
        X1T = self.dscr("X1T", [128, 8, NTOK + NS])
        bncA32 = self.dscr("bncA", [NT, 10240])
        gatA32 = self.dscr("gatA", [4 * NT, 10240])
        bncA = bncA32.bitcast(BF16)
        gatA = gatA32.bitcast(BF16)
        bncK2 = [self.dscr(f"bncK{i}", [8 * 128, 192]) for i in range(2)]
        gatK2 = [self.dscr(f"gatK{i}", [4 * 8 * 128, 192]) for i in range(2)]
        bncB32 = self.dscr("bncB", [NT, 1024])
        gatB32 = self.dscr("gatB", [4 * NT, 1024])
        bncB = bncB32.bitcast(BF16)
        gatB = gatB32.bitcast(BF16)

        identF = self.sb("identF", [128, 128])
        identB = self.sb("identB", [128, 128], BF16)
        bd64 = self.sb("bd64", [128, 128])
        ones = self.sb("ones", [128, 128])
        P.dma("sp", identF[:], identf, w=["consts"])
        P.dma("sp", bd64[:], bd64d, w=["consts"])
        P.dma("sp", ones[:], onesd, w=["consts"])
        P.dma("pool", identB[:], identf, w=["consts"])
        kdecT = self.sb("kdecT", [128, 2, 384])
        P.dma("sp", kdecT[:], kdecd.rearrange("a p f -> p a f"), w=["consts"])
        wtok = {"p": self.sb("wtokp", [128, NT, 8]), "s": self.sb("wtoks", [128, 4, 8])}
        vecs = self.sb("vecs", [128, 127])
        self.pss = pss = [self.ps(f"ps{i}") for i in range(8)]
        self.sqb = [self.sb(f"sqb{i}", [128, 512]) for i in range(2)]
        self.st_mean = self.sb("st_mean", [128, 512])
        self.st_rstd = self.sb("st_rstd", [128, 512])
        self.st_tmp = self.sb("st_tmp", [128, 512])
        ktS = self.sb("ktS", [128, 4, 16], BF16)
        vS = self.sb("vS", [16, 4, 64], BF16)
        convo = self.sb("convo", [128, NFC, 2])
        convos = self.sb("convos", [128, NFC, 4, 2])
        V_AG, V_AB, V_GN, V_L1G, V_L1B, V_L2G, V_L2B, V_CW, V_CB = 0, 2, 4, 7, 15, 23, 31, 39, 105

        with ExitStack() as ph:
            xin = [self.sb(f"xin{i}", [128, D], st=ph) for i in range(2)]
            xtg = [self.sb(f"xtg{i}", [128, 8, 128], st=ph) for i in range(2)]
            for sq_, src in ((PR, xp), (SM, xs_in)):
                n = sq_.n
                for m in range(sq_.ntile):
                    b = m % 2
                    P.dma("sp", xin[b][0:n, :], src[m * n:(m + 1) * n, :], w=[f"xin{b}"])
                    for c in range(8):
                        pb = pss[c % 4]
                        self.tr(pb[:, 0:n], xin[b][0:n, c * 128:(c + 1) * 128], identF[0:n, 0:n], r=[f"xin{b}", "consts"], w=[f"ps{c % 4}"])
                        self.copy(self.alt(), xtg[b][:, c, 0:n], pb[:, 0:n], r=[f"ps{c % 4}"], w=[f"xtg{b}"])
                    P.dma("sp", XT[:, :, sq_.off + m * n:sq_.off + (m + 1) * n], xtg[b][:, :, 0:n], r=[f"xtg{b}"], w=["XT"])
            P.barrier()

        for l in range(DEPTH if st >= 9 else 1):
            P.dma("sp", vecs[:], vecd[l], w=["vecs"])
            with ExitStack() as ph:
                WA = self.sb("WA", [128, 8, 1416], BF16, st=ph)
                for kc in range(8):
                    P.dma("pool", WA[:, kc, :], wA[l][:, kc, :], w=["WA"])
                xTf = [self.sb(f"xTf{i}", [128, 8, 128], st=ph) for i in range(2)]
                xTb = [self.sb(f"xTb{i}", [128, 8, 128], BF16, st=ph) for i in range(2)]
                vtok = self.sb("vtok", [128, 384], BF16, st=ph)
                vb16 = self.sb("vb16", [128, 64], BF16, st=ph)
                kvo = [self.sb(f"kvo{i}", [128, 168], st=ph) for i in range(2)]
                ktb = self.sb("ktb", [128, 128], BF16, st=ph)
                rt1 = self.sb("rt1", [128, 3, 128], st=ph)
                krot = self.sb("krot", [128, 3, 128], st=ph)
                kd = self.sb("kd", [128, 384], BF16, st=ph)
                kvsb = [self.sb(f"kvsb{i}", [128, 192], st=ph) for i in range(2)]
                csa = [self.sb(f"csa{i}", [128, 2, 128], st=ph) for i in range(2)]
                stS = self.sb("stS", [128, 4, 192], st=ph)
                cdecS = self.sb("cdecS", [128, 192], st=ph)
                P.dma("sp", stS[:], stSd[l], w=["stS"])
                P.dma("sp", cdecS[:], cdecSd, w=["cdecS"])
                it = 0
                for sq_ in (PR, SM):
                    n = sq_.n
                    if sq_ is SM:
                        P.cc([bncA32], [gatA32], r=["bncA"], w=["gatA"])
                        for i in range(2):
                            P.cc([bncK2[i]], [gatK2[i]], r=[f"bncK{i}"], w=[f"gatK{i}"])
                    ok, ov, oki = (o_k, o_v, o_ki) if sq_ is PR else (o_ks, o_vs, o_kis)
                    kdi = 0 if sq_ is PR else 1
                    for m in range(sq_.ntile):
                        b = it % 2
                        it += 1
                        cols = slice(sq_.off + m * n, sq_.off + (m + 1) * n)
                        rows = slice(m * n, (m + 1) * n)
                        P.dma("sp", xTf[b][:, :, 0:n], XT[:, :, cols], r=["XT"], w=[f"xTf{b}"])
                        self.copy("act", xTb[b][:, :, 0:n], xTf[b][:, :, 0:n], r=[f"xTf{b}"], w=[f"xTb{b}"])
                        xb = xTb[b]
                        rx = [f"xTb{b}", "WA"]
                        for kc in range(8):
                            self.mm(pss[0][0:n, 0:512], xb[:, kc, 0:n], WA[:, kc, 0:512], kc == 0, kc == 7, r=rx, w=["ps0"])
                        for kc in range(8):
                            self.mm(pss[1][0:n, 0:40], xb[:, kc, 0:n], WA[:, kc, 512:552], kc == 0, kc == 7, r=rx, w=["ps1"])
                        self.copy("act", vtok[0:n, :], pss[0][0:n, 0:384], r=["ps0"], w=["vtok"])
                        self.copy("act", vb16[0:n, :], pss[0][0:n, 448:512], r=["ps0"], w=["vb16"])
                        self.copy("dve", kvo[b][0:n, 0:128], pss[0][0:n, 384:512], r=["ps0"], w=[f"kvo{b}"])
                        self.copy("dve", kvo[b][0:n, 128:168], pss[1][0:n, 0:40], r=["ps1"], w=[f"kvo{b}"])
                        self.copy("dve", wtok[sq_.name][0:n, m, :], kvo[b][0:n, 160:168], r=[f"kvo{b}"], w=["wtok"])
                        P.dma("sp", ok[l, rows, :], kvo[b][0:n, 0:64], r=[f"kvo{b}"])
                        P.dma("sp", ov[l, rows, :], kvo[b][0:n, 64:128], r=[f"kvo{b}"])
                        P.dma("sp", oki[l, rows, :], kvo[b][0:n, 128:160], r=[f"kvo{b}"])
                        for kc in range(8):
                            self.mm(pss[2][0:64, 0:n], WA[:, kc, 552:616], xb[:, kc, 0:n], kc == 0, kc == 7, r=rx, w=["ps2"])
                        for kc in range(8):
                            self.mm(pss[2][64:96, 0:n], WA[:, kc, 616:648], xb[:, kc, 0:n], kc == 0, kc == 7, r=rx, w=["ps2"])
                        if sq_ is PR:
                            self.copy("act", ktb[0:96, 0:n], pss[2][0:96, 0:n], r=["ps2"], w=["ktb"])
                            P.dma("sp", bncA[m, 0:12288].rearrange("(d t) -> d t", t=128), ktb[0:96, :], r=["ktb"], w=["bncA"])
                            P.dma("sp", bncA[m, 12288:20480].rearrange("(s e) -> s e", e=64), vb16[:], r=["vb16"], w=["bncA"])
                        else:
                            self.copy("act", ktS[0:96, m, :], pss[2][0:96, 0:n], r=["ps2"], w=["ktS"])
                            self.copy("dve", vS[0:n, m, :], vb16[0:n, :], r=["vb16"], w=["vS"])
                        for p in range(3):
                            for kc in range(8):
                                self.mm(pss[3][:, p * 128:p * 128 + n], WA[:, kc, 648 + p * 128:648 + (p + 1) * 128], xb[:, kc, 0:n],
                                        kc == 0, kc == 7, r=rx, w=["ps3"])
                        for p in range(3):
                            for kc in range(8):
                                self.mm(pss[4][:, p * 128:p * 128 + n], WA[:, kc, 1032 + p * 128:1032 + (p + 1) * 128], xb[:, kc, 0:n],
                                        kc == 0, kc == 7, r=rx, w=["ps4"])
                        P.dma("sp", csa[b][:, 0, 0:n], cosd[:, cols], w=[f"csa{b}"])
                        P.dma("sp", csa[b][:, 1, 0:n], sind[:, cols], w=[f"csa{b}"])
                        cb = csa[b][:, 0, 0:n].unsqueeze(1).to_broadcast([128, 3, n])
                        sbb = csa[b][:, 1, 0:n].unsqueeze(1).to_broadcast([128, 3, n])
                        p3 = pss[3][:, 0:384].rearrange("p (c t) -> p c t", c=3)[:, :, 0:n]
                        p4 = pss[4][:, 0:384].rearrange("p (c t) -> p c t", c=3)[:, :, 0:n]
                        self.tt("dve", rt1[:, :, 0:n], p4, sbb, ALU.mult, r=["ps4", f"csa{b}"], w=["rt1"])
                        self.tt("dve", krot[:, :, 0:n], p3, cb, ALU.mult, r=["ps3", f"csa{b}"], w=["krot"])
                        self.tt("pool", krot[:, :, 0:n], krot[:, :, 0:n], rt1[:, :, 0:n], ALU.add, r=["krot", "rt1"], w=["krot"])
                        for p in range(3):
                            self.tr(pss[5][0:n, p * 128:(p + 1) * 128], krot[:, p, 0:n], identF[:], r=["krot", "consts"], w=["ps5"])
                        self.tt("dve", kd[0:n, :], pss[5][0:n, 0:384], kdecT[0:n, kdi, :], ALU.mult, r=["ps5", "consts"], w=["kd"])
                        for h in range(6):
                            po = (h % 2) * 64
                            self.mm(pss[6][po:po + 64, (h // 2) * 64:(h // 2) * 64 + 64], kd[0:n, h * 64:(h + 1) * 64], vtok[0:n, h * 64:(h + 1) * 64],
                                    True, True, r=["kd", "vtok"], w=["ps6"])
                        if sq_ is PR:
                            self.copy("act", kvsb[b][:], pss[6][:, 0:192], r=["ps6"], w=[f"kvsb{b}"])
                            P.dma("sp", bncK2[m // 8][(m % 8) * 128:(m % 8 + 1) * 128, :], kvsb[b][:], r=[f"kvsb{b}"], w=[f"bncK{m // 8}"])
                        else:
                            self.tt("dve", kvsb[b][:], stS[:, m, :], cdecS[:], ALU.mult, r=["stS", "cdecS"], w=[f"kvsb{b}"])
                            self.tt("dve", kvsb[b][:], kvsb[b][:], pss[6][:, 0:192], ALU.add, r=[f"kvsb{b}", "ps6"], w=[f"kvsb{b}"])
                            P.dma("sp", o_rets[l, :, m, :], kvsb[b][:], r=[f"kvsb{b}"])
                P.barrier()
            if st <= 1:
                break
            self.phaseB(l, locals())
            if st <= 5:
                break
            self.phaseC(l, locals())

        P.emit(self.stack)
        return nc
    def phaseB(self, l, L):
        P, pss = self.P, self.pss
        st = self.stage
        ones, bd64, identF, identB, vecs = L["ones"], L["bd64"], L["identF"], L["identB"], L["vecs"]
        XT, X1T, gatA, gatK2, bncB = L["XT"], L["X1T"], L["gatA"], L["gatK2"], L["bncB"]
        wB = L["wB"][l]
        V_AG, V_AB, V_GN, V_L1G, V_L1B = L["V_AG"], L["V_AB"], L["V_GN"], L["V_L1G"], L["V_L1B"]
        wtok = L["wtok"]
        with ExitStack() as pp:
            KTI = self.sb("KTI", [128, 16, 4, 128], BF16, st=pp)
            VA = self.sb("VA", [128, 64, 65], BF16, st=pp)
            rst = self.sb("rst", [128, 16, 192], BF16, st=pp)
            self.memset("dve", VA[:], 1.0, w=["VA"])
            for j in range(4):
                for m in range(NT):
                    P.dma("sp", KTI[0:96, m, j, :], gatA[j * 16 + m, 0:12288].rearrange("(d t) -> d t", t=128), r=["gatA"], w=["KTI"])
                    P.dma("sp", VA[:, m * 4 + j, 0:64], gatA[j * 16 + m, 12288:20480].rearrange("(s e) -> s e", e=64), r=["gatA"], w=["VA"])
            with ExitStack() as sc:
                rc = self.sb("rc", [128, 9, 192], st=sc)
                kvg = [self.sb(f"kvg{i}", [128, 4, 192], st=sc) for i in range(2)]
                Sst = self.sb("Sst", [128, 192], st=sc)
                ta = self.sb("ta", [128, 192], st=sc)
                tb = self.sb("tb", [128, 192], st=sc)
                P.dma("sp", rc[:], L["rcoefd"], w=["rc"])
                self.memset("dve", Sst[:], 0.0, w=["Sst"])
                for m in range(NT):
                    b = m % 2
                    P.dma("sp", kvg[b][:], gatK2[m // 8].rearrange("(j m p) f -> p j m f", j=4, m=8)[:, :, m % 8, :], r=[f"gatK{m // 8}"], w=[f"kvg{b}"])
                    self.tt("dve", ta[:], Sst[:], rc[:, 0, :], ALU.mult, r=["Sst", "rc"], w=["ta"])
                    for jp in range(3):
                        self.tt("dve", tb[:], kvg[b][:, jp, :], rc[:, 1 + jp, :], ALU.mult, r=[f"kvg{b}", "rc"], w=["tb"])
                        self.tt("dve", ta[:], ta[:], tb[:], ALU.add, r=["ta", "tb"], w=["ta"])
                    self.copy("dve", rst[:, m, :], ta[:], r=["ta"], w=["rst"])
                    self.tt("dve", ta[:], Sst[:], rc[:, 4, :], ALU.mult, r=["Sst", "rc"], w=["ta"])
                    for jp in range(4):
                        self.tt("dve", tb[:], kvg[b][:, jp, :], rc[:, 5 + jp, :], ALU.mult, r=[f"kvg{b}", "rc"], w=["tb"])
                        self.tt("dve", ta[:], ta[:], tb[:], ALU.add, r=["ta", "tb"], w=["ta"])
                    self.copy("dve", Sst[:], ta[:], r=["ta"], w=["Sst"])
                P.dma("sp", L["o_ret"][l], Sst[:], r=["Sst"])
                P.barrier()
            if st <= 2:
                return
            self.phaseB2(l, L, KTI, VA, rst)

    def phaseB2(self, l, L, KTI, VA, rst):
        P, pss = self.P, self.pss
        st = self.stage
        ones, bd64, identF, identB, vecs = L["ones"], L["bd64"], L["identF"], L["identB"], L["vecs"]
        XT, X1T, gatA, gatK2, bncB = L["XT"], L["X1T"], L["gatA"], L["gatK2"], L["bncB"]
        wB = L["wB"][l]
        V_AG, V_AB, V_GN, V_L1G, V_L1B = L["V_AG"], L["V_AB"], L["V_GN"], L["V_L1G"], L["V_L1B"]
        wtok = L["wtok"]
        with ExitStack() as ph:
            sb = lambda name, shape, dt=F32: self.sb(name, shape, dt, st=ph)
            self.wring = [sb(f"wr{i}", [128, 8, 128], BF16) for i in range(3)]
            scores = sb("scores", [128, 8192])
            junk = sb("junk", [128, 3840], mybir.dt.uint8)
            xfg = sb("xfg", [128, 8, 512])
            xbg = sb("xbg", [128, 8, 512], BF16)
            mixT = sb("mixT", [128, 8, 512], BF16)
            uT = sb("uT", [128, 2, 512])
            gT = sb("gT", [128, 2, 512])
            vtokA = sb("vtokA", [128, 4, 256], BF16)
            avf = scores[0:16, 0:1024].rearrange("p (t f) -> p t f", t=4)
            csg = sb("csg", [128, 2, 512])
            qrot = sb("qrot", [128, 3, 512], BF16)
            qd = sb("qd", [128, 3, 512], BF16)
            krot = sb("krotB", [128, 3, 512], BF16)
            sil = sb("sil", [128, 3, 512], BF16)
            vtokB = sb("vtokB", [128, 4, 384], BF16)
            Sm = sb("Sm", [128, 6, 128], BF16)
            ysb = sb("ysb", [128, 384])
            ycn = sb("ycn", [128, 384])
            DT = sb("DT", [128, 6, 128])
            qdecP = sb("qdecP", [128, 3, 128])
            qdecS = sb("qdecS", [128, 3, 16])
            qq = sb("qq", [128, 4096], BF16)
            Amat = sb("Amat", [128, 128], BF16)
            Wd = sb("Wd", [128, 8, 128], BF16)
            rz = [sb(f"rz{i}", [128, 512], BF16) for i in range(4)]
            pT = [sb(f"pT{i}", [128, 768], BF16) for i in range(2)]
            mb = [sb(f"mb{i}", [128, 128], BF16) for i in range(2)]
            col = sb("col", [128, 16])
            ob = sb("ob", [128, 384])
            Sel = sb("Sel", [128, 768], BF16)
            Sel16 = sb("Sel16", [16, 96], BF16)
            CM = sb("CM", [128, 8, 128], BF16)
            E16 = sb("E16", [128, 16])
            dmask = sb("dmask", [128, 512])
            WmT = sb("WmT", [128, 4, 128], BF16)
            wmf = scores[:, 0:512].rearrange("p (g i) -> p g i", g=4)
            um = scores[:, 512:640]
            bsP = sb("bsP", [128, 2, 128])
            bsS = sb("bsS", [128, 2, 16])
            qi2T = qq[64:96, :].rearrange("p (t h) -> p t h", h=8)
            P.dma("sp", DT[:], L["dtd"], w=["DT"])
            P.dma("sp", qdecP[:], L["qdecPd"][:, :, 0:128], w=["qdec"])
            P.dma("sp", qdecS[:], L["qdecSd"][:, :, 0:16], w=["qdec"])
            P.dma("pool", Sel[:], L["seld"], w=["Sel"])
            P.dma("pool", Sel16[:], L["sel16d"], w=["Sel"])
            P.dma("pool", CM[:], L["cmd"], w=["CM"])
            P.dma("sp", E16[:], L["e16d"], w=["E16"])
            P.dma("sp", dmask[:], L["dmaskd"], w=["dmask"])
            P.dma("sp", wmf, L["wmTd"][l], w=["scores"])
            P.dma("sp", um, L["umd"], w=["scores"])
            P.dma("sp", bsP[:], L["bsPd"][l][:, :, 0:128], w=["bs"])
            P.dma("sp", bsS[:], L["bsSd"][l][:, :, 0:16], w=["bs"])
            self.tt("dve", WmT[:], wmf, um.unsqueeze(1).to_broadcast([128, 4, 128]), ALU.mult, r=["scores"], w=["WmT"])

            def chunk(ci):
                return self.wchunk(wB[ci])

            def group(sq_, g, keysrc):
                n, N = sq_.n, sq_.N
                c0 = sq_.off + g * N
                gcols = slice(c0, c0 + N)
                xk = [f"xfg{c}" for c in range(8)]
                P.dma("sp", xfg[:, :, 0:N], XT[:, :, gcols], r=["XT"], w=xk)
                self.copy("act", xbg[:, :, 0:N], xfg[:, :, 0:N], r=xk, w=["xbg"])
                P.dma("sp", csg[:, 0, 0:N], L["cosd"][:, gcols], w=["csg"])
                P.dma("sp", csg[:, 1, 0:N], L["sind"][:, gcols], w=["csg"])
                bs = bsP if sq_ is PR else bsS
                qdec = qdecP if sq_ is PR else qdecS

                for ch in range(2):
                    wt, wk = chunk(CH["ua"] + ch)
                    self.proj(pss[ch], f"ps{ch}", wt, wk, 0, 128, xbg, "xbg", N)
                    self.act(uT[:, ch, 0:N], pss[ch][:, 0:N], AF.Gelu_apprx_tanh, r=[f"ps{ch}"], w=["uT"])
                for ch in range(2):
                    wt, wk = chunk(CH["va"] + ch)
                    self.proj(pss[ch], f"ps{ch}", wt, wk, 0, 128, xbg, "xbg", N)
                    self.act(gT[:, ch, 0:N], pss[ch][:, 0:N], AF.Gelu_apprx_tanh, r=[f"ps{ch}"], w=["gT"])
                if 'a2' in EXP:
                    return
                for ch in range(2):
                    mean, rstd = self.stats([gT[:, ch, 0:N]], ["gT"], N, bd64, "a")
                    self.tt("dve", gT[:, ch, 0:N], gT[:, ch, 0:N], mean[:, 0:N], ALU.subtract, r=["gT", "st_mean"], w=["gT"])
                    self.tt("dve", gT[:, ch, 0:N], gT[:, ch, 0:N], rstd[:, 0:N], ALU.mult, r=["gT", "st_rstd"], w=["gT"])
                    self.ts("dve", gT[:, ch, 0:N], gT[:, ch, 0:N], vecs[:, V_AG + ch:V_AG + ch + 1], vecs[:, V_AB + ch:V_AB + ch + 1],
                            ALU.mult, ALU.add, r=["gT", "vecs"], w=["gT"])
                if 'a3' in EXP:
                    return
                for t in range(4):
                    pb = pss[2 + t % 2]
                    for ch in range(2):
                        self.tr(pb[0:n, ch * 128:(ch + 1) * 128], gT[:, ch, t * n:(t + 1) * n], identF[:], r=["gT", "consts"], w=[f"ps{2 + t % 2}"])
                    self.copy(self.alt(), vtokA[0:n, t, :], pb[0:n, 0:256], r=[f"ps{2 + t % 2}"], w=["vtokA"])
                    if sq_ is SM:
                        self.copy("dve", avf[0:n, t, :], pb[0:n, 0:256], r=[f"ps{2 + t % 2}"], w=["scores"])
                if sq_ is SM:
                    P.dma("sp", L["o_avs"][l].rearrange("(b t) f -> t b f", t=16), avf, r=["scores"])
                if 'a4' in EXP:
                    return
                for t in range(4):
                    for gr in range(4):
                        po = (gr % 2) * 64
                        self.mm(pss[gr // 2][po:po + 64, t * n:(t + 1) * n], vtokA[0:n, t, gr * 64:(gr + 1) * 64], WmT[0:n, gr, 0:n],
                                True, True, r=["vtokA", "WmT"], w=[f"ps{gr // 2}"])
                if 'a5' in EXP:
                    return
                for ch in range(2):
                    bb = bs[:, ch, 0:n].unsqueeze(1).to_broadcast([128, 4, n])
                    self.copy("act", gT[:, ch, 0:N], pss[ch][:, 0:N], r=[f"ps{ch}"], w=["gT"])
                    g3 = gT[:, ch, 0:N].rearrange("p (t i) -> p t i", t=4)
                    self.tt("dve", g3, g3, bb, ALU.add, r=["gT", "bs"], w=["gT"])
                    if 'a6' in EXP:
                        continue
                    self.tt("dve", mixT[:, ch, 0:N], gT[:, ch, 0:N], uT[:, ch, 0:N], ALU.mult, r=["gT", "uT"], w=["mixT"])

                if 'A' in EXP:
                    return
                def rotary(cq, cqs, dst, with_qd):
                    for p in range(3):
                        wt, wk = chunk(cq + p)
                        self.proj(pss[0], "ps0", wt, wk, 0, 128, xbg, "xbg", N)
                        wt, wk = chunk(cqs + p)
                        self.proj(pss[1], "ps1", wt, wk, 0, 128, xbg, "xbg", N)
                        t1, t2 = self.sqb[0], self.sqb[1]
                        self.tt("dve", t1[:, 0:N], pss[0][:, 0:N], csg[:, 0, 0:N], ALU.mult, r=["ps0", "csg"], w=["sqb0"])
                        self.tt("dve", t2[:, 0:N], pss[1][:, 0:N], csg[:, 1, 0:N], ALU.mult, r=["ps1", "csg"], w=["sqb1"])
                        self.tt("pool", t1[:, 0:N], t1[:, 0:N], t2[:, 0:N], ALU.add, r=["sqb0", "sqb1"], w=["sqb0"])
                        self.copy("act", dst[:, p, 0:N], t1[:, 0:N], r=["sqb0"], w=["qrot" if with_qd else "krotB"])
                        if with_qd:
                            qb_ = qdec[:, p, 0:n].unsqueeze(1).to_broadcast([128, 4, n])
                            self.tt("dve", qd[:, p, 0:N].rearrange("p (t i) -> p t i", t=4), t1[:, 0:N].rearrange("p (t i) -> p t i", t=4), qb_,
                                    ALU.mult, r=["sqb0", "qdec"], w=["qd"])

                rotary(CH["qc"], CH["qcs"], qrot, True)
                if 'r1' in EXP:
                    return
                rotary(CH["kc"], CH["kcs"], krot, False)
                for p in range(3):
                    wt, wk = chunk(CH["gc"] + p)
                    self.proj(pss[p % 2], f"ps{p % 2}", wt, wk, 0, 128, xbg, "xbg", N)
                    self.act(sil[:, p, 0:N], pss[p % 2][:, 0:N], AF.Silu, r=[f"ps{p % 2}"], w=["sil"])
                for p in range(3):
                    wt, wk = chunk(CH["vc"] + p)
                    for t in range(4):
                        pb = pss[2 + t % 2]
                        for kc in range(8):
                            self.mm(pb[0:n, 0:128], xbg[:, kc, t * n:(t + 1) * n], wt[:, kc, :], kc == 0, kc == 7, r=["xbg", wk], w=[f"ps{2 + t % 2}"])
                        self.copy(self.alt(), vtokB[0:n, t, p * 128:(p + 1) * 128], pb[0:n, 0:128], r=[f"ps{2 + t % 2}"], w=["vtokB"])
                if 'r2' in EXP:
                    return
                for t in range(4):
                    tc_ = slice(t * n, (t + 1) * n)
                    m = g * 4 + t
                    for h in range(6):
                        po, p = (h % 2) * 64, h // 2
                        pb, pk = (pss[2], "ps2") if h % 2 == 0 else (pss[3], "ps3")
                        self.mm(pb[0:n, p * 128:p * 128 + n], krot[po:po + 64, p, tc_], qrot[po:po + 64, p, tc_], True, True, r=["krotB", "qrot"], w=[pk])
                    s1, s2 = self.sqb[0], self.sqb[1]
                    self.copy("act", s1[0:n, 0:384], pss[2][0:n, 0:384], r=["ps2"], w=["sqb0"])
                    self.copy("act", s2[0:n, 0:384], pss[3][0:n, 0:384], r=["ps3"], w=["sqb1"])
                    for h in range(6):
                        src, sk = (s1, "sqb0") if h % 2 == 0 else (s2, "sqb1")
                        p = h // 2
                        self.tt("dve", Sm[0:n, h, 0:n], src[0:n, p * 128:p * 128 + n], DT[0:n, h, 0:n], ALU.mult, r=[sk, "DT"], w=["Sm"])
                    if 'r3' in EXP:
                        continue
                    rsrc = keysrc["rst"](m)
                    for h in range(6):
                        po, p = (h % 2) * 64, h // 2
                        pb, pk = (pss[0], "ps0") if h % 2 == 0 else (pss[1], "ps1")
                        self.mm(pb[po:po + 64, p * 128:p * 128 + n], vtokB[0:n, t, h * 64:(h + 1) * 64], Sm[0:n, h, 0:n], True, False,
                                r=["vtokB", "Sm"], w=[pk])
                        self.mm(pb[po:po + 64, p * 128:p * 128 + n], rsrc[po:po + 64, p * 64:(p + 1) * 64], qd[po:po + 64, p, tc_], False, True,
                                r=["rst", "qd"], w=[pk])
                    if 'r4' in EXP:
                        continue
                    ysv = ysb[:, 0:3 * n].rearrange("p (c i) -> p c i", c=3)
                    self.copy("act", ysv[0:64], pss[0][0:64, 0:384].rearrange("p (c i) -> p c i", c=3)[:, :, 0:n], r=["ps0"], w=["ysb"])
                    self.copy("act", ysv[64:128], pss[1][64:128, 0:384].rearrange("p (c i) -> p c i", c=3)[:, :, 0:n], r=["ps1"], w=["ysb"])
                    mean, rstd = self.stats([ysb[:, 0:3 * n]], ["ysb"], 3 * n, bd64, "r")
                    self.tt("dve", ycn[:, 0:3 * n], ysb[:, 0:3 * n], mean[:, 0:3 * n], ALU.subtract, r=["ysb", "st_mean"], w=["ycn"])
                    self.tt("dve", ycn[:, 0:3 * n], ycn[:, 0:3 * n], rstd[:, 0:3 * n], ALU.mult, r=["ycn", "st_rstd"], w=["ycn"])
                    for p in range(3):
                        self.stt("dve", mixT[:, 5 + p, tc_], ycn[:, p * n:(p + 1) * n], vecs[:, V_GN + p:V_GN + p + 1], sil[:, p, tc_], ALU.mult, ALU.mult,
                                 r=["ycn", "vecs", "sil"], w=["mixT"])

                if 'R' in EXP:
                    return
                for c in range(3):
                    wt, wk = chunk(CH["qb"] + c)
                    for hh in range(2):
                        self.proj(pss[hh], f"ps{hh}", wt, wk, hh * 64, 64, xbg, "xbg", N)
                        q2v = qq[0:64, 0:24 * n].rearrange("p (t h i) -> p t h i", t=4, h=6)
                        self.copy(self.alt(), q2v[:, :, 2 * c + hh, :], pss[hh][0:64, 0:N].rearrange("p (t i) -> p t i", t=4), r=[f"ps{hh}"], w=["qq"])
                for c in range(2):
                    wt, wk = chunk(CH["qib"] + c)
                    for hh in range(4):
                        pb = pss[hh % 2]
                        self.proj(pb, f"ps{hh % 2}", wt, wk, hh * 32, 32, xbg, "xbg", N, po=64)
                        self.copy(self.alt(), qi2T[:, 0:N, 4 * c + hh], pb[64:96, 0:N], r=[f"ps{hh % 2}"], w=["qq"])
                for t in range(4):
                    self.dsa_tile(sq_, g, t, l, L, keysrc)

                if 'D' in EXP:
                    return
                for c in range(8):
                    wt, wk = chunk(CH["wo"] + c)
                    pb, pk = pss[c % 2], f"ps{c % 2}"
                    for kc in range(8):
                        self.mm(pb[:, 0:N], wt[:, kc, :], mixT[:, kc, 0:N], kc == 0, kc == 7, r=[wk, "mixT"], w=[pk])
                    self.stt("dve", xfg[:, c, 0:N], xfg[:, c, 0:N], ALPHA, pb[:, 0:N], ALU.mult, ALU.add, r=[xk[c], pk], w=[xk[c]])
                mean, rstd = self.stats([xfg[:, c, 0:N] for c in range(8)], xk, N, ones, "l1")
                for c in range(8):
                    self.tt("dve", xfg[:, c, 0:N], xfg[:, c, 0:N], mean[:, 0:N], ALU.subtract, r=[xk[c], "st_mean"], w=[xk[c]])
                    self.tt("pool", xfg[:, c, 0:N], xfg[:, c, 0:N], rstd[:, 0:N], ALU.mult, r=[xk[c], "st_rstd"], w=[xk[c]])
                    self.act(xfg[:, c, 0:N], xfg[:, c, 0:N], AF.Identity, r=[xk[c], "vecs"], w=[xk[c]],
                             scale=vecs[:, V_L1G + c:V_L1G + c + 1], bias=vecs[:, V_L1B + c:V_L1B + c + 1])
                P.dma("sp", X1T[:, :, gcols], xfg[:, :, 0:N], r=xk, w=["X1T"])
                if sq_ is PR:
                    bnd = self.bnd
                    for t in range(4):
                        self.copy("act", bnd[:, t, :, :], xfg[:, :, t * 128 + 126:t * 128 + 128], r=xk, w=["bnd"])
                    P.dma("sp", bncB[g * 4:(g + 1) * 4, :].rearrange("t (p x) -> p t x", x=16), bnd[:].rearrange("p t k c -> p t (k c)"),
                          r=["bnd"], w=["bncB"])

            self.bnd = sb("bnd", [128, 4, 8, 2], BF16)
            self._scores, self._junk = scores, junk
            self._junk2 = sb("junk2", [128, 4368], mybir.dt.uint8)
            self._sacc = sb("sacc", [128, 1])
            self._dsa = dict(Amat=Amat, Wd=Wd, rz=rz, pT=pT, mb=mb, col=col, ob=ob, Sel=Sel, Sel16=Sel16, CM=CM, E16=E16, dmask=dmask,
                             qq=qq, qi2T=qi2T, mixT=mixT, wtok=wtok, identF=identF, identB=identB)

            if 'a1' in EXP:
                P.barrier()
                return
            ksrc = dict(rst=lambda m: rst[:, m, :], kind="p", KTI=KTI, VA=VA)
            for g in range(PR.ng if st >= 4 else 1):
                group(PR, g, ksrc)
            if st >= 9:
                P.cc([L["bncB32"]], [L["gatB32"]], r=["bncB"], w=["gatB"])
            else:
                P.barrier()
            if st <= 3:
                return
            with ExitStack() as ss:
                KTIs = self.sb("KTIs", [128, 2064], BF16, st=ss)
                VAs = self.sb("VAs", [128, 17, 65], BF16, st=ss)
                cK = self.sb("cK", [128, 16, 96], st=ss)
                rstS = self.sb("rstS", [128, 4, 192], BF16, st=ss)
                stSf = self.sb("stSf", [128, 4, 192], st=ss)
                P.dma("sp", stSf[:], L["stSd"][l], w=["stSf"])
                self.copy("act", rstS[:], stSf[:], r=["stSf"], w=["rst"])
                ksrc = dict(rst=lambda m: rstS[:, m, :], kind="s", KTIs=KTIs, VAs=VAs, cK=cK, ck=L["ckd"][l], cv=L["cvd"][l], cki=L["ckid"][l],
                            ktS=L["ktS"], vS=L["vS"])
                group(SM, 0, ksrc)
                P.barrier()

    def dsa_tile(self, sq_, g, t, l, L, ks):
        P, pss, d = self.P, self.pss, self._dsa
        n = sq_.n
        ng = n // 16
        m = g * 4 + t
        tc_ = slice(t * n, (t + 1) * n)
        scores, junk = self._scores, self._junk
        Amat, Wd, rz, pT, mb, col, ob = d["Amat"], d["Wd"], d["rz"], d["pT"], d["mb"], d["col"], d["ob"]
        identF, identB, mixT = d["identF"], d["identB"], d["mixT"]
        qi2T = d["qi2T"]
        q2f = d["qq"][0:64, 0:24 * n].rearrange("p (t x) -> p t x", t=4)
        SelX = d["Sel"] if n == 128 else d["Sel16"]
        if ks["kind"] == "p":
            KTI, VA = ks["KTI"], ks["VA"]
            kkey, vkey = "KTI", "VA"
            iblocks = [(KTI[64:96, kb].rearrange("p j t -> p (j t)"), 512, kb * 512) for kb in range(m + 1)]
            ablocks = [(KTI[0:64, kb, jj, :], VA[:, kb * 4 + jj, :], 128, (kb * 4 + jj) * 128) for kb in range(m + 1) for jj in range(4)]
            Lk = (m + 1) * 512
        else:
            KTIs, VAs, cK = ks["KTIs"], ks["VAs"], ks["cK"]
            kkey, vkey = "KTIs", "VAs"
            b = t
            for k in range(16):
                P.dma("sp", cK[:, k, 0:64], ks["ck"][b][k * 128:(k + 1) * 128, :], w=["cK"])
                P.dma("sp", cK[:, k, 64:96], ks["cki"][b][k * 128:(k + 1) * 128, :], w=["cK"])
            for k in range(16):
                pb, pk = pss[k % 2], f"ps{k % 2}"
                self.tr(pb[0:96, 0:128], cK[:, k, :], identF[:], r=["cK", "consts"], w=[pk])
                self.copy(self.alt(), KTIs[0:96, k * 128:(k + 1) * 128], pb[0:96, 0:128], r=[pk], w=["KTIs"])
            self.copy("dve", KTIs[0:96, 2048:2064], ks["ktS"][0:96, b, :], r=["ktS"], w=["KTIs"])
            self.memset("dve", VAs[:], 1.0, w=["VAs"])
            for k in range(16):
                P.dma("pool", VAs[:, k, 0:64], ks["cv"][b][k * 128:(k + 1) * 128, :], w=["VAs"])
            self.copy("dve", VAs[0:16, 16, 0:64], ks["vS"][0:16, b, :], r=["vS"], w=["VAs"])

            iblocks = [(KTIs[64:96, kb * 512:(kb + 1) * 512], 512, kb * 512) for kb in range(4)] + [(KTIs[64:96, 2048:2064], 16, 2048)]
            ablocks = [(KTIs[0:64, k * 128:(k + 1) * 128], VAs[:, k, :], 128, k * 128) for k in range(16)] + [(KTIs[0:64, 2048:2064], VAs[0:16, 16, :], 16, 2048)]
            Lk = 2064
        wv = d["wtok"][sq_.name]
        for a in range(16):
            self.ts("dve", Amat[0:n, a * 8:(a + 1) * 8], wv[0:n, m, :], d["E16"][0:n, a:a + 1], None, ALU.mult, r=["E16", "wtok"], w=["Amat"])
        self.mm(pss[4][:, 0:n], Amat[0:n, :], identB[0:n, 0:n], True, True, r=["Amat", "consts"], w=["ps4"])
        wall = self.sqb[1]
        self.copy("act", wall[:, 0:n], pss[4][:, 0:n], r=["ps4"], w=["sqb1"])
        for gq in range(ng):
            self.tt("dve", Wd[:, gq, 0:n], wall[:, 0:n], d["CM"][:, gq, 0:n], ALU.mult, r=["sqb1", "CM"], w=["Wd"])
        tmpm = self.sqb[0]
        steps = [(bi, gq) for bi in range(len(iblocks)) for gq in range(ng)]

        def emit_z(si):
            bi, gq = steps[si]
            rhs_ap, nk, c0 = iblocks[bi]
            zb = (2, 3, 0, 1)[si % 4]
            lhs = qi2T[:, t * n + 16 * gq:t * n + 16 * gq + 16, :].rearrange("p a h -> p (a h)")
            self.mm(pss[zb][:, 0:nk], lhs, rhs_ap, True, True, r=["qq", kkey], w=[f"ps{zb}"])

        for si in range(min(2, len(steps))):
            emit_z(si)
        for si, (bi, gq) in enumerate(steps):
            rhs_ap, nk, c0 = iblocks[bi]
            zb = (2, 3, 0, 1)[si % 4]
            pz, pzk = pss[zb], f"ps{zb}"
            rzb, rk = rz[si % 4], f"rz{si % 4}"
            if self.alt() == "act":
                self.act(rzb[:, 0:nk], pz[:, 0:nk], AF.Relu, r=[pzk], w=[rk])
            else:
                self.ts("dve", rzb[:, 0:nk], pz[:, 0:nk], 0.0, None, ALU.max, r=[pzk], w=[rk])
            self.mm(pss[4][0:n, 0:nk], Wd[:, gq, 0:n], rzb[:, 0:nk], gq == 0, gq == ng - 1, r=["Wd", rk], w=["ps4"])
            if si + 2 < len(steps):
                emit_z(si + 2)
            if gq == ng - 1:
                if ks["kind"] == "p" and bi == len(iblocks) - 1:
                    self.tt("dve", tmpm[0:n, 0:nk], pss[4][0:n, 0:nk], d["dmask"][0:n, 0:nk], ALU.subtract, r=["ps4", "dmask"], w=["sqb0"])
                    self.tt("dve", scores[0:n, c0:c0 + nk], pss[4][0:n, 0:nk], d["dmask"][0:n, 0:nk], ALU.add, r=["ps4", "dmask"], w=["scores"])
                else:
                    self.copy(self.alt(), scores[0:n, c0:c0 + nk], pss[4][0:n, 0:nk], r=["ps4"], w=["scores"])
        lo, w0, mid, cnt, gk, t5 = (col[0:n, i:i + 1] for i in range(6))
        if ks["kind"] == "p":
            self.red(lo, tmpm[0:n, 0:512], ALU.min, r=["sqb0"], w=["col"])
            if m > 0:
                self.red(t5, scores[0:n, 0:m * 512], ALU.min, r=["scores"], w=["col"])
                self.tt("dve", lo, lo, t5, ALU.min, r=["col"], w=["col"])
        else:
            self.red(lo, scores[0:n, 0:Lk], ALU.min, r=["scores"], w=["col"])
        self.red(w0, scores[0:n, 0:Lk], ALU.max, r=["scores"], w=["col"])
        self.tt("dve", w0, w0, lo, ALU.subtract, r=["col"], w=["col"])
        Ld = (Lk * 15 // 32) // 16 * 16 if Lk >= 1024 else Lk
        La = Lk - Ld
        sc_ap, jk_ap = scores[0:n, 0:Ld], junk[0:n, 0:Ld]
        if La:
            sa_ap, ja_ap = scores[0:n, Ld:Lk], self._junk2[0:n, 0:La]
            sacc = self._sacc[0:n, 0:1]
        thr = 255.5 - 0.5 * La
        for k in range(NBIS):
            hw = 2.0 ** (-(k + 1))
            self.ts("dve", mid, w0, hw, lo, ALU.mult, ALU.add, r=["col"], w=["mid"])
            P.op("dve", lambda E: E.tensor_scalar(out=jk_ap, in0=sc_ap, scalar1=mid, scalar2=None, op0=ALU.is_ge, op1=ALU.add, accum_out=cnt),
                 r=["scores", "mid"], w=["junk", "col"])
            if La:
                P.op("act", lambda E: E.activation(out=ja_ap, in_=sa_ap, func=AF.Sign, bias=mid, scale=-1.0, accum_out=sacc),
                     r=["scores", "mid"], w=["junk2", "sacc"])
                self.stt("dve", cnt, sacc, -0.5, cnt, ALU.mult, ALU.add, r=["col", "sacc"], w=["col"])
            self.ts("dve", gk, cnt, thr, hw, ALU.is_ge, ALU.mult, r=["col"], w=["col"])
            self.stt("dve", lo, gk, w0, lo, ALU.mult, ALU.add, r=["col", "mid"], w=["col"])
        pO = pss[7]
        nb = len(ablocks)

        def emit_logits(bi):
            kt, v, nk, c0 = ablocks[bi]
            i2 = bi % 2
            mbb, mk = mb[i2], f"mb{i2}"
            self.ts("dve", mbb[0:n, 0:nk], scores[0:n, c0:c0 + nk], lo, -30000.0, ALU.is_lt, ALU.mult, r=["scores", "col"], w=[mk])
            la, lb = (5, 6) if i2 == 0 else (0, 1)
            self.mm(pss[la][0:nk, 0:4 * n], kt, q2f[:, t, 0:4 * n], True, False, r=[kkey, "qq"], w=[f"ps{la}"])
            self.mm(pss[la][0:nk, 0:4 * n], mbb[0:n, 0:nk], SelX[0:n, 0:4 * n], False, True, r=[mk, "Sel"], w=[f"ps{la}"])
            self.mm(pss[lb][0:nk, 0:2 * n], kt, q2f[:, t, 4 * n:6 * n], True, False, r=[kkey, "qq"], w=[f"ps{lb}"])
            self.mm(pss[lb][0:nk, 0:2 * n], mbb[0:n, 0:nk], SelX[0:n, 4 * n:6 * n], False, True, r=[mk, "Sel"], w=[f"ps{lb}"])

        emit_logits(0)
        for bi, (kt, v, nk, c0) in enumerate(ablocks):
            if bi + 1 < nb:
                emit_logits(bi + 1)
            i2 = bi % 2
            la, lb = (5, 6) if i2 == 0 else (0, 1)
            ptb, pk = pT[i2], f"pT{i2}"
            self.act(ptb[0:nk, 0:4 * n], pss[la][0:nk, 0:4 * n], AF.Exp, r=[f"ps{la}"], w=[pk], scale=0.125)
            self.act(ptb[0:nk, 4 * n:6 * n], pss[lb][0:nk, 0:2 * n], AF.Exp, r=[f"ps{lb}"], w=[pk], scale=0.125)
            for h in range(6):
                self.mm(pO[0:n, h * 65:(h + 1) * 65], ptb[0:nk, h * n:(h + 1) * n], v[0:nk, :], bi == 0 and h == 0, bi == nb - 1 and h == 5,
                        r=[pk, vkey], w=["ps7"])
        posb = self.sqb[1]
        self.copy("act", posb[0:n, 0:390], pO[0:n, 0:390], r=["ps7"], w=["sqb1"])
        pov = posb[0:n, 0:390].rearrange("p (h e) -> p h e", e=65)
        rden = col[0:n, 8:14]
        for h in range(6):
            P.op("dve", (lambda hh: (lambda E: E.reciprocal(out=col[0:n, 8 + hh:9 + hh], in_=posb[0:n, hh * 65 + 64:hh * 65 + 65])))(h), r=["sqb1"], w=["col"])
        for h in range(6):
            self.ts("dve", ob[0:n, h * 64:(h + 1) * 64], posb[0:n, h * 65:h * 65 + 64], col[0:n, 8 + h:9 + h], None, ALU.mult, r=["sqb1", "col"], w=["ob"])
        for c in range(3):
            pb, pk = pss[c % 2], f"ps{c % 2}"
            self.tr(pb[:, 0:n], ob[0:n, c * 128:(c + 1) * 128], identF[0:n, 0:n], r=["ob", "consts"], w=[pk])
            self.copy(self.alt(), mixT[:, 2 + c, tc_], pb[:, 0:n], r=[pk], w=["mixT"])

    def phaseC(self, l, L):
        P, pss = self.P, self.pss
        st = self.stage
        ones, identF, vecs = L["ones"], L["identF"], L["vecs"]
        XT, X1T, gatB, bncB32, gatB32 = L["XT"], L["X1T"], L["gatB"], L["bncB32"], L["gatB32"]
        wCg, wCu, wCd = L["wCg"][l], L["wCu"][l], L["wCd"][l]
        V_L2G, V_L2B, V_CW, V_CB = L["V_L2G"], L["V_L2B"], L["V_CW"], L["V_CB"]
        convo, convos = L["convo"], L["convos"]
        last = (l == DEPTH - 1)
        P.barrier()
        with ExitStack() as ph:
            sb = lambda name, shape, dt=F32: self.sb(name, shape, dt, st=ph)
            self.wring = [sb(f"wr{i}", [128, 8, 128], BF16) for i in range(4)]
            wdr = [sb(f"wdr{i}", [128, NFC, 128], BF16) for i in range(2)]
            x1f = sb("x1f", [128, 8, 512])
            x1b = sb("x1b", [128, 8, 512], BF16)
            hT = sb("hT", [128, NFC, 512], BF16)
            hgx = sb("hgx", [128, 4, 130])
            c1 = sb("c1", [128, 512])
            gl = sb("gl", [128, 512])
            prevb = sb("prevb", [128, 8, 4, 2], BF16)
            gbt = sb("gbt", [128, 5, 4, 16], BF16)
            acc = sb("acc", [128, 4, 16])
            tmp8 = sb("tmp8", [128, 8])
            sel5 = sb("sel5", [128, 5])
            cvpS = sb("cvpS", [128, NFC, 4, 2])
            ytok = [sb(f"ytok{i}", [128, D]) for i in range(2)]
            P.dma("sp", sel5[:], L["sel5d"], w=["sel5"])
            P.dma("sp", cvpS[:], L["cvpd"][l], w=["cvpS"])
            gBv = gatB.rearrange("r (p x) -> p r x", x=16)
            yi = 0
            for sq_ in (PR, SM):
                n, N = sq_.n, sq_.N
                for g in range(sq_.ng):
                    c0 = sq_.off + g * N
                    gcols = slice(c0, c0 + N)
                    xk = [f"x1f{c}" for c in range(8)]
                    P.dma("sp", x1f[:, :, 0:N], X1T[:, :, gcols], r=["X1T"], w=xk)
                    self.copy("act", x1b[:, :, 0:N], x1f[:, :, 0:N], r=xk, w=["x1b"])
                    if sq_ is PR:
                        for k in range(4):
                            P.dma("sp", gbt[:, k, :, :], gBv[:, k * 16 + 4 * g:k * 16 + 4 * g + 4, :], r=["gatB"], w=["gbt"])
                        if g == 0:
                            self.memset("dve", gbt[:, 4, 0, :], 0.0, w=["gbt"])
                            P.dma("sp", gbt[:, 4, 1:4, :], gBv[:, 48:51, :], r=["gatB"], w=["gbt"])
                        else:
                            P.dma("sp", gbt[:, 4, :, :], gBv[:, 48 + 4 * g - 1:48 + 4 * g + 3, :], r=["gatB"], w=["gbt"])
                        self.ts("dve", acc[:], gbt[:, 0, :, :], sel5[:, 0:1], None, ALU.mult, r=["gbt", "sel5"], w=["acc"])
                        for k in range(1, 5):
                            self.stt("dve", acc[:], gbt[:, k, :, :], sel5[:, k:k + 1], acc[:], ALU.mult, ALU.add, r=["gbt", "sel5", "acc"], w=["acc"])
                        for t in range(4):
                            self.copy("dve", prevb[:, :, t, :], acc[:, t, :].rearrange("p (k c) -> p k c", c=2), r=["acc"], w=["prevb"])
                    for f in range(NFC):
                        wg, wgk = self.wchunk(wCg[f])
                        wu, wuk = self.wchunk(wCu[f])
                        for kc in range(8):
                            self.mm(pss[0][:, 0:N], wg[:, kc, :], x1b[:, kc, 0:N], kc == 0, kc == 7, r=[wgk, "x1b"], w=["ps0"])
                        if sq_ is PR:
                            for kc in range(8):
                                self.mm(pss[2][:, 0:8], wg[:, kc, :], prevb[:, kc, :, :].rearrange("p t c -> p (t c)"), kc == 0, kc == 7,
                                        r=[wgk, "prevb"], w=["ps2"])
                        for kc in range(8):
                            self.mm(pss[1][:, 0:N], wu[:, kc, :], x1b[:, kc, 0:N], kc == 0, kc == 7, r=[wuk, "x1b"], w=["ps1"])
                        for t in range(4):
                            self.copy("act", hgx[:, t, 2:2 + n], pss[0][:, t * n:(t + 1) * n], r=["ps0"], w=["hgx"])
                        if sq_ is PR:
                            self.copy("act", tmp8[:], pss[2][:, 0:8], r=["ps2"], w=["tmp8"])
                            self.copy("dve", hgx[:, :, 0:2], tmp8[:].rearrange("p (t c) -> p t c", c=2), r=["tmp8"], w=["hgx"])
                        else:
                            self.copy("dve", hgx[:, :, 0:2], cvpS[:, f, :, :], r=["cvpS"], w=["hgx"])
                        cw = lambda j: vecs[:, V_CW + f * 3 + j:V_CW + f * 3 + j + 1]
                        c1v = c1[:, 0:N].rearrange("p (t i) -> p t i", t=4)
                        self.ts("dve", c1v, hgx[:, :, 2:2 + n], cw(2), vecs[:, V_CB + f:V_CB + f + 1], ALU.mult, ALU.add, r=["hgx", "vecs"], w=["c1"])
                        self.stt("dve", c1v, hgx[:, :, 1:1 + n], cw(1), c1v, ALU.mult, ALU.add, r=["hgx", "vecs", "c1"], w=["c1"])
                        self.stt("dve", c1v, hgx[:, :, 0:n], cw(0), c1v, ALU.mult, ALU.add, r=["hgx", "vecs", "c1"], w=["c1"])
                        self.act(gl[:, 0:N], c1[:, 0:N], AF.Gelu_apprx_tanh, r=["c1"], w=["gl"])
                        self.tt("dve", hT[:, f, 0:N], gl[:, 0:N], pss[1][:, 0:N], ALU.mult, r=["gl", "ps1"], w=["hT"])
                        if sq_ is PR and g == 3:
                            self.copy("dve", convo[:, f, :], hgx[:, 3, n:n + 2], r=["hgx"], w=["convo"])
                        if sq_ is SM:
                            self.copy("dve", convos[:, f, :, :], hgx[:, :, n:n + 2], r=["hgx"], w=["convos"])
                    for c in range(8):
                        i = c % 2
                        P.dma("pool", wdr[i][:], wCd[c], w=[f"wdr{i}"])
                        pb, pk = pss[c % 2], f"ps{c % 2}"
                        for f in range(NFC):
                            self.mm(pb[:, 0:N], wdr[i][:, f, :], hT[:, f, 0:N], f == 0, f == NFC - 1, r=[f"wdr{i}", "hT"], w=[pk])
                        self.stt("dve", x1f[:, c, 0:N], x1f[:, c, 0:N], ALPHA, pb[:, 0:N], ALU.mult, ALU.add, r=[xk[c], pk], w=[xk[c]])
                    mean, rstd = self.stats([x1f[:, c, 0:N] for c in range(8)], xk, N, ones, "l2")
                    for c in range(8):
                        self.tt("dve", x1f[:, c, 0:N], x1f[:, c, 0:N], mean[:, 0:N], ALU.subtract, r=[xk[c], "st_mean"], w=[xk[c]])
                        self.tt("pool", x1f[:, c, 0:N], x1f[:, c, 0:N], rstd[:, 0:N], ALU.mult, r=[xk[c], "st_rstd"], w=[xk[c]])
                        self.act(x1f[:, c, 0:N], x1f[:, c, 0:N], AF.Identity, r=[xk[c], "vecs"], w=[xk[c]],
                                 scale=vecs[:, V_L2G + c:V_L2G + c + 1], bias=vecs[:, V_L2B + c:V_L2B + c + 1])
                    if not last:
                        P.dma("sp", XT[:, :, gcols], x1f[:, :, 0:N], r=xk, w=["XT"])
                    else:
                        oy = L["o_y"] if sq_ is PR else L["o_ys"]
                        for t in range(4):
                            yb = ytok[yi % 2]
                            yk = f"ytok{yi % 2}"
                            yi += 1
                            for c in range(8):
                                bank = (2, 3, 4, 7)[c % 4]
                                self.tr(pss[bank][0:n, 0:128], x1f[:, c, t * n:(t + 1) * n], identF[:], r=[xk[c], "consts"], w=[f"ps{bank}"])
                                self.copy(self.alt(), yb[0:n, c * 128:(c + 1) * 128], pss[bank][0:n, 0:128], r=[f"ps{bank}"], w=[yk])
                            r0 = (g * 4 + t) * n
                            P.dma("sp", oy[r0:r0 + n, :], yb[0:n, :], r=[yk])
            P.dma("sp", L["o_conv"][l], convo[:], r=["convo"])
            P.dma("sp", L["o_convs"][l], convos[:], r=["convos"])
            P.barrier()
SPL = dict(ua=0, va=256, qb=512, kb=896, vb=960, qib=1024, kib=1280, wib=1312, qc=1320, kc=1704, vc=2088, gc=2472)
LOG_G = np.log(1.0 - 2.0 ** (-5.0 - np.arange(6, dtype=np.float64)))


def _chunked(w):
    return np.ascontiguousarray(w.reshape(8, 128, -1).transpose(1, 0, 2))


def _swap_heads(w):
    m = w.reshape(w.shape[0], -1, 2, 32)
    return np.ascontiguousarray(m[:, :, ::-1, :]).reshape(w.shape[0], -1)


def _rot_tables(pos):
    half = 32
    freqs = (10000.0 ** (-np.arange(half, dtype=np.float32) / half)).astype(np.float32)
    ang = pos.astype(np.float32)[None, :] * freqs[:, None]
    cos = np.cos(ang)
    sin = np.sin(ang)
    cosT = np.concatenate([cos, cos, cos, cos], 0).astype(np.float32)
    sinT = np.concatenate([-sin, sin, -sin, sin], 0).astype(np.float32)
    return cosT, sinT


def _pairlay(per_head):
    t = np.zeros((128, 192), np.float64)
    for h in range(6):
        t[(h % 2) * 64:(h % 2) * 64 + 64, (h // 2) * 64:(h // 2) * 64 + 64] = per_head[h]
    return t


def _const_tables():
    c = {}
    c["identf"] = np.eye(128, dtype=np.float32)
    jj = np.arange(128, dtype=np.float64)
    kd0 = np.exp((127 - jj)[:, None] * LOG_G[None, :]) * 0.125
    kd1 = np.zeros((128, 6))
    kd1[:16] = np.exp((15 - jj[:16])[:, None] * LOG_G[None, :]) * 0.125
    c["kdec"] = np.stack([np.repeat(kd0, 64, 1), np.repeat(kd1, 64, 1)]).astype(np.float32)
    diff = jj[None, :] - jj[:, None]
    dt = np.where(diff[:, None, :] >= 0, np.exp(np.maximum(diff, 0)[:, None, :] * LOG_G[None, :, None]), 0.0) * 0.125
    c["dtT"] = dt.astype(np.float32)
    lane_h = lambda pair: np.array([2 * pair + (p // 64) for p in range(128)])
    qp = np.zeros((128, 3, 512))
    qs = np.zeros((128, 3, 64))
    for pair in range(3):
        lg = LOG_G[lane_h(pair)]
        qp[:, pair, :] = np.exp(((np.arange(512) % 128) + 1)[None, :] * lg[:, None])
        qs[:, pair, :] = np.exp(((np.arange(64) % 16) + 1)[None, :] * lg[:, None])
    c["qdecP"], c["qdecS"] = qp.astype(np.float32), qs.astype(np.float32)
    c["sel"] = (np.arange(128)[:, None] == (np.arange(768) % 128)[None, :]).astype(np.float32)
    c["sel16"] = (np.arange(16)[:, None] == (np.arange(96) % 16)[None, :]).astype(np.float32)
    c["cm"] = np.broadcast_to(((np.arange(128) // 16)[None, None, :] == np.arange(8)[None, :, None]), (128, 8, 128)).astype(np.float32).copy()
    c["e16"] = ((np.arange(128) % 16)[:, None] == np.arange(16)[None, :]).astype(np.float32)
    bd = np.zeros((128, 128), np.float32)
    bd[:64, :64] = 1 / 64
    bd[64:, 64:] = 1 / 64
    c["bd64"] = bd
    c["onesd"] = np.full((128, 128), 1 / 1024, np.float32)
    c["umask"] = (np.arange(128)[None, :] >= np.arange(128)[:, None]).astype(np.float32)
    c["cdecS"] = _pairlay(np.exp(16 * LOG_G)).astype(np.float32)
    return c


def make_inputs(inp):
    maps = []
    cst = _const_tables()
    per_layer = []
    for l in range(DEPTH):
        W = inp["w_in"][l]
        sl = lambda k, n: W[:, SPL[k]:SPL[k] + n]
        d = {}
        d["wA"] = _chunked(np.concatenate([sl("vc", 384), sl("kb", 64), sl("vb", 64), sl("kib", 32), sl("wib", 8),
                                           sl("kb", 64), sl("kib", 32), sl("kc", 384), _swap_heads(sl("kc", 384))], 1))
        wb = np.concatenate([sl("ua", 256), sl("va", 256), sl("qc", 384), _swap_heads(sl("qc", 384)), sl("kc", 384),
                             _swap_heads(sl("kc", 384)), sl("gc", 384), sl("vc", 384), sl("qb", 384), sl("qib", 256), inp["w_out"][l]], 1)
        d["wB"] = np.ascontiguousarray(_chunked(wb).reshape(128, 8, 35, 128).transpose(2, 0, 1, 3))
        d["wCg"] = np.ascontiguousarray(_chunked(inp["w_gate"][l]).reshape(128, 8, NFC, 128).transpose(2, 0, 1, 3))
        d["wCu"] = np.ascontiguousarray(_chunked(inp["w_up"][l]).reshape(128, 8, NFC, 128).transpose(2, 0, 1, 3))
        wd = inp["w_down"][l].reshape(NFC, 128, 8, 128)
        d["wCd"] = np.ascontiguousarray(wd.transpose(2, 1, 0, 3))
        fm = lambda v, k: v.reshape(k, 128).T
        cw = inp["conv_w"][l].reshape(3, NFC, 128).transpose(2, 1, 0).reshape(128, NFC * 3)
        d["vec"] = np.ascontiguousarray(np.concatenate([
            fm(inp["a_ln_g"][l], 2), fm(inp["a_ln_b"][l], 2), fm(inp["c_gn_g"][l], 3), fm(inp["ln1_g"][l], 8), fm(inp["ln1_b"][l], 8),
            fm(inp["ln2_g"][l], 8), fm(inp["ln2_b"][l], 8), cw, fm(inp["conv_b"][l], NFC)], 1).astype(np.float32))
        bs = inp["a_bs"][l]
        g_of = np.array([[ch * 2 + p // 64 for ch in range(2)] for p in range(128)])
        d["bsP"] = np.ascontiguousarray(np.tile(bs[g_of], (1, 1, 4)).astype(np.float32))
        d["bsS"] = np.ascontiguousarray(np.tile(bs[g_of][:, :, :16], (1, 1, 4)).astype(np.float32))
        d["wmT"] = np.ascontiguousarray(inp["a_ws"][l].transpose(2, 0, 1))
        per_layer.append(d)
    for c in range(8):
        b, j = c // 4, c % 4
        m = dict(cst)
        tiles = [4 * mm + j for mm in range(NT)]
        pos = np.concatenate([np.arange(t * 128, (t + 1) * 128) for t in tiles])
        m["xp"] = np.ascontiguousarray(inp["x_prompt"][b][pos])
        m["xs"] = np.ascontiguousarray(inp["x_sample"][4 * c:4 * c + 4].reshape(NS, D))
        pos_all = np.concatenate([pos, np.tile(2048 + np.arange(16), 4)])
        m["cosT"], m["sinT"] = _rot_tables(pos_all)
        dm = np.zeros((128, 512), np.float32)
        for jp in range(4):
            if jp > j:
                dm[:, jp * 128:(jp + 1) * 128] = NEG
            elif jp == j:
                dm[0:64, jp * 128 + 64:(jp + 1) * 128] = NEG
        m["dmask"] = dm
        rc = np.zeros((128, 9, 192))
        rc[:, 0] = _pairlay(np.exp(128 * j * LOG_G))
        for jp in range(3):
            if jp < j:
                rc[:, 1 + jp] = _pairlay(np.exp(128 * (j - 1 - jp) * LOG_G))
        rc[:, 4] = _pairlay(np.exp(512 * LOG_G))
        for jp in range(4):
            rc[:, 5 + jp] = _pairlay(np.exp(128 * (3 - jp) * LOG_G))
        m["rcoef"] = rc.astype(np.float32)
        s5 = np.zeros((128, 5), np.float32)
        s5[:, (j - 1) if j >= 1 else 4] = 1.0
        m["sel5"] = s5
        for l in range(DEPTH):
            for k, v in per_layer[l].items():
                m[f"{k}{l}"] = v
            m[f"ck{l}"] = np.ascontiguousarray(inp["cache_b_k"][l, 4 * c:4 * c + 4])
            m[f"cv{l}"] = np.ascontiguousarray(inp["cache_b_v"][l, 4 * c:4 * c + 4])
            m[f"cki{l}"] = np.ascontiguousarray(inp["cache_b_kidx"][l, 4 * c:4 * c + 4])
            sr = inp["state_ret"][l, 4 * c:4 * c + 4]
            t = np.zeros((128, 4, 192), np.float32)
            for h in range(6):
                t[(h % 2) * 64:(h % 2) * 64 + 64, :, (h // 2) * 64:(h // 2) * 64 + 64] = sr[:, h].transpose(1, 0, 2)
            m[f"stS{l}"] = t
            cv = inp["state_ffn_conv"][l, 4 * c:4 * c + 4]
            m[f"cvp{l}"] = np.ascontiguousarray(cv.reshape(4, 2, NFC, 128).transpose(3, 2, 0, 1))
        maps.append(m)
    return maps


def _unpair(t):
    out = np.zeros(t.shape[:-2] + (6, 64, 64), np.float32)
    for h in range(6):
        out[..., h, :, :] = t[..., (h % 2) * 64:(h % 2) * 64 + 64, (h // 2) * 64:(h // 2) * 64 + 64]
    return out


def assemble(R):
    y = np.zeros((2, 8192, D), np.float32)
    ys = np.zeros((32, 16, D), np.float32)
    kp = np.zeros((DEPTH, 2, 8192, 64), np.float32)
    vp = np.zeros((DEPTH, 2, 8192, 64), np.float32)
    kip = np.zeros((DEPTH, 2, 8192, 32), np.float32)
    rp = np.zeros((DEPTH, 2, 6, 64, 64), np.float32)
    cp = np.zeros((DEPTH, 2, 2, DFF), np.float32)
    ks = np.zeros((DEPTH, 32, 16, 64), np.float32)
    vs = np.zeros((DEPTH, 32, 16, 64), np.float32)
    kis = np.zeros((DEPTH, 32, 16, 32), np.float32)
    rs = np.zeros((DEPTH, 32, 6, 64, 64), np.float32)
    cs = np.zeros((DEPTH, 32, 2, DFF), np.float32)
    avs = np.zeros((DEPTH, 32, 16, 256), np.float32)
    for c in range(8):
        b, j = c // 4, c % 4
        r = R[c]
        pos = np.concatenate([np.arange((4 * mm + j) * 128, (4 * mm + j + 1) * 128) for mm in range(NT)])
        y[b, pos] = r["o_y"]
        ys[4 * c:4 * c + 4] = r["o_ys"].reshape(4, 16, D)
        kp[:, b, pos] = r["o_k"]
        vp[:, b, pos] = r["o_v"]
        kip[:, b, pos] = r["o_ki"]
        if j == 0:
            rp[:, b] = _unpair(r["o_ret"])
        if j == 3:
            cp[:, b] = r["o_conv"].transpose(0, 3, 2, 1).reshape(DEPTH, 2, DFF)
        ks[:, 4 * c:4 * c + 4] = r["o_ks"].reshape(DEPTH, 4, 16, 64)
        vs[:, 4 * c:4 * c + 4] = r["o_vs"].reshape(DEPTH, 4, 16, 64)
        kis[:, 4 * c:4 * c + 4] = r["o_kis"].reshape(DEPTH, 4, 16, 32)
        rs[:, 4 * c:4 * c + 4] = _unpair(r["o_rets"].transpose(0, 2, 1, 3))
        cs[:, 4 * c:4 * c + 4] = r["o_convs"].transpose(0, 3, 4, 2, 1).reshape(DEPTH, 4, 2, DFF)
        avs[:, 4 * c:4 * c + 4] = r["o_avs"].reshape(DEPTH, 4, 16, 256)
    return (y, ys, kp, vp, kip, rp, cp, ks, vs, kis, rs, cs, avs)


_CACHE = {}


def kernel(**inputs):
    inp = {k: np.asarray(v, dtype=np.float32) for k, v in inputs.items()}
    stage = _CACHE.get("stage", 99)
    if "nc" not in _CACHE:
        bld = Builder(stage)
        _CACHE["nc"] = bld.build()
        _CACHE["bld"] = bld
    nc = _CACHE["nc"]
    maps = make_inputs(inp)
    res = run_bass_kernel_spmd(nc, maps, core_ids=list(range(8)))
    _CACHE["raw"] = res.results
    return assemble(res.results)
```

```python
import numpy as np
import os
EXP = os.environ.get('EXP', '')
from contextlib import ExitStack
import concourse.bass as bass
import concourse.mybir as mybir
from concourse.bass_utils import run_bass_kernel_spmd

F32 = mybir.dt.float32
BF16 = mybir.dt.bfloat16
AF = mybir.ActivationFunctionType
ALU = mybir.AluOpType
AX = mybir.AxisListType

D = 1024
NT = 16
TS = 128
NTOK = NT * TS
NS = 64
DFF = 2816
NFC = DFF // 128
DEPTH = 2
ALPHA = (2 * DEPTH) ** 0.25
EPS = 1e-5
GROUPS = [[0, 1, 2, 3], [4, 5, 6, 7]]
NEG = -1.0e30


class Prog:
    NSLOT = 8

    def __init__(self, nc):
        self.nc = nc
        self.ops = []
        self.buf = {}
        self.slot_last = {}
        self.slot_rr = {"sp": 0, "pool": 0, "act": 0}
        self.last = {}
        self.dma_pending = []

    def _deps(self, eng, r, w):
        deps = set()
        for k in r:
            b = self.buf.setdefault(k, [None, []])
            if b[0] is not None:
                deps.add(b[0])
            if isinstance(k, str) and k.startswith("ps"):
                deps.update(d for d in b[1] if self.ops[d]["eng"] != eng)
        for k in w:
            b = self.buf.setdefault(k, [None, []])
            if b[0] is not None:
                deps.add(b[0])
            deps.update(b[1])
        if eng == "pe":
            deps = {d for d in deps if not (self.ops[d]["eng"] == "pe" and self.ops[d]["kind"] == "op")}
        return deps

    def _record(self, oid, r, w):
        me = self.ops[oid]
        for k in r:
            rl = self.buf[k][1]
            if me["kind"] == "op":
                rl[:] = [d for d in rl if not (self.ops[d]["kind"] == "op" and self.ops[d]["eng"] == me["eng"])]
            rl.append(oid)
        for k in w:
            self.buf[k] = [oid, []]

    def op(self, eng, fn, r=(), w=()):
        deps = self._deps(eng, r, w)
        oid = len(self.ops)
        self.ops.append(dict(eng=eng, kind="op", fn=fn, deps=deps))
        self._record(oid, r, w)
        self.last[eng] = oid
        return oid

    def barrier(self):
        ids = set(self.last.values()) | set(self.dma_pending)
        for e in ["pe", "act", "dve", "pool", "sp"]:
            self.ops.append(dict(eng=e, kind="bar", deps=set(ids)))
        self.buf = {}
        self.dma_pending = []

    def dma(self, q, out, in_, r=(), w=()):
        deps = self._deps(q, r, w)
        slot = self.slot_rr[q] % self.NSLOT
        self.slot_rr[q] += 1
        prev = self.slot_last.get((q, slot))
        if prev is not None:
            deps.add(prev)
        oid = len(self.ops)
        self.ops.append(dict(eng=q, kind="dma", out=out, in_=in_, deps=deps, slot=slot))
        self.slot_last[(q, slot)] = oid
        self._record(oid, r, w)
        self.dma_pending.append(oid)
        return oid

    def cc(self, ins, outs, r=(), w=()):
        deps = self._deps("pool", r, w)
        oid = len(self.ops)
        self.ops.append(dict(eng="pool", kind="cc", ins=ins, outs=outs, deps=deps))
        self._record(oid, r, w)
        self.dma_pending.append(oid)
        return oid

    def emit(self, stack):
        nc = self.nc
        ops = self.ops
        needed = set()
        for o in ops:
            needed.update(o["deps"])
        engs = ["pe", "act", "dve", "pool", "sp"]
        sem_e = {e: stack.enter_context(nc.semaphore("pg_" + e)) for e in engs}
        sem_d = {(q, s): stack.enter_context(nc.semaphore(f"dq_{q}{s}"))
                 for q in ("sp", "pool", "act") for s in range(self.NSLOT)}
        sem_cc = stack.enter_context(nc.semaphore("ccsem"))
        cnt = {e: 0 for e in engs}
        dcnt = {k: 0 for k in sem_d}
        ccn = 0
        ev = {}
        for i, o in enumerate(ops):
            if o["kind"] == "op":
                if i in needed:
                    cnt[o["eng"]] += 1
                    ev[i] = (sem_e[o["eng"]], cnt[o["eng"]], ("e", o["eng"]))
            elif o["kind"] == "dma":
                k = (o["eng"], o["slot"])
                dcnt[k] += 16
                ev[i] = (sem_d[k], dcnt[k], ("d",) + k)
            elif o["kind"] == "bar":
                pass
            else:
                ccn += 1
                ev[i] = (sem_cc, ccn, ("c",))
        per = {e: [] for e in engs}
        known = {e: {} for e in engs}
        for i, o in enumerate(ops):
            e = o["eng"]
            best = {}
            for d in o["deps"]:
                s, v, key = ev[d]
                if known[e].get(key, 0) >= v:
                    continue
                if key not in best or best[key][1] < v:
                    best[key] = (s, v)
            for key, (s, v) in best.items():
                per[e].append(("wait", s, v))
                known[e][key] = v
            per[e].append(("ins", i))
        for (q, s), c in dcnt.items():
            if c:
                per[q].append(("wait", sem_d[(q, s)], c))
        if ccn:
            per["pool"].append(("wait", sem_cc, ccn))
        if os.environ.get("DUMP"):
            for e in engs:
                print("ENGINE", e)
                for it in per[e]:
                    if it[0] == "wait":
                        print("   wait", it[1], it[2])
                    else:
                        o = ops[it[1]]
                        print("   ", it[1], o["kind"], o.get("tag", ""), "sig" if it[1] in ev else "", ev.get(it[1], ("", ""))[1])
        self.stats = {e: sum(1 for x in per[e] if x[0] == "ins") for e in engs}
        self.stats["sem"] = dict(cnt)

        if os.environ.get("CHECK"):
            semv = {}
            pos = {e: 0 for e in engs}
            prog = True
            while prog:
                prog = False
                for e in engs:
                    while pos[e] < len(per[e]):
                        it = per[e][pos[e]]
                        if it[0] == "wait":
                            if semv.get(it[1].num, 0) >= it[2]:
                                pos[e] += 1
                                prog = True
                            else:
                                break
                        else:
                            i = it[1]
                            if i in ev:
                                o = ops[i]
                                inc = 16 if o["kind"] == "dma" else 1
                                semv[ev[i][0].num] = semv.get(ev[i][0].num, 0) + inc
                            pos[e] += 1
                            prog = True
            for e in engs:
                if pos[e] < len(per[e]):
                    it = per[e][pos[e]]
                    print("DEADLOCK", e, "at", pos[e], "/", len(per[e]), it[0], it[1] if it[0] == "wait" else "", it[2] if it[0] == "wait" else "",
                          "have", semv.get(it[1].num, 0) if it[0] == "wait" else "")
            print("CHECK done", {e: (pos[e], len(per[e])) for e in engs})

        def run(E, lst):
            for it in lst:
                if it[0] == "wait":
                    E.wait_ge(it[1], it[2])
                else:
                    i = it[1]
                    o = ops[i]
                    if o["kind"] == "op":
                        ins = o["fn"](E)
                        if i in ev:
                            ins.then_inc(ev[i][0], 1)
                    elif o["kind"] == "dma":
                        E.dma_start(out=o["out"], in_=o["in_"]).then_inc(ev[i][0], 16)
                    elif o["kind"] == "bar":
                        pass
                    else:
                        E.collective_compute("AllGather", ALU.bypass, replica_groups=GROUPS,
                                             ins=[a.opt() for a in o["ins"]], outs=[a.opt() for a in o["outs"]]).then_inc(ev[i][0])

        block = stack.enter_context(nc.Block())

        @block.sync
        def _(E):
            run(E, per["sp"])

        @block.scalar
        def _(E):
            run(E, per["act"])

        @block.vector
        def _(E):
            run(E, per["dve"])

        @block.gpsimd
        def _(E):
            run(E, per["pool"])

        @block.tensor
        def _(E):
            run(E, per["pe"])


class Seq:
    def __init__(self, name, n, ntile, off):
        self.name, self.n, self.ntile, self.off = name, n, ntile, off
        self.G = 4
        self.ng = ntile // 4
        self.N = 4 * n


PR = Seq("p", 128, NT, 0)
SM = Seq("s", 16, 4, NTOK)
NBIS = 16
CH = dict(ua=0, va=2, qc=4, qcs=7, kc=10, kcs=13, gc=16, vc=19, qb=22, qib=25, wo=27)


class Builder:
    def __init__(self, stage=99):
        self.stage = stage
        self.nc = bass.Bass("TRN2", target_bir_lowering=False)
        self.P = Prog(self.nc)
        self.stack = ExitStack()
        self.rr = 0
        self.wi = 0

    def din(self, name, shape, dt=F32):
        return self.nc.dram_tensor(name, list(shape), dt, kind="ExternalInput").ap()

    def dout(self, name, shape, dt=F32):
        return self.nc.dram_tensor(name, list(shape), dt, kind="ExternalOutput").ap()

    def dscr(self, name, shape, dt=F32):
        return self.nc.dram_tensor(name, list(shape), dt).ap()

    def sb(self, name, shape, dt=F32, st=None):
        self.uid = getattr(self, "uid", 0) + 1
        return (st or self.stack).enter_context(self.nc.sbuf_tensor(f"s{self.uid}_{name}", list(shape), dt))

    def ps(self, name):
        return self.stack.enter_context(self.nc.psum_tensor(name, [128, 512], F32))

    def mm(self, out, lhsT, rhs, start, stop, r, w):
        self.P.op("pe", lambda E: E.matmul(out, lhsT, rhs, start=start, stop=stop, skip_group_check=True), r=r, w=w)

    def tr(self, out, in_, ident, r, w):
        self.P.op("pe", lambda E: E.transpose(out, in_, ident), r=r, w=w)

    def copy(self, eng, out, in_, r, w):
        if eng == "act":
            self.P.op("act", lambda E: E.activation(out=out, in_=in_, func=AF.Copy), r=r, w=w)
        else:
            self.P.op(eng, lambda E: E.tensor_copy(out=out, in_=in_), r=r, w=w)

    def act(self, out, in_, func, r, w, bias=0.0, scale=1.0):
        self.P.op("act", lambda E: E.activation(out=out, in_=in_, func=func, bias=bias, scale=scale), r=r, w=w)

    def tt(self, eng, out, in0, in1, op, r, w):
        self.P.op(eng, lambda E: E.tensor_tensor(out=out, in0=in0, in1=in1, op=op), r=r, w=w)

    def ts(self, eng, out, in0, s1, s2, op0, op1=None, r=(), w=(), accum_out=None):
        kw = {}
        if accum_out is not None:
            kw["accum_out"] = accum_out
        if op1 is None:
            self.P.op(eng, lambda E: E.tensor_scalar(out=out, in0=in0, scalar1=s1, scalar2=None, op0=op0, **kw), r=r, w=w)
        else:
            self.P.op(eng, lambda E: E.tensor_scalar(out=out, in0=in0, scalar1=s1, scalar2=s2, op0=op0, op1=op1, **kw), r=r, w=w)

    def stt(self, eng, out, in0, scalar, in1, op0, op1, r, w):
        self.P.op(eng, lambda E: E.scalar_tensor_tensor(out=out, in0=in0, scalar=scalar, in1=in1, op0=op0, op1=op1), r=r, w=w)

    def red(self, out, in_, op, r, w):
        self.P.op("dve", lambda E: E.tensor_reduce(out=out, in_=in_, axis=AX.X, op=op), r=r, w=w)

    def memset(self, eng, ap, val, w):
        self.P.op(eng, lambda E: E.memset(ap, val), r=(), w=w)

    def alt(self):
        self.rr += 1
        return "act" if self.rr % 2 else "dve"

    def wchunk(self, src):
        i = self.wi % len(self.wring)
        self.wi += 1
        t = self.wring[i]
        self.P.dma("pool", t[:], src, w=[f"wr{i}"])
        return t, f"wr{i}"

    def proj(self, pst, psk, wt, wk, c0, M, xb, xk, N, po=0):
        for kc in range(8):
            self.mm(pst[po:po + M, 0:N], wt[:, kc, c0:c0 + M], xb[:, kc, 0:N], kc == 0, kc == 7, r=[wk, xk], w=[psk])

    def stats(self, xs, keys, N, ones, tag):
        pm, pe2 = self.pss[5], self.pss[6]
        nx = len(xs)
        for i, (x, k) in enumerate(zip(xs, keys)):
            sq = self.sqb[i % 2]
            self.act(sq[:, 0:N], x, AF.Square, r=[k], w=[f"sqb{i % 2}"])
            self.mm(pm[:, 0:N], ones[:], x, i == 0, i == nx - 1, r=[k, "consts"], w=["ps5"])
            self.mm(pe2[:, 0:N], ones[:], sq[:, 0:N], i == 0, i == nx - 1, r=[f"sqb{i % 2}", "consts"], w=["ps6"])
        mean, rstd, tmp = self.st_mean, self.st_rstd, self.st_tmp
        self.act(tmp[:, 0:N], pm[:, 0:N], AF.Square, r=["ps5"], w=["st_tmp"])
        self.copy("act", mean[:, 0:N], pm[:, 0:N], r=["ps5"], w=["st_mean"])
        self.tt("dve", tmp[:, 0:N], pe2[:, 0:N], tmp[:, 0:N], ALU.subtract, r=["ps6", "st_tmp"], w=["st_tmp"])
        self.ts("dve", tmp[:, 0:N], tmp[:, 0:N], 0.0, EPS, ALU.max, ALU.add, r=["st_tmp"], w=["st_tmp"])
        self.act(tmp[:, 0:N], tmp[:, 0:N], AF.Sqrt, r=["st_tmp"], w=["st_tmp"])
        self.P.op("dve", lambda E: E.reciprocal(out=rstd[:, 0:N], in_=tmp[:, 0:N]), r=["st_tmp"], w=["st_rstd"])
        return mean, rstd

    def build(self):
        nc, P = self.nc, self.P
        st = self.stage
        S = self.stack
        xp = self.din("xp", [NTOK, D])
        xs_in = self.din("xs", [NS, D])
        identf = self.din("identf", [128, 128])
        cosd = self.din("cosT", [128, NTOK + NS])
        sind = self.din("sinT", [128, NTOK + NS])
        kdecd = self.din("kdec", [2, 128, 384])
        dtd = self.din("dtT", [128, 6, 128])
        qdecPd = self.din("qdecP", [128, 3, 512])
        qdecSd = self.din("qdecS", [128, 3, 64])
        seld = self.din("sel", [128, 768])
        sel16d = self.din("sel16", [16, 96])
        cmd = self.din("cm", [128, 8, 128])
        e16d = self.din("e16", [128, 16])
        bd64d = self.din("bd64", [128, 128])
        onesd = self.din("onesd", [128, 128])
        umd = self.din("umask", [128, 128])
        dmaskd = self.din("dmask", [128, 512])
        rcoefd = self.din("rcoef", [128, 9, 192])
        sel5d = self.din("sel5", [128, 5])
        cdecSd = self.din("cdecS", [128, 192])
        wA = [self.din(f"wA{l}", [128, 8, 1416]) for l in range(DEPTH)]
        wB = [self.din(f"wB{l}", [35, 128, 8, 128]) for l in range(DEPTH)]
        wCg = [self.din(f"wCg{l}", [NFC, 128, 8, 128]) for l in range(DEPTH)]
        wCu = [self.din(f"wCu{l}", [NFC, 128, 8, 128]) for l in range(DEPTH)]
        wCd = [self.din(f"wCd{l}", [8, 128, NFC, 128]) for l in range(DEPTH)]
        vecd = [self.din(f"vec{l}", [128, 127]) for l in range(DEPTH)]
        bsPd = [self.din(f"bsP{l}", [128, 2, 512]) for l in range(DEPTH)]
        bsSd = [self.din(f"bsS{l}", [128, 2, 64]) for l in range(DEPTH)]
        wmTd = [self.din(f"wmT{l}", [128, 4, 128]) for l in range(DEPTH)]
        ckd = [self.din(f"ck{l}", [4, 2048, 64]) for l in range(DEPTH)]
        cvd = [self.din(f"cv{l}", [4, 2048, 64]) for l in range(DEPTH)]
        ckid = [self.din(f"cki{l}", [4, 2048, 32]) for l in range(DEPTH)]
        stSd = [self.din(f"stS{l}", [128, 4, 192]) for l in range(DEPTH)]
        cvpd = [self.din(f"cvp{l}", [128, NFC, 4, 2]) for l in range(DEPTH)]
        o_y = self.dout("o_y", [NTOK, D])
        o_ys = self.dout("o_ys", [NS, D])
        o_k = self.dout("o_k", [DEPTH, NTOK, 64])
        o_v = self.dout("o_v", [DEPTH, NTOK, 64])
        o_ki = self.dout("o_ki", [DEPTH, NTOK, 32])
        o_ret = self.dout("o_ret", [DEPTH, 128, 192])
        o_conv = self.dout("o_conv", [DEPTH, 128, NFC, 2])
        o_ks = self.dout("o_ks", [DEPTH, NS, 64])
        o_vs = self.dout("o_vs", [DEPTH, NS, 64])
        o_kis = self.dout("o_kis", [DEPTH, NS, 32])
        o_rets = self.dout("o_rets", [DEPTH, 128, 4, 192])
        o_convs = self.dout("o_convs", [DEPTH, 128, NFC, 4, 2])
        o_avs = self.dout("o_avs", [DEPTH, NS, 256])
        XT = self.dscr("XT", [128, 8, NTOK + NS])
        X1T = self.dscr("X1T", [128, 8, NTOK + NS])
        bncA32 = self.dscr("bncA", [NT, 10240])
        gatA32 = self.dscr("gatA", [4 * NT, 10240])
        bncA = bncA32.bitcast(BF16)
        gatA = gatA32.bitcast(BF16)
        bncK2 = [self.dscr(f"bncK{i}", [8 * 128, 192]) for i in range(2)]
        gatK2 = [self.dscr(f"gatK{i}", [4 * 8 * 128, 192]) for i in range(2)]
        bncB32 = self.dscr("bncB", [NT, 1024])
        gatB32 = self.dscr("gatB", [4 * NT, 1024])
        bncB = bncB32.bitcast(BF16)
        gatB = gatB32.bitcast(BF16)

        identF = self.sb("identF", [128, 128])
        identB = self.sb("identB", [128, 128], BF16)
        bd64 = self.sb("bd64", [128, 128])
        ones = self.sb("ones", [128, 128])
        P.dma("sp", identF[:], identf, w=["consts"])
        P.dma("sp", bd64[:], bd64d, w=["consts"])
        P.dma("sp", ones[:], onesd, w=["consts"])
        P.dma("pool", identB[:], identf, w=["consts"])
        kdecT = self.sb("kdecT", [128, 2, 384])
        P.dma("sp", kdecT[:], kdecd.rearrange("a p f -> p a f"), w=["consts"])
        wtok = {"p": self.sb("wtokp", [128, NT, 8]), "s": self.sb("wtoks", [128, 4, 8])}
        vecs = self.sb("vecs", [128, 127])
        self.pss = pss = [self.ps(f"ps{i}") for i in range(8)]
        self.sqb = [self.sb(f"sqb{i}", [128, 512]) for i in range(2)]
        self.st_mean = self.sb("st_mean", [128, 512])
        self.st_rstd = self.sb("st_rstd", [128, 512])
        self.st_tmp = self.sb("st_tmp", [128, 512])
        ktS = self.sb("ktS", [128, 4, 16], BF16)
        vS = self.sb("vS", [16, 4, 64], BF16)
        convo = self.sb("convo", [128, NFC, 2])
        convos = self.sb("convos", [128, NFC, 4, 2])
        V_AG, V_AB, V_GN, V_L1G, V_L1B, V_L2G, V_L2B, V_CW, V_CB = 0, 2, 4, 7, 15, 23, 31, 39, 105

        with ExitStack() as ph:
            xin = [self.sb(f"xin{i}", [128, D], st=ph) for i in range(2)]
            xtg = [self.sb(f"xtg{i}", [128, 8, 128], st=ph) for i in range(2)]
            for sq_, src in ((PR, xp), (SM, xs_in)):
                n = sq_.n
                for m in range(sq_.ntile):
                    b = m % 2
                    P.dma("sp", xin[b][0:n, :], src[m * n:(m + 1) * n, :], w=[f"xin{b}"])
                    for c in range(8):
                        pb = pss[c % 4]
                        self.tr(pb[:, 0:n], xin[b][0:n, c * 128:(c + 1) * 128], identF[0:n, 0:n], r=[f"xin{b}", "consts"], w=[f"ps{c % 4}"])
                        self.copy(self.alt(), xtg[b][:, c, 0:n], pb[:, 0:n], r=[f"ps{c % 4}"], w=[f"xtg{b}"])
                    P.dma("sp", XT[:, :, sq_.off + m * n:sq_.off + (m + 1) * n], xtg[b][:, :, 0:n], r=[f"xtg{b}"], w=["XT"])
            P.barrier()

        for l in range(DEPTH if st >= 9 else 1):
            P.dma("sp", vecs[:], vecd[l], w=["vecs"])
            with ExitStack() as ph:
                WA = self.sb("WA", [128, 8, 1416], BF16, st=ph)
                for kc in range(8):
                    P.dma("pool", WA[:, kc, :], wA[l][:, kc, :], w=["WA"])
                xTf = [self.sb(f"xTf{i}", [128, 8, 128], st=ph) for i in range(2)]
                xTb = [self.sb(f"xTb{i}", [128, 8, 128], BF16, st=ph) for i in range(2)]
                vtok = self.sb("vtok", [128, 384], BF16, st=ph)
                vb16 = self.sb("vb16", [128, 64], BF16, st=ph)
                kvo = [self.sb(f"kvo{i}", [128, 168], st=ph) for i in range(2)]
                ktb = self.sb("ktb", [128, 128], BF16, st=ph)
                rt1 = self.sb("rt1", [128, 3, 128], st=ph)
                krot = self.sb("krot", [128, 3, 128], st=ph)
                kd = self.sb("kd", [128, 384], BF16, st=ph)
                kvsb = [self.sb(f"kvsb{i}", [128, 192], st=ph) for i in range(2)]
                csa = [self.sb(f"csa{i}", [128, 2, 128], st=ph) for i in range(2)]
                stS = self.sb("stS", [128, 4, 192], st=ph)
                cdecS = self.sb("cdecS", [128, 192], st=ph)
                P.dma("sp", stS[:], stSd[l], w=["stS"])
                P.dma("sp", cdecS[:], cdecSd, w=["cdecS"])
                it = 0
                for sq_ in (PR, SM):
                    n = sq_.n
                    if sq_ is SM:
                        P.cc([bncA32], [gatA32], r=["bncA"], w=["gatA"])
                        for i in range(2):
                            P.cc([bncK2[i]], [gatK2[i]], r=[f"bncK{i}"], w=[f"gatK{i}"])
                    ok, ov, oki = (o_k, o_v, o_ki) if sq_ is PR else (o_ks, o_vs, o_kis)
                    kdi = 0 if sq_ is PR else 1
                    for m in range(sq_.ntile):
                        b = it % 2
                        it += 1
                        cols = slice(sq_.off + m * n, sq_.off + (m + 1) * n)
                        rows = slice(m * n, (m + 1) * n)
                        P.dma("sp", xTf[b][:, :, 0:n], XT[:, :, cols], r=["XT"], w=[f"xTf{b}"])
                        self.copy("act", xTb[b][:, :, 0:n], xTf[b][:, :, 0:n], r=[f"xTf{b}"], w=[f"xTb{b}"])
                        xb = xTb[b]
                        rx = [f"xTb{b}", "WA"]
                        for kc in range(8):
                            self.mm(pss[0][0:n, 0:512], xb[:, kc, 0:n], WA[:, kc, 0:512], kc == 0, kc == 7, r=rx, w=["ps0"])
                        for kc in range(8):
                            self.mm(pss[1][0:n, 0:40], xb[:, kc, 0:n], WA[:, kc, 512:552], kc == 0, kc == 7, r=rx, w=["ps1"])
                        self.copy("act", vtok[0:n, :], pss[0][0:n, 0:384], r=["ps0"], w=["vtok"])
                        self.copy("act", vb16[0:n, :], pss[0][0:n, 448:512], r=["ps0"], w=["vb16"])
                        self.copy("dve", kvo[b][0:n, 0:128], pss[0][0:n, 384:512], r=["ps0"], w=[f"kvo{b}"])
                        self.copy("dve", kvo[b][0:n, 128:168], pss[1][0:n, 0:40], r=["ps1"], w=[f"kvo{b}"])
                        self.copy("dve", wtok[sq_.name][0:n, m, :], kvo[b][0:n, 160:168], r=[f"kvo{b}"], w=["wtok"])
                        P.dma("sp", ok[l, rows, :], kvo[b][0:n, 0:64], r=[f"kvo{b}"])
                        P.dma("sp", ov[l, rows, :], kvo[b][0:n, 64:128], r=[f"kvo{b}"])
                        P.dma("sp", oki[l, rows, :], kvo[b][0:n, 128:160], r=[f"kvo{b}"])
                        for kc in range(8):
                            self.mm(pss[2][0:64, 0:n], WA[:, kc, 552:616], xb[:, kc, 0:n], kc == 0, kc == 7, r=rx, w=["ps2"])
                        for kc in range(8):
                            self.mm(pss[2][64:96, 0:n], WA[:, kc, 616:648], xb[:, kc, 0:n], kc == 0, kc == 7, r=rx, w=["ps2"])
                        if sq_ is PR:
                            self.copy("act", ktb[0:96, 0:n], pss[2][0:96, 0:n], r=["ps2"], w=["ktb"])
                            P.dma("sp", bncA[m, 0:12288].rearrange("(d t) -> d t", t=128), ktb[0:96, :], r=["ktb"], w=["bncA"])
                            P.dma("sp", bncA[m, 12288:20480].rearrange("(s e) -> s e", e=64), vb16[:], r=["vb16"], w=["bncA"])
                        else:
                            self.copy("act", ktS[0:96, m, :], pss[2][0:96, 0:n], r=["ps2"], w=["ktS"])
                            self.copy("dve", vS[0:n, m, :], vb16[0:n, :], r=["vb16"], w=["vS"])
                        for p in range(3):
                            for kc in range(8):
                                self.mm(pss[3][:, p * 128:p * 128 + n], WA[:, kc, 648 + p * 128:648 + (p + 1) * 128], xb[:, kc, 0:n],
                                        kc == 0, kc == 7, r=rx, w=["ps3"])
                        for p in range(3):
                            for kc in range(8):
                                self.mm(pss[4][:, p * 128:p * 128 + n], WA[:, kc, 1032 + p * 128:1032 + (p + 1) * 128], xb[:, kc, 0:n],
                                        kc == 0, kc == 7, r=rx, w=["ps4"])
                        P.dma("sp", csa[b][:, 0, 0:n], cosd[:, cols], w=[f"csa{b}"])
                        P.dma("sp", csa[b][:, 1, 0:n], sind[:, cols], w=[f"csa{b}"])
                        cb = csa[b][:, 0, 0:n].unsqueeze(1).to_broadcast([128, 3, n])
                        sbb = csa[b][:, 1, 0:n].unsqueeze(1).to_broadcast([128, 3, n])
                        p3 = pss[3][:, 0:384].rearrange("p (c t) -> p c t", c=3)[:, :, 0:n]
                        p4 = pss[4][:, 0:384].rearrange("p (c t) -> p c t", c=3)[:, :, 0:n]
                        self.tt("dve", rt1[:, :, 0:n], p4, sbb, ALU.mult, r=["ps4", f"csa{b}"], w=["rt1"])
                        self.tt("dve", krot[:, :, 0:n], p3, cb, ALU.mult, r=["ps3", f"csa{b}"], w=["krot"])
                        self.tt("pool", krot[:, :, 0:n], krot[:, :, 0:n], rt1[:, :, 0:n], ALU.add, r=["krot", "rt1"], w=["krot"])
                        for p in range(3):
                            self.tr(pss[5][0:n, p * 128:(p + 1) * 128], krot[:, p, 0:n], identF[:], r=["krot", "consts"], w=["ps5"])
                        self.tt("dve", kd[0:n, :], pss[5][0:n, 0:384], kdecT[0:n, kdi, :], ALU.mult, r=["ps5", "consts"], w=["kd"])
                        for h in range(6):
                            po = (h % 2) * 64
                            self.mm(pss[6][po:po + 64, (h // 2) * 64:(h // 2) * 64 + 64], kd[0:n, h * 64:(h + 1) * 64], vtok[0:n, h * 64:(h + 1) * 64],
                                    True, True, r=["kd", "vtok"], w=["ps6"])
                        if sq_ is PR:
                            self.copy("act", kvsb[b][:], pss[6][:, 0:192], r=["ps6"], w=[f"kvsb{b}"])
                            P.dma("sp", bncK2[m // 8][(m % 8) * 128:(m % 8 + 1) * 128, :], kvsb[b][:], r=[f"kvsb{b}"], w=[f"bncK{m // 8}"])
                        else:
                            self.tt("dve", kvsb[b][:], stS[:, m, :], cdecS[:], ALU.mult, r=["stS", "cdecS"], w=[f"kvsb{b}"])
                            self.tt("dve", kvsb[b][:], kvsb[b][:], pss[6][:, 0:192], ALU.add, r=[f"kvsb{b}", "ps6"], w=[f"kvsb{b}"])
                            P.dma("sp", o_rets[l, :, m, :], kvsb[b][:], r=[f"kvsb{b}"])
                P.barrier()
            if st <= 1:
                break
            self.phaseB(l, locals())
            if st <= 5:
                break
            self.phaseC(l, locals())

        P.emit(self.stack)
        return nc
    def phaseB(self, l, L):
        P, pss = self.P, self.pss
        st = self.stage
        ones, bd64, identF, identB, vecs = L["ones"], L["bd64"], L["identF"], L["identB"], L["vecs"]
        XT, X1T, gatA, gatK2, bncB = L["XT"], L["X1T"], L["gatA"], L["gatK2"], L["bncB"]
        wB = L["wB"][l]
        V_AG, V_AB, V_GN, V_L1G, V_L1B = L["V_AG"], L["V_AB"], L["V_GN"], L["V_L1G"], L["V_L1B"]
        wtok = L["wtok"]
        with ExitStack() as pp:
            KTI = self.sb("KTI", [128, 16, 4, 128], BF16, st=pp)
            VA = self.sb("VA", [128, 64, 65], BF16, st=pp)
            rst = self.sb("rst", [128, 16, 192], BF16, st=pp)
            self.memset("dve", VA[:], 1.0, w=["VA"])
            for j in range(4):
                for m in range(NT):
                    P.dma("sp", KTI[0:96, m, j, :], gatA[j * 16 + m, 0:12288].rearrange("(d t) -> d t", t=128), r=["gatA"], w=["KTI"])
                    P.dma("sp", VA[:, m * 4 + j, 0:64], gatA[j * 16 + m, 12288:20480].rearrange("(s e) -> s e", e=64), r=["gatA"], w=["VA"])
            with ExitStack() as sc:
                rc = self.sb("rc", [128, 9, 192], st=sc)
                kvg = [self.sb(f"kvg{i}", [128, 4, 192], st=sc) for i in range(2)]
                Sst = self.sb("Sst", [128, 192], st=sc)
                ta = self.sb("ta", [128, 192], st=sc)
                tb = self.sb("tb", [128, 192], st=sc)
                P.dma("sp", rc[:], L["rcoefd"], w=["rc"])
                self.memset("dve", Sst[:], 0.0, w=["Sst"])
                for m in range(NT):
                    b = m % 2
                    P.dma("sp", kvg[b][:], gatK2[m // 8].rearrange("(j m p) f -> p j m f", j=4, m=8)[:, :, m % 8, :], r=[f"gatK{m // 8}"], w=[f"kvg{b}"])
                    self.tt("dve", ta[:], Sst[:], rc[:, 0, :], ALU.mult, r=["Sst", "rc"], w=["ta"])
                    for jp in range(3):
                        self.tt("dve", tb[:], kvg[b][:, jp, :], rc[:, 1 + jp, :], ALU.mult, r=[f"kvg{b}", "rc"], w=["tb"])
                        self.tt("dve", ta[:], ta[:], tb[:], ALU.add, r=["ta", "tb"], w=["ta"])
                    self.copy("dve", rst[:, m, :], ta[:], r=["ta"], w=["rst"])
                    self.tt("dve", ta[:], Sst[:], rc[:, 4, :], ALU.mult, r=["Sst", "rc"], w=["ta"])
                    for jp in range(4):
                        self.tt("dve", tb[:], kvg[b][:, jp, :], rc[:, 5 + jp, :], ALU.mult, r=[f"kvg{b}", "rc"], w=["tb"])
                        self.tt("dve", ta[:], ta[:], tb[:], ALU.add, r=["ta", "tb"], w=["ta"])
                    self.copy("dve", Sst[:], ta[:], r=["ta"], w=["Sst"])
                P.dma("sp", L["o_ret"][l], Sst[:], r=["Sst"])
                P.barrier()
            if st <= 2:
                return
            self.phaseB2(l, L, KTI, VA, rst)

    def phaseB2(self, l, L, KTI, VA, rst):
        P, pss = self.P, self.pss
        st = self.stage
        ones, bd64, identF, identB, vecs = L["ones"], L["bd64"], L["identF"], L["identB"], L["vecs"]
        XT, X1T, gatA, gatK2, bncB = L["XT"], L["X1T"], L["gatA"], L["gatK2"], L["bncB"]
        wB = L["wB"][l]
        V_AG, V_AB, V_GN, V_L1G, V_L1B = L["V_AG"], L["V_AB"], L["V_GN"], L["V_L1G"], L["V_L1B"]
        wtok = L["wtok"]
        with ExitStack() as ph:
            sb = lambda name, shape, dt=F32: self.sb(name, shape, dt, st=ph)
            self.wring = [sb(f"wr{i}", [128, 8, 128], BF16) for i in range(3)]
            scores = sb("scores", [128, 8192])
            junk = sb("junk", [128, 3840], mybir.dt.uint8)
            xfg = sb("xfg", [128, 8, 512])
            xbg = sb("xbg", [128, 8, 512], BF16)
            mixT = sb("mixT", [128, 8, 512], BF16)
            uT = sb("uT", [128, 2, 512])
            gT = sb("gT", [128, 2, 512])
            vtokA = sb("vtokA", [128, 4, 256], BF16)
            avf = scores[0:16, 0:1024].rearrange("p (t f) -> p t f", t=4)
            csg = sb("csg", [128, 2, 512])
            qrot = sb("qrot", [128, 3, 512], BF16)
            qd = sb("qd", [128, 3, 512], BF16)
            krot = sb("krotB", [128, 3, 512], BF16)
            sil = sb("sil", [128, 3, 512], BF16)
            vtokB = sb("vtokB", [128, 4, 384], BF16)
            Sm = sb("Sm", [128, 6, 128], BF16)
            ysb = sb("ysb", [128, 384])
            ycn = sb("ycn", [128, 384])
            DT = sb("DT", [128, 6, 128])
            qdecP = sb("qdecP", [128, 3, 128])
            qdecS = sb("qdecS", [128, 3, 16])
            qq = sb("qq", [128, 4096], BF16)
            Amat = sb("Amat", [128, 128], BF16)
            Wd = sb("Wd", [128, 8, 128], BF16)
            rz = [sb(f"rz{i}", [128, 512], BF16) for i in range(4)]
            pT = [sb(f"pT{i}", [128, 768], BF16) for i in range(2)]
            mb = [sb(f"mb{i}", [128, 128], BF16) for i in range(2)]
            col = sb("col", [128, 16])
            ob = sb("ob", [128, 384])
            Sel = sb("Sel", [128, 768], BF16)
            Sel16 = sb("Sel16", [16, 96], BF16)
            CM = sb("CM", [128, 8, 128], BF16)
            E16 = sb("E16", [128, 16])
            dmask = sb("dmask", [128, 512])
            WmT = sb("WmT", [128, 4, 128], BF16)
            wmf = scores[:, 0:512].rearrange("p (g i) -> p g i", g=4)
            um = scores[:, 512:640]
            bsP = sb("bsP", [128, 2, 128])
            bsS = sb("bsS", [128, 2, 16])
            qi2T = qq[64:96, :].rearrange("p (t h) -> p t h", h=8)
            P.dma("sp", DT[:], L["dtd"], w=["DT"])
            P.dma("sp", qdecP[:], L["qdecPd"][:, :, 0:128], w=["qdec"])
            P.dma("sp", qdecS[:], L["qdecSd"][:, :, 0:16], w=["qdec"])
            P.dma("pool", Sel[:], L["seld"], w=["Sel"])
            P.dma("pool", Sel16[:], L["sel16d"], w=["Sel"])
            P.dma("pool", CM[:], L["cmd"], w=["CM"])
            P.dma("sp", E16[:], L["e16d"], w=["E16"])
            P.dma("sp", dmask[:], L["dmaskd"], w=["dmask"])
            P.dma("sp", wmf, L["wmTd"][l], w=["scores"])
            P.dma("sp", um, L["umd"], w=["scores"])
            P.dma("sp", bsP[:], L["bsPd"][l][:, :, 0:128], w=["bs"])
            P.dma("sp", bsS[:], L["bsSd"][l][:, :, 0:16], w=["bs"])
            self.tt("dve", WmT[:], wmf, um.unsqueeze(1).to_broadcast([128, 4, 128]), ALU.mult, r=["scores"], w=["WmT"])

            def chunk(ci):
                return self.wchunk(wB[ci])

            def group(sq_, g, keysrc):
                n, N = sq_.n, sq_.N
                c0 = sq_.off + g * N
                gcols = slice(c0, c0 + N)
                xk = [f"xfg{c}" for c in range(8)]
                P.dma("sp", xfg[:, :, 0:N], XT[:, :, gcols], r=["XT"], w=xk)
                self.copy("act", xbg[:, :, 0:N], xfg[:, :, 0:N], r=xk, w=["xbg"])
                P.dma("sp", csg[:, 0, 0:N], L["cosd"][:, gcols], w=["csg"])
                P.dma("sp", csg[:, 1, 0:N], L["sind"][:, gcols], w=["csg"])
                bs = bsP if sq_ is PR else bsS
                qdec = qdecP if sq_ is PR else qdecS

                for ch in range(2):
                    wt, wk = chunk(CH["ua"] + ch)
                    self.proj(pss[ch], f"ps{ch}", wt, wk, 0, 128, xbg, "xbg", N)
                    self.act(uT[:, ch, 0:N], pss[ch][:, 0:N], AF.Gelu_apprx_tanh, r=[f"ps{ch}"], w=["uT"])
                for ch in range(2):
                    wt, wk = chunk(CH["va"] + ch)
                    self.proj(pss[ch], f"ps{ch}", wt, wk, 0, 128, xbg, "xbg", N)
                    self.act(gT[:, ch, 0:N], pss[ch][:, 0:N], AF.Gelu_apprx_tanh, r=[f"ps{ch}"], w=["gT"])
                if 'a2' in EXP:
                    return
                for ch in range(2):
                    mean, rstd = self.stats([gT[:, ch, 0:N]], ["gT"], N, bd64, "a")
                    self.tt("dve", gT[:, ch, 0:N], gT[:, ch, 0:N], mean[:, 0:N], ALU.subtract, r=["gT", "st_mean"], w=["gT"])
                    self.tt("dve", gT[:, ch, 0:N], gT[:, ch, 0:N], rstd[:, 0:N], ALU.mult, r=["gT", "st_rstd"], w=["gT"])
                    self.ts("dve", gT[:, ch, 0:N], gT[:, ch, 0:N], vecs[:, V_AG + ch:V_AG + ch + 1], vecs[:, V_AB + ch:V_AB + ch + 1],
                            ALU.mult, ALU.add, r=["gT", "vecs"], w=["gT"])
                if 'a3' in EXP:
                    return
                for t in range(4):
                    pb = pss[2 + t % 2]
                    for ch in range(2):
                        self.tr(pb[0:n, ch * 128:(ch + 1) * 128], gT[:, ch, t * n:(t + 1) * n], identF[:], r=["gT", "consts"], w=[f"ps{2 + t % 2}"])
                    self.copy(self.alt(), vtokA[0:n, t, :], pb[0:n, 0:256], r=[f"ps{2 + t % 2}"], w=["vtokA"])
                    if sq_ is SM:
                        self.copy("dve", avf[0:n, t, :], pb[0:n, 0:256], r=[f"ps{2 + t % 2}"], w=["scores"])
                if sq_ is SM:
                    P.dma("sp", L["o_avs"][l].rearrange("(b t) f -> t b f", t=16), avf, r=["scores"])
                if 'a4' in EXP:
                    return
                for t in range(4):
                    for gr in range(4):
                        po = (gr % 2) * 64
                        self.mm(pss[gr // 2][po:po + 64, t * n:(t + 1) * n], vtokA[0:n, t, gr * 64:(gr + 1) * 64], WmT[0:n, gr, 0:n],
                                True, True, r=["vtokA", "WmT"], w=[f"ps{gr // 2}"])
                if 'a5' in EXP:
                    return
                for ch in range(2):
                    bb = bs[:, ch, 0:n].unsqueeze(1).to_broadcast([128, 4, n])
                    self.copy("act", gT[:, ch, 0:N], pss[ch][:, 0:N], r=[f"ps{ch}"], w=["gT"])
                    g3 = gT[:, ch, 0:N].rearrange("p (t i) -> p t i", t=4)
                    self.tt("dve", g3, g3, bb, ALU.add, r=["gT", "bs"], w=["gT"])
                    if 'a6' in EXP:
                        continue
                    self.tt("dve", mixT[:, ch, 0:N], gT[:, ch, 0:N], uT[:, ch, 0:N], ALU.mult, r=["gT", "uT"], w=["mixT"])

                if 'A' in EXP:
                    return
                def rotary(cq, cqs, dst, with_qd):
                    for p in range(3):
                        wt, wk = chunk(cq + p)
                        self.proj(pss[0], "ps0", wt, wk, 0, 128, xbg, "xbg", N)
                        wt, wk = chunk(cqs + p)
                        self.proj(pss[1], "ps1", wt, wk, 0, 128, xbg, "xbg", N)
                        t1, t2 = self.sqb[0], self.sqb[1]
                        self.tt("dve", t1[:, 0:N], pss[0][:, 0:N], csg[:, 0, 0:N], ALU.mult, r=["ps0", "csg"], w=["sqb0"])
                        self.tt("dve", t2[:, 0:N], pss[1][:, 0:N], csg[:, 1, 0:N], ALU.mult, r=["ps1", "csg"], w=["sqb1"])
                        self.tt("pool", t1[:, 0:N], t1[:, 0:N], t2[:, 0:N], ALU.add, r=["sqb0", "sqb1"], w=["sqb0"])
                        self.copy("act", dst[:, p, 0:N], t1[:, 0:N], r=["sqb0"], w=["qrot" if with_qd else "krotB"])
                        if with_qd:
                            qb_ = qdec[:, p, 0:n].unsqueeze(1).to_broadcast([128, 4, n])
                            self.tt("dve", qd[:, p, 0:N].rearrange("p (t i) -> p t i", t=4), t1[:, 0:N].rearrange("p (t i) -> p t i", t=4), qb_,
                                    ALU.mult, r=["sqb0", "qdec"], w=["qd"])

                rotary(CH["qc"], CH["qcs"], qrot, True)
                if 'r1' in EXP:
                    return
                rotary(CH["kc"], CH["kcs"], krot, False)
                for p in range(3):
                    wt, wk = chunk(CH["gc"] + p)
                    self.proj(pss[p % 2], f"ps{p % 2}", wt, wk, 0, 128, xbg, "xbg", N)
                    self.act(sil[:, p, 0:N], pss[p % 2][:, 0:N], AF.Silu, r=[f"ps{p % 2}"], w=["sil"])
                for p in range(3):
                    wt, wk = chunk(CH["vc"] + p)
                    for t in range(4):
                        pb = pss[2 + t % 2]
                        for kc in range(8):
                            self.mm(pb[0:n, 0:128], xbg[:, kc, t * n:(t + 1) * n], wt[:, kc, :], kc == 0, kc == 7, r=["xbg", wk], w=[f"ps{2 + t % 2}"])
                        self.copy(self.alt(), vtokB[0:n, t, p * 128:(p + 1) * 128], pb[0:n, 0:128], r=[f"ps{2 + t % 2}"], w=["vtokB"])
                if 'r2' in EXP:
                    return
                for t in range(4):
                    tc_ = slice(t * n, (t + 1) * n)
                    m = g * 4 + t
                    for h in range(6):
                        po, p = (h % 2) * 64, h // 2
                        pb, pk = (pss[2], "ps2") if h % 2 == 0 else (pss[3], "ps3")
                        self.mm(pb[0:n, p * 128:p * 128 + n], krot[po:po + 64, p, tc_], qrot[po:po + 64, p, tc_], True, True, r=["krotB", "qrot"], w=[pk])
                    s1, s2 = self.sqb[0], self.sqb[1]
                    self.copy("act", s1[0:n, 0:384], pss[2][0:n, 0:384], r=["ps2"], w=["sqb0"])
                    self.copy("act", s2[0:n, 0:384], pss[3][0:n, 0:384], r=["ps3"], w=["sqb1"])
                    for h in range(6):
                        src, sk = (s1, "sqb0") if h % 2 == 0 else (s2, "sqb1")
                        p = h // 2
                        self.tt("dve", Sm[0:n, h, 0:n], src[0:n, p * 128:p * 128 + n], DT[0:n, h, 0:n], ALU.mult, r=[sk, "DT"], w=["Sm"])
                    if 'r3' in EXP:
                        continue
                    rsrc = keysrc["rst"](m)
                    for h in range(6):
                        po, p = (h % 2) * 64, h // 2
                        pb, pk = (pss[0], "ps0") if h % 2 == 0 else (pss[1], "ps1")
                        self.mm(pb[po:po + 64, p * 128:p * 128 + n], vtokB[0:n, t, h * 64:(h + 1) * 64], Sm[0:n, h, 0:n], True, False,
                                r=["vtokB", "Sm"], w=[pk])
                        self.mm(pb[po:po + 64, p * 128:p * 128 + n], rsrc[po:po + 64, p * 64:(p + 1) * 64], qd[po:po + 64, p, tc_], False, True,
                                r=["rst", "qd"], w=[pk])
                    if 'r4' in EXP:
                        continue
                    ysv = ysb[:, 0:3 * n].rearrange("p (c i) -> p c i", c=3)
                    self.copy("act", ysv[0:64], pss[0][0:64, 0:384].rearrange("p (c i) -> p c i", c=3)[:, :, 0:n], r=["ps0"], w=["ysb"])
                    self.copy("act", ysv[64:128], pss[1][64:128, 0:384].rearrange("p (c i) -> p c i", c=3)[:, :, 0:n], r=["ps1"], w=["ysb"])
                    mean, rstd = self.stats([ysb[:, 0:3 * n]], ["ysb"], 3 * n, bd64, "r")
                    self.tt("dve", ycn[:, 0:3 * n], ysb[:, 0:3 * n], mean[:, 0:3 * n], ALU.subtract, r=["ysb", "st_mean"], w=["ycn"])
                    self.tt("dve", ycn[:, 0:3 * n], ycn[:, 0:3 * n], rstd[:, 0:3 * n], ALU.mult, r=["ycn", "st_rstd"], w=["ycn"])
                    for p in range(3):
                        self.stt("dve", mixT[:, 5 + p, tc_], ycn[:, p * n:(p + 1) * n], vecs[:, V_GN + p:V_GN + p + 1], sil[:, p, tc_], ALU.mult, ALU.mult,
                                 r=["ycn", "vecs", "sil"], w=["mixT"])

                if 'R' in EXP:
                    return
                for c in range(3):
                    wt, wk = chunk(CH["qb"] + c)
                    for hh in range(2):
                        self.proj(pss[hh], f"ps{hh}", wt, wk, hh * 64, 64, xbg, "xbg", N)
                        q2v = qq[0:64, 0:24 * n].rearrange("p (t h i) -> p t h i", t=4, h=6)
                        self.copy(self.alt(), q2v[:, :, 2 * c + hh, :], pss[hh][0:64, 0:N].rearrange("p (t i) -> p t i", t=4), r=[f"ps{hh}"], w=["qq"])
                for c in range(2):
                    wt, wk = chunk(CH["qib"] + c)
                    for hh in range(4):
                        pb = pss[hh % 2]
                        self.proj(pb, f"ps{hh % 2}", wt, wk, hh * 32, 32, xbg, "xbg", N, po=64)
                        self.copy(self.alt(), qi2T[:, 0:N, 4 * c + hh], pb[64:96, 0:N], r=[f"ps{hh % 2}"], w=["qq"])
                for t in range(4):
                    self.dsa_tile(sq_, g, t, l, L, keysrc)

                if 'D' in EXP:
                    return
                for c in range(8):
                    wt, wk = chunk(CH["wo"] + c)
                    pb, pk = pss[c % 2], f"ps{c % 2}"
                    for kc in range(8):
                        self.mm(pb[:, 0:N], wt[:, kc, :], mixT[:, kc, 0:N], kc == 0, kc == 7, r=[wk, "mixT"], w=[pk])
                    self.stt("dve", xfg[:, c, 0:N], xfg[:, c, 0:N], ALPHA, pb[:, 0:N], ALU.mult, ALU.add, r=[xk[c], pk], w=[xk[c]])
                mean, rstd = self.stats([xfg[:, c, 0:N] for c in range(8)], xk, N, ones, "l1")
                for c in range(8):
                    self.tt("dve", xfg[:, c, 0:N], xfg[:, c, 0:N], mean[:, 0:N], ALU.subtract, r=[xk[c], "st_mean"], w=[xk[c]])
                    self.tt("pool", xfg[:, c, 0:N], xfg[:, c, 0:N], rstd[:, 0:N], ALU.mult, r=[xk[c], "st_rstd"], w=[xk[c]])
                    self.act(xfg[:, c, 0:N], xfg[:, c, 0:N], AF.Identity, r=[xk[c], "vecs"], w=[xk[c]],
                             scale=vecs[:, V_L1G + c:V_L1G + c + 1], bias=vecs[:, V_L1B + c:V_L1B + c + 1])
                P.dma("sp", X1T[:, :, gcols], xfg[:, :, 0:N], r=xk, w=["X1T"])
                if sq_ is PR:
                    bnd = self.bnd
                    for t in range(4):
                        self.copy("act", bnd[:, t, :, :], xfg[:, :, t * 128 + 126:t * 128 + 128], r=xk, w=["bnd"])
                    P.dma("sp", bncB[g * 4:(g + 1) * 4, :].rearrange("t (p x) -> p t x", x=16), bnd[:].rearrange("p t k c -> p t (k c)"),
                          r=["bnd"], w=["bncB"])

            self.bnd = sb("bnd", [128, 4, 8, 2], BF16)
            self._scores, self._junk = scores, junk
            self._junk2 = sb("junk2", [128, 4368], mybir.dt.uint8)
            self._sacc = sb("sacc", [128, 1])
            self._dsa = dict(Amat=Amat, Wd=Wd, rz=rz, pT=pT, mb=mb, col=col, ob=ob, Sel=Sel, Sel16=Sel16, CM=CM, E16=E16, dmask=dmask,
                             qq=qq, qi2T=qi2T, mixT=mixT, wtok=wtok, identF=identF, identB=identB)

            if 'a1' in EXP:
                P.barrier()
                return
            ksrc = dict(rst=lambda m: rst[:, m, :], kind="p", KTI=KTI, VA=VA)
            for g in range(PR.ng if st >= 4 else 1):
                group(PR, g, ksrc)
            P.barrier()
            if st >= 9:
                P.cc([L["bncB32"]], [L["gatB32"]], r=[], w=["gatB"])
            if st <= 3:
                return
            with ExitStack() as ss:
                KTIs = self.sb("KTIs", [128, 2064], BF16, st=ss)
                VAs = self.sb("VAs", [128, 17, 65], BF16, st=ss)
                cK = self.sb("cK", [128, 16, 96], st=ss)
                rstS = self.sb("rstS", [128, 4, 192], BF16, st=ss)
                stSf = self.sb("stSf", [128, 4, 192], st=ss)
                P.dma("sp", stSf[:], L["stSd"][l], w=["stSf"])
                self.copy("act", rstS[:], stSf[:], r=["stSf"], w=["rst"])
                ksrc = dict(rst=lambda m: rstS[:, m, :], kind="s", KTIs=KTIs, VAs=VAs, cK=cK, ck=L["ckd"][l], cv=L["cvd"][l], cki=L["ckid"][l],
                            ktS=L["ktS"], vS=L["vS"])
                group(SM, 0, ksrc)
                P.barrier()

    def dsa_tile(self, sq_, g, t, l, L, ks):
        P, pss, d = self.P, self.pss, self._dsa
        n = sq_.n
        ng = n // 16
        m = g * 4 + t
        tc_ = slice(t * n, (t + 1) * n)
        scores, junk = self._scores, self._junk
        Amat, Wd, rz, pT, mb, col, ob = d["Amat"], d["Wd"], d["rz"], d["pT"], d["mb"], d["col"], d["ob"]
        identF, identB, mixT = d["identF"], d["identB"], d["mixT"]
        qi2T = d["qi2T"]
        q2f = d["qq"][0:64, 0:24 * n].rearrange("p (t x) -> p t x", t=4)
        SelX = d["Sel"] if n == 128 else d["Sel16"]
        if ks["kind"] == "p":
            KTI, VA = ks["KTI"], ks["VA"]
            kkey, vkey = "KTI", "VA"
            iblocks = [(KTI[64:96, kb].rearrange("p j t -> p (j t)"), 512, kb * 512) for kb in range(m + 1)]
            ablocks = [(KTI[0:64, kb, jj, :], VA[:, kb * 4 + jj, :], 128, (kb * 4 + jj) * 128) for kb in range(m + 1) for jj in range(4)]
            Lk = (m + 1) * 512
        else:
            KTIs, VAs, cK = ks["KTIs"], ks["VAs"], ks["cK"]
            kkey, vkey = "KTIs", "VAs"
            b = t
            for k in range(16):
                P.dma("sp", cK[:, k, 0:64], ks["ck"][b][k * 128:(k + 1) * 128, :], w=["cK"])
                P.dma("sp", cK[:, k, 64:96], ks["cki"][b][k * 128:(k + 1) * 128, :], w=["cK"])
            for k in range(16):
                pb, pk = pss[k % 2], f"ps{k % 2}"
                self.tr(pb[0:96, 0:128], cK[:, k, :], identF[:], r=["cK", "consts"], w=[pk])
                self.copy(self.alt(), KTIs[0:96, k * 128:(k + 1) * 128], pb[0:96, 0:128], r=[pk], w=["KTIs"])
            self.copy("dve", KTIs[0:96, 2048:2064], ks["ktS"][0:96, b, :], r=["ktS"], w=["KTIs"])
            self.memset("dve", VAs[:], 1.0, w=["VAs"])
            for k in range(16):
                P.dma("pool", VAs[:, k, 0:64], ks["cv"][b][k * 128:(k + 1) * 128, :], w=["VAs"])
            self.copy("dve", VAs[0:16, 16, 0:64], ks["vS"][0:16, b, :], r=["vS"], w=["VAs"])

            iblocks = [(KTIs[64:96, kb * 512:(kb + 1) * 512], 512, kb * 512) for kb in range(4)] + [(KTIs[64:96, 2048:2064], 16, 2048)]
            ablocks = [(KTIs[0:64, k * 128:(k + 1) * 128], VAs[:, k, :], 128, k * 128) for k in range(16)] + [(KTIs[0:64, 2048:2064], VAs[0:16, 16, :], 16, 2048)]
            Lk = 2064
        wv = d["wtok"][sq_.name]
        for a in range(16):
            self.ts("dve", Amat[0:n, a * 8:(a + 1) * 8], wv[0:n, m, :], d["E16"][0:n, a:a + 1], None, ALU.mult, r=["E16", "wtok"], w=["Amat"])
        self.mm(pss[4][:, 0:n], Amat[0:n, :], identB[0:n, 0:n], True, True, r=["Amat", "consts"], w=["ps4"])
        wall = self.sqb[1]
        self.copy("act", wall[:, 0:n], pss[4][:, 0:n], r=["ps4"], w=["sqb1"])
        for gq in range(ng):
            self.tt("dve", Wd[:, gq, 0:n], wall[:, 0:n], d["CM"][:, gq, 0:n], ALU.mult, r=["sqb1", "CM"], w=["Wd"])
        tmpm = self.sqb[0]
        steps = [(bi, gq) for bi in range(len(iblocks)) for gq in range(ng)]

        def emit_z(si):
            bi, gq = steps[si]
            rhs_ap, nk, c0 = iblocks[bi]
            zb = (2, 3, 0, 1)[si % 4]
            lhs = qi2T[:, t * n + 16 * gq:t * n + 16 * gq + 16, :].rearrange("p a h -> p (a h)")
            self.mm(pss[zb][:, 0:nk], lhs, rhs_ap, True, True, r=["qq", kkey], w=[f"ps{zb}"])

        for si in range(min(2, len(steps))):
            emit_z(si)
        for si, (bi, gq) in enumerate(steps):
            rhs_ap, nk, c0 = iblocks[bi]
            zb = (2, 3, 0, 1)[si % 4]
            pz, pzk = pss[zb], f"ps{zb}"
            rzb, rk = rz[si % 4], f"rz{si % 4}"
            if self.alt() == "act":
                self.act(rzb[:, 0:nk], pz[:, 0:nk], AF.Relu, r=[pzk], w=[rk])
            else:
                self.ts("dve", rzb[:, 0:nk], pz[:, 0:nk], 0.0, None, ALU.max, r=[pzk], w=[rk])
            ab = (4, 7)[bi % 2]
            abk = f"ps{ab}"
            self.mm(pss[ab][0:n, 0:nk], Wd[:, gq, 0:n], rzb[:, 0:nk], gq == 0, gq == ng - 1, r=["Wd", rk], w=[abk])
            if si + 2 < len(steps):
                emit_z(si + 2)
            if gq == ng - 1:
                if ks["kind"] == "p" and bi == len(iblocks) - 1:
                    self.tt("dve", tmpm[0:n, 0:nk], pss[ab][0:n, 0:nk], d["dmask"][0:n, 0:nk], ALU.subtract, r=[abk, "dmask"], w=["sqb0"])
                    self.tt("dve", scores[0:n, c0:c0 + nk], pss[ab][0:n, 0:nk], d["dmask"][0:n, 0:nk], ALU.add, r=[abk, "dmask"], w=["scores"])
                else:
                    self.copy(self.alt(), scores[0:n, c0:c0 + nk], pss[ab][0:n, 0:nk], r=[abk], w=["scores"])
        lo, w0, mid, cnt, gk, t5 = (col[0:n, i:i + 1] for i in range(6))
        if ks["kind"] == "p":
            self.red(lo, tmpm[0:n, 0:512], ALU.min, r=["sqb0"], w=["col"])
            if m > 0:
                self.red(t5, scores[0:n, 0:m * 512], ALU.min, r=["scores"], w=["col"])
                self.tt("dve", lo, lo, t5, ALU.min, r=["col"], w=["col"])
        else:
            self.red(lo, scores[0:n, 0:Lk], ALU.min, r=["scores"], w=["col"])
        self.red(w0, scores[0:n, 0:Lk], ALU.max, r=["scores"], w=["col"])
        self.tt("dve", w0, w0, lo, ALU.subtract, r=["col"], w=["col"])
        Ld = (Lk * 15 // 32) // 16 * 16 if Lk >= 1024 else Lk
        La = Lk - Ld
        sc_ap, jk_ap = scores[0:n, 0:Ld], junk[0:n, 0:Ld]
        if La:
            sa_ap, ja_ap = scores[0:n, Ld:Lk], self._junk2[0:n, 0:La]
            sacc = self._sacc[0:n, 0:1]
        thr = 255.5 - 0.5 * La
        for k in range(NBIS):
            hw = 2.0 ** (-(k + 1))
            self.ts("dve", mid, w0, hw, lo, ALU.mult, ALU.add, r=["col"], w=["mid"])
            P.op("dve", lambda E: E.tensor_scalar(out=jk_ap, in0=sc_ap, scalar1=mid, scalar2=None, op0=ALU.is_ge, op1=ALU.add, accum_out=cnt),
                 r=["scores", "mid"], w=["junk", "col"])
            if La:
                P.op("act", lambda E: E.activation(out=ja_ap, in_=sa_ap, func=AF.Sign, bias=mid, scale=-1.0, accum_out=sacc),
                     r=["scores", "mid"], w=["junk2", "sacc"])
                self.stt("dve", cnt, sacc, -0.5, cnt, ALU.mult, ALU.add, r=["col", "sacc"], w=["col"])
            self.ts("dve", gk, cnt, thr, hw, ALU.is_ge, ALU.mult, r=["col"], w=["col"])
            self.stt("dve", lo, gk, w0, lo, ALU.mult, ALU.add, r=["col", "mid"], w=["col"])
        pO = pss[7]
        nb = len(ablocks)

        def emit_logits(bi):
            kt, v, nk, c0 = ablocks[bi]
            i2 = bi % 2
            mbb, mk = mb[i2], f"mb{i2}"
            self.ts("dve", mbb[0:n, 0:nk], scores[0:n, c0:c0 + nk], lo, -30000.0, ALU.is_lt, ALU.mult, r=["scores", "col"], w=[mk])
            la, lb = (5, 6) if i2 == 0 else (0, 1)
            self.mm(pss[la][0:nk, 0:4 * n], kt, q2f[:, t, 0:4 * n], True, False, r=[kkey, "qq"], w=[f"ps{la}"])
            self.mm(pss[la][0:nk, 0:4 * n], mbb[0:n, 0:nk], SelX[0:n, 0:4 * n], False, True, r=[mk, "Sel"], w=[f"ps{la}"])
            self.mm(pss[lb][0:nk, 0:2 * n], kt, q2f[:, t, 4 * n:6 * n], True, False, r=[kkey, "qq"], w=[f"ps{lb}"])
            self.mm(pss[lb][0:nk, 0:2 * n], mbb[0:n, 0:nk], SelX[0:n, 4 * n:6 * n], False, True, r=[mk, "Sel"], w=[f"ps{lb}"])

        emit_logits(0)
        for bi, (kt, v, nk, c0) in enumerate(ablocks):
            if bi + 1 < nb:
                emit_logits(bi + 1)
            i2 = bi % 2
            la, lb = (5, 6) if i2 == 0 else (0, 1)
            ptb, pk = pT[i2], f"pT{i2}"
            self.act(ptb[0:nk, 0:4 * n], pss[la][0:nk, 0:4 * n], AF.Exp, r=[f"ps{la}"], w=[pk], scale=0.125)
            self.act(ptb[0:nk, 4 * n:6 * n], pss[lb][0:nk, 0:2 * n], AF.Exp, r=[f"ps{lb}"], w=[pk], scale=0.125)
            for h in range(6):
                self.mm(pO[0:n, h * 65:(h + 1) * 65], ptb[0:nk, h * n:(h + 1) * n], v[0:nk, :], bi == 0 and h == 0, bi == nb - 1 and h == 5,
                        r=[pk, vkey], w=["ps7"])
        posb = self.sqb[1]
        self.copy("act", posb[0:n, 0:390], pO[0:n, 0:390], r=["ps7"], w=["sqb1"])
        pov = posb[0:n, 0:390].rearrange("p (h e) -> p h e", e=65)
        rden = col[0:n, 8:14]
        for h in range(6):
            P.op("dve", (lambda hh: (lambda E: E.reciprocal(out=col[0:n, 8 + hh:9 + hh], in_=posb[0:n, hh * 65 + 64:hh * 65 + 65])))(h), r=["sqb1"], w=["col"])
        for h in range(6):
            self.ts("dve", ob[0:n, h * 64:(h + 1) * 64], posb[0:n, h * 65:h * 65 + 64], col[0:n, 8 + h:9 + h], None, ALU.mult, r=["sqb1", "col"], w=["ob"])
        for c in range(3):
            pb, pk = pss[c % 2], f"ps{c % 2}"
            self.tr(pb[:, 0:n], ob[0:n, c * 128:(c + 1) * 128], identF[0:n, 0:n], r=["ob", "consts"], w=[pk])
            self.copy(self.alt(), mixT[:, 2 + c, tc_], pb[:, 0:n], r=[pk], w=["mixT"])

    def phaseC(self, l, L):
        P, pss = self.P, self.pss
        st = self.stage
        ones, identF, vecs = L["ones"], L["identF"], L["vecs"]
        XT, X1T, gatB, bncB32, gatB32 = L["XT"], L["X1T"], L["gatB"], L["bncB32"], L["gatB32"]
        wCg, wCu, wCd = L["wCg"][l], L["wCu"][l], L["wCd"][l]
        V_L2G, V_L2B, V_CW, V_CB = L["V_L2G"], L["V_L2B"], L["V_CW"], L["V_CB"]
        convo, convos = L["convo"], L["convos"]
        last = (l == DEPTH - 1)
        P.barrier()
        with ExitStack() as ph:
            sb = lambda name, shape, dt=F32: self.sb(name, shape, dt, st=ph)
            self.wring = [sb(f"wr{i}", [128, 8, 128], BF16) for i in range(4)]
            wdr = [sb(f"wdr{i}", [128, NFC, 128], BF16) for i in range(2)]
            x1f = sb("x1f", [128, 8, 512])
            x1b = sb("x1b", [128, 8, 512], BF16)
            hT = sb("hT", [128, NFC, 512], BF16)
            hgx = sb("hgx", [128, 4, 130])
            c1 = sb("c1", [128, 512])
            gl = sb("gl", [128, 512])
            prevb = sb("prevb", [128, 8, 4, 2], BF16)
            gbt = sb("gbt", [128, 5, 4, 16], BF16)
            acc = sb("acc", [128, 4, 16])
            tmp8 = sb("tmp8", [128, 8])
            sel5 = sb("sel5", [128, 5])
            cvpS = sb("cvpS", [128, NFC, 4, 2])
            ytok = [sb(f"ytok{i}", [128, D]) for i in range(2)]
            P.dma("sp", sel5[:], L["sel5d"], w=["sel5"])
            P.dma("sp", cvpS[:], L["cvpd"][l], w=["cvpS"])
            gBv = gatB.rearrange("r (p x) -> p r x", x=16)
            yi = 0
            for sq_ in (PR, SM):
                n, N = sq_.n, sq_.N
                for g in range(sq_.ng):
                    c0 = sq_.off + g * N
                    gcols = slice(c0, c0 + N)
                    xk = [f"x1f{c}" for c in range(8)]
                    P.dma("sp", x1f[:, :, 0:N], X1T[:, :, gcols], r=["X1T"], w=xk)
                    self.copy("act", x1b[:, :, 0:N], x1f[:, :, 0:N], r=xk, w=["x1b"])
                    if sq_ is PR:
                        for k in range(4):
                            P.dma("sp", gbt[:, k, :, :], gBv[:, k * 16 + 4 * g:k * 16 + 4 * g + 4, :], r=["gatB"], w=["gbt"])
                        if g == 0:
                            self.memset("dve", gbt[:, 4, 0, :], 0.0, w=["gbt"])
                            P.dma("sp", gbt[:, 4, 1:4, :], gBv[:, 48:51, :], r=["gatB"], w=["gbt"])
                        else:
                            P.dma("sp", gbt[:, 4, :, :], gBv[:, 48 + 4 * g - 1:48 + 4 * g + 3, :], r=["gatB"], w=["gbt"])
                        self.ts("dve", acc[:], gbt[:, 0, :, :], sel5[:, 0:1], None, ALU.mult, r=["gbt", "sel5"], w=["acc"])
                        for k in range(1, 5):
                            self.stt("dve", acc[:], gbt[:, k, :, :], sel5[:, k:k + 1], acc[:], ALU.mult, ALU.add, r=["gbt", "sel5", "acc"], w=["acc"])
                        for t in range(4):
                            self.copy("dve", prevb[:, :, t, :], acc[:, t, :].rearrange("p (k c) -> p k c", c=2), r=["acc"], w=["prevb"])
                    for f in range(NFC):
                        wg, wgk = self.wchunk(wCg[f])
                        wu, wuk = self.wchunk(wCu[f])
                        for kc in range(8):
                            self.mm(pss[0][:, 0:N], wg[:, kc, :], x1b[:, kc, 0:N], kc == 0, kc == 7, r=[wgk, "x1b"], w=["ps0"])
                        if sq_ is PR:
                            for kc in range(8):
                                self.mm(pss[2][:, 0:8], wg[:, kc, :], prevb[:, kc, :, :].rearrange("p t c -> p (t c)"), kc == 0, kc == 7,
                                        r=[wgk, "prevb"], w=["ps2"])
                        for kc in range(8):
                            self.mm(pss[1][:, 0:N], wu[:, kc, :], x1b[:, kc, 0:N], kc == 0, kc == 7, r=[wuk, "x1b"], w=["ps1"])
                        for t in range(4):
                            self.copy("act", hgx[:, t, 2:2 + n], pss[0][:, t * n:(t + 1) * n], r=["ps0"], w=["hgx"])
                        if sq_ is PR:
                            self.copy("act", tmp8[:], pss[2][:, 0:8], r=["ps2"], w=["tmp8"])
                            self.copy("dve", hgx[:, :, 0:2], tmp8[:].rearrange("p (t c) -> p t c", c=2), r=["tmp8"], w=["hgx"])
                        else:
                            self.copy("dve", hgx[:, :, 0:2], cvpS[:, f, :, :], r=["cvpS"], w=["hgx"])
                        cw = lambda j: vecs[:, V_CW + f * 3 + j:V_CW + f * 3 + j + 1]
                        c1v = c1[:, 0:N].rearrange("p (t i) -> p t i", t=4)
                        self.ts("dve", c1v, hgx[:, :, 2:2 + n], cw(2), vecs[:, V_CB + f:V_CB + f + 1], ALU.mult, ALU.add, r=["hgx", "vecs"], w=["c1"])
                        self.stt("dve", c1v, hgx[:, :, 1:1 + n], cw(1), c1v, ALU.mult, ALU.add, r=["hgx", "vecs", "c1"], w=["c1"])
                        self.stt("dve", c1v, hgx[:, :, 0:n], cw(0), c1v, ALU.mult, ALU.add, r=["hgx", "vecs", "c1"], w=["c1"])
                        self.act(gl[:, 0:N], c1[:, 0:N], AF.Gelu_apprx_tanh, r=["c1"], w=["gl"])
                        self.tt("dve", hT[:, f, 0:N], gl[:, 0:N], pss[1][:, 0:N], ALU.mult, r=["gl", "ps1"], w=["hT"])
                        if sq_ is PR and g == 3:
                            self.copy("dve", convo[:, f, :], hgx[:, 3, n:n + 2], r=["hgx"], w=["convo"])
                        if sq_ is SM:
                            self.copy("dve", convos[:, f, :, :], hgx[:, :, n:n + 2], r=["hgx"], w=["convos"])
                    for c in range(8):
                        i = c % 2
                        P.dma("pool", wdr[i][:], wCd[c], w=[f"wdr{i}"])
                        pb, pk = pss[c % 2], f"ps{c % 2}"
                        for f in range(NFC):
                            self.mm(pb[:, 0:N], wdr[i][:, f, :], hT[:, f, 0:N], f == 0, f == NFC - 1, r=[f"wdr{i}", "hT"], w=[pk])
                        self.stt("dve", x1f[:, c, 0:N], x1f[:, c, 0:N], ALPHA, pb[:, 0:N], ALU.mult, ALU.add, r=[xk[c], pk], w=[xk[c]])
                    mean, rstd = self.stats([x1f[:, c, 0:N] for c in range(8)], xk, N, ones, "l2")
                    for c in range(8):
                        self.tt("dve", x1f[:, c, 0:N], x1f[:, c, 0:N], mean[:, 0:N], ALU.subtract, r=[xk[c], "st_mean"], w=[xk[c]])
                        self.tt("pool", x1f[:, c, 0:N], x1f[:, c, 0:N], rstd[:, 0:N], ALU.mult, r=[xk[c], "st_rstd"], w=[xk[c]])
                        self.act(x1f[:, c, 0:N], x1f[:, c, 0:N], AF.Identity, r=[xk[c], "vecs"], w=[xk[c]],
                                 scale=vecs[:, V_L2G + c:V_L2G + c + 1], bias=vecs[:, V_L2B + c:V_L2B + c + 1])
                    if not last:
                        P.dma("sp", XT[:, :, gcols], x1f[:, :, 0:N], r=xk, w=["XT"])
                    else:
                        oy = L["o_y"] if sq_ is PR else L["o_ys"]
                        for t in range(4):
                            yb = ytok[yi % 2]
                            yk = f"ytok{yi % 2}"
                            yi += 1
                            for c in range(8):
                                bank = (2, 3, 4, 7)[c % 4]
                                self.tr(pss[bank][0:n, 0:128], x1f[:, c, t * n:(t + 1) * n], identF[:], r=[xk[c], "consts"], w=[f"ps{bank}"])
                                self.copy(self.alt(), yb[0:n, c * 128:(c + 1) * 128], pss[bank][0:n, 0:128], r=[f"ps{bank}"], w=[yk])
                            r0 = (g * 4 + t) * n
                            P.dma("sp", oy[r0:r0 + n, :], yb[0:n, :], r=[yk])
            P.dma("sp", L["o_conv"][l], convo[:], r=["convo"])
            P.dma("sp", L["o_convs"][l], convos[:], r=["convos"])
            P.barrier()
SPL = dict(ua=0, va=256, qb=512, kb=896, vb=960, qib=1024, kib=1280, wib=1312, qc=1320, kc=1704, vc=2088, gc=2472)
LOG_G = np.log(1.0 - 2.0 ** (-5.0 - np.arange(6, dtype=np.float64)))


def _chunked(w):
    return np.ascontiguousarray(w.reshape(8, 128, -1).transpose(1, 0, 2))


def _swap_heads(w):
    m = w.reshape(w.shape[0], -1, 2, 32)
    return np.ascontiguousarray(m[:, :, ::-1, :]).reshape(w.shape[0], -1)


def _rot_tables(pos):
    half = 32
    freqs = (10000.0 ** (-np.arange(half, dtype=np.float32) / half)).astype(np.float32)
    ang = pos.astype(np.float32)[None, :] * freqs[:, None]
    cos = np.cos(ang)
    sin = np.sin(ang)
    cosT = np.concatenate([cos, cos, cos, cos], 0).astype(np.float32)
    sinT = np.concatenate([-sin, sin, -sin, sin], 0).astype(np.float32)
    return cosT, sinT


def _pairlay(per_head):
    t = np.zeros((128, 192), np.float64)
    for h in range(6):
        t[(h % 2) * 64:(h % 2) * 64 + 64, (h // 2) * 64:(h // 2) * 64 + 64] = per_head[h]
    return t


def _const_tables():
    c = {}
    c["identf"] = np.eye(128, dtype=np.float32)
    jj = np.arange(128, dtype=np.float64)
    kd0 = np.exp((127 - jj)[:, None] * LOG_G[None, :]) * 0.125
    kd1 = np.zeros((128, 6))
    kd1[:16] = np.exp((15 - jj[:16])[:, None] * LOG_G[None, :]) * 0.125
    c["kdec"] = np.stack([np.repeat(kd0, 64, 1), np.repeat(kd1, 64, 1)]).astype(np.float32)
    diff = jj[None, :] - jj[:, None]
    dt = np.where(diff[:, None, :] >= 0, np.exp(np.maximum(diff, 0)[:, None, :] * LOG_G[None, :, None]), 0.0) * 0.125
    c["dtT"] = dt.astype(np.float32)
    lane_h = lambda pair: np.array([2 * pair + (p // 64) for p in range(128)])
    qp = np.zeros((128, 3, 512))
    qs = np.zeros((128, 3, 64))
    for pair in range(3):
        lg = LOG_G[lane_h(pair)]
        qp[:, pair, :] = np.exp(((np.arange(512) % 128) + 1)[None, :] * lg[:, None])
        qs[:, pair, :] = np.exp(((np.arange(64) % 16) + 1)[None, :] * lg[:, None])
    c["qdecP"], c["qdecS"] = qp.astype(np.float32), qs.astype(np.float32)
    c["sel"] = (np.arange(128)[:, None] == (np.arange(768) % 128)[None, :]).astype(np.float32)
    c["sel16"] = (np.arange(16)[:, None] == (np.arange(96) % 16)[None, :]).astype(np.float32)
    c["cm"] = np.broadcast_to(((np.arange(128) // 16)[None, None, :] == np.arange(8)[None, :, None]), (128, 8, 128)).astype(np.float32).copy()
    c["e16"] = ((np.arange(128) % 16)[:, None] == np.arange(16)[None, :]).astype(np.float32)
    bd = np.zeros((128, 128), np.float32)
    bd[:64, :64] = 1 / 64
    bd[64:, 64:] = 1 / 64
    c["bd64"] = bd
    c["onesd"] = np.full((128, 128), 1 / 1024, np.float32)
    c["umask"] = (np.arange(128)[None, :] >= np.arange(128)[:, None]).astype(np.float32)
    c["cdecS"] = _pairlay(np.exp(16 * LOG_G)).astype(np.float32)
    return c


def make_inputs(inp):
    maps = []
    cst = _const_tables()
    per_layer = []
    for l in range(DEPTH):
        W = inp["w_in"][l]
        sl = lambda k, n: W[:, SPL[k]:SPL[k] + n]
        d = {}
        d["wA"] = _chunked(np.concatenate([sl("vc", 384), sl("kb", 64), sl("vb", 64), sl("kib", 32), sl("wib", 8),
                                           sl("kb", 64), sl("kib", 32), sl("kc", 384), _swap_heads(sl("kc", 384))], 1))
        wb = np.concatenate([sl("ua", 256), sl("va", 256), sl("qc", 384), _swap_heads(sl("qc", 384)), sl("kc", 384),
                             _swap_heads(sl("kc", 384)), sl("gc", 384), sl("vc", 384), sl("qb", 384), sl("qib", 256), inp["w_out"][l]], 1)
        d["wB"] = np.ascontiguousarray(_chunked(wb).reshape(128, 8, 35, 128).transpose(2, 0, 1, 3))
        d["wCg"] = np.ascontiguousarray(_chunked(inp["w_gate"][l]).reshape(128, 8, NFC, 128).transpose(2, 0, 1, 3))
        d["wCu"] = np.ascontiguousarray(_chunked(inp["w_up"][l]).reshape(128, 8, NFC, 128).transpose(2, 0, 1, 3))
        wd = inp["w_down"][l].reshape(NFC, 128, 8, 128)
        d["wCd"] = np.ascontiguousarray(wd.transpose(2, 1, 0, 3))
        fm = lambda v, k: v.reshape(k, 128).T
        cw = inp["conv_w"][l].reshape(3, NFC, 128).transpose(2, 1, 0).reshape(128, NFC * 3)
        d["vec"] = np.ascontiguousarray(np.concatenate([
            fm(inp["a_ln_g"][l], 2), fm(inp["a_ln_b"][l], 2), fm(inp["c_gn_g"][l], 3), fm(inp["ln1_g"][l], 8), fm(inp["ln1_b"][l], 8),
            fm(inp["ln2_g"][l], 8), fm(inp["ln2_b"][l], 8), cw, fm(inp["conv_b"][l], NFC)], 1).astype(np.float32))
        bs = inp["a_bs"][l]
        g_of = np.array([[ch * 2 + p // 64 for ch in range(2)] for p in range(128)])
        d["bsP"] = np.ascontiguousarray(np.tile(bs[g_of], (1, 1, 4)).astype(np.float32))
        d["bsS"] = np.ascontiguousarray(np.tile(bs[g_of][:, :, :16], (1, 1, 4)).astype(np.float32))
        d["wmT"] = np.ascontiguousarray(inp["a_ws"][l].transpose(2, 0, 1))
        per_layer.append(d)
    for c in range(8):
        b, j = c // 4, c % 4
        m = dict(cst)
        tiles = [4 * mm + j for mm in range(NT)]
        pos = np.concatenate([np.arange(t * 128, (t + 1) * 128) for t in tiles])
        m["xp"] = np.ascontiguousarray(inp["x_prompt"][b][pos])
        m["xs"] = np.ascontiguousarray(inp["x_sample"][4 * c:4 * c + 4].reshape(NS, D))
        pos_all = np.concatenate([pos, np.tile(2048 + np.arange(16), 4)])
        m["cosT"], m["sinT"] = _rot_tables(pos_all)
        dm = np.zeros((128, 512), np.float32)
        for jp in range(4):
            if jp > j:
                dm[:, jp * 128:(jp + 1) * 128] = NEG
            elif jp == j:
                dm[0:64, jp * 128 + 64:(jp + 1) * 128] = NEG
        m["dmask"] = dm
        rc = np.zeros((128, 9, 192))
        rc[:, 0] = _pairlay(np.exp(128 * j * LOG_G))
        for jp in range(3):
            if jp < j:
                rc[:, 1 + jp] = _pairlay(np.exp(128 * (j - 1 - jp) * LOG_G))
        rc[:, 4] = _pairlay(np.exp(512 * LOG_G))
        for jp in range(4):
            rc[:, 5 + jp] = _pairlay(np.exp(128 * (3 - jp) * LOG_G))
        m["rcoef"] = rc.astype(np.float32)
        s5 = np.zeros((128, 5), np.float32)
        s5[:, (j - 1) if j >= 1 else 4] = 1.0
        m["sel5"] = s5
        for l in range(DEPTH):
            for k, v in per_layer[l].items():
                m[f"{k}{l}"] = v
            m[f"ck{l}"] = np.ascontiguousarray(inp["cache_b_k"][l, 4 * c:4 * c + 4])
            m[f"cv{l}"] = np.ascontiguousarray(inp["cache_b_v"][l, 4 * c:4 * c + 4])
            m[f"cki{l}"] = np.ascontiguousarray(inp["cache_b_kidx"][l, 4 * c:4 * c + 4])
            sr = inp["state_ret"][l, 4 * c:4 * c + 4]
            t = np.zeros((128, 4, 192), np.float32)
            for h in range(6):
                t[(h % 2) * 64:(h % 2) * 64 + 64, :, (h // 2) * 64:(h // 2) * 64 + 64] = sr[:, h].transpose(1, 0, 2)
            m[f"stS{l}"] = t
            cv = inp["state_ffn_conv"][l, 4 * c:4 * c + 4]
            m[f"cvp{l}"] = np.ascontiguousarray(cv.reshape(4, 2, NFC, 128).transpose(3, 2, 0, 1))
        maps.append(m)
    return maps


def _unpair(t):
    out = np.zeros(t.shape[:-2] + (6, 64, 64), np.float32)
    for h in range(6):
        out[..., h, :, :] = t[..., (h % 2) * 64:(h % 2) * 64 + 64, (h // 2) * 64:(h // 2) * 64 + 64]
    return out


def assemble(R):
    y = np.zeros((2, 8192, D), np.float32)
    ys = np.zeros((32, 16, D), np.float32)
    kp = np.zeros((DEPTH, 2, 8192, 64), np.float32)
    vp = np.zeros((DEPTH, 2, 8192, 64), np.float32)
    kip = np.zeros((DEPTH, 2, 8192, 32), np.float32)
    rp = np.zeros((DEPTH, 2, 6, 64, 64), np.float32)
    cp = np.zeros((DEPTH, 2, 2, DFF), np.float32)
    ks = np.zeros((DEPTH, 32, 16, 64), np.float32)
    vs = np.zeros((DEPTH, 32, 16, 64), np.float32)
    kis = np.zeros((DEPTH, 32, 16, 32), np.float32)
    rs = np.zeros((DEPTH, 32, 6, 64, 64), np.float32)
    cs = np.zeros((DEPTH, 32, 2, DFF), np.float32)
    avs = np.zeros((DEPTH, 32, 16, 256), np.float32)
    for c in range(8):
        b, j = c // 4, c % 4
        r = R[c]
        pos = np.concatenate([np.arange((4 * mm + j) * 128, (4 * mm + j + 1) * 128) for mm in range(NT)])
        y[b, pos] = r["o_y"]
        ys[4 * c:4 * c + 4] = r["o_ys"].reshape(4, 16, D)
        kp[:, b, pos] = r["o_k"]
        vp[:, b, pos] = r["o_v"]
        kip[:, b, pos] = r["o_ki"]
        if j == 0:
            rp[:, b] = _unpair(r["o_ret"])
        if j == 3:
            cp[:, b] = r["o_conv"].transpose(0, 3, 2, 1).reshape(DEPTH, 2, DFF)
        ks[:, 4 * c:4 * c + 4] = r["o_ks"].reshape(DEPTH, 4, 16, 64)
        vs[:, 4 * c:4 * c + 4] = r["o_vs"].reshape(DEPTH, 4, 16, 64)
        kis[:, 4 * c:4 * c + 4] = r["o_kis"].reshape(DEPTH, 4, 16, 32)
        rs[:, 4 * c:4 * c + 4] = _unpair(r["o_rets"].transpose(0, 2, 1, 3))
        cs[:, 4 * c:4 * c + 4] = r["o_convs"].transpose(0, 3, 4, 2, 1).reshape(DEPTH, 4, 2, DFF)
        avs[:, 4 * c:4 * c + 4] = r["o_avs"].reshape(DEPTH, 4, 16, 256)
    return (y, ys, kp, vp, kip, rp, cp, ks, vs, kis, rs, cs, avs)


_CACHE = {}


def kernel(**inputs):
    inp = {k: np.asarray(v, dtype=np.float32) for k, v in inputs.items()}
    stage = _CACHE.get("stage", 99)
    if "nc" not in _CACHE:
        bld = Builder(stage)
        _CACHE["nc"] = bld.build()
        _CACHE["bld"] = bld
    nc = _CACHE["nc"]
    maps = make_inputs(inp)
    res = run_bass_kernel_spmd(nc, maps, core_ids=list(range(8)))
    _CACHE["raw"] = res.results
    return assemble(res.results)
```
